# Optimizing a Trainium2 kernel written in Bass

```python
import math
import jax, jax.numpy as jnp
from jax import lax
import numpy as np

D_MODEL = 2048
BATCH = 2
SEQ = 16384
DEPTH = 1
DEC_BATCH = 32
DEC_SEQ = 64
PAST_LEN = 2048

CHUNK = 64
Q_BLOCK = 128
N_HEADS_A = 8
HEAD_DIM_A = 64
VAL_DIM_A = 2 * HEAD_DIM_A
WIDTH_A = N_HEADS_A * VAL_DIM_A
ROT_DIM = HEAD_DIM_A // 4
ROPE_THETA = 500000.0
HEAD_DIM_B = 64
WIDTH_B = D_MODEL - WIDTH_A
N_HEADS_B = WIDTH_B // HEAD_DIM_B
LORA_W = 64
LORA_A = 64
LORA_G = 64
ATTN_COLS = 3 * WIDTH_A
RWKV_COLS = 3 * WIDTH_B + LORA_W + LORA_A + LORA_G
IN_COLS = ATTN_COLS + RWKV_COLS
D_FF = -(-8 * D_MODEL // (3 * 256)) * 256
RMS_EPS = 1e-6
GN_EPS = 64e-5

kernel_name = "hybrid_diffattn_rwkv7_stream_step"


def rmsnorm(x, g):
    xf = x.astype(jnp.float32)
    y = xf * lax.rsqrt(jnp.mean(xf * xf, axis=-1, keepdims=True) + RMS_EPS)
    return (y * g.astype(jnp.float32)).astype(x.dtype)


def rope_partial(x, pos):
    half = ROT_DIM // 2
    inv = jnp.float32(ROPE_THETA) ** (-jnp.arange(0, ROT_DIM, 2, dtype=jnp.float32) / ROT_DIM)
    ang = pos.astype(jnp.float32)[:, None] * inv
    cos = jnp.cos(ang)[:, None, None, :]
    sin = jnp.sin(ang)[:, None, None, :]
    xr = x[..., :ROT_DIM].astype(jnp.float32)
    x1, x2 = xr[..., :half], xr[..., half:]
    rot = jnp.concatenate([x1 * cos - x2 * sin, x2 * cos + x1 * sin], axis=-1)
    return jnp.concatenate([rot.astype(x.dtype), x[..., ROT_DIM:]], axis=-1)


def diff_attend(q, k, v, q_chunk, k_chunk, lam):
    s = jnp.einsum('bqhcd,bkhcd->bchqk', q, k).astype(jnp.float32) * (HEAD_DIM_A ** -0.5)
    mask = k_chunk[None, :] <= q_chunk[:, None]
    s = jnp.where(mask, s, -jnp.inf)
    p = jax.nn.softmax(s, axis=-1)
    a = p[:, 0] - lam * p[:, 1]
    return jnp.einsum('bhqk,bkhe->bqhe', a.astype(v.dtype), v)


def rwkv_scan(S0, r, w, k, v, a, b):
    def step(S, inp):
        r_t, w_t, k_t, v_t, a_t, b_t = inp
        sa = jnp.einsum('bhvk,bhk->bhv', S, a_t)
        S = S * w_t[:, :, None, :] + sa[..., None] * b_t[:, :, None, :] + v_t[..., None] * k_t[:, :, None, :]
        return S, jnp.einsum('bhvk,bhk->bhv', S, r_t)
    xs = tuple(jnp.moveaxis(t.astype(jnp.float32), 1, 0) for t in (r, w, k, v, a, b))
    S, y = lax.scan(step, S0.astype(jnp.float32), xs)
    return S, jnp.moveaxis(y, 0, 1)


def rwkv_mix(p, shift0, S0, mu, w0, w2, a0, a2, g2, k_k, k_a, r_k, lnx_g, lnx_b):
    B, T, _ = p.shape
    prev = jnp.concatenate([shift0.astype(p.dtype), p[:, :-1]], axis=1)
    xs = p + (prev - p) * mu
    r, k, v, wd, ad, gd = jnp.split(
        xs, [WIDTH_B, 2 * WIDTH_B, 3 * WIDTH_B, 3 * WIDTH_B + LORA_W, 3 * WIDTH_B + LORA_W + LORA_A], axis=-1)
    w = -jax.nn.softplus(-(w0 + jnp.tanh(wd) @ w2)) - 0.5
    decay = jnp.exp(-jnp.exp(w.astype(jnp.float32)))
    a = jax.nn.sigmoid(a0 + ad @ a2)
    g = jax.nn.sigmoid(gd) @ g2
    kk = (k * k_k).astype(jnp.float32).reshape(B, T, N_HEADS_B, HEAD_DIM_B)
    kk = kk / jnp.maximum(jnp.sqrt(jnp.sum(kk * kk, axis=-1, keepdims=True)), 1e-12)
    k = k * (1 + (a - 1) * k_a)
    hd = lambda t: t.reshape(B, T, N_HEADS_B, HEAD_DIM_B)
    r_h, k_h, v_h, a_h, d_h = hd(r), hd(k), hd(v), hd(a), hd(decay)
    S, y = rwkv_scan(S0, r_h, d_h, k_h, v_h, -kk, kk * a_h.astype(jnp.float32))
    mean = jnp.mean(y, axis=-1, keepdims=True)
    var = jnp.mean(jnp.square(y - mean), axis=-1, keepdims=True)
    yn = ((y - mean) * lax.rsqrt(var + GN_EPS)).reshape(B, T, WIDTH_B) * lnx_g + lnx_b
    bonus = jnp.sum((r_h * k_h * r_k).astype(jnp.float32), axis=-1, keepdims=True) * v_h.astype(jnp.float32)
    out = (yn + bonus.reshape(B, T, WIDTH_B)) * g.astype(jnp.float32)
    return out.astype(p.dtype), S, p[:, -1:]


def layer(x, pos, past_k, past_v, past_pos, S0, shift0, lam_init,
          norm1_g, w_in, q_norm_g, k_norm_g, lambda_q1, lambda_k1, lambda_q2, lambda_k2, subln_g,
          mu_rwkv, w0, w2, a0, a2, g2, k_k, k_a, r_k, lnx_g, lnx_b,
          w_out, norm2_g, w_gate, w_up, w_down):
    B, T, _ = x.shape
    xn = rmsnorm(x, norm1_g)
    P = xn @ w_in
    p_attn, p_rwkv = P[..., :ATTN_COLS], P[..., ATTN_COLS:]
    q, k, v = jnp.split(p_attn, 3, axis=-1)
    q = rope_partial(rmsnorm(q.reshape(B, T, N_HEADS_A, 2, HEAD_DIM_A), q_norm_g), pos)
    k = rope_partial(rmsnorm(k.reshape(B, T, N_HEADS_A, 2, HEAD_DIM_A), k_norm_g), pos)
    k_new = k.reshape(B, T, N_HEADS_A, 2 * HEAD_DIM_A)
    v_new = v.reshape(B, T, N_HEADS_A, VAL_DIM_A)
    k_all = jnp.concatenate([past_k.astype(x.dtype), k_new], axis=1)
    k_all = k_all.reshape(B, k_all.shape[1], N_HEADS_A, 2, HEAD_DIM_A)
    v_all = jnp.concatenate([past_v.astype(x.dtype), v_new], axis=1)
    k_chunk = jnp.concatenate([past_pos, pos]) // CHUNK
    q_chunk = pos // CHUNK
    lam = (jnp.exp(jnp.sum(lambda_q1.astype(jnp.float32) * lambda_k1.astype(jnp.float32)))
           - jnp.exp(jnp.sum(lambda_q2.astype(jnp.float32) * lambda_k2.astype(jnp.float32))) + lam_init)
    if T > Q_BLOCK and T % Q_BLOCK == 0:
        nb = T // Q_BLOCK
        qb = q.reshape(B, nb, Q_BLOCK, N_HEADS_A, 2, HEAD_DIM_A).swapaxes(0, 1)
        qcb = q_chunk.reshape(nb, Q_BLOCK)
        o = lax.map(lambda args: diff_attend(args[0], k_all, v_all, args[1], k_chunk, lam), (qb, qcb))
        o = o.swapaxes(0, 1).reshape(B, T, N_HEADS_A, VAL_DIM_A)
    else:
        o = diff_attend(q, k_all, v_all, q_chunk, k_chunk, lam)
    attn_out = (rmsnorm(o, subln_g) * (1 - lam_init)).reshape(B, T, WIDTH_A)
    rwkv_out, S, shift = rwkv_mix(p_rwkv, shift0, S0, mu_rwkv, w0, w2, a0, a2, g2,
                                  k_k, k_a, r_k, lnx_g, lnx_b)
    h = x + jnp.concatenate([attn_out, rwkv_out], axis=-1) @ w_out
    hn = rmsnorm(h, norm2_g)
    y = h + (jax.nn.silu(hn @ w_gate) * (hn @ w_up)) @ w_down
    return y, k_new, v_new, S.astype(x.dtype), shift


def setup_inputs(seed: int = 0) -> dict:
    key = jax.random.key(seed)
    ks = jax.random.split(key, 40)
    n = lambda i, shape, s=1.0: jax.random.normal(ks[i], shape, jnp.float32) * s
    L = DEPTH
    return {
        "x_prompt": n(0, (BATCH, SEQ, D_MODEL)),
        "x_sample": n(1, (DEC_BATCH, DEC_SEQ, D_MODEL)),
        "cache_attn_k": n(2, (L, DEC_BATCH, PAST_LEN, N_HEADS_A, 2 * HEAD_DIM_A)),
        "cache_attn_v": n(3, (L, DEC_BATCH, PAST_LEN, N_HEADS_A, VAL_DIM_A)),
        "state_rwkv": n(4, (L, DEC_BATCH, N_HEADS_B, HEAD_DIM_B, HEAD_DIM_B), 0.3),
        "state_rwkv_shift": n(5, (L, DEC_BATCH, 1, RWKV_COLS)),
        "norm1_g": 1.0 + n(6, (L, D_MODEL), 0.05),
        "w_in": n(7, (L, D_MODEL, IN_COLS), D_MODEL ** -0.5),
        "q_norm_g": 1.0 + n(8, (L, 2, HEAD_DIM_A), 0.05),
        "k_norm_g": 1.0 + n(9, (L, 2, HEAD_DIM_A), 0.05),
        "lambda_q1": n(10, (L, HEAD_DIM_A), 0.1),
        "lambda_k1": n(11, (L, HEAD_DIM_A), 0.1),
        "lambda_q2": n(12, (L, HEAD_DIM_A), 0.1),
        "lambda_k2": n(13, (L, HEAD_DIM_A), 0.1),
        "subln_g": 1.0 + n(14, (L, VAL_DIM_A), 0.05),
        "mu_rwkv": jax.random.uniform(ks[15], (L, RWKV_COLS), jnp.float32),
        "w0": jax.random.uniform(ks[16], (L, WIDTH_B), jnp.float32, minval=-6.0, maxval=-1.0),
        "w2": n(17, (L, LORA_W, WIDTH_B), 0.5 * LORA_W ** -0.5),
        "a0": n(18, (L, WIDTH_B), 0.1),
        "a2": n(19, (L, LORA_A, WIDTH_B), 0.5 * LORA_A ** -0.5),
        "g2": n(20, (L, LORA_G, WIDTH_B), LORA_G ** -0.5),
        "k_k": 0.85 + n(21, (L, WIDTH_B), 0.05),
        "k_a": 1.0 + n(22, (L, WIDTH_B), 0.05),
        "r_k": n(23, (L, N_HEADS_B, HEAD_DIM_B), 0.1),
        "lnx_g": 1.0 + n(24, (L, WIDTH_B), 0.05),
        "lnx_b": n(25, (L, WIDTH_B), 0.01),
        "w_out": n(26, (L, D_MODEL, D_MODEL), D_MODEL ** -0.5),
        "norm2_g": 1.0 + n(27, (L, D_MODEL), 0.05),
        "w_gate": n(28, (L, D_MODEL, D_FF), D_MODEL ** -0.5),
        "w_up": n(29, (L, D_MODEL, D_FF), D_MODEL ** -0.5),
        "w_down": n(30, (L, D_FF, D_MODEL), D_FF ** -0.5),
    }


def reference(x_prompt, x_sample, cache_attn_k, cache_attn_v, state_rwkv, state_rwkv_shift,
              norm1_g, w_in, q_norm_g, k_norm_g, lambda_q1, lambda_k1, lambda_q2, lambda_k2, subln_g,
              mu_rwkv, w0, w2, a0, a2, g2, k_k, k_a, r_k, lnx_g, lnx_b,
              w_out, norm2_g, w_gate, w_up, w_down):
    Bp, Tp, _ = x_prompt.shape
    Bs, Ts, _ = x_sample.shape
    past_len = cache_attn_k.shape[2]
    dt = x_prompt.dtype
    pos_p = jnp.arange(Tp, dtype=jnp.int32)
    pos_s = past_len + jnp.arange(Ts, dtype=jnp.int32)
    past_pos_s = jnp.arange(past_len, dtype=jnp.int32)
    empty_k = jnp.zeros((Bp, 0, N_HEADS_A, 2 * HEAD_DIM_A), dt)
    empty_v = jnp.zeros((Bp, 0, N_HEADS_A, VAL_DIM_A), dt)
    empty_pos = jnp.zeros((0,), jnp.int32)
    S0_p = jnp.zeros((Bp, N_HEADS_B, HEAD_DIM_B, HEAD_DIM_B), jnp.float32)
    shift0_p = jnp.zeros((Bp, 1, RWKV_COLS), dt)
    xp, xs = x_prompt, x_sample
    kp_l, vp_l, Sp_l, shp_l, ks_l, vs_l, Ss_l, shs_l = [], [], [], [], [], [], [], []
    for l in range(DEPTH):
        lam_init = 0.8 - 0.6 * math.exp(-0.3 * l)
        lw = (norm1_g[l], w_in[l], q_norm_g[l], k_norm_g[l], lambda_q1[l], lambda_k1[l],
              lambda_q2[l], lambda_k2[l], subln_g[l], mu_rwkv[l], w0[l], w2[l], a0[l], a2[l], g2[l],
              k_k[l], k_a[l], r_k[l], lnx_g[l], lnx_b[l], w_out[l], norm2_g[l], w_gate[l], w_up[l], w_down[l])
        xp, kp, vp, Sp, shp = layer(xp, pos_p, empty_k, empty_v, empty_pos, S0_p, shift0_p, lam_init, *lw)
        xs, ks, vs, Ss, shs = layer(xs, pos_s, cache_attn_k[l], cache_attn_v[l], past_pos_s,
                                    state_rwkv[l], state_rwkv_shift[l], lam_init, *lw)
        kp_l.append(kp); vp_l.append(vp); Sp_l.append(Sp); shp_l.append(shp)
        ks_l.append(ks); vs_l.append(vs); Ss_l.append(Ss); shs_l.append(shs)
    return (xp, xs,
            jnp.stack(kp_l), jnp.stack(vp_l), jnp.stack(Sp_l), jnp.stack(shp_l),
            jnp.stack(ks_l), jnp.stack(vs_l), jnp.stack(Ss_l), jnp.stack(shs_l))
```

```python
import math
import contextlib
import numpy as np
import concourse.bass as bass
import concourse.mybir as mybir
from concourse.bass_utils import run_bass_kernel_spmd

F32 = mybir.dt.float32
BF16 = mybir.dt.bfloat16
AF = mybir.ActivationFunctionType
ALU = mybir.AluOpType
AX = mybir.AxisListType

D = 2048
NCH = 16
UC = 960
DFF = 5632
NFF = 44
RMS_EPS = 1e-6
GN_EPS = 64e-5
LAM_INIT = 0.2
NPAR = 1472
ROPE_THETA = 500000.0


class _Stop(Exception):
    pass


def ckpt(name):
    import os
    if os.environ.get("KSTOP") == name:
        raise _Stop()


class Buf:
    __slots__ = ("name", "w", "r", "excl")

    def __init__(self, name, excl=False):
        self.name = name
        self.w = {}
        self.r = {}
        self.excl = excl


class Sched:
    LIMIT = 30000

    def __init__(self, nc, es, n_dma_sems=40):
        self.nc = nc
        self.es = es
        self.eng = {"pe": nc.tensor, "act": nc.scalar, "dve": nc.vector, "pool": nc.gpsimd, "sp": nc.sync}
        self.esem = {}
        self.cnt = {}
        self.seen = {e: {} for e in self.eng}
        self.uncommitted = {e: False for e in self.eng}
        self.nsem = 0
        for e in ("pe", "act", "dve", "pool"):
            self._new_esem(e)
        self.rings = {}
        for q, nq_ in (("sp", 30), ("pool", 12), ("cc", 1)):
            self.rings[q] = dict(sem=[self._sem("dma_%s%d" % (q, i)) for i in range(nq_)], cnt=[0] * nq_, nxt=0)
        self.nops = 0

    def _sem(self, name):
        self.nsem += 1
        return self.es.enter_context(self.nc.semaphore(name))

    def _new_esem(self, e):
        self.esem[e] = self._sem("e_%s_%d" % (e, self.nsem))
        self.cnt[e] = 0

    def _collect(self, reads, writes, pwrites):
        deps = {}

        def add(d):
            for k, (s, v) in d.items():
                if k not in deps or deps[k][1] < v:
                    deps[k] = (s, v)
        for b in reads:
            add(b.w)
            if b.excl:
                add(b.r)
        for b in writes:
            add(b.w)
            add(b.r)
        for b in pwrites:
            add(b.r)
        return deps

    def _emit_waits(self, e, deps):
        eng = self.eng[e]
        seen = self.seen[e]
        for k, (s, v) in deps.items():
            if seen.get(k, 0) >= v:
                continue
            if e in self.esem and s is self.esem[e] and v > self.cnt[e]:
                continue
            for e2 in self.esem:
                if s is self.esem[e2] and v > self.cnt[e2]:
                    raise RuntimeError("wait on uncommitted event of %s from %s" % (e2, e))
            eng.wait_ge(s, v)
            seen[k] = v

    def _record(self, ev, reads, writes, pwrites):
        k = id(ev[0])
        for b in reads:
            if k not in b.r or b.r[k][1] < ev[1]:
                b.r[k] = ev
        for b in writes:
            b.w = {k: ev}
            b.r = {}
        for b in pwrites:
            if k not in b.w or b.w[k][1] < ev[1]:
                b.w[k] = ev

    def op(self, e, fn, reads=(), writes=(), inc=True, pwrites=()):
        self.nops += 1
        if self.cnt[e] >= self.LIMIT and not self.uncommitted[e]:
            self._new_esem(e)
        deps = self._collect(reads, writes, pwrites)
        self._emit_waits(e, deps)
        ins = fn(self.eng[e])
        ev = (self.esem[e], self.cnt[e] + 1)
        if inc:
            self.cnt[e] += 1
            ins.then_inc(self.esem[e], 1)
            self.uncommitted[e] = False
        else:
            self.uncommitted[e] = True
        self._record(ev, reads, writes, pwrites)
        return ins

    def _ring_next(self, q):
        r = self.rings[q]
        i = r["nxt"]
        r["nxt"] = (i + 1) % len(r["sem"])
        return r, i

    def dma(self, q, out, in_, reads=(), writes=(), pwrites=(), **kw):
        self.nops += 1
        r, i = self._ring_next(q)
        s = r["sem"][i]
        deps = self._collect(reads, writes, pwrites)
        if r["cnt"][i] > 0:
            deps[id(s)] = (s, r["cnt"][i])
        self._emit_waits(q, deps)
        ins = self.eng[q].dma_start(out=out, in_=in_, **kw)
        r["cnt"][i] += 16
        ins.then_inc(s, 16)
        ev = (s, r["cnt"][i])
        self._record(ev, reads, writes, pwrites)
        return ev

    def custom(self, q, fn, reads=(), writes=(), pwrites=()):
        r, i = self._ring_next("cc")
        s = r["sem"][i]
        deps = self._collect(reads, writes, pwrites)
        if r["cnt"][i] > 0:
            deps[id(s)] = (s, r["cnt"][i])
        self._emit_waits(q, deps)
        ins = fn(self.eng[q])
        r["cnt"][i] += 1
        ins.then_inc(s, 1)
        ev = (s, r["cnt"][i])
        self._record(ev, reads, writes, pwrites)
        return ev

    def all_dma(self):
        for r in self.rings.values():
            for s, c in zip(r["sem"], r["cnt"]):
                if c > 0:
                    yield s, c

    def barrier(self):
        for e in self.eng:
            for e2 in self.esem:
                if self.cnt[e2] > 0 and self.seen[e].get(id(self.esem[e2]), 0) < self.cnt[e2]:
                    self.eng[e].wait_ge(self.esem[e2], self.cnt[e2])
                    self.seen[e][id(self.esem[e2])] = self.cnt[e2]
            for s, c in self.all_dma():
                if self.seen[e].get(id(s), 0) < c:
                    self.eng[e].wait_ge(s, c)
                    self.seen[e][id(s)] = c

    def wait_all(self, q, bufs):
        deps = self._collect(bufs, bufs, ())
        self._emit_waits(q, deps)


class Dims:
    def __init__(self, T, NB, PAST):
        self.T = T
        self.NB = NB
        self.PAST = PAST
        self.NT = T // 128
        self.TR = T // 4
        self.NP = PAST // 128


def build_nc(dm):
    T, NB, PAST, NT, TR, NP = dm.T, dm.NB, dm.PAST, dm.NT, dm.TR, dm.NP
    nc = bass.Bass("TRN2", target_bir_lowering=False)

    def din(name, shape, dt=F32):
        return nc.dram_tensor(name, list(shape), dt, kind="ExternalInput").ap()

    def dout(name, shape, dt=F32):
        return nc.dram_tensor(name, list(shape), dt, kind="ExternalOutput").ap()

    def dint(name, shape, dt):
        return nc.dram_tensor(name, list(shape), dt, kind="Internal").ap()

    xb = din("xb", [T, D])
    xr = din("xr", [TR, D])
    xsm = din("xsm", [NB * 64, D])
    ck = din("ck", [NB, PAST, 8, 128])
    cv = din("cv", [NB, PAST, 8, 128])
    st0 = din("st0", [NB, 16, 64, 64])
    sh0 = din("sh0", [NB, 8, 576])
    wu = din("wu", [8, D, UC])
    wmy = din("wmy", [2, D, UC])
    up = din("up", [8, NPAR])
    upmy = din("upmy", [2, NPAR])
    lw = din("lw", [8, 128, 128])
    lwmy = din("lwmy", [2, 128, 128])
    lg2 = din("lg2", [8, 64, 128])
    lg2my = din("lg2my", [2, 64, 128])
    g1T = din("g1T", [128, NCH])
    g2T = din("g2T", [128, NCH])
    qkg = din("qkg", [1, 256])
    lamv = din("lamv", [1, 256])
    subg = din("subg", [1, 128])
    sel = din("sel", [128, 4])
    w_out = din("w_out", [D, D])
    w_gate = din("w_gate", [D, DFF])
    w_up = din("w_up", [D, DFF])
    w_down = din("w_down", [DFF, D])
    csp = din("csp", [T, 16])
    css = din("css", [64, 16])
    cid = din("cid", [128, 128])
    cut = din("cut", [128, 128])
    cs1 = din("cs1", [128, 128])
    cc128 = din("cc128", [128, 128])
    cc64 = din("cc64", [64, 64])
    cgm = din("cgm", [128, 512])
    clow = din("clow", [128, 128])
    cam = din("cam", [2, 128, 512])
    yp = dout("yp", [TR, D])
    ys = dout("ys", [NB * 64, D])
    kpo = dout("kpo", [T, 2, 128])
    vpo = dout("vpo", [T, 2, 128])
    spo = dout("spo", [2, 2, 64, 64])
    shpo = dout("shpo", [2, 576])
    kso = dout("kso", [NB, 64, 8, 128])
    vso = dout("vso", [NB, 64, 8, 128])
    sso = dout("sso", [NB, 8, 2, 64, 64])
    shso = dout("shso", [NB, 8, 576])
    XI = dint("XI", [4 * 16 * 128, TR], BF16)
    XO = dint("XO", [16 * 128, TR], BF16)
    XS = dint("XS", [16 * 128, NB * 64], BF16)
    wub = dint("wub", [8, D, UC], BF16)
    wmyb = dint("wmyb", [2, D, UC], BF16)
    wob = dint("wob", [D, D], BF16)
    wgb = dint("wgb", [D, DFF], BF16)
    wupb = dint("wupb", [D, DFF], BF16)
    wdb = dint("wdb", [DFF, D], BF16)

    es = contextlib.ExitStack()
    with es:
        S = Sched(nc, es)
        bufs = {}

        es1 = contextlib.ExitStack()
        cur = [es]

        def sb(name, shape, dt=F32):
            t = cur[0].enter_context(nc.sbuf_tensor(name, list(shape), dt))
            bufs[name] = Buf(name)
            return t

        def B(name):
            if name not in bufs:
                bufs[name] = Buf(name)
            return bufs[name]

        PS = [es.enter_context(nc.psum_tensor("ps%d" % i, [128, 512], F32)) for i in range(8)]
        PB = [Buf("ps%d" % i, excl=True) for i in range(8)]
        gen_rr = [0]

        def gbank():
            i = gen_rr[0]
            gen_rr[0] = (gen_rr[0] + 1) % 4
            return i

        try:
            def cast_w(dst, src, rows, cols, bname):
                b = B(bname)
                r = 0
                while r < rows:
                    rr = min(128, rows - r)
                    c = 0
                    while c < cols:
                        cc = min(2048, cols - c)
                        S.dma("pool", dst[r:r + rr, c:c + cc], src[r:r + rr, c:c + cc], pwrites=[b])
                        c += cc
                    r += rr

            for u in range(2):
                cast_w(wmyb[u], wmy[u], D, UC, "wmyb%d" % u)

            ident = sb("ident", [128, 128]); identb = sb("identb", [128, 128], BF16)
            ut = sb("ut", [128, 128]); s1m = sb("s1m", [128, 128]); c128 = sb("c128", [128, 128]); c64 = sb("c64", [64, 64])
            ones = sb("ones", [128, 128]); onesb = sb("onesb", [128, 1], BF16)
            gmask = sb("gmask", [128, 512]); lowm = sb("lowm", [128, 128])
            amask = sb("amask", [128, 2, 512], BF16)
            g1t = sb("g1t", [128, NCH]); g2t = sb("g2t", [128, NCH])
            qkgb = sb("qkgb", [128, 256]); lamb = sb("lamb", [128, 256]); subgb = sb("subgb", [128, 128])
            selt = sb("selt", [128, 4])
            cstt = [sb("cstt%d" % i, [128, 16]) for i in range(2)]; csst = sb("csst", [64, 16])
            lamt = sb("lamt", [128, 4])
            neglam = sb("neglam", [128, 1])
            for t_, src in ((ident, cid), (ut, cut), (s1m, cs1), (c128, cc128), (gmask, cgm), (lowm, clow),
                            (g1t, g1T), (g2t, g2T), (selt, sel)):
                S.dma("sp", t_[:], src[:, :], writes=[B(t_.name)])
            S.dma("sp", c64[:], cc64[:, :], writes=[B("c64")])
            S.dma("sp", csst[:], css[:, :], writes=[B("csst")])
            S.dma("sp", qkgb[:], qkg.partition_broadcast(128), writes=[B("qkgb")])
            S.dma("sp", lamb[:], lamv.partition_broadcast(128), writes=[B("lamb")])
            S.dma("sp", subgb[:], subg.partition_broadcast(128), writes=[B("subgb")])
            S.op("dve", lambda e: e.memset(ones[:], 1.0), writes=[B("ones")])
            S.op("dve", lambda e: e.memset(onesb[:], 1.0), writes=[B("onesb")])
            S.op("dve", lambda e: e.tensor_copy(out=identb[:], in_=ident[:]), reads=[B("ident")], writes=[B("identb")])
            S.dma("pool", amask[:], cam.rearrange("a p c -> p a c"), writes=[B("amask")])
            S.op("dve", lambda e: e.tensor_scalar(out=subgb[:], in0=subgb[:], scalar1=1.0 - LAM_INIT, scalar2=None,
                                                  op0=ALU.mult), reads=[B("subgb")], writes=[B("subgb")])
            lscr = sb("lscr", [128, 64])
            S.op("dve", lambda e: e.scalar_tensor_tensor(out=lscr[:], in0=lamb[:, 0:64], scalar=1.0, in1=lamb[:, 64:128], op0=ALU.mult, op1=ALU.mult, accum_out=lamt[:, 0:1]),
                 reads=[B("lamb")], writes=[B("lscr"), B("lamt")])
            S.op("dve", lambda e: e.scalar_tensor_tensor(out=lscr[:], in0=lamb[:, 128:192], scalar=1.0, in1=lamb[:, 192:256], op0=ALU.mult, op1=ALU.mult, accum_out=lamt[:, 1:2]),
                 reads=[B("lamb"), B("lamt")], writes=[B("lscr"), B("lamt")])
            S.op("act", lambda e: e.activation(out=lamt[:, 2:4], in_=lamt[:, 0:2], func=AF.Exp),
                 reads=[B("lamt")], writes=[B("lamt")])
            S.op("dve", lambda e: e.tensor_tensor(out=neglam[:], in0=lamt[:, 3:4], in1=lamt[:, 2:3], op=ALU.subtract),
                 reads=[B("lamt")], writes=[B("neglam")])
            S.op("dve", lambda e: e.tensor_scalar(out=neglam[:], in0=neglam[:], scalar1=-LAM_INIT, scalar2=None, op0=ALU.add),
                 reads=[B("neglam")], writes=[B("neglam")])
            ckpt("consts")

            NKT = max(NT, NP + 1)
            xsb = sb("xsb", [128, D], BF16)
            junk = sb("junk", [128, D], BF16)
            st4 = sb("st4", [128, 8])
            cur[0] = es1
            wub_sb = sb("wub_sb", [128, NCH, UC], BF16)
            KT = sb("KT", [128, NKT * 128], BF16)
            VV = sb("VV", [128, NKT, 128], BF16)
            KTB = [Buf("KT%d" % i) for i in range(NKT)]
            VB = [Buf("V%d" % i) for i in range(NKT)]
            xt = [sb("xt%d" % i, [128, D]) for i in range(2)]
            xnTall = sb("xnTall", [128, NCH, 256], BF16)
            XNB = [Buf("xnTslot0"), Buf("xnTslot1")]
            gsb = sb("gsb", [128, 384])
            prw = [sb("prw%d" % i, [128, 576]) for i in range(2)]
            shbuf = sb("shbuf", [128, 576])
            tq = sb("tq", [128, 256]); qkn = sb("qkn", [128, 256]); rtmp = sb("rtmp", [128, 4, 4, 8])
            qkb = sb("qkb", [128, 256], BF16)
            QTs = [sb("QT%d" % i, [128, 2, 256], BF16) for i in range(2)]
            Eb = [sb("Eb%d" % i, [128, 512], BF16) for i in range(2)]
            OT = sb("OT", [128, 512]); zsb = sb("zsb", [1, 512]); Zacc = sb("Zacc", [128, 512]); ajunk = sb("ajunk", [128, 128], BF16); one11 = sb("one11", [1, 1])
            S.op("dve", lambda e: e.memset(one11[:], 1.0), writes=[B("one11")])
            for q_ in QTs:
                S.op("dve", lambda e, q_=q_: e.memset(q_[:, :, :], 0.0), writes=[B(q_.name)])
            osb = sb("osb", [128, 128]); onb = sb("onb", [128, 128], BF16); ast = sb("ast", [128, 8])
            catA = sb("catA", [128, 4, 128], BF16); catB = sb("catB", [128, 4, 128], BF16)
            parb = sb("parb", [128, NPAR])
            lwt = sb("lwt", [64, 2, 128]); lg2t = sb("lg2t", [64, 128])
            xs_ = sb("xs_", [128, 576])
            Et = sb("Et", [128, 128]); LT = sb("LT", [128, 192]); LTT = sb("LTT", [64, 3, 128])
            za = sb("za", [128, 256]); sa = sb("sa", [128, 256]); ld = sb("ld", [128, 128]); g_sb = sb("g_sb", [128, 128])
            kkv = sb("kkv", [128, 128]); rst = sb("rst", [128, 8]); k2 = sb("k2", [128, 128]); mm_ = sb("mm_", [128, 128])
            bvec = sb("bvec", [128, 128]); bs = sb("bs", [128, 2])
            cum = sb("cum", [128, 128]); ec = sb("ec", [128, 128]); eci = sb("eci", [128, 128]); ee = sb("ee", [128, 128])
            eh = sb("eh", [128, 128]); gC = sb("gC", [64, 2])
            rt = sb("rt", [128, 128]); bt = sb("bt", [128, 128]); ktl = sb("ktl", [128, 128])
            bh = sb("bh", [128, 128]); kh = sb("kh", [128, 128])
            FT = [sb("FT%d" % h, [64, 512]) for h in range(2)]
            GM = [sb("GM%d" % h, [128, 512]) for h in range(2)]
            Xa = [[sb("Xa%d_%d" % (h, i), [128, 128]) for i in range(2)] for h in range(2)]
            Xb = [[sb("Xb%d_%d" % (h, i), [128, 128]) for i in range(2)] for h in range(2)]
            ACC = [[sb("ACC%d_%d" % (h, i), [128, 128]) for i in range(2)] for h in range(2)]
            RH = [sb("RH%d" % h, [128, 128]) for h in range(2)]
            PU = [sb("PU%d" % h, [128, 128]) for h in range(2)]
            Y1T = [sb("Y1T%d" % h, [64, 128]) for h in range(2)]
            Y2 = [sb("Y2%d" % h, [128, 64]) for h in range(2)]
            T1T = [sb("T1T%d" % h, [64, 64]) for h in range(2)]
            T2 = [sb("T2%d" % h, [64, 64]) for h in range(2)]
            Hs = [[sb("H%d_%d" % (h, i), [64, 64]) for i in range(2)] for h in range(2)]
            Hld = sb("Hld", [64, 2, 64]); Hout = sb("Hout", [64, 2, 64])
            yb = sb("yb", [128, 128]); yc = sb("yc", [128, 128]); ysq = sb("ysq", [128, 128]); obb = sb("obb", [128, 128], BF16)

            def bc3(ap2, n, a, b):
                return ap2.unsqueeze(2).to_broadcast([n, a, b])

            def rstd_chain(src_ap, dst_ap, n, k, scale, eps, bsrc, bdst):
                S.op("dve", lambda e: e.tensor_scalar(out=dst_ap, in0=src_ap, scalar1=scale, scalar2=eps, op0=ALU.mult,
                                                      op1=ALU.add), reads=[bsrc], writes=[bdst])
                S.op("act", lambda e: e.activation(out=dst_ap, in_=dst_ap, func=AF.Ln), reads=[bdst], writes=[bdst])
                S.op("act", lambda e: e.activation(out=dst_ap, in_=dst_ap, func=AF.Exp, scale=-0.5), reads=[bdst], writes=[bdst])

            def front(x_rows_ap, n, dst, dst_off, dstB, gt, slot):
                xtile = xt[slot]
                bx = B(xtile.name)
                S.dma("sp", xtile[:n, :], x_rows_ap, writes=[bx])
                S.op("dve", lambda e: e.scalar_tensor_tensor(out=junk[:n, :], in0=xtile[:n, :], scalar=1.0, in1=xtile[:n, :], op0=ALU.mult, op1=ALU.mult, accum_out=st4[:n, 0:1]),
                     reads=[bx], writes=[B("junk"), B("st4")])
                rstd_chain(st4[:n, 0:1], st4[:n, 0:1], n, 1, 1.0 / D, RMS_EPS, B("st4"), B("st4"))
                S.op("act", lambda e: e.activation(out=xsb[:n, :], in_=xtile[:n, :], func=AF.Copy, scale=st4[:n, 0:1]),
                     reads=[bx, B("st4")], writes=[B("xsb")])
                yield
                for half in range(2):
                    pb = gbank()
                    pv = PS[pb][:].bitcast(BF16)
                    for c8 in range(8):
                        c = half * 8 + c8
                        S.op("pe", lambda e, c=c, c8=c8, pv=pv: e.transpose(out=pv[:, c8 * n:(c8 + 1) * n],
                                                                           in_=xsb[:n, c * 128:(c + 1) * 128],
                                                                           identity=identb[:n, :n]),
                             reads=[B("xsb"), B("identb")], writes=[PB[pb]] if c8 == 0 else [], pwrites=[] if c8 == 0 else [PB[pb]],
                             inc=(c8 == 7))
                    S.op("dve", lambda e, half=half, pv=pv: e.tensor_tensor(
                        out=dst[:, half * 8:(half + 1) * 8, dst_off:dst_off + n],
                        in0=pv[:, 0:8 * n].rearrange("p (c n) -> p c n", n=n),
                        in1=bc3(gt[:, half * 8:(half + 1) * 8], 128, 8, n), op=ALU.mult),
                        reads=[PB[pb], B(gt.name)], writes=[] if half else [dstB], pwrites=[dstB] if half else [])
                    yield

            def load_unit_params(up_row, lw_ap, lg2_ap):
                S.dma("sp", parb[:], up_row.partition_broadcast(128), writes=[B("parb")])
                S.dma("sp", lwt[:], lw_ap.rearrange("(a p) n -> p a n", p=64), writes=[B("lwt")])
                S.dma("sp", lg2t[:], lg2_ap, writes=[B("lg2t")])
            mu_bc = parb[:, 0:576]; w0a0_bc = parb[:, 576:832]; kk_bc = parb[:, 832:960]; ka_bc = parb[:, 960:1088]
            rk_bc = parb[:, 1088:1216]; lg_bc = parb[:, 1216:1344]; lb_bc = parb[:, 1344:1472]

            def load_unit_w(wsrc):
                S.dma("sp", wub_sb[:], wsrc.rearrange("(c p) n -> p c n", p=128), reads=[B(wsrc_name[0])], writes=[B("wub_sb")])
            wsrc_name = [None]

            def project(xsrc, xoff, xB, n, pslot):
                pa, pb2 = gbank(), gbank()
                for k in range(NCH):
                    S.op("pe", lambda e, k=k: e.matmul(PS[pa][:n, 0:512], lhsT=xsrc[:, k, xoff:xoff + n], rhs=wub_sb[:, k, 0:512],
                                                       start=(k == 0), stop=(k == NCH - 1), skip_group_check=True),
                         reads=[xB, B("wub_sb")], writes=[PB[pa]] if k == 0 else [], pwrites=[] if k == 0 else [PB[pa]],
                         inc=(k == NCH - 1))
                for k in range(NCH):
                    S.op("pe", lambda e, k=k: e.matmul(PS[pb2][:n, 0:448], lhsT=xsrc[:, k, xoff:xoff + n], rhs=wub_sb[:, k, 512:960],
                                                       start=(k == 0), stop=(k == NCH - 1), skip_group_check=True),
                         reads=[xB, B("wub_sb")], writes=[PB[pb2]] if k == 0 else [], pwrites=[] if k == 0 else [PB[pb2]],
                         inc=(k == NCH - 1))
                pr = prw[pslot]
                yield
                S.op("act", lambda e: e.activation(out=gsb[:n, :], in_=PS[pa][:n, 0:384], func=AF.Copy),
                     reads=[PB[pa]], writes=[B("gsb")])
                S.op("act", lambda e: e.activation(out=pr[:n, 0:128], in_=PS[pa][:n, 384:512], func=AF.Copy),
                     reads=[PB[pa]], writes=[B(pr.name)])
                S.op("act", lambda e: e.activation(out=pr[:n, 128:576], in_=PS[pb2][:n, 0:448], func=AF.Copy),
                     reads=[PB[pb2]], pwrites=[B(pr.name)])

            def attn_prep(n, cs_ap, csB, k_out_ap, v_out_ap, kt_idx, kt_off, qoff, QT, QTB_):
                S.op("dve", lambda e: e.tensor_tensor(out=tq[:n, :], in0=gsb[:n, 0:256], in1=gsb[:n, 0:256], op=ALU.mult),
                     reads=[B("gsb")], writes=[B("tq")])
                S.op("dve", lambda e: e.tensor_reduce(out=st4[:n, 4:8], in_=tq[:n, :].rearrange("p (a b) -> p a b", b=64),
                                                      axis=AX.X, op=ALU.add), reads=[B("tq")], writes=[B("st4")])
                rstd_chain(st4[:n, 4:8], st4[:n, 4:8], n, 4, 1.0 / 64, RMS_EPS, B("st4"), B("st4"))
                q3 = qkn[:n, :].rearrange("p (a b) -> p a b", b=64)
                S.op("dve", lambda e: e.tensor_tensor(out=q3, in0=gsb[:n, 0:256].rearrange("p (a b) -> p a b", b=64),
                                                      in1=bc3(st4[:n, 4:8], n, 4, 64), op=ALU.mult),
                     reads=[B("gsb"), B("st4")], writes=[B("qkn")])
                S.op("dve", lambda e: e.tensor_tensor(out=qkn[:n, :], in0=qkn[:n, :], in1=qkgb[:n, :], op=ALU.mult),
                     reads=[B("qkn"), B("qkgb")], writes=[B("qkn")])
                yield
                x1 = q3[:, :, 0:8]; x2 = q3[:, :, 8:16]
                cosb = cs_ap[:, 0:8].unsqueeze(1).to_broadcast([n, 4, 8])
                sinb = cs_ap[:, 8:16].unsqueeze(1).to_broadcast([n, 4, 8])
                for idx, (a_, b_) in enumerate(((x1, cosb), (x2, sinb), (x2, cosb), (x1, sinb))):
                    S.op("dve", lambda e, idx=idx, a_=a_, b_=b_: e.tensor_tensor(out=rtmp[:n, :, idx, :], in0=a_, in1=b_, op=ALU.mult),
                         reads=[B("qkn"), csB], writes=[B("rtmp")] if idx == 0 else [], pwrites=[] if idx == 0 else [B("rtmp")])
                S.op("dve", lambda e: e.tensor_tensor(out=x1, in0=rtmp[:n, :, 0, :], in1=rtmp[:n, :, 1, :], op=ALU.subtract),
                     reads=[B("rtmp")], writes=[B("qkn")])
                S.op("dve", lambda e: e.tensor_tensor(out=x2, in0=rtmp[:n, :, 2, :], in1=rtmp[:n, :, 3, :], op=ALU.add),
                     reads=[B("rtmp")], writes=[B("qkn")])
                yield
                S.dma("sp", k_out_ap, qkn[:n, 128:256], reads=[B("qkn")])
                S.dma("sp", v_out_ap, gsb[:n, 256:384], reads=[B("gsb")])
                S.op("act", lambda e: e.activation(out=qkb[:n, :], in_=qkn[:n, :], func=AF.Copy), reads=[B("qkn")], writes=[B("qkb")])
                S.op("act", lambda e: e.activation(out=VV[:n, kt_idx, :], in_=gsb[:n, 256:384], func=AF.Copy),
                     reads=[B("gsb")], writes=[VB[kt_idx]])
                pb = gbank()
                pv = PS[pb][:].bitcast(BF16)
                S.op("pe", lambda e: e.transpose(out=pv[:, 0:n], in_=qkb[:n, 0:128], identity=identb[:n, :n]),
                     reads=[B("qkb"), B("identb")], writes=[PB[pb]], inc=False)
                S.op("pe", lambda e: e.transpose(out=pv[:, 128:128 + n], in_=qkb[:n, 128:256], identity=identb[:n, :n]),
                     reads=[B("qkb"), B("identb")], pwrites=[PB[pb]])
                S.op("act", lambda e: e.activation(out=QT[0:64, 0, qoff:qoff + n], in_=pv[0:64, 0:n], func=AF.Copy),
                     reads=[PB[pb]], writes=[] if qoff else [QTB_], pwrites=[QTB_] if qoff else [])
                S.op("act", lambda e: e.activation(out=QT[64:128, 1, qoff:qoff + n], in_=pv[64:128, 0:n], func=AF.Copy),
                     reads=[PB[pb]], pwrites=[QTB_])
                yield
                S.op("act", lambda e: e.activation(out=KT[:, kt_off:kt_off + n], in_=pv[:, 128:128 + n], func=AF.Copy),
                     reads=[PB[pb]], writes=[KTB[kt_idx]])

            def attention(nq, n, key_tiles, cat_writer, QT, QTB_, tile_base):
                W2 = 2 * nq
                for ki, (koff, nk, kidx, mk) in enumerate(key_tiles):
                    sbk = 4 + (ki % 2)
                    Ebt = Eb[ki % 2]
                    S.op("pe", lambda e: e.matmul(PS[sbk][:nk, 0:W2].rearrange("p (a b) -> p a b", a=2), lhsT=KT[:, koff:koff + nk], rhs=QT[:, :, 0:nq],
                                                  start=True, stop=True, skip_group_check=True),
                         reads=[KTB[kidx], QTB_], writes=[PB[sbk]])
                    S.op("act", lambda e: e.activation(out=Ebt[:nk, 0:W2], in_=PS[sbk][:nk, 0:W2], func=AF.Exp, scale=0.125),
                         reads=[PB[sbk]], writes=[B(Ebt.name)])
                    if mk is not None:
                        S.op("dve", lambda e: e.tensor_tensor(out=Ebt[:nk, 0:W2], in0=Ebt[:nk, 0:W2], in1=amask[:nk, mk, 0:W2], op=ALU.mult),
                             reads=[B(Ebt.name), B("amask")], writes=[B(Ebt.name)])
                    first = (ki == 0)
                    last = (ki == len(key_tiles) - 1)
                    S.op("pe", lambda e: e.matmul(PS[6][:, 0:W2], lhsT=VV[:nk, kidx, :], rhs=Ebt[:nk, 0:W2], start=first, stop=last,
                                                  skip_group_check=True),
                         reads=[VB[kidx], B(Ebt.name)], writes=[PB[6]] if first else [], pwrites=[] if first else [PB[6]], inc=True)
                    if first:
                        S.op("pool", lambda e: e.tensor_copy(out=Zacc[:nk, 0:W2], in_=Ebt[:nk, 0:W2]), reads=[B(Ebt.name)], writes=[B("Zacc")])
                    else:
                        S.op("pool", lambda e: e.tensor_tensor(out=Zacc[:nk, 0:W2], in0=Zacc[:nk, 0:W2], in1=Ebt[:nk, 0:W2], op=ALU.add),
                             reads=[B(Ebt.name), B("Zacc")], writes=[B("Zacc")])
                    yield
                S.op("pe", lambda e: e.matmul(PS[7][0:1, 0:W2], lhsT=ones[:, 0:1], rhs=Zacc[:, 0:W2], start=True, stop=True, skip_group_check=True),
                     reads=[B("ones"), B("Zacc")], writes=[PB[7]])
                S.op("act", lambda e: e.activation(out=OT[:, 0:W2], in_=PS[6][:, 0:W2], func=AF.Copy), reads=[PB[6]], writes=[B("OT")])
                S.op("dve", lambda e: e.tensor_copy(out=zsb[0:1, 0:W2], in_=PS[7][0:1, 0:W2]), reads=[PB[7]], writes=[B("zsb")])
                for qt in range(nq // n):
                    pb = gbank()
                    S.op("pe", lambda e: e.transpose(out=PS[pb][:n, 0:128], in_=OT[:, qt * n:(qt + 1) * n], identity=ident[:, :]),
                         reads=[B("OT"), B("ident")], writes=[PB[pb]], inc=False)
                    S.op("pe", lambda e: e.transpose(out=PS[pb][:n, 128:256], in_=OT[:, nq + qt * n:nq + (qt + 1) * n], identity=ident[:, :]),
                         reads=[B("OT"), B("ident")], pwrites=[PB[pb]], inc=False)
                    S.op("pe", lambda e: e.matmul(PS[pb][:n, 256:257], lhsT=zsb[0:1, qt * n:(qt + 1) * n], rhs=one11[0:1, 0:1],
                                                  start=False, stop=False, skip_group_check=True),
                         reads=[B("zsb"), B("one11")], pwrites=[PB[pb]], inc=False)
                    S.op("pe", lambda e: e.matmul(PS[pb][:n, 257:258], lhsT=zsb[0:1, nq + qt * n:nq + (qt + 1) * n], rhs=one11[0:1, 0:1],
                                                  start=False, stop=True, skip_group_check=True),
                         reads=[B("zsb"), B("one11")], pwrites=[PB[pb]])
                    S.op("dve", lambda e: e.reciprocal(out=ast[:n, 0:2], in_=PS[pb][:n, 256:258]), reads=[PB[pb]], writes=[B("ast")])
                    S.op("dve", lambda e: e.tensor_tensor(out=ast[:n, 2:3], in0=ast[:n, 1:2], in1=neglam[:n, 0:1], op=ALU.mult),
                         reads=[B("ast"), B("neglam")], writes=[B("ast")])
                    S.op("dve", lambda e: e.tensor_scalar(out=osb[:n, :], in0=PS[pb][:n, 0:128], scalar1=ast[:n, 0:1], scalar2=None,
                                                          op0=ALU.mult), reads=[PB[pb], B("ast")], writes=[B("osb")])
                    S.op("dve", lambda e: e.scalar_tensor_tensor(out=osb[:n, :], in0=PS[pb][:n, 128:256], scalar=ast[:n, 2:3],
                                                                 in1=osb[:n, :], op0=ALU.mult, op1=ALU.add),
                         reads=[PB[pb], B("ast"), B("osb")], writes=[B("osb")])
                    S.op("dve", lambda e: e.scalar_tensor_tensor(out=ajunk[:n, 0:128], in0=osb[:n, :], scalar=1.0, in1=osb[:n, :], op0=ALU.mult, op1=ALU.mult, accum_out=ast[:n, 4:5]),
                         reads=[B("osb"), B("ast")], writes=[B("ajunk"), B("ast")])
                    rstd_chain(ast[:n, 4:5], ast[:n, 4:5], n, 1, 1.0 / 128, RMS_EPS, B("ast"), B("ast"))
                    S.op("dve", lambda e: e.scalar_tensor_tensor(out=onb[:n, :], in0=osb[:n, :], scalar=ast[:n, 4:5], in1=subgb[:n, :],
                                                                 op0=ALU.mult, op1=ALU.mult),
                         reads=[B("osb"), B("ast"), B("subgb")], writes=[B("onb")])
                    pb2 = gbank()
                    pv = PS[pb2][:].bitcast(BF16)
                    S.op("pe", lambda e: e.transpose(out=pv[:, 0:n], in_=onb[:n, :], identity=identb[:n, :n]),
                         reads=[B("onb"), B("identb")], writes=[PB[pb2]])
                    cat_writer(0, tile_base + qt, pv[:, 0:n], PB[pb2])
                    yield

            def rwkv(n, pslot, prev_ap, prevB, cmat, first_tile, hslot, cat_writer, tile_idx):
                pr = prw[pslot]
                prB = B(pr.name)
                pa, pb2 = gbank(), gbank()
                for (bank, c0, c1) in ((pa, 0, 512), (pb2, 512, 576)):
                    S.op("pe", lambda e, bank=bank, c0=c0, c1=c1: e.matmul(PS[bank][:n, 0:c1 - c0], lhsT=s1m[:n, :n], rhs=pr[:n, c0:c1],
                                                                          start=True, stop=(prev_ap is None), skip_group_check=True),
                         reads=[B("s1m"), prB], writes=[PB[bank]], inc=(prev_ap is None))
                    if prev_ap is not None:
                        S.op("pe", lambda e, bank=bank, c0=c0, c1=c1: e.matmul(PS[bank][:n, 0:c1 - c0], lhsT=cmat, rhs=prev_ap[:, c0:c1],
                                                                              start=False, stop=True, skip_group_check=True),
                             reads=[prevB, B("c128"), B("c64")], pwrites=[PB[bank]])
                S.op("dve", lambda e: e.tensor_tensor(out=xs_[:n, 0:512], in0=PS[pa][:n, 0:512], in1=pr[:n, 0:512], op=ALU.subtract),
                     reads=[PB[pa], prB], writes=[B("xs_")])
                S.op("dve", lambda e: e.tensor_tensor(out=xs_[:n, 512:576], in0=PS[pb2][:n, 0:64], in1=pr[:n, 512:576], op=ALU.subtract),
                     reads=[PB[pb2], prB], pwrites=[B("xs_")])
                S.op("dve", lambda e: e.tensor_tensor(out=xs_[:n, :], in0=xs_[:n, :], in1=mu_bc[:n, :], op=ALU.mult),
                     reads=[B("xs_"), B("parb")], writes=[B("xs_")])
                S.op("dve", lambda e: e.tensor_tensor(out=xs_[:n, :], in0=xs_[:n, :], in1=pr[:n, :], op=ALU.add),
                     reads=[B("xs_"), prB], writes=[B("xs_")])
                xr_, xk, xv = xs_[:n, 0:128], xs_[:n, 128:256], xs_[:n, 256:384]
                yield
                S.op("act", lambda e: e.activation(out=Et[:n, 0:64], in_=xs_[:n, 384:448], func=AF.Exp, scale=-2.0),
                     reads=[B("xs_")], writes=[B("Et")])
                S.op("act", lambda e: e.activation(out=Et[:n, 64:128], in_=xs_[:n, 512:576], func=AF.Exp, scale=-1.0),
                     reads=[B("xs_")], pwrites=[B("Et")])
                S.op("dve", lambda e: e.tensor_scalar(out=Et[:n, :], in0=Et[:n, :], scalar1=1.0, scalar2=None, op0=ALU.add),
                     reads=[B("Et")], writes=[B("Et")])
                S.op("dve", lambda e: e.reciprocal(out=Et[:n, :], in_=Et[:n, :]), reads=[B("Et")], writes=[B("Et")])
                S.op("dve", lambda e: e.tensor_scalar(out=LT[:n, 0:64], in0=Et[:n, 0:64], scalar1=2.0, scalar2=-1.0, op0=ALU.mult, op1=ALU.add),
                     reads=[B("Et")], writes=[B("LT")])
                S.op("dve", lambda e: e.tensor_copy(out=LT[:n, 64:128], in_=xs_[:n, 448:512]), reads=[B("xs_")], pwrites=[B("LT")])
                S.op("dve", lambda e: e.tensor_copy(out=LT[:n, 128:192], in_=Et[:n, 64:128]), reads=[B("Et")], pwrites=[B("LT")])
                yield
                pb = gbank()
                for j3 in range(3):
                    S.op("pe", lambda e, j3=j3: e.transpose(out=PS[pb][0:64, j3 * 128:j3 * 128 + n], in_=LT[:n, j3 * 64:(j3 + 1) * 64], identity=ident[:n, :n]),
                         reads=[B("LT"), B("ident")], writes=[PB[pb]] if j3 == 0 else [], pwrites=[] if j3 == 0 else [PB[pb]], inc=(j3 == 2))
                S.op("act", lambda e: e.activation(out=LTT[:, :, 0:n], in_=PS[pb][0:64, 0:384].rearrange("p (a b) -> p a b", b=128)[:, :, 0:n], func=AF.Copy),
                     reads=[PB[pb]], writes=[B("LTT")])
                yield
                pl = gbank()
                S.op("pe", lambda e: e.matmul(PS[pl][:n, 0:128], lhsT=LTT[:, 0, 0:n], rhs=lwt[:, 0, :], start=True, stop=False, skip_group_check=True),
                     reads=[B("LTT"), B("lwt")], writes=[PB[pl]], inc=False)
                S.op("pe", lambda e: e.matmul(PS[pl][:n, 128:256], lhsT=LTT[:, 1, 0:n], rhs=lwt[:, 1, :], start=False, stop=False, skip_group_check=True),
                     reads=[B("LTT"), B("lwt")], pwrites=[PB[pl]], inc=False)
                S.op("pe", lambda e: e.matmul(PS[pl][:n, 256:384], lhsT=LTT[:, 2, 0:n], rhs=lg2t[:, :], start=False, stop=True, skip_group_check=True),
                     reads=[B("LTT"), B("lg2t")], pwrites=[PB[pl]])
                yield
                S.op("dve", lambda e: e.tensor_tensor(out=za[:n, :], in0=PS[pl][:n, 0:256], in1=w0a0_bc[:n, :], op=ALU.add),
                     reads=[PB[pl], B("parb")], writes=[B("za")])
                yield
                S.op("dve", lambda e: e.tensor_copy(out=g_sb[:n, :], in_=PS[pl][:n, 256:384]), reads=[PB[pl]], writes=[B("g_sb")])
                yield
                S.op("act", lambda e: e.activation(out=za[:n, :], in_=za[:n, :], func=AF.Exp, scale=-1.0), reads=[B("za")], writes=[B("za")])
                S.op("dve", lambda e: e.tensor_scalar(out=za[:n, :], in0=za[:n, :], scalar1=1.0, scalar2=None, op0=ALU.add),
                     reads=[B("za")], writes=[B("za")])
                yield
                S.op("dve", lambda e: e.reciprocal(out=sa[:n, :], in_=za[:n, :]), reads=[B("za")], writes=[B("sa")])
                yield
                S.op("dve", lambda e: e.tensor_scalar(out=ld[:n, :], in0=sa[:n, 0:128], scalar1=-math.exp(-0.5), scalar2=None, op0=ALU.mult),
                     reads=[B("sa")], writes=[B("ld")])
                av = sa[:n, 128:256]
                yield
                S.op("dve", lambda e: e.tensor_tensor(out=kkv[:n, :], in0=xk, in1=kk_bc[:n, :], op=ALU.mult), reads=[B("xs_"), B("parb")], writes=[B("kkv")])
                S.op("dve", lambda e: e.tensor_tensor(out=mm_[:n, :], in0=kkv[:n, :], in1=kkv[:n, :], op=ALU.mult), reads=[B("kkv")], writes=[B("mm_")])
                S.op("dve", lambda e: e.tensor_reduce(out=rst[:n, 0:2], in_=mm_[:n, :].rearrange("p (a b) -> p a b", b=64), axis=AX.X, op=ALU.add),
                     reads=[B("mm_")], writes=[B("rst")])
                S.op("dve", lambda e: e.tensor_scalar(out=rst[:n, 0:2], in0=rst[:n, 0:2], scalar1=1e-18, scalar2=None, op0=ALU.max),
                     reads=[B("rst")], writes=[B("rst")])
                S.op("act", lambda e: e.activation(out=rst[:n, 0:2], in_=rst[:n, 0:2], func=AF.Ln), reads=[B("rst")], writes=[B("rst")])
                S.op("act", lambda e: e.activation(out=rst[:n, 0:2], in_=rst[:n, 0:2], func=AF.Exp, scale=-0.5), reads=[B("rst")], writes=[B("rst")])
                k3 = kkv[:n, :].rearrange("p (a b) -> p a b", b=64)
                S.op("dve", lambda e: e.tensor_tensor(out=k3, in0=k3, in1=bc3(rst[:n, 0:2], n, 2, 64), op=ALU.mult),
                     reads=[B("kkv"), B("rst")], writes=[B("kkv")])
                S.op("dve", lambda e: e.scalar_tensor_tensor(out=mm_[:n, :], in0=av, scalar=-1.0, in1=ka_bc[:n, :], op0=ALU.add, op1=ALU.mult),
                     reads=[B("sa"), B("parb")], writes=[B("mm_")])
                S.op("dve", lambda e: e.scalar_tensor_tensor(out=k2[:n, :], in0=mm_[:n, :], scalar=1.0, in1=xk, op0=ALU.add, op1=ALU.mult),
                     reads=[B("mm_"), B("xs_")], writes=[B("k2")])
                S.op("dve", lambda e: e.tensor_tensor(out=bvec[:n, :], in0=kkv[:n, :], in1=av, op=ALU.mult), reads=[B("kkv"), B("sa")], writes=[B("bvec")])
                S.op("dve", lambda e: e.tensor_tensor(out=mm_[:n, :], in0=xr_, in1=k2[:n, :], op=ALU.mult), reads=[B("xs_"), B("k2")], writes=[B("mm_")])
                S.op("dve", lambda e: e.tensor_tensor(out=mm_[:n, :], in0=mm_[:n, :], in1=rk_bc[:n, :], op=ALU.mult), reads=[B("mm_"), B("parb")], writes=[B("mm_")])
                S.op("dve", lambda e: e.tensor_reduce(out=bs[:n, 0:2], in_=mm_[:n, :].rearrange("p (a b) -> p a b", b=64), axis=AX.X, op=ALU.add),
                     reads=[B("mm_")], writes=[B("bs")])
                yield
                pc = gbank()
                S.op("pe", lambda e: e.matmul(PS[pc][:n, 0:128], lhsT=ut[:n, :n], rhs=ld[:n, :], start=True, stop=False, skip_group_check=True),
                     reads=[B("ut"), B("ld")], writes=[PB[pc]], inc=False)
                S.op("pe", lambda e: e.matmul(PS[pc][:n, 128:256], lhsT=ones[:n, :n], rhs=ld[:n, :], start=False, stop=False, skip_group_check=True),
                     reads=[B("ones"), B("ld")], pwrites=[PB[pc]], inc=False)
                for hh in range(2):
                    S.op("pe", lambda e, hh=hh: e.matmul(PS[pc][0:64, 256 + hh:257 + hh], lhsT=ld[:n, hh * 64:(hh + 1) * 64], rhs=ones[:n, 0:1],
                                                         start=False, stop=(hh == 1), skip_group_check=True),
                         reads=[B("ones"), B("ld")], pwrites=[PB[pc]], inc=(hh == 1))
                S.op("act", lambda e: e.activation(out=cum[:n, :], in_=PS[pc][:n, 0:128], func=AF.Copy), reads=[PB[pc]], writes=[B("cum")])
                S.op("act", lambda e: e.activation(out=ec[:n, :], in_=PS[pc][:n, 0:128], func=AF.Exp), reads=[PB[pc]], writes=[B("ec")])
                S.op("act", lambda e: e.activation(out=eci[:n, :], in_=PS[pc][:n, 0:128], func=AF.Exp, scale=-1.0), reads=[PB[pc]], writes=[B("eci")])
                S.op("act", lambda e: e.activation(out=gC[:, 0:2], in_=PS[pc][0:64, 256:258], func=AF.Exp), reads=[PB[pc]], writes=[B("gC")])
                S.op("dve", lambda e: e.tensor_tensor(out=ee[:n, :], in0=cum[:n, :], in1=ld[:n, :], op=ALU.subtract), reads=[B("cum"), B("ld")], writes=[B("ee")])
                S.op("act", lambda e: e.activation(out=ee[:n, :], in_=ee[:n, :], func=AF.Exp), reads=[B("ee")], writes=[B("ee")])
                S.op("dve", lambda e: e.tensor_tensor(out=eh[:n, :], in0=PS[pc][:n, 128:256], in1=cum[:n, :], op=ALU.subtract), reads=[PB[pc], B("cum")], writes=[B("eh")])
                S.op("act", lambda e: e.activation(out=eh[:n, :], in_=eh[:n, :], func=AF.Exp), reads=[B("eh")], writes=[B("eh")])
                yield
                S.op("dve", lambda e: e.tensor_tensor(out=rt[:n, :], in0=xr_, in1=ec[:n, :], op=ALU.mult), reads=[B("xs_"), B("ec")], writes=[B("rt")])
                for hh in range(2):
                    S.op("dve", lambda e, hh=hh: e.scalar_tensor_tensor(out=RH[hh][:n, 0:64], in0=kkv[:n, hh * 64:(hh + 1) * 64], scalar=-1.0,
                                                                        in1=ee[:n, hh * 64:(hh + 1) * 64], op0=ALU.mult, op1=ALU.mult),
                         reads=[B("kkv"), B("ee")], writes=[B(RH[hh].name)])
                S.op("dve", lambda e: e.tensor_tensor(out=bt[:n, :], in0=bvec[:n, :], in1=eci[:n, :], op=ALU.mult), reads=[B("bvec"), B("eci")], writes=[B("bt")])
                S.op("dve", lambda e: e.tensor_tensor(out=ktl[:n, :], in0=k2[:n, :], in1=eci[:n, :], op=ALU.mult), reads=[B("k2"), B("eci")], writes=[B("ktl")])
                S.op("dve", lambda e: e.tensor_tensor(out=bh[:n, :], in0=bvec[:n, :], in1=eh[:n, :], op=ALU.mult), reads=[B("bvec"), B("eh")], writes=[B("bh")])
                S.op("dve", lambda e: e.tensor_tensor(out=kh[:n, :], in0=k2[:n, :], in1=eh[:n, :], op=ALU.mult), reads=[B("k2"), B("eh")], writes=[B("kh")])
                nlev = 7 if n == 128 else 6
                yield
                def head_gen(hh):
                    hs = slice(hh * 64, (hh + 1) * 64)
                    pf = gbank()
                    srcs = ((RH[hh][:n, 0:64], B(RH[hh].name)), (rt[:n, hs], B("rt")), (bt[:n, hs], B("bt")), (ktl[:n, hs], B("ktl")))
                    for i4, (sap, sB) in enumerate(srcs):
                        S.op("pe", lambda e, i4=i4, sap=sap: e.transpose(out=PS[pf][0:64, i4 * n:(i4 + 1) * n], in_=sap, identity=ident[:n, :n]),
                             reads=[sB, B("ident")], writes=[PB[pf]] if i4 == 0 else [], pwrites=[] if i4 == 0 else [PB[pf]], inc=(i4 == 3))
                    S.op("act", lambda e, hh=hh: e.activation(out=FT[hh][:, 0:4 * n], in_=PS[pf][0:64, 0:4 * n], func=AF.Copy),
                         reads=[PB[pf]], writes=[B(FT[hh].name)])
                    F_ = FT[hh]; FB = B(F_.name)
                    yield
                    pg = gbank()
                    S.op("pe", lambda e, F_=F_: e.matmul(PS[pg][:n, 0:2 * n], lhsT=F_[:, 2 * n:3 * n], rhs=F_[:, 0:2 * n], start=True, stop=False, skip_group_check=True),
                         reads=[FB], writes=[PB[pg]], inc=False)
                    S.op("pe", lambda e, F_=F_: e.matmul(PS[pg][:n, 2 * n:4 * n], lhsT=F_[:, 3 * n:4 * n], rhs=F_[:, 0:2 * n], start=False, stop=True, skip_group_check=True),
                         reads=[FB], pwrites=[PB[pg]])
                    G_ = GM[hh]; GB = B(G_.name)
                    S.op("dve", lambda e, G_=G_: e.tensor_tensor(out=G_[:n, 0:4 * n].rearrange("p (a b) -> p a b", b=n),
                                                                 in0=PS[pg][:n, 0:4 * n].rearrange("p (a b) -> p a b", b=n),
                                                                 in1=gmask[:n, :].rearrange("p (a b) -> p a b", b=128)[:, :, 0:n], op=ALU.mult),
                         reads=[PB[pg], B("gmask")], writes=[GB])
                    px = gbank()
                    S.op("pe", lambda e, F_=F_: e.matmul(PS[px][:n, 0:n], lhsT=F_[:, 0:n], rhs=F_[:, 2 * n:3 * n], start=True, stop=True, skip_group_check=True),
                         reads=[FB], writes=[PB[px]])
                    xa, xb_ = Xa[hh], Xb[hh]
                    S.op("dve", lambda e, xb_=xb_: e.tensor_tensor(out=xb_[0][:n, :n], in0=PS[px][:n, 0:n], in1=lowm[:n, :n], op=ALU.mult),
                         reads=[PB[px], B("lowm")], writes=[B(xb_[0].name)])
                    yield
                    acc = ACC[hh]
                    S.op("dve", lambda e, acc=acc, G_=G_: e.tensor_tensor(out=acc[0][:n, :n], in0=G_[:n, 0:n], in1=ident[:n, :n], op=ALU.add),
                         reads=[GB, B("ident")], writes=[B(acc[0].name)])
                    curX_ap, curXB = G_[:n, 0:n], GB
                    cs_ = 0
                    for lev in range(1, nlev):
                        curXp = xb_[cs_]
                        nxt = 1 - cs_
                        p2 = gbank()
                        lastlev = (lev == nlev - 1)
                        S.op("pe", lambda e, curX_ap=curX_ap, curXp=curXp: e.matmul(PS[p2][:n, 0:n], lhsT=curX_ap, rhs=curXp[:n, :n], start=True, stop=lastlev, skip_group_check=True),
                             reads=[curXB, B(curXp.name)], writes=[PB[p2]], inc=lastlev)
                        if not lastlev:
                            S.op("pe", lambda e, curX_ap=curX_ap, curXp=curXp: e.matmul(PS[p2][:n, 128:128 + n], lhsT=curXp[:n, :n], rhs=curX_ap, start=False, stop=True, skip_group_check=True),
                                 reads=[curXB, B(curXp.name)], pwrites=[PB[p2]])
                        S.op("act", lambda e, xb_=xb_, nxt=nxt: e.activation(out=xb_[nxt][:n, :n], in_=PS[p2][:n, 0:n], func=AF.Copy),
                             reads=[PB[p2]], writes=[B(xb_[nxt].name)])
                        if not lastlev:
                            S.op("dve", lambda e, xa=xa, nxt=nxt: e.tensor_copy(out=xa[nxt][:n, :n], in_=PS[p2][:n, 128:128 + n]),
                                 reads=[PB[p2]], writes=[B(xa[nxt].name)])
                        a_cur = acc[(lev - 1) % 2]; a_nxt = acc[lev % 2]
                        p3 = gbank()
                        S.op("pe", lambda e, xb_=xb_, nxt=nxt, a_cur=a_cur: e.matmul(PS[p3][:n, 0:n], lhsT=xb_[nxt][:n, :n], rhs=a_cur[:n, :n], start=True, stop=True, skip_group_check=True),
                             reads=[B(xb_[nxt].name), B(a_cur.name)], writes=[PB[p3]])
                        S.op("dve", lambda e, a_cur=a_cur, a_nxt=a_nxt: e.tensor_tensor(out=a_nxt[:n, :n], in0=PS[p3][:n, 0:n], in1=a_cur[:n, :n], op=ALU.add),
                             reads=[PB[p3], B(a_cur.name)], writes=[B(a_nxt.name)])
                        curX_ap, curXB = xa[nxt][:n, :n], B(xa[nxt].name)
                        cs_ = nxt
                        yield
                    MT = acc[(nlev - 1) % 2]
                    yield
                    MTB = B(MT.name)
                    vh = xs_[:n, 256 + hh * 64:256 + (hh + 1) * 64]
                    pa_ = gbank()
                    S.op("pe", lambda e, G_=G_, vh=vh: e.matmul(PS[pa_][:n, 0:64], lhsT=G_[:n, 2 * n:3 * n], rhs=vh, start=True, stop=True, skip_group_check=True),
                         reads=[GB, B("xs_")], writes=[PB[pa_]])
                    S.op("act", lambda e, hh=hh: e.activation(out=RH[hh][:n, 64:128], in_=PS[pa_][:n, 0:64], func=AF.Copy),
                         reads=[PB[pa_]], pwrites=[B(RH[hh].name)])
                    ppu = gbank()
                    S.op("pe", lambda e, MT=MT, hh=hh: e.matmul(PS[ppu][:n, 0:128], lhsT=MT[:n, :n], rhs=RH[hh][:n, :], start=True, stop=True, skip_group_check=True),
                         reads=[MTB, B(RH[hh].name)], writes=[PB[ppu]])
                    S.op("act", lambda e, hh=hh: e.activation(out=PU[hh][:n, :], in_=PS[ppu][:n, 0:128], func=AF.Copy), reads=[PB[ppu]], writes=[B(PU[hh].name)])
                    PUB = B(PU[hh].name)
                    yield
                    py = gbank()
                    S.op("pe", lambda e, hh=hh, G_=G_: e.matmul(PS[py][:n, 128:192], lhsT=G_[:n, n:2 * n], rhs=PU[hh][:n, 64:128], start=True, stop=False, skip_group_check=True),
                         reads=[PUB, GB], writes=[PB[py]], inc=False)
                    S.op("pe", lambda e, hh=hh, G_=G_, vh=vh: e.matmul(PS[py][:n, 128:192], lhsT=G_[:n, 3 * n:4 * n], rhs=vh, start=False, stop=False, skip_group_check=True),
                         reads=[GB, B("xs_")], pwrites=[PB[py]], inc=False)
                    S.op("pe", lambda e, hh=hh, G_=G_: e.matmul(PS[py][0:64, 0:n], lhsT=PU[hh][:n, 0:64], rhs=G_[:n, n:2 * n], start=False, stop=False, skip_group_check=True),
                         reads=[PUB, GB], pwrites=[PB[py]], inc=False)
                    S.op("pe", lambda e, hh=hh, hs=hs: e.matmul(PS[py][0:64, 192:256], lhsT=PU[hh][:n, 0:64], rhs=bh[:n, hs], start=False, stop=False, skip_group_check=True),
                         reads=[PUB, B("bh")], pwrites=[PB[py]], inc=False)
                    S.op("pe", lambda e, hh=hh, hs=hs: e.matmul(PS[py][0:64, 256:320], lhsT=bh[:n, hs], rhs=PU[hh][:n, 64:128], start=False, stop=False, skip_group_check=True),
                         reads=[PUB, B("bh")], pwrites=[PB[py]], inc=False)
                    S.op("pe", lambda e, hh=hh, hs=hs, vh=vh: e.matmul(PS[py][0:64, 256:320], lhsT=kh[:n, hs], rhs=vh, start=False, stop=True, skip_group_check=True),
                         reads=[B("kh"), B("xs_")], pwrites=[PB[py]])
                    S.op("dve", lambda e, hh=hh, F_=F_: e.tensor_tensor(out=Y1T[hh][:, 0:n], in0=PS[py][0:64, 0:n], in1=F_[:, n:2 * n], op=ALU.add),
                         reads=[PB[py], FB], writes=[B(Y1T[hh].name)])
                    S.op("act", lambda e, hh=hh: e.activation(out=Y2[hh][:n, :], in_=PS[py][:n, 128:192], func=AF.Copy), reads=[PB[py]], writes=[B(Y2[hh].name)])
                    S.op("dve", lambda e, hh=hh: e.scalar_tensor_tensor(out=T1T[hh][:, :], in0=ident[0:64, 0:64], scalar=gC[:, hh:hh + 1], in1=PS[py][0:64, 192:256],
                                                                        op0=ALU.mult, op1=ALU.add),
                         reads=[PB[py], B("ident"), B("gC")], writes=[B(T1T[hh].name)])
                    S.op("act", lambda e, hh=hh: e.activation(out=T2[hh][:, :], in_=PS[py][0:64, 256:320], func=AF.Copy), reads=[PB[py]], writes=[B(T2[hh].name)])
                    yield
                    Hc = Hs[hh][hslot]; Hn = Hs[hh][1 - hslot]
                    ph = gbank()
                    S.op("pe", lambda e, hh=hh, Hc=Hc: e.matmul(PS[ph][:n, 0:64], lhsT=Y1T[hh][:, 0:n], rhs=Hc[:, :], start=True, stop=False, skip_group_check=True),
                         reads=[B(Y1T[hh].name), B(Hc.name)], writes=[PB[ph]], inc=False)
                    S.op("pe", lambda e, hh=hh, Hc=Hc: e.matmul(PS[ph][0:64, 64:128], lhsT=T1T[hh][:, :], rhs=Hc[:, :], start=False, stop=True, skip_group_check=True),
                         reads=[B(T1T[hh].name), B(Hc.name)], pwrites=[PB[ph]])
                    S.op("dve", lambda e, hh=hh, hs=hs: e.tensor_tensor(out=yb[:n, hs], in0=PS[ph][:n, 0:64], in1=Y2[hh][:n, :], op=ALU.add),
                         reads=[PB[ph], B(Y2[hh].name)], writes=[B("yb")] if hh == 0 else [], pwrites=[] if hh == 0 else [B("yb")])
                    S.op("dve", lambda e, hh=hh, Hn=Hn: e.tensor_tensor(out=Hn[:, :], in0=PS[ph][0:64, 64:128], in1=T2[hh][:, :], op=ALU.add),
                         reads=[PB[ph], B(T2[hh].name)], writes=[B(Hn.name)])
                gens = [head_gen(0), head_gen(1)]
                while gens:
                    for g_ in list(gens):
                        try:
                            next(g_)
                        except StopIteration:
                            gens.remove(g_)
                    yield
                yield
                y3 = yb[:n, :].rearrange("p (a b) -> p a b", b=64)
                yc3 = yc[:n, :].rearrange("p (a b) -> p a b", b=64)
                S.op("dve", lambda e: e.tensor_reduce(out=rst[:n, 2:4], in_=y3, axis=AX.X, op=ALU.add), reads=[B("yb")], writes=[B("rst")])
                S.op("dve", lambda e: e.tensor_scalar(out=rst[:n, 2:4], in0=rst[:n, 2:4], scalar1=-1.0 / 64, scalar2=None, op0=ALU.mult), reads=[B("rst")], writes=[B("rst")])
                S.op("dve", lambda e: e.tensor_tensor(out=yc3, in0=y3, in1=bc3(rst[:n, 2:4], n, 2, 64), op=ALU.add), reads=[B("yb"), B("rst")], writes=[B("yc")])
                S.op("dve", lambda e: e.tensor_tensor(out=ysq[:n, :], in0=yc[:n, :], in1=yc[:n, :], op=ALU.mult), reads=[B("yc")], writes=[B("ysq")])
                S.op("dve", lambda e: e.tensor_reduce(out=rst[:n, 4:6], in_=ysq[:n, :].rearrange("p (a b) -> p a b", b=64), axis=AX.X, op=ALU.add),
                     reads=[B("ysq")], writes=[B("rst")])
                rstd_chain(rst[:n, 4:6], rst[:n, 4:6], n, 2, 1.0 / 64, GN_EPS, B("rst"), B("rst"))
                S.op("dve", lambda e: e.tensor_tensor(out=yc3, in0=yc3, in1=bc3(rst[:n, 4:6], n, 2, 64), op=ALU.mult), reads=[B("yc"), B("rst")], writes=[B("yc")])
                S.op("dve", lambda e: e.tensor_tensor(out=yc[:n, :], in0=yc[:n, :], in1=lg_bc[:n, :], op=ALU.mult), reads=[B("yc"), B("parb")], writes=[B("yc")])
                S.op("dve", lambda e: e.tensor_tensor(out=yc[:n, :], in0=yc[:n, :], in1=lb_bc[:n, :], op=ALU.add), reads=[B("yc"), B("parb")], writes=[B("yc")])
                S.op("dve", lambda e: e.tensor_tensor(out=ysq[:n, :].rearrange("p (a b) -> p a b", b=64), in0=xv.rearrange("p (a b) -> p a b", b=64),
                                                      in1=bc3(bs[:n, 0:2], n, 2, 64), op=ALU.mult), reads=[B("xs_"), B("bs")], writes=[B("ysq")])
                S.op("dve", lambda e: e.tensor_tensor(out=yc[:n, :], in0=yc[:n, :], in1=ysq[:n, :], op=ALU.add), reads=[B("yc"), B("ysq")], writes=[B("yc")])
                S.op("dve", lambda e: e.tensor_tensor(out=obb[:n, :], in0=yc[:n, :], in1=g_sb[:n, :], op=ALU.mult), reads=[B("yc"), B("g_sb")], writes=[B("obb")])
                pb = gbank()
                pv = PS[pb][:].bitcast(BF16)
                S.op("pe", lambda e: e.transpose(out=pv[:, 0:n], in_=obb[:n, :], identity=identb[:n, :n]), reads=[B("obb"), B("identb")], writes=[PB[pb]])
                cat_writer(1, tile_idx, pv[:, 0:n], PB[pb])
                yield

            def state_out(hslot, dst_ap):
                pb = gbank()
                for hh in range(2):
                    Hc = Hs[hh][hslot]
                    S.op("pe", lambda e, hh=hh, Hc=Hc: e.transpose(out=PS[pb][0:64, hh * 64:(hh + 1) * 64], in_=Hc[:, :], identity=ident[0:64, 0:64]),
                         reads=[B(Hc.name), B("ident")], writes=[PB[pb]] if hh == 0 else [], pwrites=[] if hh == 0 else [PB[pb]], inc=(hh == 1))
                S.op("act", lambda e: e.activation(out=Hout[:, :, :], in_=PS[pb][0:64, 0:128].rearrange("p (a b) -> p a b", b=64), func=AF.Copy),
                     reads=[PB[pb]], writes=[B("Hout")])
                S.dma("sp", dst_ap.rearrange("h v k -> v h k"), Hout[:, :, :], reads=[B("Hout")])

            def run(g):
                for _ in g:
                    pass

            def drive(main, side, ratio):
                acc_ = 0.0
                for _ in main:
                    drive.n += 1
                    assert not any(S.uncommitted.values()), "yield inside an uncommitted group"
                    if side is not None:
                        acc_ += ratio
                        while acc_ >= 1.0 and side is not None:
                            acc_ -= 1.0
                            try:
                                next(side)
                            except StopIteration:
                                side = None
                if side is not None:
                    run(side)

            drive.n = 0
            XIv = XI.rearrange("(r c two p) t -> r c two p t", r=4, c=8, two=2, p=128)

            def make_writer(hp):
                def w_(kind, ti, src_ap, srcB):
                    rng_ = (ti * 128) // TR
                    toff_ = ti * 128 - rng_ * TR
                    ct = catA if kind == 0 else catB
                    cB = B(ct.name)
                    S.op("dve", lambda e: e.tensor_tensor(out=ct[:, :, :], in0=src_ap.unsqueeze(1).to_broadcast([128, 4, 128]),
                                                          in1=selt[:, 0:4].unsqueeze(2).to_broadcast([128, 4, 128]), op=ALU.mult),
                         reads=[srcB, B("selt")], writes=[cB])
                    c0 = 4 if kind == 1 else 0
                    S.dma("sp", XIv[rng_, c0:c0 + 4, hp, :, toff_:toff_ + 128].rearrange("j p t -> p j t"), ct[:, :, :],
                          reads=[cB], pwrites=[B("XI")])
                return w_

            def stage1(hp, i):
                slot = i % 2
                qb = (i // 2) % 2
                S.dma("sp", cstt[slot][:, :], csp[i * 128:(i + 1) * 128, :], writes=[B(cstt[slot].name)])
                yield from front(xb[i * 128:(i + 1) * 128, :], 128, xnTall, slot * 128, XNB[slot], g1t, slot)
                yield from project(xnTall, slot * 128, XNB[slot], 128, slot)
                yield from attn_prep(128, cstt[slot][:, :], B(cstt[slot].name), kpo[i * 128:(i + 1) * 128, hp, :], vpo[i * 128:(i + 1) * 128, hp, :],
                                     i, i * 128, (i % 2) * 128, QTs[qb], B(QTs[qb].name))

            def stage2(hp, i, writer):
                slot = i % 2
                prev_ap = None if i == 0 else prw[1 - slot]
                yield from rwkv(128, slot, prev_ap, None if i == 0 else B(prw[1 - slot].name), c128[:, :], i == 0, i % 2, writer, i)

            def interleave(g1, g2):
                gens = [g for g in (g1, g2) if g is not None]
                while gens:
                    for g_ in list(gens):
                        try:
                            next(g_)
                        except StopIteration:
                            gens.remove(g_)
                    yield

            def drive_keep(main, side, ratio):
                acc_ = 0.0
                for _ in main:
                    drive.n += 1
                    assert not any(S.uncommitted.values()), "yield inside an uncommitted group"
                    if side is not None:
                        acc_ += ratio
                        while acc_ >= 1.0 and side is not None:
                            acc_ -= 1.0
                            try:
                                next(side)
                            except StopIteration:
                                side = None
                return side

            def attn_stream(hp, Q, writer):
                kts = [(j * 128, 128, j, None) for j in range(2 * Q)]
                kts.append((2 * Q * 128, 128, 2 * Q, 0))
                kts.append(((2 * Q + 1) * 128, 128, 2 * Q + 1, 1))
                qb = Q % 2
                yield from attention(256, 128, kts, writer, QTs[qb], B(QTs[qb].name), 2 * Q)

            SEG_YIELDS = 25.0
            for hp in range(2):
                wsrc_name[0] = "wmyb%d" % hp
                load_unit_w(wmyb[hp])
                load_unit_params(upmy[hp:hp + 1, :], lwmy[hp], lg2my[hp])
                for hh in range(2):
                    S.op("dve", lambda e, hh=hh: e.memset(Hs[hh][0][:, :], 0.0), writes=[B(Hs[hh][0].name)])
                writer = make_writer(hp)
                run(stage1(hp, 0))
                side = None
                ratio = 0.0
                for i in range(NT):
                    if i % 2 == 1:
                        if side is not None:
                            run(side)
                        Q = i // 2
                        side = attn_stream(hp, Q, writer)
                        ratio = (2 * Q + 6) / (2 * SEG_YIELDS)
                    seg = interleave(stage2(hp, i, writer), stage1(hp, i + 1) if i + 1 < NT else None)
                    side = drive_keep(seg, side, ratio)
                if side is not None:
                    run(side)
                build_nc.main_yields = drive.n / float(NT) / (hp + 1)
                state_out(NT % 2, spo[hp])
                lastp = prw[(NT - 1) % 2]
                S.dma("sp", shpo[hp:hp + 1, :], lastp[127:128, :], reads=[B(lastp.name)])

            S.custom("pool", lambda e: e.collective_compute("ReduceScatter", ALU.add, replica_groups=[[0, 1, 2, 3], [4, 5, 6, 7]],
                                                             ins=[XI[:, :]], outs=[XO[:, :]]),
                     reads=[B("XI")], writes=[B("XO")])
            ckpt("rs")

            for u in range(8):
                cast_w(wub[u], wu[u], D, UC, "wub%d" % u)
            cast_w(wob, w_out, D, D, "wob")
            cast_w(wgb, w_gate, D, DFF, "wgb")
            cast_w(wupb, w_up, D, DFF, "wupb")
            cast_w(wdb, w_down, DFF, D, "wdb")
            ckpt("casts")

            ntile_s = (NB * 64 + 127) // 128
            for t_ in range(ntile_s):
                n_ = min(128, NB * 64 - t_ * 128)
                run(front(xsm[t_ * 128:t_ * 128 + n_, :], n_, xnTall, t_ * 128, XNB[t_], g1t, t_ % 2))
            kst = sb("kst", [128, 128]); kstb = sb("kstb", [128, 128], BF16)
            for u in range(8):
                wsrc_name[0] = "wub%d" % u
                load_unit_w(wub[u])
                load_unit_params(up[u:u + 1, :], lw[u], lg2[u])
                for bb in range(NB):
                    for pt in range(NP):
                        S.dma("sp", kst[:, :], ck[bb, pt * 128:(pt + 1) * 128, u, :], writes=[B("kst")])
                        S.op("dve", lambda e: e.tensor_copy(out=kstb[:, :], in_=kst[:, :]), reads=[B("kst")], writes=[B("kstb")])
                        pb = gbank()
                        pv = PS[pb][:].bitcast(BF16)
                        S.op("pe", lambda e, pv=pv: e.transpose(out=pv[:, 0:128], in_=kstb[:, :], identity=identb[:, :]),
                             reads=[B("kstb"), B("identb")], writes=[PB[pb]])
                        S.op("act", lambda e, pv=pv, pt=pt: e.activation(out=KT[:, pt * 128:(pt + 1) * 128], in_=pv[:, 0:128], func=AF.Copy),
                             reads=[PB[pb]], writes=[KTB[pt]])
                        S.dma("sp", kst[:, :], cv[bb, pt * 128:(pt + 1) * 128, u, :], writes=[B("kst")])
                        S.op("dve", lambda e, pt=pt: e.tensor_copy(out=VV[:, pt, :], in_=kst[:, :]), reads=[B("kst")], writes=[VB[pt]])
                    S.dma("sp", Hld[:, :, :], st0[bb, 2 * u:2 * u + 2].rearrange("h v k -> v h k"), writes=[B("Hld")])
                    pb = gbank()
                    for hh in range(2):
                        S.op("pe", lambda e, hh=hh: e.transpose(out=PS[pb][0:64, hh * 64:(hh + 1) * 64], in_=Hld[:, hh, :], identity=ident[0:64, 0:64]),
                             reads=[B("Hld"), B("ident")], writes=[PB[pb]] if hh == 0 else [], pwrites=[] if hh == 0 else [PB[pb]], inc=(hh == 1))
                    for hh in range(2):
                        S.op("act", lambda e, hh=hh: e.activation(out=Hs[hh][0][:, :], in_=PS[pb][0:64, hh * 64:(hh + 1) * 64], func=AF.Copy),
                             reads=[PB[pb]], writes=[B(Hs[hh][0].name)])
                    S.op("dve", lambda e: e.memset(shbuf[0:63, :], 0.0), writes=[B("shbuf")])
                    S.dma("sp", shbuf[63:64, :], sh0[bb, u:u + 1, :], reads=[], writes=[], pwrites=[B("shbuf")])
                    run(project(xnTall, bb * 64, XNB[bb // 2], 64, 0))
                    run(attn_prep(64, csst[:, :], B("csst"), kso[bb, :, u, :], vso[bb, :, u, :], NP, NP * 128, 0, QTs[0], B(QTs[0].name)))

                    def cat_writer_s(kind, qt_, src_ap, srcB, u=u, bb=bb):
                        chunk = u + (8 if kind == 1 else 0)
                        ct = catA if kind == 0 else catB
                        cB = B(ct.name)
                        S.op("act", lambda e: e.activation(out=ct[:, 0, 0:64], in_=src_ap, func=AF.Copy), reads=[srcB], writes=[cB])
                        S.dma("sp", XS[chunk * 128:(chunk + 1) * 128, bb * 64:(bb + 1) * 64], ct[:, 0, 0:64], reads=[cB], pwrites=[B("XS")])

                    run(rwkv(64, 0, shbuf[0:64, :], B("shbuf"), c64[:, :], False, 0, cat_writer_s, 0))
                    kts = [(j * 128, 128, j, None) for j in range(NP)] + [(NP * 128, 64, NP, None)]
                    run(attention(64, 64, kts, cat_writer_s, QTs[0], B(QTs[0].name), 0))
                    state_out(1, sso[bb, u])
                    S.dma("sp", shso[bb, u:u + 1, :], prw[0][63:64, :], reads=[B(prw[0].name)])

            ckpt("sample")
            S.barrier()
            es1.close()
            cur[0] = es
            NTOK = 512
            catT = sb("catT", [128, NCH, NTOK], BF16)
            hsb = sb("hsb", [128, NTOK // 128, D])
            hnT = sb("hnT", [128, NCH, NTOK], BF16)
            actT = sb("actT", [128, NFF, NTOK], BF16)
            wring = [sb("wring%d" % i, [128, 16 * 512], BF16) for i in range(3)]
            wrr = [0]
            sg = sb("sg", [128, 512])

            def wbuf():
                i = wrr[0]
                wrr[0] = (wrr[0] + 1) % 3
                return wring[i]

            blocks = []
            t0 = 0
            while t0 < TR:
                nb_ = min(NTOK, TR - t0)
                blocks.append(("p", t0, nb_))
                t0 += nb_
            t0 = 0
            while t0 < NB * 64:
                nb_ = min(NTOK, NB * 64 - t0)
                blocks.append(("s", t0, nb_))
                t0 += nb_
            for (kind, t0, nb_) in blocks:
                src = XO if kind == "p" else XS
                srcB = B("XO") if kind == "p" else B("XS")
                xsrc = xr if kind == "p" else xsm
                ydst = yp if kind == "p" else ys
                S.dma("sp", catT[:, :, 0:nb_], src[:, t0:t0 + nb_].rearrange("(c p) n -> p c n", p=128), reads=[srcB], writes=[B("catT")])
                ntl = (nb_ + 127) // 128
                tls = [(tt * 128, min(128, nb_ - tt * 128)) for tt in range(ntl)]
                for tt, (o_, n_) in enumerate(tls):
                    S.dma("sp", hsb[:n_, tt, :], xsrc[t0 + o_:t0 + o_ + n_, :], writes=[B("hsb%d" % tt)])
                for ng in range(4):
                    wb = wbuf(); wB = B(wb.name)
                    w3 = wb[:, :].rearrange("p (c n) -> p c n", n=512)
                    S.dma("sp", w3, wob[:, ng * 512:(ng + 1) * 512].rearrange("(c p) n -> p c n", p=128), reads=[B("wob")], writes=[wB])
                    for tt, (o_, n_) in enumerate(tls):
                        pb = tt % 4
                        for k in range(NCH):
                            S.op("pe", lambda e, k=k, pb=pb, o_=o_, n_=n_: e.matmul(PS[pb][:n_, 0:512], lhsT=catT[:, k, o_:o_ + n_], rhs=w3[:, k, :],
                                                                                  start=(k == 0), stop=(k == NCH - 1), skip_group_check=True),
                                 reads=[B("catT"), wB], writes=[PB[pb]] if k == 0 else [], pwrites=[] if k == 0 else [PB[pb]], inc=(k == NCH - 1))
                        S.op("dve", lambda e, tt=tt, pb=pb, n_=n_, ng=ng: e.tensor_tensor(out=hsb[:n_, tt, ng * 512:(ng + 1) * 512], in0=PS[pb][:n_, 0:512],
                                                                                         in1=hsb[:n_, tt, ng * 512:(ng + 1) * 512], op=ALU.add),
                             reads=[PB[pb], B("hsb%d" % tt)], writes=[B("hsb%d" % tt)])
                for tt, (o_, n_) in enumerate(tls):
                    hB = B("hsb%d" % tt)
                    S.op("dve", lambda e, tt=tt, n_=n_: e.scalar_tensor_tensor(out=junk[:n_, :], in0=hsb[:n_, tt, :], scalar=1.0, in1=hsb[:n_, tt, :], op0=ALU.mult, op1=ALU.mult, accum_out=st4[:n_, 0:1]),
                         reads=[hB], writes=[B("junk"), B("st4")])
                    rstd_chain(st4[:n_, 0:1], st4[:n_, 0:1], n_, 1, 1.0 / D, RMS_EPS, B("st4"), B("st4"))
                    S.op("act", lambda e, tt=tt, n_=n_: e.activation(out=xsb[:n_, :], in_=hsb[:n_, tt, :], func=AF.Copy, scale=st4[:n_, 0:1]),
                         reads=[hB, B("st4")], writes=[B("xsb")])
                    for half in range(2):
                        pb = 4 + half
                        pv = PS[pb][:].bitcast(BF16)
                        for c8 in range(8):
                            c = half * 8 + c8
                            S.op("pe", lambda e, c=c, c8=c8, pv=pv, n_=n_: e.transpose(out=pv[:, c8 * n_:(c8 + 1) * n_], in_=xsb[:n_, c * 128:(c + 1) * 128],
                                                                                     identity=identb[:n_, :n_]),
                                 reads=[B("xsb"), B("identb")], writes=[PB[pb]] if c8 == 0 else [], pwrites=[] if c8 == 0 else [PB[pb]], inc=(c8 == 7))
                        S.op("dve", lambda e, half=half, pv=pv, n_=n_, o_=o_: e.tensor_tensor(
                            out=hnT[:, half * 8:(half + 1) * 8, o_:o_ + n_], in0=pv[:, 0:8 * n_].rearrange("p (c n) -> p c n", n=n_),
                            in1=bc3(g2t[:, half * 8:(half + 1) * 8], 128, 8, n_), op=ALU.mult),
                            reads=[PB[pb], B("g2t")], writes=[B("hnT")] if (tt == 0 and half == 0) else [], pwrites=[] if (tt == 0 and half == 0) else [B("hnT")])
                for fg in range(NFF // 4):
                    wg_ = wbuf(); wgB = B(wg_.name)
                    wg3 = wg_[:, :].rearrange("p (c n) -> p c n", n=512)
                    S.dma("sp", wg3, wgb[:, fg * 512:(fg + 1) * 512].rearrange("(c p) n -> p c n", p=128), reads=[B("wgb")], writes=[wgB])
                    wu_ = wbuf(); wuB = B(wu_.name)
                    wu3 = wu_[:, :].rearrange("p (c n) -> p c n", n=512)
                    S.dma("sp", wu3, wupb[:, fg * 512:(fg + 1) * 512].rearrange("(c p) n -> p c n", p=128), reads=[B("wupb")], writes=[wuB])
                    for f4 in range(4):
                        f = fg * 4 + f4
                        pg_, pu_ = (f % 2) * 2, (f % 2) * 2 + 1
                        for k in range(NCH):
                            S.op("pe", lambda e, k=k, pg_=pg_, f4=f4: e.matmul(PS[pg_][:, 0:nb_], lhsT=wg3[:, k, f4 * 128:(f4 + 1) * 128], rhs=hnT[:, k, 0:nb_],
                                                                             start=(k == 0), stop=(k == NCH - 1), skip_group_check=True),
                                 reads=[B("hnT"), wgB], writes=[PB[pg_]] if k == 0 else [], pwrites=[] if k == 0 else [PB[pg_]], inc=(k == NCH - 1))
                        for k in range(NCH):
                            S.op("pe", lambda e, k=k, pu_=pu_, f4=f4: e.matmul(PS[pu_][:, 0:nb_], lhsT=wu3[:, k, f4 * 128:(f4 + 1) * 128], rhs=hnT[:, k, 0:nb_],
                                                                             start=(k == 0), stop=(k == NCH - 1), skip_group_check=True),
                                 reads=[B("hnT"), wuB], writes=[PB[pu_]] if k == 0 else [], pwrites=[] if k == 0 else [PB[pu_]], inc=(k == NCH - 1))
                        S.op("act", lambda e, pg_=pg_: e.activation(out=sg[:, 0:nb_], in_=PS[pg_][:, 0:nb_], func=AF.Silu), reads=[PB[pg_]], writes=[B("sg")])
                        S.op("dve", lambda e, pu_=pu_, f=f: e.tensor_tensor(out=actT[:, f, 0:nb_], in0=PS[pu_][:, 0:nb_], in1=sg[:, 0:nb_], op=ALU.mult),
                             reads=[PB[pu_], B("sg")], writes=[B("actT")] if f == 0 else [], pwrites=[] if f == 0 else [B("actT")])
                for ng in range(4):
                    for kq in range(4):
                        wb = wbuf(); wB = B(wb.name)
                        w3 = wb[:, 0:11 * 512].rearrange("p (c n) -> p c n", n=512)
                        S.dma("sp", w3, wdb[kq * 11 * 128:(kq + 1) * 11 * 128, ng * 512:(ng + 1) * 512].rearrange("(c p) n -> p c n", p=128),
                              reads=[B("wdb")], writes=[wB])
                        for tt, (o_, n_) in enumerate(tls):
                            pb = 4 + tt
                            for kc in range(11):
                                kk_ = kq * 11 + kc
                                first = (kk_ == 0); last = (kk_ == NFF - 1)
                                S.op("pe", lambda e, kc=kc, kk_=kk_, pb=pb, o_=o_, n_=n_, first=first, last=last: e.matmul(
                                    PS[pb][:n_, 0:512], lhsT=actT[:, kk_, o_:o_ + n_], rhs=w3[:, kc, :], start=first, stop=last, skip_group_check=True),
                                     reads=[B("actT"), wB], writes=[PB[pb]] if first else [], pwrites=[] if first else [PB[pb]], inc=(kc == 10))
                    for tt, (o_, n_) in enumerate(tls):
                        pb = 4 + tt
                        S.op("dve", lambda e, tt=tt, pb=pb, n_=n_, ng=ng: e.tensor_tensor(out=hsb[:n_, tt, ng * 512:(ng + 1) * 512], in0=PS[pb][:n_, 0:512],
                                                                                         in1=hsb[:n_, tt, ng * 512:(ng + 1) * 512], op=ALU.add),
                             reads=[PB[pb], B("hsb%d" % tt)], writes=[B("hsb%d" % tt)])
                for tt, (o_, n_) in enumerate(tls):
                    S.dma("sp", ydst[t0 + o_:t0 + o_ + n_, :], hsb[:n_, tt, :], reads=[B("hsb%d" % tt)])

        except _Stop:
            es1.close()
        for s, c in S.all_dma():
            nc.sync.wait_ge(s, c)
        for e in ("pe", "act", "dve", "pool"):
            if S.cnt[e] > 0:
                nc.sync.wait_ge(S.esem[e], S.cnt[e])
        build_nc.nops = S.nops
    return nc


def _unit_cols(u):
    A = 3072
    q = np.arange(128) + u * 128
    k = 1024 + q
    v = 2048 + q
    r = A + u * 128 + np.arange(128)
    kr = A + 1024 + u * 128 + np.arange(128)
    vr = A + 2048 + u * 128 + np.arange(128)
    lo = A + 3072 + np.arange(192)
    return np.concatenate([q, k, v, r, kr, vr, lo])


def _rw_cols(u):
    r = u * 128 + np.arange(128)
    return np.concatenate([r, 1024 + r, 2048 + r, 3072 + np.arange(192)])


def _consts(T):
    c = {}
    idx = np.arange(128)
    c["cid"] = np.eye(128, dtype=np.float32)
    c["cut"] = (idx[:, None] <= idx[None, :]).astype(np.float32)
    c["cs1"] = (idx[:, None] + 1 == idx[None, :]).astype(np.float32)
    cc = np.zeros((128, 128), np.float32); cc[127, 0] = 1.0
    c["cc128"] = cc
    cc = np.zeros((64, 64), np.float32); cc[63, 0] = 1.0
    c["cc64"] = cc
    mA = (idx[:, None] < idx[None, :]).astype(np.float32)
    mL = (idx[:, None] <= idx[None, :]).astype(np.float32)
    c["cgm"] = np.concatenate([mA, mL, mA, mL], axis=1)
    c["clow"] = (idx[None, :] < idx[:, None]).astype(np.float32)
    cm = np.ones((128, 128), np.float32); cm[64:, :64] = 0.0
    on = np.ones((128, 128), np.float32); ze = np.zeros((128, 128), np.float32)
    d0 = np.concatenate([cm, on], axis=1); d1 = np.concatenate([ze, cm], axis=1)
    c["cam"] = np.stack([np.concatenate([d0, d0], axis=1), np.concatenate([d1, d1], axis=1)]).astype(np.float32)
    inv = (np.float32(ROPE_THETA) ** (-np.arange(0, 16, 2, dtype=np.float32) / np.float32(16))).astype(np.float32)

    def tab(pos):
        ang = (pos.astype(np.float32)[:, None] * inv[None, :]).astype(np.float32)
        return np.concatenate([np.cos(ang), np.sin(ang)], axis=1).astype(np.float32)
    c["csp"] = tab(np.arange(T))
    return c, tab


_NC_CACHE = {}


def kernel(x_prompt, x_sample, cache_attn_k, cache_attn_v, state_rwkv, state_rwkv_shift,
           norm1_g, w_in, q_norm_g, k_norm_g, lambda_q1, lambda_k1, lambda_q2, lambda_k2, subln_g,
           mu_rwkv, w0, w2, a0, a2, g2, k_k, k_a, r_k, lnx_g, lnx_b,
           w_out, norm2_g, w_gate, w_up, w_down):
    f = lambda a: np.ascontiguousarray(np.asarray(a, dtype=np.float32))
    x_prompt, x_sample = f(x_prompt), f(x_sample)
    Bp, T, _ = x_prompt.shape
    Bs, Ts, _ = x_sample.shape
    PAST = cache_attn_k.shape[2]
    NB = Bs // 8
    assert Bp == 2 and Ts == 64
    dm = Dims(T, NB, PAST)
    key = (T, NB, PAST)
    if key not in _NC_CACHE:
        _NC_CACHE[key] = build_nc(dm)
    nc = _NC_CACHE[key]
    TR = T // 4
    w_in0 = f(w_in)[0]
    wu = np.stack([w_in0[:, _unit_cols(u)] for u in range(8)])
    mu0 = f(mu_rwkv)[0]; w00 = f(w0)[0]; a00 = f(a0)[0]; kk0 = f(k_k)[0]; ka0 = f(k_a)[0]
    rk0 = f(r_k)[0].reshape(-1); lg0 = f(lnx_g)[0]; lb0 = f(lnx_b)[0]
    w20 = f(w2)[0]; a20 = f(a2)[0]; g20 = f(g2)[0]
    ups, lws, lg2s = [], [], []
    for u in range(8):
        hs = slice(u * 128, (u + 1) * 128)
        ups.append(np.concatenate([mu0[_rw_cols(u)], w00[hs], a00[hs], kk0[hs], ka0[hs], rk0[hs], lg0[hs], lb0[hs]]))
        lws.append(np.concatenate([w20[:, hs], a20[:, hs]], axis=0))
        lg2s.append(g20[:, hs])
    up = np.stack(ups).astype(np.float32); lw = np.stack(lws).astype(np.float32); lg2_ = np.stack(lg2s).astype(np.float32)
    consts, tab = _consts(T)
    consts["css"] = tab(PAST + np.arange(64))
    common = dict(
        wu=wu, up=up, lw=lw, lg2=lg2_,
        g1T=np.ascontiguousarray(f(norm1_g)[0].reshape(16, 128).T), g2T=np.ascontiguousarray(f(norm2_g)[0].reshape(16, 128).T),
        qkg=np.concatenate([f(q_norm_g)[0].reshape(-1), f(k_norm_g)[0].reshape(-1)])[None, :],
        lamv=np.concatenate([f(lambda_q1)[0], f(lambda_k1)[0], f(lambda_q2)[0], f(lambda_k2)[0]])[None, :],
        subg=f(subln_g)[0][None, :],
        w_out=f(w_out)[0], w_gate=f(w_gate)[0], w_up=f(w_up)[0], w_down=f(w_down)[0], **consts)
    ck = f(cache_attn_k)[0]; cv = f(cache_attn_v)[0]; st = f(state_rwkv)[0]; sh = f(state_rwkv_shift)[0][:, 0, :]
    sh_u = np.stack([sh[:, _rw_cols(u)] for u in range(8)], axis=1)
    in_maps = []
    for c in range(8):
        b, j = c // 4, c % 4
        selm = np.zeros((128, 4), np.float32); selm[:, j] = 1.0
        m = dict(common)
        m.update(xb=x_prompt[b], xr=np.ascontiguousarray(x_prompt[b, j * TR:(j + 1) * TR]),
                 xsm=np.ascontiguousarray(x_sample[c * NB:(c + 1) * NB].reshape(NB * 64, D)),
                 ck=np.ascontiguousarray(ck[c * NB:(c + 1) * NB]), cv=np.ascontiguousarray(cv[c * NB:(c + 1) * NB]),
                 st0=np.ascontiguousarray(st[c * NB:(c + 1) * NB]), sh0=np.ascontiguousarray(sh_u[c * NB:(c + 1) * NB]),
                 wmy=np.ascontiguousarray(wu[2 * j:2 * j + 2]), upmy=np.ascontiguousarray(up[2 * j:2 * j + 2]),
                 lwmy=np.ascontiguousarray(lw[2 * j:2 * j + 2]), lg2my=np.ascontiguousarray(lg2_[2 * j:2 * j + 2]), sel=selm)
        in_maps.append({k: np.ascontiguousarray(v, dtype=np.float32) for k, v in m.items()})
    res = run_bass_kernel_spmd(nc, in_maps, core_ids=list(range(8)))
    R = res.results
    y_p = np.zeros((2, T, D), np.float32); y_s = np.zeros((Bs, 64, D), np.float32)
    k_p = np.zeros((1, 2, T, 8, 128), np.float32); v_p = np.zeros((1, 2, T, 8, 128), np.float32)
    S_p = np.zeros((1, 2, 16, 64, 64), np.float32); sh_p = np.zeros((1, 2, 1, 3264), np.float32)
    k_s = np.zeros((1, Bs, 64, 8, 128), np.float32); v_s = np.zeros((1, Bs, 64, 8, 128), np.float32)
    S_s = np.zeros((1, Bs, 16, 64, 64), np.float32); sh_s = np.zeros((1, Bs, 1, 3264), np.float32)
    for c in range(8):
        b, j = c // 4, c % 4
        r = R[c]
        y_p[b, j * TR:(j + 1) * TR] = r["yp"]
        y_s[c * NB:(c + 1) * NB] = np.asarray(r["ys"]).reshape(NB, 64, D)
        for hp in range(2):
            u = 2 * j + hp
            k_p[0, b, :, u, :] = r["kpo"][:, hp, :]
            v_p[0, b, :, u, :] = r["vpo"][:, hp, :]
            S_p[0, b, 2 * u:2 * u + 2] = r["spo"][hp]
            sh_p[0, b, 0, _rw_cols(u)] = r["shpo"][hp]
        k_s[0, c * NB:(c + 1) * NB] = r["kso"]
        v_s[0, c * NB:(c + 1) * NB] = r["vso"]
        S_s[0, c * NB:(c + 1) * NB] = np.asarray(r["sso"]).reshape(NB, 16, 64, 64)
        for u in range(8):
            sh_s[0, c * NB:(c + 1) * NB, 0, _rw_cols(u)] = np.asarray(r["shso"])[:, u, :].T if False else 0
        shso = np.asarray(r["shso"])
        for u in range(8):
            for bb in range(NB):
                sh_s[0, c * NB + bb, 0, _rw_cols(u)] = shso[bb, u]
    return (y_p, y_s, k_p, v_p, S_p, sh_p, k_s, v_s, S_s, sh_s)
```

```python
import math
import contextlib
import numpy as np
import concourse.bass as bass
import concourse.mybir as mybir
from concourse.bass_utils import run_bass_kernel_spmd

F32 = mybir.dt.float32
BF16 = mybir.dt.bfloat16
AF = mybir.ActivationFunctionType
ALU = mybir.AluOpType
AX = mybir.AxisListType

D = 2048
NCH = 16
UC = 960
DFF = 5632
NFF = 44
RMS_EPS = 1e-6
GN_EPS = 64e-5
LAM_INIT = 0.2
NPAR = 1472
ROPE_THETA = 500000.0


class _Stop(Exception):
    pass


def ckpt(name):
    import os
    if os.environ.get("KSTOP") == name:
        raise _Stop()


class Buf:
    __slots__ = ("name", "w", "r", "excl")

    def __init__(self, name, excl=False):
        self.name = name
        self.w = {}
        self.r = {}
        self.excl = excl


class Sched:
    LIMIT = 30000

    def __init__(self, nc, es, n_dma_sems=40):
        self.nc = nc
        self.es = es
        self.eng = {"pe": nc.tensor, "act": nc.scalar, "dve": nc.vector, "pool": nc.gpsimd, "sp": nc.sync}
        self.esem = {}
        self.cnt = {}
        self.seen = {e: {} for e in self.eng}
        self.uncommitted = {e: False for e in self.eng}
        self.nsem = 0
        for e in ("pe", "act", "dve", "pool"):
            self._new_esem(e)
        self.rings = {}
        for q, nq_ in (("sp", 30), ("pool", 12), ("cc", 1)):
            self.rings[q] = dict(sem=[self._sem("dma_%s%d" % (q, i)) for i in range(nq_)], cnt=[0] * nq_, nxt=0)
        self.nops = 0

    def _sem(self, name):
        self.nsem += 1
        return self.es.enter_context(self.nc.semaphore(name))

    def _new_esem(self, e):
        self.esem[e] = self._sem("e_%s_%d" % (e, self.nsem))
        self.cnt[e] = 0

    def _collect(self, reads, writes, pwrites):
        deps = {}

        def add(d):
            for k, (s, v) in d.items():
                if k not in deps or deps[k][1] < v:
                    deps[k] = (s, v)
        for b in reads:
            add(b.w)
            if b.excl:
                add(b.r)
        for b in writes:
            add(b.w)
            add(b.r)
        for b in pwrites:
            add(b.r)
        return deps

    def _emit_waits(self, e, deps):
        eng = self.eng[e]
        seen = self.seen[e]
        for k, (s, v) in deps.items():
            if seen.get(k, 0) >= v:
                continue
            if e in self.esem and s is self.esem[e] and v > self.cnt[e]:
                continue
            for e2 in self.esem:
                if s is self.esem[e2] and v > self.cnt[e2]:
                    raise RuntimeError("wait on uncommitted event of %s from %s" % (e2, e))
            eng.wait_ge(s, v)
            seen[k] = v

    def _record(self, ev, reads, writes, pwrites):
        k = id(ev[0])
        for b in reads:
            if k not in b.r or b.r[k][1] < ev[1]:
                b.r[k] = ev
        for b in writes:
            b.w = {k: ev}
            b.r = {}
        for b in pwrites:
            if k not in b.w or b.w[k][1] < ev[1]:
                b.w[k] = ev

    def op(self, e, fn, reads=(), writes=(), inc=True, pwrites=()):
        self.nops += 1
        if self.cnt[e] >= self.LIMIT and not self.uncommitted[e]:
            self._new_esem(e)
        deps = self._collect(reads, writes, pwrites)
        self._emit_waits(e, deps)
        ins = fn(self.eng[e])
        ev = (self.esem[e], self.cnt[e] + 1)
        if inc:
            self.cnt[e] += 1
            ins.then_inc(self.esem[e], 1)
            self.uncommitted[e] = False
        else:
            self.uncommitted[e] = True
        self._record(ev, reads, writes, pwrites)
        return ins

    def _ring_next(self, q):
        r = self.rings[q]
        i = r["nxt"]
        r["nxt"] = (i + 1) % len(r["sem"])
        return r, i

    def dma(self, q, out, in_, reads=(), writes=(), pwrites=(), **kw):
        self.nops += 1
        r, i = self._ring_next(q)
        s = r["sem"][i]
        deps = self._collect(reads, writes, pwrites)
        if r["cnt"][i] > 0:
            deps[id(s)] = (s, r["cnt"][i])
        self._emit_waits(q, deps)
        ins = self.eng[q].dma_start(out=out, in_=in_, **kw)
        r["cnt"][i] += 16
        ins.then_inc(s, 16)
        ev = (s, r["cnt"][i])
        self._record(ev, reads, writes, pwrites)
        return ev

    def custom(self, q, fn, reads=(), writes=(), pwrites=()):
        r, i = self._ring_next("cc")
        s = r["sem"][i]
        deps = self._collect(reads, writes, pwrites)
        if r["cnt"][i] > 0:
            deps[id(s)] = (s, r["cnt"][i])
        self._emit_waits(q, deps)
        ins = fn(self.eng[q])
        r["cnt"][i] += 1
        ins.then_inc(s, 1)
        ev = (s, r["cnt"][i])
        self._record(ev, reads, writes, pwrites)
        return ev

    def all_dma(self):
        for r in self.rings.values():
            for s, c in zip(r["sem"], r["cnt"]):
                if c > 0:
                    yield s, c

    def barrier(self):
        for e in self.eng:
            for e2 in self.esem:
                if self.cnt[e2] > 0 and self.seen[e].get(id(self.esem[e2]), 0) < self.cnt[e2]:
                    self.eng[e].wait_ge(self.esem[e2], self.cnt[e2])
                    self.seen[e][id(self.esem[e2])] = self.cnt[e2]
            for s, c in self.all_dma():
                if self.seen[e].get(id(s), 0) < c:
                    self.eng[e].wait_ge(s, c)
                    self.seen[e][id(s)] = c

    def wait_all(self, q, bufs):
        deps = self._collect(bufs, bufs, ())
        self._emit_waits(q, deps)


class Dims:
    def __init__(self, T, NB, PAST):
        self.T = T
        self.NB = NB
        self.PAST = PAST
        self.NT = T // 128
        self.TR = T // 4
        self.NP = PAST // 128


def build_nc(dm):
    T, NB, PAST, NT, TR, NP = dm.T, dm.NB, dm.PAST, dm.NT, dm.TR, dm.NP
    nc = bass.Bass("TRN2", target_bir_lowering=False)

    def din(name, shape, dt=F32):
        return nc.dram_tensor(name, list(shape), dt, kind="ExternalInput").ap()

    def dout(name, shape, dt=F32):
        return nc.dram_tensor(name, list(shape), dt, kind="ExternalOutput").ap()

    def dint(name, shape, dt):
        return nc.dram_tensor(name, list(shape), dt, kind="Internal").ap()

    xb = din("xb", [T, D])
    xr = din("xr", [TR, D])
    xsm = din("xsm", [NB * 64, D])
    ck = din("ck", [NB, PAST, 8, 128])
    cv = din("cv", [NB, PAST, 8, 128])
    st0 = din("st0", [NB, 16, 64, 64])
    sh0 = din("sh0", [NB, 8, 576])
    wu = din("wu", [8, D, UC])
    wmy = din("wmy", [2, D, UC])
    up = din("up", [8, NPAR])
    upmy = din("upmy", [2, NPAR])
    lw = din("lw", [8, 128, 128])
    lwmy = din("lwmy", [2, 128, 128])
    lg2 = din("lg2", [8, 64, 128])
    lg2my = din("lg2my", [2, 64, 128])
    g1T = din("g1T", [128, NCH])
    g2T = din("g2T", [128, NCH])
    qkg = din("qkg", [1, 256])
    lamv = din("lamv", [1, 256])
    subg = din("subg", [1, 128])
    sel = din("sel", [128, 4])
    w_out = din("w_out", [D, D])
    w_gate = din("w_gate", [D, DFF])
    w_up = din("w_up", [D, DFF])
    w_down = din("w_down", [DFF, D])
    csp = din("csp", [T, 16])
    css = din("css", [64, 16])
    cid = din("cid", [128, 128])
    cut = din("cut", [128, 128])
    cs1 = din("cs1", [128, 128])
    cc128 = din("cc128", [128, 128])
    cc64 = din("cc64", [64, 64])
    cgm = din("cgm", [128, 512])
    clow = din("clow", [128, 128])
    cam = din("cam", [2, 128, 512])
    yp = dout("yp", [TR, D])
    ys = dout("ys", [NB * 64, D])
    kpo = dout("kpo", [T, 2, 128])
    vpo = dout("vpo", [T, 2, 128])
    spo = dout("spo", [2, 2, 64, 64])
    shpo = dout("shpo", [2, 576])
    kso = dout("kso", [NB, 64, 8, 128])
    vso = dout("vso", [NB, 64, 8, 128])
    sso = dout("sso", [NB, 8, 2, 64, 64])
    shso = dout("shso", [NB, 8, 576])
    XI = dint("XI", [4 * 16 * 128, TR], BF16)
    XO = dint("XO", [16 * 128, TR], BF16)
    XS = dint("XS", [16 * 128, NB * 64], BF16)
    wub = dint("wub", [8, D, UC], BF16)
    wmyb = dint("wmyb", [2, D, UC], BF16)
    wob = dint("wob", [D, D], BF16)
    wgb = dint("wgb", [D, DFF], BF16)
    wupb = dint("wupb", [D, DFF], BF16)
    wdb = dint("wdb", [DFF, D], BF16)

    es = contextlib.ExitStack()
    with es:
        S = Sched(nc, es)
        bufs = {}

        es1 = contextlib.ExitStack()
        cur = [es]

        def sb(name, shape, dt=F32):
            t = cur[0].enter_context(nc.sbuf_tensor(name, list(shape), dt))
            bufs[name] = Buf(name)
            return t

        def B(name):
            if name not in bufs:
                bufs[name] = Buf(name)
            return bufs[name]

        PS = [es.enter_context(nc.psum_tensor("ps%d" % i, [128, 512], F32)) for i in range(8)]
        PB = [Buf("ps%d" % i, excl=True) for i in range(8)]
        gen_rr = [0]

        def gbank():
            i = gen_rr[0]
            gen_rr[0] = (gen_rr[0] + 1) % 4
            return i

        try:
            def cast_w(dst, src, rows, cols, bname):
                b = B(bname)
                r = 0
                while r < rows:
                    rr = min(128, rows - r)
                    c = 0
                    while c < cols:
                        cc = min(2048, cols - c)
                        S.dma("pool", dst[r:r + rr, c:c + cc], src[r:r + rr, c:c + cc], pwrites=[b])
                        c += cc
                    r += rr

            for u in range(2):
                cast_w(wmyb[u], wmy[u], D, UC, "wmyb%d" % u)
            for u in range(8):
                cast_w(wub[u], wu[u], D, UC, "wub%d" % u)
            cast_w(wob, w_out, D, D, "wob")
            cast_w(wgb, w_gate, D, DFF, "wgb")
            cast_w(wupb, w_up, D, DFF, "wupb")
            cast_w(wdb, w_down, DFF, D, "wdb")

            ident = sb("ident", [128, 128]); identb = sb("identb", [128, 128], BF16)
            ut = sb("ut", [128, 128]); s1m = sb("s1m", [128, 128]); c128 = sb("c128", [128, 128]); c64 = sb("c64", [64, 64])
            ones = sb("ones", [128, 128]); onesb = sb("onesb", [128, 1], BF16)
            gmask = sb("gmask", [128, 512]); lowm = sb("lowm", [128, 128])
            amask = sb("amask", [128, 2, 512], BF16)
            g1t = sb("g1t", [128, NCH]); g2t = sb("g2t", [128, NCH])
            qkgb = sb("qkgb", [128, 256]); lamb = sb("lamb", [128, 256]); subgb = sb("subgb", [128, 128])
            selt = sb("selt", [128, 4])
            cstt = [sb("cstt%d" % i, [128, 16]) for i in range(2)]; csst = sb("csst", [64, 16])
            lamt = sb("lamt", [128, 4])
            neglam = sb("neglam", [128, 1])
            for t_, src in ((ident, cid), (ut, cut), (s1m, cs1), (c128, cc128), (gmask, cgm), (lowm, clow),
                            (g1t, g1T), (g2t, g2T), (selt, sel)):
                S.dma("sp", t_[:], src[:, :], writes=[B(t_.name)])
            S.dma("sp", c64[:], cc64[:, :], writes=[B("c64")])
            S.dma("sp", csst[:], css[:, :], writes=[B("csst")])
            S.dma("sp", qkgb[:], qkg.partition_broadcast(128), writes=[B("qkgb")])
            S.dma("sp", lamb[:], lamv.partition_broadcast(128), writes=[B("lamb")])
            S.dma("sp", subgb[:], subg.partition_broadcast(128), writes=[B("subgb")])
            S.op("dve", lambda e: e.memset(ones[:], 1.0), writes=[B("ones")])
            S.op("dve", lambda e: e.memset(onesb[:], 1.0), writes=[B("onesb")])
            S.op("dve", lambda e: e.tensor_copy(out=identb[:], in_=ident[:]), reads=[B("ident")], writes=[B("identb")])
            S.dma("pool", amask[:], cam.rearrange("a p c -> p a c"), writes=[B("amask")])
            S.op("dve", lambda e: e.tensor_scalar(out=subgb[:], in0=subgb[:], scalar1=1.0 - LAM_INIT, scalar2=None,
                                                  op0=ALU.mult), reads=[B("subgb")], writes=[B("subgb")])
            lscr = sb("lscr", [128, 64])
            S.op("dve", lambda e: e.scalar_tensor_tensor(out=lscr[:], in0=lamb[:, 0:64], scalar=1.0, in1=lamb[:, 64:128], op0=ALU.mult, op1=ALU.mult, accum_out=lamt[:, 0:1]),
                 reads=[B("lamb")], writes=[B("lscr"), B("lamt")])
            S.op("dve", lambda e: e.scalar_tensor_tensor(out=lscr[:], in0=lamb[:, 128:192], scalar=1.0, in1=lamb[:, 192:256], op0=ALU.mult, op1=ALU.mult, accum_out=lamt[:, 1:2]),
                 reads=[B("lamb"), B("lamt")], writes=[B("lscr"), B("lamt")])
            S.op("act", lambda e: e.activation(out=lamt[:, 2:4], in_=lamt[:, 0:2], func=AF.Exp),
                 reads=[B("lamt")], writes=[B("lamt")])
            S.op("dve", lambda e: e.tensor_tensor(out=neglam[:], in0=lamt[:, 3:4], in1=lamt[:, 2:3], op=ALU.subtract),
                 reads=[B("lamt")], writes=[B("neglam")])
            S.op("dve", lambda e: e.tensor_scalar(out=neglam[:], in0=neglam[:], scalar1=-LAM_INIT, scalar2=None, op0=ALU.add),
                 reads=[B("neglam")], writes=[B("neglam")])
            ckpt("consts")

            NKT = max(NT, 2 * (NP + 1))
            xsb = sb("xsb", [128, D], BF16)
            junk = sb("junk", [128, D], BF16)
            st4 = sb("st4", [128, 8])
            cur[0] = es1
            wub_sb = sb("wub_sb", [128, NCH, UC], BF16)
            KT = sb("KT", [128, NKT * 128], BF16)
            VV = sb("VV", [128, NKT, 128], BF16)
            KTB = [Buf("KT%d" % i) for i in range(NKT)]
            VB = [Buf("V%d" % i) for i in range(NKT)]
            xt = [sb("xt%d" % i, [128, D]) for i in range(2)]
            xnTall = sb("xnTall", [128, NCH, 256], BF16)
            XNB = [Buf("xnTslot0"), Buf("xnTslot1")]
            gsb = sb("gsb", [128, 384])
            prw = [sb("prw%d" % i, [128, 576]) for i in range(2)]
            shbuf = sb("shbuf", [128, 576])
            tq = sb("tq", [128, 256]); qkn = sb("qkn", [128, 256]); rtmp = sb("rtmp", [128, 4, 4, 8])
            qkb = sb("qkb", [128, 256], BF16)
            QTs = [sb("QT%d" % i, [128, 2, 256], BF16) for i in range(2)]
            Eb = [sb("Eb%d" % i, [128, 512], BF16) for i in range(2)]
            OT = sb("OT", [128, 512]); zsb = sb("zsb", [1, 512]); Zacc = sb("Zacc", [128, 512]); ajunk = sb("ajunk", [128, 128], BF16); one11 = sb("one11", [1, 1])
            S.op("dve", lambda e: e.memset(one11[:], 1.0), writes=[B("one11")])
            for q_ in QTs:
                S.op("dve", lambda e, q_=q_: e.memset(q_[:, :, :], 0.0), writes=[B(q_.name)])
            osb = sb("osb", [128, 128]); onb = sb("onb", [128, 128], BF16); ast = sb("ast", [128, 8])
            catA = sb("catA", [128, 4, 128], BF16); catB = sb("catB", [128, 4, 128], BF16)
            parb = sb("parb", [128, NPAR])
            lwt = sb("lwt", [64, 2, 128]); lg2t = sb("lg2t", [64, 128])
            xs_ = sb("xs_", [128, 576])
            Et = sb("Et", [128, 128]); LT = sb("LT", [128, 192]); LTT = sb("LTT", [64, 3, 128])
            za = sb("za", [128, 256]); sa = sb("sa", [128, 256]); ld = sb("ld", [128, 128]); g_sb = sb("g_sb", [128, 128])
            kkv = sb("kkv", [128, 128]); rst = sb("rst", [128, 8]); k2 = sb("k2", [128, 128]); mm_ = sb("mm_", [128, 128])
            bvec = sb("bvec", [128, 128]); bs = sb("bs", [128, 2])
            cum = sb("cum", [128, 128]); ec = sb("ec", [128, 128]); eci = sb("eci", [128, 128]); ee = sb("ee", [128, 128])
            eh = sb("eh", [128, 128]); gC = sb("gC", [64, 2])
            rt = sb("rt", [128, 128]); bt = sb("bt", [128, 128]); ktl = sb("ktl", [128, 128])
            bh = sb("bh", [128, 128]); kh = sb("kh", [128, 128])
            FT = [sb("FT%d" % h, [64, 512]) for h in range(2)]
            GM = [sb("GM%d" % h, [128, 512]) for h in range(2)]
            Xa = [[sb("Xa%d_%d" % (h, i), [128, 128]) for i in range(2)] for h in range(2)]
            Xb = [[sb("Xb%d_%d" % (h, i), [128, 128]) for i in range(2)] for h in range(2)]
            ACC = [[sb("ACC%d_%d" % (h, i), [128, 128]) for i in range(2)] for h in range(2)]
            RH = [sb("RH%d" % h, [128, 128]) for h in range(2)]
            PU = [sb("PU%d" % h, [128, 128]) for h in range(2)]
            Y1T = [sb("Y1T%d" % h, [64, 128]) for h in range(2)]
            Y2 = [sb("Y2%d" % h, [128, 64]) for h in range(2)]
            T1T = [sb("T1T%d" % h, [64, 64]) for h in range(2)]
            T2 = [sb("T2%d" % h, [64, 64]) for h in range(2)]
            Hs = [[sb("H%d_%d" % (h, i), [64, 64]) for i in range(2)] for h in range(2)]
            Hld = sb("Hld", [64, 2, 64]); Hout = sb("Hout", [64, 2, 64])
            yb = sb("yb", [128, 128]); yc = sb("yc", [128, 128]); ysq = sb("ysq", [128, 128]); obb = sb("obb", [128, 128], BF16)

            def bc3(ap2, n, a, b):
                return ap2.unsqueeze(2).to_broadcast([n, a, b])

            def rstd_chain(src_ap, dst_ap, n, k, scale, eps, bsrc, bdst):
                S.op("dve", lambda e: e.tensor_scalar(out=dst_ap, in0=src_ap, scalar1=scale, scalar2=eps, op0=ALU.mult,
                                                      op1=ALU.add), reads=[bsrc], writes=[bdst])
                S.op("act", lambda e: e.activation(out=dst_ap, in_=dst_ap, func=AF.Ln), reads=[bdst], writes=[bdst])
                S.op("act", lambda e: e.activation(out=dst_ap, in_=dst_ap, func=AF.Exp, scale=-0.5), reads=[bdst], writes=[bdst])

            def front(x_rows_ap, n, dst, dst_off, dstB, gt, slot):
                xtile = xt[slot]
                bx = B(xtile.name)
                S.dma("sp", xtile[:n, :], x_rows_ap, writes=[bx])
                S.op("dve", lambda e: e.scalar_tensor_tensor(out=junk[:n, :], in0=xtile[:n, :], scalar=1.0, in1=xtile[:n, :], op0=ALU.mult, op1=ALU.mult, accum_out=st4[:n, 0:1]),
                     reads=[bx], writes=[B("junk"), B("st4")])
                rstd_chain(st4[:n, 0:1], st4[:n, 0:1], n, 1, 1.0 / D, RMS_EPS, B("st4"), B("st4"))
                S.op("act", lambda e: e.activation(out=xsb[:n, :], in_=xtile[:n, :], func=AF.Copy, scale=st4[:n, 0:1]),
                     reads=[bx, B("st4")], writes=[B("xsb")])
                yield
                for half in range(2):
                    pb = gbank()
                    pv = PS[pb][:].bitcast(BF16)
                    for c8 in range(8):
                        c = half * 8 + c8
                        S.op("pe", lambda e, c=c, c8=c8, pv=pv: e.transpose(out=pv[:, c8 * n:(c8 + 1) * n],
                                                                           in_=xsb[:n, c * 128:(c + 1) * 128],
                                                                           identity=identb[:n, :n]),
                             reads=[B("xsb"), B("identb")], writes=[PB[pb]] if c8 == 0 else [], pwrites=[] if c8 == 0 else [PB[pb]],
                             inc=(c8 == 7))
                    S.op("dve", lambda e, half=half, pv=pv: e.tensor_tensor(
                        out=dst[:, half * 8:(half + 1) * 8, dst_off:dst_off + n],
                        in0=pv[:, 0:8 * n].rearrange("p (c n) -> p c n", n=n),
                        in1=bc3(gt[:, half * 8:(half + 1) * 8], 128, 8, n), op=ALU.mult),
                        reads=[PB[pb], B(gt.name)], writes=[] if half else [dstB], pwrites=[dstB] if half else [])
                    yield

            def load_unit_params(up_row, lw_ap, lg2_ap):
                S.dma("sp", parb[:], up_row.partition_broadcast(128), writes=[B("parb")])
                S.dma("sp", lwt[:], lw_ap.rearrange("(a p) n -> p a n", p=64), writes=[B("lwt")])
                S.dma("sp", lg2t[:], lg2_ap, writes=[B("lg2t")])
            mu_bc = parb[:, 0:576]; w0a0_bc = parb[:, 576:832]; kk_bc = parb[:, 832:960]; ka_bc = parb[:, 960:1088]
            rk_bc = parb[:, 1088:1216]; lg_bc = parb[:, 1216:1344]; lb_bc = parb[:, 1344:1472]

            def load_unit_w(wsrc):
                S.dma("sp", wub_sb[:], wsrc.rearrange("(c p) n -> p c n", p=128), reads=[B(wsrc_name[0])], writes=[B("wub_sb")])
            wsrc_name = [None]

            def project(xsrc, xoff, xB, n, pslot):
                pa, pb2 = gbank(), gbank()
                for k in range(NCH):
                    S.op("pe", lambda e, k=k: e.matmul(PS[pa][:n, 0:512], lhsT=xsrc[:, k, xoff:xoff + n], rhs=wub_sb[:, k, 0:512],
                                                       start=(k == 0), stop=(k == NCH - 1), skip_group_check=True),
                         reads=[xB, B("wub_sb")], writes=[PB[pa]] if k == 0 else [], pwrites=[] if k == 0 else [PB[pa]],
                         inc=(k == NCH - 1))
                for k in range(NCH):
                    S.op("pe", lambda e, k=k: e.matmul(PS[pb2][:n, 0:448], lhsT=xsrc[:, k, xoff:xoff + n], rhs=wub_sb[:, k, 512:960],
                                                       start=(k == 0), stop=(k == NCH - 1), skip_group_check=True),
                         reads=[xB, B("wub_sb")], writes=[PB[pb2]] if k == 0 else [], pwrites=[] if k == 0 else [PB[pb2]],
                         inc=(k == NCH - 1))
                pr = prw[pslot]
                yield
                S.op("act", lambda e: e.activation(out=gsb[:n, :], in_=PS[pa][:n, 0:384], func=AF.Copy),
                     reads=[PB[pa]], writes=[B("gsb")])
                S.op("act", lambda e: e.activation(out=pr[:n, 0:128], in_=PS[pa][:n, 384:512], func=AF.Copy),
                     reads=[PB[pa]], writes=[B(pr.name)])
                S.op("act", lambda e: e.activation(out=pr[:n, 128:576], in_=PS[pb2][:n, 0:448], func=AF.Copy),
                     reads=[PB[pb2]], pwrites=[B(pr.name)])

            def attn_prep(n, cs_ap, csB, k_out_ap, v_out_ap, kt_idx, kt_off, qoff, QT, QTB_):
                S.op("dve", lambda e: e.tensor_tensor(out=tq[:n, :], in0=gsb[:n, 0:256], in1=gsb[:n, 0:256], op=ALU.mult),
                     reads=[B("gsb")], writes=[B("tq")])
                S.op("dve", lambda e: e.tensor_reduce(out=st4[:n, 4:8], in_=tq[:n, :].rearrange("p (a b) -> p a b", b=64),
                                                      axis=AX.X, op=ALU.add), reads=[B("tq")], writes=[B("st4")])
                rstd_chain(st4[:n, 4:8], st4[:n, 4:8], n, 4, 1.0 / 64, RMS_EPS, B("st4"), B("st4"))
                q3 = qkn[:n, :].rearrange("p (a b) -> p a b", b=64)
                S.op("dve", lambda e: e.tensor_tensor(out=q3, in0=gsb[:n, 0:256].rearrange("p (a b) -> p a b", b=64),
                                                      in1=bc3(st4[:n, 4:8], n, 4, 64), op=ALU.mult),
                     reads=[B("gsb"), B("st4")], writes=[B("qkn")])
                S.op("dve", lambda e: e.tensor_tensor(out=qkn[:n, :], in0=qkn[:n, :], in1=qkgb[:n, :], op=ALU.mult),
                     reads=[B("qkn"), B("qkgb")], writes=[B("qkn")])
                yield
                x1 = q3[:, :, 0:8]; x2 = q3[:, :, 8:16]
                cosb = cs_ap[:, 0:8].unsqueeze(1).to_broadcast([n, 4, 8])
                sinb = cs_ap[:, 8:16].unsqueeze(1).to_broadcast([n, 4, 8])
                for idx, (a_, b_) in enumerate(((x1, cosb), (x2, sinb), (x2, cosb), (x1, sinb))):
                    S.op("dve", lambda e, idx=idx, a_=a_, b_=b_: e.tensor_tensor(out=rtmp[:n, :, idx, :], in0=a_, in1=b_, op=ALU.mult),
                         reads=[B("qkn"), csB], writes=[B("rtmp")] if idx == 0 else [], pwrites=[] if idx == 0 else [B("rtmp")])
                S.op("dve", lambda e: e.tensor_tensor(out=x1, in0=rtmp[:n, :, 0, :], in1=rtmp[:n, :, 1, :], op=ALU.subtract),
                     reads=[B("rtmp")], writes=[B("qkn")])
                S.op("dve", lambda e: e.tensor_tensor(out=x2, in0=rtmp[:n, :, 2, :], in1=rtmp[:n, :, 3, :], op=ALU.add),
                     reads=[B("rtmp")], writes=[B("qkn")])
                yield
                S.dma("sp", k_out_ap, qkn[:n, 128:256], reads=[B("qkn")])
                S.dma("sp", v_out_ap, gsb[:n, 256:384], reads=[B("gsb")])
                S.op("act", lambda e: e.activation(out=qkb[:n, :], in_=qkn[:n, :], func=AF.Copy), reads=[B("qkn")], writes=[B("qkb")])
                S.op("act", lambda e: e.activation(out=VV[:n, kt_idx, :], in_=gsb[:n, 256:384], func=AF.Copy),
                     reads=[B("gsb")], writes=[VB[kt_idx]])
                pb = gbank()
                pv = PS[pb][:].bitcast(BF16)
                S.op("pe", lambda e: e.transpose(out=pv[:, 0:n], in_=qkb[:n, 0:128], identity=identb[:n, :n]),
                     reads=[B("qkb"), B("identb")], writes=[PB[pb]], inc=False)
                S.op("pe", lambda e: e.transpose(out=pv[:, 128:128 + n], in_=qkb[:n, 128:256], identity=identb[:n, :n]),
                     reads=[B("qkb"), B("identb")], pwrites=[PB[pb]])
                S.op("act", lambda e: e.activation(out=QT[0:64, 0, qoff:qoff + n], in_=pv[0:64, 0:n], func=AF.Copy),
                     reads=[PB[pb]], writes=[] if qoff else [QTB_], pwrites=[QTB_] if qoff else [])
                S.op("act", lambda e: e.activation(out=QT[64:128, 1, qoff:qoff + n], in_=pv[64:128, 0:n], func=AF.Copy),
                     reads=[PB[pb]], pwrites=[QTB_])
                yield
                S.op("act", lambda e: e.activation(out=KT[:, kt_off:kt_off + n], in_=pv[:, 128:128 + n], func=AF.Copy),
                     reads=[PB[pb]], writes=[KTB[kt_idx]])

            def attention(nq, n, key_tiles, cat_writer, QT, QTB_, tile_base):
                W2 = 2 * nq
                for ki, (koff, nk, kidx, mk) in enumerate(key_tiles):
                    sbk = 4 + (ki % 2)
                    Ebt = Eb[ki % 2]
                    S.op("pe", lambda e: e.matmul(PS[sbk][:nk, 0:W2].rearrange("p (a b) -> p a b", a=2), lhsT=KT[:, koff:koff + nk], rhs=QT[:, :, 0:nq],
                                                  start=True, stop=True, skip_group_check=True),
                         reads=[KTB[kidx], QTB_], writes=[PB[sbk]])
                    S.op("act", lambda e: e.activation(out=Ebt[:nk, 0:W2], in_=PS[sbk][:nk, 0:W2], func=AF.Exp, scale=0.125),
                         reads=[PB[sbk]], writes=[B(Ebt.name)])
                    if mk is not None:
                        S.op("dve", lambda e: e.tensor_tensor(out=Ebt[:nk, 0:W2], in0=Ebt[:nk, 0:W2], in1=amask[:nk, mk, 0:W2], op=ALU.mult),
                             reads=[B(Ebt.name), B("amask")], writes=[B(Ebt.name)])
                    first = (ki == 0)
                    last = (ki == len(key_tiles) - 1)
                    S.op("pe", lambda e: e.matmul(PS[6][:, 0:W2], lhsT=VV[:nk, kidx, :], rhs=Ebt[:nk, 0:W2], start=first, stop=last,
                                                  skip_group_check=True),
                         reads=[VB[kidx], B(Ebt.name)], writes=[PB[6]] if first else [], pwrites=[] if first else [PB[6]], inc=True)
                    if first:
                        S.op("pool", lambda e: e.tensor_copy(out=Zacc[:nk, 0:W2], in_=Ebt[:nk, 0:W2]), reads=[B(Ebt.name)], writes=[B("Zacc")])
                    else:
                        S.op("pool", lambda e: e.tensor_tensor(out=Zacc[:nk, 0:W2], in0=Zacc[:nk, 0:W2], in1=Ebt[:nk, 0:W2], op=ALU.add),
                             reads=[B(Ebt.name), B("Zacc")], writes=[B("Zacc")])
                    yield
                S.op("pe", lambda e: e.matmul(PS[7][0:1, 0:W2], lhsT=ones[:, 0:1], rhs=Zacc[:, 0:W2], start=True, stop=True, skip_group_check=True),
                     reads=[B("ones"), B("Zacc")], writes=[PB[7]])
                S.op("act", lambda e: e.activation(out=OT[:, 0:W2], in_=PS[6][:, 0:W2], func=AF.Copy), reads=[PB[6]], writes=[B("OT")])
                S.op("dve", lambda e: e.tensor_copy(out=zsb[0:1, 0:W2], in_=PS[7][0:1, 0:W2]), reads=[PB[7]], writes=[B("zsb")])
                for qt in range(nq // n):
                    pb = gbank()
                    S.op("pe", lambda e: e.transpose(out=PS[pb][:n, 0:128], in_=OT[:, qt * n:(qt + 1) * n], identity=ident[:, :]),
                         reads=[B("OT"), B("ident")], writes=[PB[pb]], inc=False)
                    S.op("pe", lambda e: e.transpose(out=PS[pb][:n, 128:256], in_=OT[:, nq + qt * n:nq + (qt + 1) * n], identity=ident[:, :]),
                         reads=[B("OT"), B("ident")], pwrites=[PB[pb]], inc=False)
                    S.op("pe", lambda e: e.matmul(PS[pb][:n, 256:257], lhsT=zsb[0:1, qt * n:(qt + 1) * n], rhs=one11[0:1, 0:1],
                                                  start=False, stop=False, skip_group_check=True),
                         reads=[B("zsb"), B("one11")], pwrites=[PB[pb]], inc=False)
                    S.op("pe", lambda e: e.matmul(PS[pb][:n, 257:258], lhsT=zsb[0:1, nq + qt * n:nq + (qt + 1) * n], rhs=one11[0:1, 0:1],
                                                  start=False, stop=True, skip_group_check=True),
                         reads=[B("zsb"), B("one11")], pwrites=[PB[pb]])
                    S.op("dve", lambda e: e.reciprocal(out=ast[:n, 0:2], in_=PS[pb][:n, 256:258]), reads=[PB[pb]], writes=[B("ast")])
                    S.op("dve", lambda e: e.tensor_tensor(out=ast[:n, 2:3], in0=ast[:n, 1:2], in1=neglam[:n, 0:1], op=ALU.mult),
                         reads=[B("ast"), B("neglam")], writes=[B("ast")])
                    S.op("dve", lambda e: e.tensor_scalar(out=osb[:n, :], in0=PS[pb][:n, 0:128], scalar1=ast[:n, 0:1], scalar2=None,
                                                          op0=ALU.mult), reads=[PB[pb], B("ast")], writes=[B("osb")])
                    S.op("dve", lambda e: e.scalar_tensor_tensor(out=osb[:n, :], in0=PS[pb][:n, 128:256], scalar=ast[:n, 2:3],
                                                                 in1=osb[:n, :], op0=ALU.mult, op1=ALU.add),
                         reads=[PB[pb], B("ast"), B("osb")], writes=[B("osb")])
                    S.op("dve", lambda e: e.scalar_tensor_tensor(out=ajunk[:n, 0:128], in0=osb[:n, :], scalar=1.0, in1=osb[:n, :], op0=ALU.mult, op1=ALU.mult, accum_out=ast[:n, 4:5]),
                         reads=[B("osb"), B("ast")], writes=[B("ajunk"), B("ast")])
                    rstd_chain(ast[:n, 4:5], ast[:n, 4:5], n, 1, 1.0 / 128, RMS_EPS, B("ast"), B("ast"))
                    S.op("dve", lambda e: e.scalar_tensor_tensor(out=onb[:n, :], in0=osb[:n, :], scalar=ast[:n, 4:5], in1=subgb[:n, :],
                                                                 op0=ALU.mult, op1=ALU.mult),
                         reads=[B("osb"), B("ast"), B("subgb")], writes=[B("onb")])
                    pb2 = gbank()
                    pv = PS[pb2][:].bitcast(BF16)
                    S.op("pe", lambda e: e.transpose(out=pv[:, 0:n], in_=onb[:n, :], identity=identb[:n, :n]),
                         reads=[B("onb"), B("identb")], writes=[PB[pb2]])
                    cat_writer(0, tile_base + qt, pv[:, 0:n], PB[pb2])
                    yield

            def rwkv(n, pslot, prev_ap, prevB, cmat, first_tile, hslot, cat_writer, tile_idx):
                pr = prw[pslot]
                prB = B(pr.name)
                pa, pb2 = gbank(), gbank()
                for (bank, c0, c1) in ((pa, 0, 512), (pb2, 512, 576)):
                    S.op("pe", lambda e, bank=bank, c0=c0, c1=c1: e.matmul(PS[bank][:n, 0:c1 - c0], lhsT=s1m[:n, :n], rhs=pr[:n, c0:c1],
                                                                          start=True, stop=(prev_ap is None), skip_group_check=True),
                         reads=[B("s1m"), prB], writes=[PB[bank]], inc=(prev_ap is None))
                    if prev_ap is not None:
                        S.op("pe", lambda e, bank=bank, c0=c0, c1=c1: e.matmul(PS[bank][:n, 0:c1 - c0], lhsT=cmat, rhs=prev_ap[:, c0:c1],
                                                                              start=False, stop=True, skip_group_check=True),
                             reads=[prevB, B("c128"), B("c64")], pwrites=[PB[bank]])
                S.op("dve", lambda e: e.tensor_tensor(out=xs_[:n, 0:512], in0=PS[pa][:n, 0:512], in1=pr[:n, 0:512], op=ALU.subtract),
                     reads=[PB[pa], prB], writes=[B("xs_")])
                S.op("dve", lambda e: e.tensor_tensor(out=xs_[:n, 512:576], in0=PS[pb2][:n, 0:64], in1=pr[:n, 512:576], op=ALU.subtract),
                     reads=[PB[pb2], prB], pwrites=[B("xs_")])
                S.op("dve", lambda e: e.tensor_tensor(out=xs_[:n, :], in0=xs_[:n, :], in1=mu_bc[:n, :], op=ALU.mult),
                     reads=[B("xs_"), B("parb")], writes=[B("xs_")])
                S.op("dve", lambda e: e.tensor_tensor(out=xs_[:n, :], in0=xs_[:n, :], in1=pr[:n, :], op=ALU.add),
                     reads=[B("xs_"), prB], writes=[B("xs_")])
                xr_, xk, xv = xs_[:n, 0:128], xs_[:n, 128:256], xs_[:n, 256:384]
                yield
                S.op("act", lambda e: e.activation(out=Et[:n, 0:64], in_=xs_[:n, 384:448], func=AF.Exp, scale=-2.0),
                     reads=[B("xs_")], writes=[B("Et")])
                S.op("act", lambda e: e.activation(out=Et[:n, 64:128], in_=xs_[:n, 512:576], func=AF.Exp, scale=-1.0),
                     reads=[B("xs_")], pwrites=[B("Et")])
                S.op("dve", lambda e: e.tensor_scalar(out=Et[:n, :], in0=Et[:n, :], scalar1=1.0, scalar2=None, op0=ALU.add),
                     reads=[B("Et")], writes=[B("Et")])
                S.op("dve", lambda e: e.reciprocal(out=Et[:n, :], in_=Et[:n, :]), reads=[B("Et")], writes=[B("Et")])
                S.op("dve", lambda e: e.tensor_scalar(out=LT[:n, 0:64], in0=Et[:n, 0:64], scalar1=2.0, scalar2=-1.0, op0=ALU.mult, op1=ALU.add),
                     reads=[B("Et")], writes=[B("LT")])
                S.op("dve", lambda e: e.tensor_copy(out=LT[:n, 64:128], in_=xs_[:n, 448:512]), reads=[B("xs_")], pwrites=[B("LT")])
                S.op("dve", lambda e: e.tensor_copy(out=LT[:n, 128:192], in_=Et[:n, 64:128]), reads=[B("Et")], pwrites=[B("LT")])
                yield
                pb = gbank()
                for j3 in range(3):
                    S.op("pe", lambda e, j3=j3: e.transpose(out=PS[pb][0:64, j3 * 128:j3 * 128 + n], in_=LT[:n, j3 * 64:(j3 + 1) * 64], identity=ident[:n, :n]),
                         reads=[B("LT"), B("ident")], writes=[PB[pb]] if j3 == 0 else [], pwrites=[] if j3 == 0 else [PB[pb]], inc=(j3 == 2))
                S.op("act", lambda e: e.activation(out=LTT[:, :, 0:n], in_=PS[pb][0:64, 0:384].rearrange("p (a b) -> p a b", b=128)[:, :, 0:n], func=AF.Copy),
                     reads=[PB[pb]], writes=[B("LTT")])
                yield
                pl = gbank()
                S.op("pe", lambda e: e.matmul(PS[pl][:n, 0:128], lhsT=LTT[:, 0, 0:n], rhs=lwt[:, 0, :], start=True, stop=False, skip_group_check=True),
                     reads=[B("LTT"), B("lwt")], writes=[PB[pl]], inc=False)
                S.op("pe", lambda e: e.matmul(PS[pl][:n, 128:256], lhsT=LTT[:, 1, 0:n], rhs=lwt[:, 1, :], start=False, stop=False, skip_group_check=True),
                     reads=[B("LTT"), B("lwt")], pwrites=[PB[pl]], inc=False)
                S.op("pe", lambda e: e.matmul(PS[pl][:n, 256:384], lhsT=LTT[:, 2, 0:n], rhs=lg2t[:, :], start=False, stop=True, skip_group_check=True),
                     reads=[B("LTT"), B("lg2t")], pwrites=[PB[pl]])
                yield
                S.op("dve", lambda e: e.tensor_tensor(out=za[:n, :], in0=PS[pl][:n, 0:256], in1=w0a0_bc[:n, :], op=ALU.add),
                     reads=[PB[pl], B("parb")], writes=[B("za")])
                yield
                S.op("dve", lambda e: e.tensor_copy(out=g_sb[:n, :], in_=PS[pl][:n, 256:384]), reads=[PB[pl]], writes=[B("g_sb")])
                yield
                S.op("act", lambda e: e.activation(out=za[:n, :], in_=za[:n, :], func=AF.Exp, scale=-1.0), reads=[B("za")], writes=[B("za")])
                S.op("dve", lambda e: e.tensor_scalar(out=za[:n, :], in0=za[:n, :], scalar1=1.0, scalar2=None, op0=ALU.add),
                     reads=[B("za")], writes=[B("za")])
                yield
                S.op("dve", lambda e: e.reciprocal(out=sa[:n, :], in_=za[:n, :]), reads=[B("za")], writes=[B("sa")])
                yield
                S.op("dve", lambda e: e.tensor_scalar(out=ld[:n, :], in0=sa[:n, 0:128], scalar1=-math.exp(-0.5), scalar2=None, op0=ALU.mult),
                     reads=[B("sa")], writes=[B("ld")])
                av = sa[:n, 128:256]
                yield
                S.op("dve", lambda e: e.tensor_tensor(out=kkv[:n, :], in0=xk, in1=kk_bc[:n, :], op=ALU.mult), reads=[B("xs_"), B("parb")], writes=[B("kkv")])
                S.op("dve", lambda e: e.tensor_tensor(out=mm_[:n, :], in0=kkv[:n, :], in1=kkv[:n, :], op=ALU.mult), reads=[B("kkv")], writes=[B("mm_")])
                S.op("dve", lambda e: e.tensor_reduce(out=rst[:n, 0:2], in_=mm_[:n, :].rearrange("p (a b) -> p a b", b=64), axis=AX.X, op=ALU.add),
                     reads=[B("mm_")], writes=[B("rst")])
                S.op("dve", lambda e: e.tensor_scalar(out=rst[:n, 0:2], in0=rst[:n, 0:2], scalar1=1e-18, scalar2=None, op0=ALU.max),
                     reads=[B("rst")], writes=[B("rst")])
                S.op("act", lambda e: e.activation(out=rst[:n, 0:2], in_=rst[:n, 0:2], func=AF.Ln), reads=[B("rst")], writes=[B("rst")])
                S.op("act", lambda e: e.activation(out=rst[:n, 0:2], in_=rst[:n, 0:2], func=AF.Exp, scale=-0.5), reads=[B("rst")], writes=[B("rst")])
                k3 = kkv[:n, :].rearrange("p (a b) -> p a b", b=64)
                S.op("dve", lambda e: e.tensor_tensor(out=k3, in0=k3, in1=bc3(rst[:n, 0:2], n, 2, 64), op=ALU.mult),
                     reads=[B("kkv"), B("rst")], writes=[B("kkv")])
                S.op("dve", lambda e: e.scalar_tensor_tensor(out=mm_[:n, :], in0=av, scalar=-1.0, in1=ka_bc[:n, :], op0=ALU.add, op1=ALU.mult),
                     reads=[B("sa"), B("parb")], writes=[B("mm_")])
                S.op("dve", lambda e: e.scalar_tensor_tensor(out=k2[:n, :], in0=mm_[:n, :], scalar=1.0, in1=xk, op0=ALU.add, op1=ALU.mult),
                     reads=[B("mm_"), B("xs_")], writes=[B("k2")])
                S.op("dve", lambda e: e.tensor_tensor(out=bvec[:n, :], in0=kkv[:n, :], in1=av, op=ALU.mult), reads=[B("kkv"), B("sa")], writes=[B("bvec")])
                S.op("dve", lambda e: e.tensor_tensor(out=mm_[:n, :], in0=xr_, in1=k2[:n, :], op=ALU.mult), reads=[B("xs_"), B("k2")], writes=[B("mm_")])
                S.op("dve", lambda e: e.tensor_tensor(out=mm_[:n, :], in0=mm_[:n, :], in1=rk_bc[:n, :], op=ALU.mult), reads=[B("mm_"), B("parb")], writes=[B("mm_")])
                S.op("dve", lambda e: e.tensor_reduce(out=bs[:n, 0:2], in_=mm_[:n, :].rearrange("p (a b) -> p a b", b=64), axis=AX.X, op=ALU.add),
                     reads=[B("mm_")], writes=[B("bs")])
                yield
                pc = gbank()
                S.op("pe", lambda e: e.matmul(PS[pc][:n, 0:128], lhsT=ut[:n, :n], rhs=ld[:n, :], start=True, stop=False, skip_group_check=True),
                     reads=[B("ut"), B("ld")], writes=[PB[pc]], inc=False)
                S.op("pe", lambda e: e.matmul(PS[pc][:n, 128:256], lhsT=ones[:n, :n], rhs=ld[:n, :], start=False, stop=False, skip_group_check=True),
                     reads=[B("ones"), B("ld")], pwrites=[PB[pc]], inc=False)
                for hh in range(2):
                    S.op("pe", lambda e, hh=hh: e.matmul(PS[pc][0:64, 256 + hh:257 + hh], lhsT=ld[:n, hh * 64:(hh + 1) * 64], rhs=ones[:n, 0:1],
                                                         start=False, stop=(hh == 1), skip_group_check=True),
                         reads=[B("ones"), B("ld")], pwrites=[PB[pc]], inc=(hh == 1))
                S.op("act", lambda e: e.activation(out=cum[:n, :], in_=PS[pc][:n, 0:128], func=AF.Copy), reads=[PB[pc]], writes=[B("cum")])
                S.op("act", lambda e: e.activation(out=ec[:n, :], in_=PS[pc][:n, 0:128], func=AF.Exp), reads=[PB[pc]], writes=[B("ec")])
                S.op("act", lambda e: e.activation(out=eci[:n, :], in_=PS[pc][:n, 0:128], func=AF.Exp, scale=-1.0), reads=[PB[pc]], writes=[B("eci")])
                S.op("act", lambda e: e.activation(out=gC[:, 0:2], in_=PS[pc][0:64, 256:258], func=AF.Exp), reads=[PB[pc]], writes=[B("gC")])
                S.op("dve", lambda e: e.tensor_tensor(out=ee[:n, :], in0=cum[:n, :], in1=ld[:n, :], op=ALU.subtract), reads=[B("cum"), B("ld")], writes=[B("ee")])
                S.op("act", lambda e: e.activation(out=ee[:n, :], in_=ee[:n, :], func=AF.Exp), reads=[B("ee")], writes=[B("ee")])
                S.op("dve", lambda e: e.tensor_tensor(out=eh[:n, :], in0=PS[pc][:n, 128:256], in1=cum[:n, :], op=ALU.subtract), reads=[PB[pc], B("cum")], writes=[B("eh")])
                S.op("act", lambda e: e.activation(out=eh[:n, :], in_=eh[:n, :], func=AF.Exp), reads=[B("eh")], writes=[B("eh")])
                yield
                S.op("dve", lambda e: e.tensor_tensor(out=rt[:n, :], in0=xr_, in1=ec[:n, :], op=ALU.mult), reads=[B("xs_"), B("ec")], writes=[B("rt")])
                for hh in range(2):
                    S.op("dve", lambda e, hh=hh: e.scalar_tensor_tensor(out=RH[hh][:n, 0:64], in0=kkv[:n, hh * 64:(hh + 1) * 64], scalar=-1.0,
                                                                        in1=ee[:n, hh * 64:(hh + 1) * 64], op0=ALU.mult, op1=ALU.mult),
                         reads=[B("kkv"), B("ee")], writes=[B(RH[hh].name)])
                S.op("dve", lambda e: e.tensor_tensor(out=bt[:n, :], in0=bvec[:n, :], in1=eci[:n, :], op=ALU.mult), reads=[B("bvec"), B("eci")], writes=[B("bt")])
                S.op("dve", lambda e: e.tensor_tensor(out=ktl[:n, :], in0=k2[:n, :], in1=eci[:n, :], op=ALU.mult), reads=[B("k2"), B("eci")], writes=[B("ktl")])
                S.op("dve", lambda e: e.tensor_tensor(out=bh[:n, :], in0=bvec[:n, :], in1=eh[:n, :], op=ALU.mult), reads=[B("bvec"), B("eh")], writes=[B("bh")])
                S.op("dve", lambda e: e.tensor_tensor(out=kh[:n, :], in0=k2[:n, :], in1=eh[:n, :], op=ALU.mult), reads=[B("k2"), B("eh")], writes=[B("kh")])
                nlev = 7 if n == 128 else 6
                yield
                def head_gen(hh):
                    hs = slice(hh * 64, (hh + 1) * 64)
                    pf = gbank()
                    srcs = ((RH[hh][:n, 0:64], B(RH[hh].name)), (rt[:n, hs], B("rt")), (bt[:n, hs], B("bt")), (ktl[:n, hs], B("ktl")))
                    for i4, (sap, sB) in enumerate(srcs):
                        S.op("pe", lambda e, i4=i4, sap=sap: e.transpose(out=PS[pf][0:64, i4 * n:(i4 + 1) * n], in_=sap, identity=ident[:n, :n]),
                             reads=[sB, B("ident")], writes=[PB[pf]] if i4 == 0 else [], pwrites=[] if i4 == 0 else [PB[pf]], inc=(i4 == 3))
                    S.op("act", lambda e, hh=hh: e.activation(out=FT[hh][:, 0:4 * n], in_=PS[pf][0:64, 0:4 * n], func=AF.Copy),
                         reads=[PB[pf]], writes=[B(FT[hh].name)])
                    F_ = FT[hh]; FB = B(F_.name)
                    yield
                    pg = gbank()
                    S.op("pe", lambda e, F_=F_: e.matmul(PS[pg][:n, 0:2 * n], lhsT=F_[:, 2 * n:3 * n], rhs=F_[:, 0:2 * n], start=True, stop=False, skip_group_check=True),
                         reads=[FB], writes=[PB[pg]], inc=False)
                    S.op("pe", lambda e, F_=F_: e.matmul(PS[pg][:n, 2 * n:4 * n], lhsT=F_[:, 3 * n:4 * n], rhs=F_[:, 0:2 * n], start=False, stop=True, skip_group_check=True),
                         reads=[FB], pwrites=[PB[pg]])
                    G_ = GM[hh]; GB = B(G_.name)
                    S.op("dve", lambda e, G_=G_: e.tensor_tensor(out=G_[:n, 0:4 * n].rearrange("p (a b) -> p a b", b=n),
                                                                 in0=PS[pg][:n, 0:4 * n].rearrange("p (a b) -> p a b", b=n),
                                                                 in1=gmask[:n, :].rearrange("p (a b) -> p a b", b=128)[:, :, 0:n], op=ALU.mult),
                         reads=[PB[pg], B("gmask")], writes=[GB])
                    px = gbank()
                    S.op("pe", lambda e, F_=F_: e.matmul(PS[px][:n, 0:n], lhsT=F_[:, 0:n], rhs=F_[:, 2 * n:3 * n], start=True, stop=True, skip_group_check=True),
                         reads=[FB], writes=[PB[px]])
                    xa, xb_ = Xa[hh], Xb[hh]
                    S.op("dve", lambda e, xb_=xb_: e.tensor_tensor(out=xb_[0][:n, :n], in0=PS[px][:n, 0:n], in1=lowm[:n, :n], op=ALU.mult),
                         reads=[PB[px], B("lowm")], writes=[B(xb_[0].name)])
                    yield
                    acc = ACC[hh]
                    S.op("dve", lambda e, acc=acc, G_=G_: e.tensor_tensor(out=acc[0][:n, :n], in0=G_[:n, 0:n], in1=ident[:n, :n], op=ALU.add),
                         reads=[GB, B("ident")], writes=[B(acc[0].name)])
                    curX_ap, curXB = G_[:n, 0:n], GB
                    cs_ = 0
                    for lev in range(1, nlev):
                        curXp = xb_[cs_]
                        nxt = 1 - cs_
                        p2 = gbank()
                        lastlev = (lev == nlev - 1)
                        S.op("pe", lambda e, curX_ap=curX_ap, curXp=curXp: e.matmul(PS[p2][:n, 0:n], lhsT=curX_ap, rhs=curXp[:n, :n], start=True, stop=lastlev, skip_group_check=True),
                             reads=[curXB, B(curXp.name)], writes=[PB[p2]], inc=lastlev)
                        if not lastlev:
                            S.op("pe", lambda e, curX_ap=curX_ap, curXp=curXp: e.matmul(PS[p2][:n, 128:128 + n], lhsT=curXp[:n, :n], rhs=curX_ap, start=False, stop=True, skip_group_check=True),
                                 reads=[curXB, B(curXp.name)], pwrites=[PB[p2]])
                        S.op("act", lambda e, xb_=xb_, nxt=nxt: e.activation(out=xb_[nxt][:n, :n], in_=PS[p2][:n, 0:n], func=AF.Copy),
                             reads=[PB[p2]], writes=[B(xb_[nxt].name)])
                        if not lastlev:
                            S.op("dve", lambda e, xa=xa, nxt=nxt: e.tensor_copy(out=xa[nxt][:n, :n], in_=PS[p2][:n, 128:128 + n]),
                                 reads=[PB[p2]], writes=[B(xa[nxt].name)])
                        a_cur = acc[(lev - 1) % 2]; a_nxt = acc[lev % 2]
                        p3 = gbank()
                        S.op("pe", lambda e, xb_=xb_, nxt=nxt, a_cur=a_cur: e.matmul(PS[p3][:n, 0:n], lhsT=xb_[nxt][:n, :n], rhs=a_cur[:n, :n], start=True, stop=True, skip_group_check=True),
                             reads=[B(xb_[nxt].name), B(a_cur.name)], writes=[PB[p3]])
                        S.op("dve", lambda e, a_cur=a_cur, a_nxt=a_nxt: e.tensor_tensor(out=a_nxt[:n, :n], in0=PS[p3][:n, 0:n], in1=a_cur[:n, :n], op=ALU.add),
                             reads=[PB[p3], B(a_cur.name)], writes=[B(a_nxt.name)])
                        curX_ap, curXB = xa[nxt][:n, :n], B(xa[nxt].name)
                        cs_ = nxt
                        yield
                    MT = acc[(nlev - 1) % 2]
                    yield
                    MTB = B(MT.name)
                    vh = xs_[:n, 256 + hh * 64:256 + (hh + 1) * 64]
                    pa_ = gbank()
                    S.op("pe", lambda e, G_=G_, vh=vh: e.matmul(PS[pa_][:n, 0:64], lhsT=G_[:n, 2 * n:3 * n], rhs=vh, start=True, stop=True, skip_group_check=True),
                         reads=[GB, B("xs_")], writes=[PB[pa_]])
                    S.op("act", lambda e, hh=hh: e.activation(out=RH[hh][:n, 64:128], in_=PS[pa_][:n, 0:64], func=AF.Copy),
                         reads=[PB[pa_]], pwrites=[B(RH[hh].name)])
                    ppu = gbank()
                    S.op("pe", lambda e, MT=MT, hh=hh: e.matmul(PS[ppu][:n, 0:128], lhsT=MT[:n, :n], rhs=RH[hh][:n, :], start=True, stop=True, skip_group_check=True),
                         reads=[MTB, B(RH[hh].name)], writes=[PB[ppu]])
                    S.op("act", lambda e, hh=hh: e.activation(out=PU[hh][:n, :], in_=PS[ppu][:n, 0:128], func=AF.Copy), reads=[PB[ppu]], writes=[B(PU[hh].name)])
                    PUB = B(PU[hh].name)
                    yield
                    py = gbank()
                    S.op("pe", lambda e, hh=hh, G_=G_: e.matmul(PS[py][:n, 128:192], lhsT=G_[:n, n:2 * n], rhs=PU[hh][:n, 64:128], start=True, stop=False, skip_group_check=True),
                         reads=[PUB, GB], writes=[PB[py]], inc=False)
                    S.op("pe", lambda e, hh=hh, G_=G_, vh=vh: e.matmul(PS[py][:n, 128:192], lhsT=G_[:n, 3 * n:4 * n], rhs=vh, start=False, stop=False, skip_group_check=True),
                         reads=[GB, B("xs_")], pwrites=[PB[py]], inc=False)
                    S.op("pe", lambda e, hh=hh, G_=G_: e.matmul(PS[py][0:64, 0:n], lhsT=PU[hh][:n, 0:64], rhs=G_[:n, n:2 * n], start=False, stop=False, skip_group_check=True),
                         reads=[PUB, GB], pwrites=[PB[py]], inc=False)
                    S.op("pe", lambda e, hh=hh, hs=hs: e.matmul(PS[py][0:64, 192:256], lhsT=PU[hh][:n, 0:64], rhs=bh[:n, hs], start=False, stop=False, skip_group_check=True),
                         reads=[PUB, B("bh")], pwrites=[PB[py]], inc=False)
                    S.op("pe", lambda e, hh=hh, hs=hs: e.matmul(PS[py][0:64, 256:320], lhsT=bh[:n, hs], rhs=PU[hh][:n, 64:128], start=False, stop=False, skip_group_check=True),
                         reads=[PUB, B("bh")], pwrites=[PB[py]], inc=False)
                    S.op("pe", lambda e, hh=hh, hs=hs, vh=vh: e.matmul(PS[py][0:64, 256:320], lhsT=kh[:n, hs], rhs=vh, start=False, stop=True, skip_group_check=True),
                         reads=[B("kh"), B("xs_")], pwrites=[PB[py]])
                    S.op("dve", lambda e, hh=hh, F_=F_: e.tensor_tensor(out=Y1T[hh][:, 0:n], in0=PS[py][0:64, 0:n], in1=F_[:, n:2 * n], op=ALU.add),
                         reads=[PB[py], FB], writes=[B(Y1T[hh].name)])
                    S.op("act", lambda e, hh=hh: e.activation(out=Y2[hh][:n, :], in_=PS[py][:n, 128:192], func=AF.Copy), reads=[PB[py]], writes=[B(Y2[hh].name)])
                    S.op("dve", lambda e, hh=hh: e.scalar_tensor_tensor(out=T1T[hh][:, :], in0=ident[0:64, 0:64], scalar=gC[:, hh:hh + 1], in1=PS[py][0:64, 192:256],
                                                                        op0=ALU.mult, op1=ALU.add),
                         reads=[PB[py], B("ident"), B("gC")], writes=[B(T1T[hh].name)])
                    S.op("act", lambda e, hh=hh: e.activation(out=T2[hh][:, :], in_=PS[py][0:64, 256:320], func=AF.Copy), reads=[PB[py]], writes=[B(T2[hh].name)])
                    yield
                    Hc = Hs[hh][hslot]; Hn = Hs[hh][1 - hslot]
                    ph = gbank()
                    S.op("pe", lambda e, hh=hh, Hc=Hc: e.matmul(PS[ph][:n, 0:64], lhsT=Y1T[hh][:, 0:n], rhs=Hc[:, :], start=True, stop=False, skip_group_check=True),
                         reads=[B(Y1T[hh].name), B(Hc.name)], writes=[PB[ph]], inc=False)
                    S.op("pe", lambda e, hh=hh, Hc=Hc: e.matmul(PS[ph][0:64, 64:128], lhsT=T1T[hh][:, :], rhs=Hc[:, :], start=False, stop=True, skip_group_check=True),
                         reads=[B(T1T[hh].name), B(Hc.name)], pwrites=[PB[ph]])
                    S.op("dve", lambda e, hh=hh, hs=hs: e.tensor_tensor(out=yb[:n, hs], in0=PS[ph][:n, 0:64], in1=Y2[hh][:n, :], op=ALU.add),
                         reads=[PB[ph], B(Y2[hh].name)], writes=[B("yb")] if hh == 0 else [], pwrites=[] if hh == 0 else [B("yb")])
                    S.op("dve", lambda e, hh=hh, Hn=Hn: e.tensor_tensor(out=Hn[:, :], in0=PS[ph][0:64, 64:128], in1=T2[hh][:, :], op=ALU.add),
                         reads=[PB[ph], B(T2[hh].name)], writes=[B(Hn.name)])
                gens = [head_gen(0), head_gen(1)]
                while gens:
                    for g_ in list(gens):
                        try:
                            next(g_)
                        except StopIteration:
                            gens.remove(g_)
                    yield
                yield
                y3 = yb[:n, :].rearrange("p (a b) -> p a b", b=64)
                yc3 = yc[:n, :].rearrange("p (a b) -> p a b", b=64)
                S.op("dve", lambda e: e.tensor_reduce(out=rst[:n, 2:4], in_=y3, axis=AX.X, op=ALU.add), reads=[B("yb")], writes=[B("rst")])
                S.op("dve", lambda e: e.tensor_scalar(out=rst[:n, 2:4], in0=rst[:n, 2:4], scalar1=-1.0 / 64, scalar2=None, op0=ALU.mult), reads=[B("rst")], writes=[B("rst")])
                S.op("dve", lambda e: e.tensor_tensor(out=yc3, in0=y3, in1=bc3(rst[:n, 2:4], n, 2, 64), op=ALU.add), reads=[B("yb"), B("rst")], writes=[B("yc")])
                S.op("dve", lambda e: e.tensor_tensor(out=ysq[:n, :], in0=yc[:n, :], in1=yc[:n, :], op=ALU.mult), reads=[B("yc")], writes=[B("ysq")])
                S.op("dve", lambda e: e.tensor_reduce(out=rst[:n, 4:6], in_=ysq[:n, :].rearrange("p (a b) -> p a b", b=64), axis=AX.X, op=ALU.add),
                     reads=[B("ysq")], writes=[B("rst")])
                rstd_chain(rst[:n, 4:6], rst[:n, 4:6], n, 2, 1.0 / 64, GN_EPS, B("rst"), B("rst"))
                S.op("dve", lambda e: e.tensor_tensor(out=yc3, in0=yc3, in1=bc3(rst[:n, 4:6], n, 2, 64), op=ALU.mult), reads=[B("yc"), B("rst")], writes=[B("yc")])
                S.op("dve", lambda e: e.tensor_tensor(out=yc[:n, :], in0=yc[:n, :], in1=lg_bc[:n, :], op=ALU.mult), reads=[B("yc"), B("parb")], writes=[B("yc")])
                S.op("dve", lambda e: e.tensor_tensor(out=yc[:n, :], in0=yc[:n, :], in1=lb_bc[:n, :], op=ALU.add), reads=[B("yc"), B("parb")], writes=[B("yc")])
                S.op("dve", lambda e: e.tensor_tensor(out=ysq[:n, :].rearrange("p (a b) -> p a b", b=64), in0=xv.rearrange("p (a b) -> p a b", b=64),
                                                      in1=bc3(bs[:n, 0:2], n, 2, 64), op=ALU.mult), reads=[B("xs_"), B("bs")], writes=[B("ysq")])
                S.op("dve", lambda e: e.tensor_tensor(out=yc[:n, :], in0=yc[:n, :], in1=ysq[:n, :], op=ALU.add), reads=[B("yc"), B("ysq")], writes=[B("yc")])
                S.op("dve", lambda e: e.tensor_tensor(out=obb[:n, :], in0=yc[:n, :], in1=g_sb[:n, :], op=ALU.mult), reads=[B("yc"), B("g_sb")], writes=[B("obb")])
                pb = gbank()
                pv = PS[pb][:].bitcast(BF16)
                S.op("pe", lambda e: e.transpose(out=pv[:, 0:n], in_=obb[:n, :], identity=identb[:n, :n]), reads=[B("obb"), B("identb")], writes=[PB[pb]])
                cat_writer(1, tile_idx, pv[:, 0:n], PB[pb])
                yield

            def state_out(hslot, dst_ap):
                pb = gbank()
                for hh in range(2):
                    Hc = Hs[hh][hslot]
                    S.op("pe", lambda e, hh=hh, Hc=Hc: e.transpose(out=PS[pb][0:64, hh * 64:(hh + 1) * 64], in_=Hc[:, :], identity=ident[0:64, 0:64]),
                         reads=[B(Hc.name), B("ident")], writes=[PB[pb]] if hh == 0 else [], pwrites=[] if hh == 0 else [PB[pb]], inc=(hh == 1))
                S.op("act", lambda e: e.activation(out=Hout[:, :, :], in_=PS[pb][0:64, 0:128].rearrange("p (a b) -> p a b", b=64), func=AF.Copy),
                     reads=[PB[pb]], writes=[B("Hout")])
                S.dma("sp", dst_ap.rearrange("h v k -> v h k"), Hout[:, :, :], reads=[B("Hout")])

            def run(g):
                for _ in g:
                    pass

            def drive(main, side, ratio):
                acc_ = 0.0
                for _ in main:
                    drive.n += 1
                    assert not any(S.uncommitted.values()), "yield inside an uncommitted group"
                    if side is not None:
                        acc_ += ratio
                        while acc_ >= 1.0 and side is not None:
                            acc_ -= 1.0
                            try:
                                next(side)
                            except StopIteration:
                                side = None
                if side is not None:
                    run(side)

            drive.n = 0
            XIv = XI.rearrange("(r c two p) t -> r c two p t", r=4, c=8, two=2, p=128)

            def make_writer(hp):
                def w_(kind, ti, src_ap, srcB):
                    rng_ = (ti * 128) // TR
                    toff_ = ti * 128 - rng_ * TR
                    ct = catA if kind == 0 else catB
                    cB = B(ct.name)
                    S.op("dve", lambda e: e.tensor_tensor(out=ct[:, :, :], in0=src_ap.unsqueeze(1).to_broadcast([128, 4, 128]),
                                                          in1=selt[:, 0:4].unsqueeze(2).to_broadcast([128, 4, 128]), op=ALU.mult),
                         reads=[srcB, B("selt")], writes=[cB])
                    c0 = 4 if kind == 1 else 0
                    S.dma("sp", XIv[rng_, c0:c0 + 4, hp, :, toff_:toff_ + 128].rearrange("j p t -> p j t"), ct[:, :, :],
                          reads=[cB], pwrites=[B("XI")])
                return w_

            def stage1(hp, i):
                slot = i % 2
                qb = (i // 2) % 2
                S.dma("sp", cstt[slot][:, :], csp[i * 128:(i + 1) * 128, :], writes=[B(cstt[slot].name)])
                yield from front(xb[i * 128:(i + 1) * 128, :], 128, xnTall, slot * 128, XNB[slot], g1t, slot)
                yield from project(xnTall, slot * 128, XNB[slot], 128, slot)
                yield from attn_prep(128, cstt[slot][:, :], B(cstt[slot].name), kpo[i * 128:(i + 1) * 128, hp, :], vpo[i * 128:(i + 1) * 128, hp, :],
                                     i, i * 128, (i % 2) * 128, QTs[qb], B(QTs[qb].name))

            def stage2(hp, i, writer):
                slot = i % 2
                prev_ap = None if i == 0 else prw[1 - slot]
                yield from rwkv(128, slot, prev_ap, None if i == 0 else B(prw[1 - slot].name), c128[:, :], i == 0, i % 2, writer, i)

            def interleave(g1, g2):
                gens = [g for g in (g1, g2) if g is not None]
                while gens:
                    for g_ in list(gens):
                        try:
                            next(g_)
                        except StopIteration:
                            gens.remove(g_)
                    yield

            def drive_keep(main, side, ratio):
                acc_ = 0.0
                for _ in main:
                    drive.n += 1
                    assert not any(S.uncommitted.values()), "yield inside an uncommitted group"
                    if side is not None:
                        acc_ += ratio
                        while acc_ >= 1.0 and side is not None:
                            acc_ -= 1.0
                            try:
                                next(side)
                            except StopIteration:
                                side = None
                return side

            def attn_stream(hp, Q, writer):
                kts = [(j * 128, 128, j, None) for j in range(2 * Q)]
                kts.append((2 * Q * 128, 128, 2 * Q, 0))
                kts.append(((2 * Q + 1) * 128, 128, 2 * Q + 1, 1))
                qb = Q % 2
                yield from attention(256, 128, kts, writer, QTs[qb], B(QTs[qb].name), 2 * Q)

            SEG_YIELDS = 25.0
            for hp in range(2):
                wsrc_name[0] = "wmyb%d" % hp
                load_unit_w(wmyb[hp])
                load_unit_params(upmy[hp:hp + 1, :], lwmy[hp], lg2my[hp])
                for hh in range(2):
                    S.op("dve", lambda e, hh=hh: e.memset(Hs[hh][0][:, :], 0.0), writes=[B(Hs[hh][0].name)])
                writer = make_writer(hp)
                run(stage1(hp, 0))
                side = None
                ratio = 0.0
                for i in range(NT):
                    if i % 2 == 1:
                        if side is not None:
                            run(side)
                        Q = i // 2
                        side = attn_stream(hp, Q, writer)
                        ratio = (2 * Q + 6) / (2 * SEG_YIELDS)
                    seg = interleave(stage2(hp, i, writer), stage1(hp, i + 1) if i + 1 < NT else None)
                    side = drive_keep(seg, side, ratio)
                if side is not None:
                    run(side)
                build_nc.main_yields = drive.n / float(NT) / (hp + 1)
                state_out(NT % 2, spo[hp])
                lastp = prw[(NT - 1) % 2]
                S.dma("sp", shpo[hp:hp + 1, :], lastp[127:128, :], reads=[B(lastp.name)])

            S.custom("pool", lambda e: e.collective_compute("ReduceScatter", ALU.add, replica_groups=[[0, 1, 2, 3], [4, 5, 6, 7]],
                                                             ins=[XI[:, :]], outs=[XO[:, :]]),
                     reads=[B("XI")], writes=[B("XO")])
            ckpt("rs")

            ckpt("casts")

            ntile_s = (NB * 64 + 127) // 128
            for t_ in range(ntile_s):
                n_ = min(128, NB * 64 - t_ * 128)
                run(front(xsm[t_ * 128:t_ * 128 + n_, :], n_, xnTall, t_ * 128, XNB[t_], g1t, t_ % 2))
            kst = sb("kst", [128, 128]); kstb = sb("kstb", [128, 128], BF16)
            kst2 = sb("kst2", [128, 128])

            def make_writer_s(u, bb):
                def cat_writer_s(kind, qt_, src_ap, srcB):
                    chunk = u + (8 if kind == 1 else 0)
                    ct = catA if kind == 0 else catB
                    cB = B(ct.name)
                    S.op("act", lambda e: e.activation(out=ct[:, 0, 0:64], in_=src_ap, func=AF.Copy), reads=[srcB], writes=[cB])
                    S.dma("sp", XS[chunk * 128:(chunk + 1) * 128, bb * 64:(bb + 1) * 64], ct[:, 0, 0:64], reads=[cB], pwrites=[B("XS")])
                return cat_writer_s

            def sample_pre(u, bb, idx):
                base = (idx % 2) * (NP + 1)
                qb = idx % 2
                for pt in range(NP):
                    kt_i = base + pt
                    S.dma("sp", kst[:, :], ck[bb, pt * 128:(pt + 1) * 128, u, :], writes=[B("kst")])
                    S.op("dve", lambda e: e.tensor_copy(out=kstb[:, :], in_=kst[:, :]), reads=[B("kst")], writes=[B("kstb")])
                    pb = gbank()
                    pv = PS[pb][:].bitcast(BF16)
                    S.op("pe", lambda e, pv=pv: e.transpose(out=pv[:, 0:128], in_=kstb[:, :], identity=identb[:, :]),
                         reads=[B("kstb"), B("identb")], writes=[PB[pb]])
                    S.op("act", lambda e, pv=pv: e.activation(out=KT[:, kt_i * 128:(kt_i + 1) * 128], in_=pv[:, 0:128], func=AF.Copy),
                         reads=[PB[pb]], writes=[KTB[kt_i]])
                    S.dma("sp", kst2[:, :], cv[bb, pt * 128:(pt + 1) * 128, u, :], writes=[B("kst2")])
                    S.op("dve", lambda e: e.tensor_copy(out=VV[:, kt_i, :], in_=kst2[:, :]), reads=[B("kst2")], writes=[VB[kt_i]])
                    if pt % 2 == 1:
                        yield
                S.dma("sp", Hld[:, :, :], st0[bb, 2 * u:2 * u + 2].rearrange("h v k -> v h k"), writes=[B("Hld")])
                pb = gbank()
                for hh in range(2):
                    S.op("pe", lambda e, hh=hh: e.transpose(out=PS[pb][0:64, hh * 64:(hh + 1) * 64], in_=Hld[:, hh, :], identity=ident[0:64, 0:64]),
                         reads=[B("Hld"), B("ident")], writes=[PB[pb]] if hh == 0 else [], pwrites=[] if hh == 0 else [PB[pb]], inc=(hh == 1))
                for hh in range(2):
                    S.op("act", lambda e, hh=hh: e.activation(out=Hs[hh][0][:, :], in_=PS[pb][0:64, hh * 64:(hh + 1) * 64], func=AF.Copy),
                         reads=[PB[pb]], writes=[B(Hs[hh][0].name)])
                S.op("dve", lambda e: e.memset(shbuf[0:63, :], 0.0), writes=[B("shbuf")])
                S.dma("sp", shbuf[63:64, :], sh0[bb, u:u + 1, :], reads=[], writes=[], pwrites=[B("shbuf")])
                yield
                yield from project(xnTall, bb * 64, XNB[bb // 2], 64, 0)
                yield from attn_prep(64, csst[:, :], B("csst"), kso[bb, :, u, :], vso[bb, :, u, :], base + NP, (base + NP) * 128, 0,
                                     QTs[qb], B(QTs[qb].name))
                yield from rwkv(64, 0, shbuf[0:64, :], B("shbuf"), c64[:, :], False, 0, make_writer_s(u, bb), 0)
                state_out(1, sso[bb, u])
                S.dma("sp", shso[bb, u:u + 1, :], prw[0][63:64, :], reads=[B(prw[0].name)])
                yield

            def sample_attn(u, bb, idx):
                base = (idx % 2) * (NP + 1)
                qb = idx % 2
                kts = [((base + j) * 128, 128, base + j, None) for j in range(NP)] + [((base + NP) * 128, 64, base + NP, None)]
                yield from attention(64, 64, kts, make_writer_s(u, bb), QTs[qb], B(QTs[qb].name), 0)

            side_s = None
            for u in range(8):
                wsrc_name[0] = "wub%d" % u
                load_unit_w(wub[u])
                load_unit_params(up[u:u + 1, :], lw[u], lg2[u])
                for bb in range(NB):
                    idx = u * NB + bb
                    side_s = drive_keep(sample_pre(u, bb, idx), side_s, (NP + 3) / 40.0)
                    if side_s is not None:
                        run(side_s)
                    side_s = sample_attn(u, bb, idx)
            if side_s is not None:
                run(side_s)

            ckpt("sample")
            S.barrier()
            es1.close()
            cur[0] = es
            NTOK = 512
            catT = sb("catT", [128, NCH, NTOK], BF16)
            hsb = sb("hsb", [128, NTOK // 128, D])
            hnT = sb("hnT", [128, NCH, NTOK], BF16)
            actT = sb("actT", [128, NFF, NTOK], BF16)
            wring = [sb("wring%d" % i, [128, 16 * 512], BF16) for i in range(3)]
            wrr = [0]
            sg = sb("sg", [128, 512])

            def wbuf():
                i = wrr[0]
                wrr[0] = (wrr[0] + 1) % 3
                return wring[i]

            blocks = []
            t0 = 0
            while t0 < TR:
                nb_ = min(NTOK, TR - t0)
                blocks.append(("p", t0, nb_))
                t0 += nb_
            t0 = 0
            while t0 < NB * 64:
                nb_ = min(NTOK, NB * 64 - t0)
                blocks.append(("s", t0, nb_))
                t0 += nb_
            for (kind, t0, nb_) in blocks:
                src = XO if kind == "p" else XS
                srcB = B("XO") if kind == "p" else B("XS")
                xsrc = xr if kind == "p" else xsm
                ydst = yp if kind == "p" else ys
                S.dma("sp", catT[:, :, 0:nb_], src[:, t0:t0 + nb_].rearrange("(c p) n -> p c n", p=128), reads=[srcB], writes=[B("catT")])
                ntl = (nb_ + 127) // 128
                tls = [(tt * 128, min(128, nb_ - tt * 128)) for tt in range(ntl)]
                for tt, (o_, n_) in enumerate(tls):
                    S.dma("sp", hsb[:n_, tt, :], xsrc[t0 + o_:t0 + o_ + n_, :], writes=[B("hsb%d" % tt)])
                for ng in range(4):
                    wb = wbuf(); wB = B(wb.name)
                    w3 = wb[:, :].rearrange("p (c n) -> p c n", n=512)
                    S.dma("sp", w3, wob[:, ng * 512:(ng + 1) * 512].rearrange("(c p) n -> p c n", p=128), reads=[B("wob")], writes=[wB])
                    for tt, (o_, n_) in enumerate(tls):
                        pb = tt % 4
                        for k in range(NCH):
                            S.op("pe", lambda e, k=k, pb=pb, o_=o_, n_=n_: e.matmul(PS[pb][:n_, 0:512], lhsT=catT[:, k, o_:o_ + n_], rhs=w3[:, k, :],
                                                                                  start=(k == 0), stop=(k == NCH - 1), skip_group_check=True),
                                 reads=[B("catT"), wB], writes=[PB[pb]] if k == 0 else [], pwrites=[] if k == 0 else [PB[pb]], inc=(k == NCH - 1))
                        S.op("dve", lambda e, tt=tt, pb=pb, n_=n_, ng=ng: e.tensor_tensor(out=hsb[:n_, tt, ng * 512:(ng + 1) * 512], in0=PS[pb][:n_, 0:512],
                                                                                         in1=hsb[:n_, tt, ng * 512:(ng + 1) * 512], op=ALU.add),
                             reads=[PB[pb], B("hsb%d" % tt)], writes=[B("hsb%d" % tt)])
                for tt, (o_, n_) in enumerate(tls):
                    hB = B("hsb%d" % tt)
                    S.op("dve", lambda e, tt=tt, n_=n_: e.scalar_tensor_tensor(out=junk[:n_, :], in0=hsb[:n_, tt, :], scalar=1.0, in1=hsb[:n_, tt, :], op0=ALU.mult, op1=ALU.mult, accum_out=st4[:n_, 0:1]),
                         reads=[hB], writes=[B("junk"), B("st4")])
                    rstd_chain(st4[:n_, 0:1], st4[:n_, 0:1], n_, 1, 1.0 / D, RMS_EPS, B("st4"), B("st4"))
                    S.op("act", lambda e, tt=tt, n_=n_: e.activation(out=xsb[:n_, :], in_=hsb[:n_, tt, :], func=AF.Copy, scale=st4[:n_, 0:1]),
                         reads=[hB, B("st4")], writes=[B("xsb")])
                    for half in range(2):
                        pb = 4 + half
                        pv = PS[pb][:].bitcast(BF16)
                        for c8 in range(8):
                            c = half * 8 + c8
                            S.op("pe", lambda e, c=c, c8=c8, pv=pv, n_=n_: e.transpose(out=pv[:, c8 * n_:(c8 + 1) * n_], in_=xsb[:n_, c * 128:(c + 1) * 128],
                                                                                     identity=identb[:n_, :n_]),
                                 reads=[B("xsb"), B("identb")], writes=[PB[pb]] if c8 == 0 else [], pwrites=[] if c8 == 0 else [PB[pb]], inc=(c8 == 7))
                        S.op("dve", lambda e, half=half, pv=pv, n_=n_, o_=o_: e.tensor_tensor(
                            out=hnT[:, half * 8:(half + 1) * 8, o_:o_ + n_], in0=pv[:, 0:8 * n_].rearrange("p (c n) -> p c n", n=n_),
                            in1=bc3(g2t[:, half * 8:(half + 1) * 8], 128, 8, n_), op=ALU.mult),
                            reads=[PB[pb], B("g2t")], writes=[B("hnT")] if (tt == 0 and half == 0) else [], pwrites=[] if (tt == 0 and half == 0) else [B("hnT")])
                for fg in range(NFF // 4):
                    wg_ = wbuf(); wgB = B(wg_.name)
                    wg3 = wg_[:, :].rearrange("p (c n) -> p c n", n=512)
                    S.dma("sp", wg3, wgb[:, fg * 512:(fg + 1) * 512].rearrange("(c p) n -> p c n", p=128), reads=[B("wgb")], writes=[wgB])
                    wu_ = wbuf(); wuB = B(wu_.name)
                    wu3 = wu_[:, :].rearrange("p (c n) -> p c n", n=512)
                    S.dma("sp", wu3, wupb[:, fg * 512:(fg + 1) * 512].rearrange("(c p) n -> p c n", p=128), reads=[B("wupb")], writes=[wuB])
                    for f4 in range(4):
                        f = fg * 4 + f4
                        pg_, pu_ = (f % 2) * 2, (f % 2) * 2 + 1
                        for k in range(NCH):
                            S.op("pe", lambda e, k=k, pg_=pg_, f4=f4: e.matmul(PS[pg_][:, 0:nb_], lhsT=wg3[:, k, f4 * 128:(f4 + 1) * 128], rhs=hnT[:, k, 0:nb_],
                                                                             start=(k == 0), stop=(k == NCH - 1), skip_group_check=True),
                                 reads=[B("hnT"), wgB], writes=[PB[pg_]] if k == 0 else [], pwrites=[] if k == 0 else [PB[pg_]], inc=(k == NCH - 1))
                        for k in range(NCH):
                            S.op("pe", lambda e, k=k, pu_=pu_, f4=f4: e.matmul(PS[pu_][:, 0:nb_], lhsT=wu3[:, k, f4 * 128:(f4 + 1) * 128], rhs=hnT[:, k, 0:nb_],
                                                                             start=(k == 0), stop=(k == NCH - 1), skip_group_check=True),
                                 reads=[B("hnT"), wuB], writes=[PB[pu_]] if k == 0 else [], pwrites=[] if k == 0 else [PB[pu_]], inc=(k == NCH - 1))
                        S.op("act", lambda e, pg_=pg_: e.activation(out=sg[:, 0:nb_], in_=PS[pg_][:, 0:nb_], func=AF.Silu), reads=[PB[pg_]], writes=[B("sg")])
                        S.op("dve", lambda e, pu_=pu_, f=f: e.tensor_tensor(out=actT[:, f, 0:nb_], in0=PS[pu_][:, 0:nb_], in1=sg[:, 0:nb_], op=ALU.mult),
                             reads=[PB[pu_], B("sg")], writes=[B("actT")] if f == 0 else [], pwrites=[] if f == 0 else [B("actT")])
                for ng in range(4):
                    for kq in range(4):
                        wb = wbuf(); wB = B(wb.name)
                        w3 = wb[:, 0:11 * 512].rearrange("p (c n) -> p c n", n=512)
                        S.dma("sp", w3, wdb[kq * 11 * 128:(kq + 1) * 11 * 128, ng * 512:(ng + 1) * 512].rearrange("(c p) n -> p c n", p=128),
                              reads=[B("wdb")], writes=[wB])
                        for tt, (o_, n_) in enumerate(tls):
                            pb = 4 + tt
                            for kc in range(11):
                                kk_ = kq * 11 + kc
                                first = (kk_ == 0); last = (kk_ == NFF - 1)
                                S.op("pe", lambda e, kc=kc, kk_=kk_, pb=pb, o_=o_, n_=n_, first=first, last=last: e.matmul(
                                    PS[pb][:n_, 0:512], lhsT=actT[:, kk_, o_:o_ + n_], rhs=w3[:, kc, :], start=first, stop=last, skip_group_check=True),
                                     reads=[B("actT"), wB], writes=[PB[pb]] if first else [], pwrites=[] if first else [PB[pb]], inc=(kc == 10))
                    for tt, (o_, n_) in enumerate(tls):
                        pb = 4 + tt
                        S.op("dve", lambda e, tt=tt, pb=pb, n_=n_, ng=ng: e.tensor_tensor(out=hsb[:n_, tt, ng * 512:(ng + 1) * 512], in0=PS[pb][:n_, 0:512],
                                                                                         in1=hsb[:n_, tt, ng * 512:(ng + 1) * 512], op=ALU.add),
                             reads=[PB[pb], B("hsb%d" % tt)], writes=[B("hsb%d" % tt)])
                for tt, (o_, n_) in enumerate(tls):
                    S.dma("sp", ydst[t0 + o_:t0 + o_ + n_, :], hsb[:n_, tt, :], reads=[B("hsb%d" % tt)])

        except _Stop:
            es1.close()
        for s, c in S.all_dma():
            nc.sync.wait_ge(s, c)
        for e in ("pe", "act", "dve", "pool"):
            if S.cnt[e] > 0:
                nc.sync.wait_ge(S.esem[e], S.cnt[e])
        build_nc.nops = S.nops
    return nc


def _unit_cols(u):
    A = 3072
    q = np.arange(128) + u * 128
    k = 1024 + q
    v = 2048 + q
    r = A + u * 128 + np.arange(128)
    kr = A + 1024 + u * 128 + np.arange(128)
    vr = A + 2048 + u * 128 + np.arange(128)
    lo = A + 3072 + np.arange(192)
    return np.concatenate([q, k, v, r, kr, vr, lo])


def _rw_cols(u):
    r = u * 128 + np.arange(128)
    return np.concatenate([r, 1024 + r, 2048 + r, 3072 + np.arange(192)])


def _consts(T):
    c = {}
    idx = np.arange(128)
    c["cid"] = np.eye(128, dtype=np.float32)
    c["cut"] = (idx[:, None] <= idx[None, :]).astype(np.float32)
    c["cs1"] = (idx[:, None] + 1 == idx[None, :]).astype(np.float32)
    cc = np.zeros((128, 128), np.float32); cc[127, 0] = 1.0
    c["cc128"] = cc
    cc = np.zeros((64, 64), np.float32); cc[63, 0] = 1.0
    c["cc64"] = cc
    mA = (idx[:, None] < idx[None, :]).astype(np.float32)
    mL = (idx[:, None] <= idx[None, :]).astype(np.float32)
    c["cgm"] = np.concatenate([mA, mL, mA, mL], axis=1)
    c["clow"] = (idx[None, :] < idx[:, None]).astype(np.float32)
    cm = np.ones((128, 128), np.float32); cm[64:, :64] = 0.0
    on = np.ones((128, 128), np.float32); ze = np.zeros((128, 128), np.float32)
    d0 = np.concatenate([cm, on], axis=1); d1 = np.concatenate([ze, cm], axis=1)
    c["cam"] = np.stack([np.concatenate([d0, d0], axis=1), np.concatenate([d1, d1], axis=1)]).astype(np.float32)
    inv = (np.float32(ROPE_THETA) ** (-np.arange(0, 16, 2, dtype=np.float32) / np.float32(16))).astype(np.float32)

    def tab(pos):
        ang = (pos.astype(np.float32)[:, None] * inv[None, :]).astype(np.float32)
        return np.concatenate([np.cos(ang), np.sin(ang)], axis=1).astype(np.float32)
    c["csp"] = tab(np.arange(T))
    return c, tab


_NC_CACHE = {}


def kernel(x_prompt, x_sample, cache_attn_k, cache_attn_v, state_rwkv, state_rwkv_shift,
           norm1_g, w_in, q_norm_g, k_norm_g, lambda_q1, lambda_k1, lambda_q2, lambda_k2, subln_g,
           mu_rwkv, w0, w2, a0, a2, g2, k_k, k_a, r_k, lnx_g, lnx_b,
           w_out, norm2_g, w_gate, w_up, w_down):
    f = lambda a: np.ascontiguousarray(np.asarray(a, dtype=np.float32))
    x_prompt, x_sample = f(x_prompt), f(x_sample)
    Bp, T, _ = x_prompt.shape
    Bs, Ts, _ = x_sample.shape
    PAST = cache_attn_k.shape[2]
    NB = Bs // 8
    assert Bp == 2 and Ts == 64
    dm = Dims(T, NB, PAST)
    key = (T, NB, PAST)
    if key not in _NC_CACHE:
        _NC_CACHE[key] = build_nc(dm)
    nc = _NC_CACHE[key]
    TR = T // 4
    w_in0 = f(w_in)[0]
    wu = np.stack([w_in0[:, _unit_cols(u)] for u in range(8)])
    mu0 = f(mu_rwkv)[0]; w00 = f(w0)[0]; a00 = f(a0)[0]; kk0 = f(k_k)[0]; ka0 = f(k_a)[0]
    rk0 = f(r_k)[0].reshape(-1); lg0 = f(lnx_g)[0]; lb0 = f(lnx_b)[0]
    w20 = f(w2)[0]; a20 = f(a2)[0]; g20 = f(g2)[0]
    ups, lws, lg2s = [], [], []
    for u in range(8):
        hs = slice(u * 128, (u + 1) * 128)
        ups.append(np.concatenate([mu0[_rw_cols(u)], w00[hs], a00[hs], kk0[hs], ka0[hs], rk0[hs], lg0[hs], lb0[hs]]))
        lws.append(np.concatenate([w20[:, hs], a20[:, hs]], axis=0))
        lg2s.append(g20[:, hs])
    up = np.stack(ups).astype(np.float32); lw = np.stack(lws).astype(np.float32); lg2_ = np.stack(lg2s).astype(np.float32)
    consts, tab = _consts(T)
    consts["css"] = tab(PAST + np.arange(64))
    common = dict(
        wu=wu, up=up, lw=lw, lg2=lg2_,
        g1T=np.ascontiguousarray(f(norm1_g)[0].reshape(16, 128).T), g2T=np.ascontiguousarray(f(norm2_g)[0].reshape(16, 128).T),
        qkg=np.concatenate([f(q_norm_g)[0].reshape(-1), f(k_norm_g)[0].reshape(-1)])[None, :],
        lamv=np.concatenate([f(lambda_q1)[0], f(lambda_k1)[0], f(lambda_q2)[0], f(lambda_k2)[0]])[None, :],
        subg=f(subln_g)[0][None, :],
        w_out=f(w_out)[0], w_gate=f(w_gate)[0], w_up=f(w_up)[0], w_down=f(w_down)[0], **consts)
    ck = f(cache_attn_k)[0]; cv = f(cache_attn_v)[0]; st = f(state_rwkv)[0]; sh = f(state_rwkv_shift)[0][:, 0, :]
    sh_u = np.stack([sh[:, _rw_cols(u)] for u in range(8)], axis=1)
    in_maps = []
    for c in range(8):
        b, j = c // 4, c % 4
        selm = np.zeros((128, 4), np.float32); selm[:, j] = 1.0
        m = dict(common)
        m.update(xb=x_prompt[b], xr=np.ascontiguousarray(x_prompt[b, j * TR:(j + 1) * TR]),
                 xsm=np.ascontiguousarray(x_sample[c * NB:(c + 1) * NB].reshape(NB * 64, D)),
                 ck=np.ascontiguousarray(ck[c * NB:(c + 1) * NB]), cv=np.ascontiguousarray(cv[c * NB:(c + 1) * NB]),
                 st0=np.ascontiguousarray(st[c * NB:(c + 1) * NB]), sh0=np.ascontiguousarray(sh_u[c * NB:(c + 1) * NB]),
                 wmy=np.ascontiguousarray(wu[2 * j:2 * j + 2]), upmy=np.ascontiguousarray(up[2 * j:2 * j + 2]),
                 lwmy=np.ascontiguousarray(lw[2 * j:2 * j + 2]), lg2my=np.ascontiguousarray(lg2_[2 * j:2 * j + 2]), sel=selm)
        in_maps.append({k: np.ascontiguousarray(v, dtype=np.float32) for k, v in m.items()})
    res = run_bass_kernel_spmd(nc, in_maps, core_ids=list(range(8)))
    R = res.results
    y_p = np.zeros((2, T, D), np.float32); y_s = np.zeros((Bs, 64, D), np.float32)
    k_p = np.zeros((1, 2, T, 8, 128), np.float32); v_p = np.zeros((1, 2, T, 8, 128), np.float32)
    S_p = np.zeros((1, 2, 16, 64, 64), np.float32); sh_p = np.zeros((1, 2, 1, 3264), np.float32)
    k_s = np.zeros((1, Bs, 64, 8, 128), np.float32); v_s = np.zeros((1, Bs, 64, 8, 128), np.float32)
    S_s = np.zeros((1, Bs, 16, 64, 64), np.float32); sh_s = np.zeros((1, Bs, 1, 3264), np.float32)
    for c in range(8):
        b, j = c // 4, c % 4
        r = R[c]
        y_p[b, j * TR:(j + 1) * TR] = r["yp"]
        y_s[c * NB:(c + 1) * NB] = np.asarray(r["ys"]).reshape(NB, 64, D)
        for hp in range(2):
            u = 2 * j + hp
            k_p[0, b, :, u, :] = r["kpo"][:, hp, :]
            v_p[0, b, :, u, :] = r["vpo"][:, hp, :]
            S_p[0, b, 2 * u:2 * u + 2] = r["spo"][hp]
            sh_p[0, b, 0, _rw_cols(u)] = r["shpo"][hp]
        k_s[0, c * NB:(c + 1) * NB] = r["kso"]
        v_s[0, c * NB:(c + 1) * NB] = r["vso"]
        S_s[0, c * NB:(c + 1) * NB] = np.asarray(r["sso"]).reshape(NB, 16, 64, 64)
        for u in range(8):
            sh_s[0, c * NB:(c + 1) * NB, 0, _rw_cols(u)] = np.asarray(r["shso"])[:, u, :].T if False else 0
        shso = np.asarray(r["shso"])
        for u in range(8):
            for bb in range(NB):
                sh_s[0, c * NB + bb, 0, _rw_cols(u)] = shso[bb, u]
    return (y_p, y_s, k_p, v_p, S_p, sh_p, k_s, v_s, S_s, sh_s)
```

```python
import math
import contextlib
import numpy as np
import concourse.bass as bass
import concourse.mybir as mybir
from concourse.bass_utils import run_bass_kernel_spmd

F32 = mybir.dt.float32
BF16 = mybir.dt.bfloat16
AF = mybir.ActivationFunctionType
ALU = mybir.AluOpType
AX = mybir.AxisListType

D = 2048
NCH = 16
UC = 960
DFF = 5632
NFF = 44
RMS_EPS = 1e-6
GN_EPS = 64e-5
LAM_INIT = 0.2
NPAR = 1472
ROPE_THETA = 500000.0


class _Stop(Exception):
    pass


def ckpt(name):
    import os
    if os.environ.get("KSTOP") == name:
        raise _Stop()


class Buf:
    __slots__ = ("name", "w", "r", "excl")

    def __init__(self, name, excl=False):
        self.name = name
        self.w = {}
        self.r = {}
        self.excl = excl


class Sched:
    LIMIT = 30000

    def __init__(self, nc, es, n_dma_sems=40):
        self.nc = nc
        self.es = es
        self.eng = {"pe": nc.tensor, "act": nc.scalar, "dve": nc.vector, "pool": nc.gpsimd, "sp": nc.sync}
        self.esem = {}
        self.cnt = {}
        self.seen = {e: {} for e in self.eng}
        self.uncommitted = {e: False for e in self.eng}
        self.nsem = 0
        for e in ("pe", "act", "dve", "pool"):
            self._new_esem(e)
        self.rings = {}
        for q, nq_ in (("sp", 30), ("pool", 12), ("cc", 1)):
            self.rings[q] = dict(sem=[self._sem("dma_%s%d" % (q, i)) for i in range(nq_)], cnt=[0] * nq_, nxt=0)
        self.nops = 0

    def _sem(self, name):
        self.nsem += 1
        return self.es.enter_context(self.nc.semaphore(name))

    def _new_esem(self, e):
        self.esem[e] = self._sem("e_%s_%d" % (e, self.nsem))
        self.cnt[e] = 0

    def _collect(self, reads, writes, pwrites):
        deps = {}

        def add(d):
            for k, (s, v) in d.items():
                if k not in deps or deps[k][1] < v:
                    deps[k] = (s, v)
        for b in reads:
            add(b.w)
            if b.excl:
                add(b.r)
        for b in writes:
            add(b.w)
            add(b.r)
        for b in pwrites:
            add(b.r)
        return deps

    def _emit_waits(self, e, deps):
        eng = self.eng[e]
        seen = self.seen[e]
        for k, (s, v) in deps.items():
            if seen.get(k, 0) >= v:
                continue
            if e in self.esem and s is self.esem[e] and v > self.cnt[e]:
                continue
            for e2 in self.esem:
                if s is self.esem[e2] and v > self.cnt[e2]:
                    raise RuntimeError("wait on uncommitted event of %s from %s" % (e2, e))
            eng.wait_ge(s, v)
            seen[k] = v

    def _record(self, ev, reads, writes, pwrites):
        k = id(ev[0])
        for b in reads:
            if k not in b.r or b.r[k][1] < ev[1]:
                b.r[k] = ev
        for b in writes:
            b.w = {k: ev}
            b.r = {}
        for b in pwrites:
            if k not in b.w or b.w[k][1] < ev[1]:
                b.w[k] = ev

    def op(self, e, fn, reads=(), writes=(), inc=True, pwrites=()):
        self.nops += 1
        if self.cnt[e] >= self.LIMIT and not self.uncommitted[e]:
            self._new_esem(e)
        deps = self._collect(reads, writes, pwrites)
        self._emit_waits(e, deps)
        ins = fn(self.eng[e])
        ev = (self.esem[e], self.cnt[e] + 1)
        if inc:
            self.cnt[e] += 1
            ins.then_inc(self.esem[e], 1)
            self.uncommitted[e] = False
        else:
            self.uncommitted[e] = True
        self._record(ev, reads, writes, pwrites)
        return ins

    def _ring_next(self, q):
        r = self.rings[q]
        i = r["nxt"]
        r["nxt"] = (i + 1) % len(r["sem"])
        return r, i

    def dma(self, q, out, in_, reads=(), writes=(), pwrites=(), **kw):
        self.nops += 1
        r, i = self._ring_next(q)
        s = r["sem"][i]
        deps = self._collect(reads, writes, pwrites)
        if r["cnt"][i] > 0:
            deps[id(s)] = (s, r["cnt"][i])
        self._emit_waits(q, deps)
        ins = self.eng[q].dma_start(out=out, in_=in_, **kw)
        r["cnt"][i] += 16
        ins.then_inc(s, 16)
        ev = (s, r["cnt"][i])
        self._record(ev, reads, writes, pwrites)
        return ev

    def custom(self, q, fn, reads=(), writes=(), pwrites=()):
        r, i = self._ring_next("cc")
        s = r["sem"][i]
        deps = self._collect(reads, writes, pwrites)
        if r["cnt"][i] > 0:
            deps[id(s)] = (s, r["cnt"][i])
        self._emit_waits(q, deps)
        ins = fn(self.eng[q])
        r["cnt"][i] += 1
        ins.then_inc(s, 1)
        ev = (s, r["cnt"][i])
        self._record(ev, reads, writes, pwrites)
        return ev

    def all_dma(self):
        for r in self.rings.values():
            for s, c in zip(r["sem"], r["cnt"]):
                if c > 0:
                    yield s, c

    def barrier(self):
        for e in self.eng:
            for e2 in self.esem:
                if self.cnt[e2] > 0 and self.seen[e].get(id(self.esem[e2]), 0) < self.cnt[e2]:
                    self.eng[e].wait_ge(self.esem[e2], self.cnt[e2])
                    self.seen[e][id(self.esem[e2])] = self.cnt[e2]
            for s, c in self.all_dma():
                if self.seen[e].get(id(s), 0) < c:
                    self.eng[e].wait_ge(s, c)
                    self.seen[e][id(s)] = c

    def wait_all(self, q, bufs):
        deps = self._collect(bufs, bufs, ())
        self._emit_waits(q, deps)


class Dims:
    def __init__(self, T, NB, PAST):
        self.T = T
        self.NB = NB
        self.PAST = PAST
        self.NT = T // 128
        self.TR = T // 4
        self.NP = PAST // 128


def build_nc(dm):
    T, NB, PAST, NT, TR, NP = dm.T, dm.NB, dm.PAST, dm.NT, dm.TR, dm.NP
    nc = bass.Bass("TRN2", target_bir_lowering=False)

    def din(name, shape, dt=F32):
        return nc.dram_tensor(name, list(shape), dt, kind="ExternalInput").ap()

    def dout(name, shape, dt=F32):
        return nc.dram_tensor(name, list(shape), dt, kind="ExternalOutput").ap()

    def dint(name, shape, dt):
        return nc.dram_tensor(name, list(shape), dt, kind="Internal").ap()

    xb = din("xb", [T, D])
    xr = din("xr", [TR, D])
    xsm = din("xsm", [NB * 64, D])
    ck = din("ck", [NB, PAST, 8, 128])
    cv = din("cv", [NB, PAST, 8, 128])
    st0 = din("st0", [NB, 16, 64, 64])
    sh0 = din("sh0", [NB, 8, 576])
    wu = din("wu", [8, D, UC])
    wmy = din("wmy", [2, D, UC])
    up = din("up", [8, NPAR])
    upmy = din("upmy", [2, NPAR])
    lw = din("lw", [8, 128, 128])
    lwmy = din("lwmy", [2, 128, 128])
    lg2 = din("lg2", [8, 64, 128])
    lg2my = din("lg2my", [2, 64, 128])
    g1T = din("g1T", [128, NCH])
    g2T = din("g2T", [128, NCH])
    qkg = din("qkg", [1, 256])
    lamv = din("lamv", [1, 256])
    subg = din("subg", [1, 128])
    sel = din("sel", [128, 4])
    w_out = din("w_out", [D, D])
    w_gate = din("w_gate", [D, DFF])
    w_up = din("w_up", [D, DFF])
    w_down = din("w_down", [DFF, D])
    csp = din("csp", [T, 16])
    css = din("css", [64, 16])
    cid = din("cid", [128, 128])
    cut = din("cut", [128, 128])
    cs1 = din("cs1", [128, 128])
    cc128 = din("cc128", [128, 128])
    cc64 = din("cc64", [64, 64])
    cgm = din("cgm", [128, 512])
    clow = din("clow", [128, 128])
    cam = din("cam", [2, 128, 512])
    yp = dout("yp", [TR, D])
    ys = dout("ys", [NB * 64, D])
    kpo = dout("kpo", [T, 2, 128])
    vpo = dout("vpo", [T, 2, 128])
    spo = dout("spo", [2, 2, 64, 64])
    shpo = dout("shpo", [2, 576])
    kso = dout("kso", [NB, 64, 8, 128])
    vso = dout("vso", [NB, 64, 8, 128])
    sso = dout("sso", [NB, 8, 2, 64, 64])
    shso = dout("shso", [NB, 8, 576])
    XI = dint("XI", [4 * 16 * 128, TR], BF16)
    XO = dint("XO", [16 * 128, TR], BF16)
    XS = dint("XS", [16 * 128, NB * 64], BF16)
    wub = dint("wub", [8, D, UC], BF16)
    wmyb = dint("wmyb", [2, D, UC], BF16)
    wob = dint("wob", [D, D], BF16)
    wgb = dint("wgb", [D, DFF], BF16)
    wupb = dint("wupb", [D, DFF], BF16)
    wdb = dint("wdb", [DFF, D], BF16)

    es = contextlib.ExitStack()
    with es:
        S = Sched(nc, es)
        bufs = {}

        es1 = contextlib.ExitStack()
        cur = [es]

        def sb(name, shape, dt=F32):
            t = cur[0].enter_context(nc.sbuf_tensor(name, list(shape), dt))
            bufs[name] = Buf(name)
            return t

        def B(name):
            if name not in bufs:
                bufs[name] = Buf(name)
            return bufs[name]

        PS = [es.enter_context(nc.psum_tensor("ps%d" % i, [128, 512], F32)) for i in range(8)]
        PB = [Buf("ps%d" % i, excl=True) for i in range(8)]
        gen_rr = [0]

        def gbank():
            i = gen_rr[0]
            gen_rr[0] = (gen_rr[0] + 1) % 4
            return i

        try:
            def cast_w(dst, src, rows, cols, bname):
                b = B(bname)
                r = 0
                while r < rows:
                    rr = min(128, rows - r)
                    c = 0
                    while c < cols:
                        cc = min(2048, cols - c)
                        S.dma("pool", dst[r:r + rr, c:c + cc], src[r:r + rr, c:c + cc], pwrites=[b])
                        c += cc
                    r += rr

            for u in range(2):
                cast_w(wmyb[u], wmy[u], D, UC, "wmyb%d" % u)
            for u in range(8):
                cast_w(wub[u], wu[u], D, UC, "wub%d" % u)
            cast_w(wob, w_out, D, D, "wob")
            cast_w(wgb, w_gate, D, DFF, "wgb")
            cast_w(wupb, w_up, D, DFF, "wupb")
            cast_w(wdb, w_down, DFF, D, "wdb")

            ident = sb("ident", [128, 128]); identb = sb("identb", [128, 128], BF16)
            ut = sb("ut", [128, 128]); s1m = sb("s1m", [128, 128]); c128 = sb("c128", [128, 128]); c64 = sb("c64", [64, 64])
            ones = sb("ones", [128, 128]); onesb = sb("onesb", [128, 1], BF16)
            gmask = sb("gmask", [128, 512]); lowm = sb("lowm", [128, 128])
            amask = sb("amask", [128, 2, 512], BF16)
            g1t = sb("g1t", [128, NCH]); g2t = sb("g2t", [128, NCH])
            qkgb = sb("qkgb", [128, 256]); lamb = sb("lamb", [128, 256]); subgb = sb("subgb", [128, 128])
            selt = sb("selt", [128, 4])
            cstt = [sb("cstt%d" % i, [128, 16]) for i in range(2)]; csst = sb("csst", [64, 16])
            lamt = sb("lamt", [128, 4])
            neglam = sb("neglam", [128, 1])
            for t_, src in ((ident, cid), (ut, cut), (s1m, cs1), (c128, cc128), (gmask, cgm), (lowm, clow),
                            (g1t, g1T), (g2t, g2T), (selt, sel)):
                S.dma("sp", t_[:], src[:, :], writes=[B(t_.name)])
            S.dma("sp", c64[:], cc64[:, :], writes=[B("c64")])
            S.dma("sp", csst[:], css[:, :], writes=[B("csst")])
            S.dma("sp", qkgb[:], qkg.partition_broadcast(128), writes=[B("qkgb")])
            S.dma("sp", lamb[:], lamv.partition_broadcast(128), writes=[B("lamb")])
            S.dma("sp", subgb[:], subg.partition_broadcast(128), writes=[B("subgb")])
            S.op("dve", lambda e: e.memset(ones[:], 1.0), writes=[B("ones")])
            S.op("dve", lambda e: e.memset(onesb[:], 1.0), writes=[B("onesb")])
            S.op("dve", lambda e: e.tensor_copy(out=identb[:], in_=ident[:]), reads=[B("ident")], writes=[B("identb")])
            S.dma("pool", amask[:], cam.rearrange("a p c -> p a c"), writes=[B("amask")])
            S.op("dve", lambda e: e.tensor_scalar(out=subgb[:], in0=subgb[:], scalar1=1.0 - LAM_INIT, scalar2=None,
                                                  op0=ALU.mult), reads=[B("subgb")], writes=[B("subgb")])
            lscr = sb("lscr", [128, 64])
            S.op("dve", lambda e: e.scalar_tensor_tensor(out=lscr[:], in0=lamb[:, 0:64], scalar=1.0, in1=lamb[:, 64:128], op0=ALU.mult, op1=ALU.mult, accum_out=lamt[:, 0:1]),
                 reads=[B("lamb")], writes=[B("lscr"), B("lamt")])
            S.op("dve", lambda e: e.scalar_tensor_tensor(out=lscr[:], in0=lamb[:, 128:192], scalar=1.0, in1=lamb[:, 192:256], op0=ALU.mult, op1=ALU.mult, accum_out=lamt[:, 1:2]),
                 reads=[B("lamb"), B("lamt")], writes=[B("lscr"), B("lamt")])
            S.op("act", lambda e: e.activation(out=lamt[:, 2:4], in_=lamt[:, 0:2], func=AF.Exp),
                 reads=[B("lamt")], writes=[B("lamt")])
            S.op("dve", lambda e: e.tensor_tensor(out=neglam[:], in0=lamt[:, 3:4], in1=lamt[:, 2:3], op=ALU.subtract),
                 reads=[B("lamt")], writes=[B("neglam")])
            S.op("dve", lambda e: e.tensor_scalar(out=neglam[:], in0=neglam[:], scalar1=-LAM_INIT, scalar2=None, op0=ALU.add),
                 reads=[B("neglam")], writes=[B("neglam")])
            ckpt("consts")

            NKT = max(NT, 2 * (NP + 1))
            xsb = sb("xsb", [128, D], BF16)
            junk = sb("junk", [128, D], BF16)
            st4 = sb("st4", [128, 8])
            cur[0] = es1
            wub_sb = sb("wub_sb", [128, NCH, UC], BF16)
            KT = sb("KT", [128, NKT * 128], BF16)
            VV = sb("VV", [128, NKT, 128], BF16)
            KTB = [Buf("KT%d" % i) for i in range(NKT)]
            VB = [Buf("V%d" % i) for i in range(NKT)]
            xt = [sb("xt%d" % i, [128, D]) for i in range(2)]
            xnTall = sb("xnTall", [128, NCH, 256], BF16)
            XNB = [Buf("xnTslot0"), Buf("xnTslot1")]
            gsb = sb("gsb", [128, 384])
            prw = [sb("prw%d" % i, [128, 576]) for i in range(2)]
            shbuf = sb("shbuf", [128, 576])
            tq = sb("tq", [128, 256]); qkn = sb("qkn", [128, 256]); rtmp = sb("rtmp", [128, 4, 4, 8])
            qkb = sb("qkb", [128, 256], BF16)
            QTs = [sb("QT%d" % i, [128, 2, 256], BF16) for i in range(2)]
            Eb = [sb("Eb%d" % i, [128, 512], BF16) for i in range(2)]
            OT = sb("OT", [128, 512]); zsb = sb("zsb", [1, 512]); Zacc = sb("Zacc", [128, 512]); ajunk = sb("ajunk", [128, 128], BF16); one11 = sb("one11", [1, 1])
            S.op("dve", lambda e: e.memset(one11[:], 1.0), writes=[B("one11")])
            for q_ in QTs:
                S.op("dve", lambda e, q_=q_: e.memset(q_[:, :, :], 0.0), writes=[B(q_.name)])
            osb = sb("osb", [128, 128]); onb = sb("onb", [128, 128], BF16); ast = sb("ast", [128, 8])
            catA = sb("catA", [128, 4, 128], BF16); catB = sb("catB", [128, 4, 128], BF16)
            parb = sb("parb", [128, NPAR])
            lwt = sb("lwt", [64, 2, 128]); lg2t = sb("lg2t", [64, 128])
            xs_ = sb("xs_", [128, 576])
            Et = sb("Et", [128, 128]); LT = sb("LT", [128, 192]); LTT = sb("LTT", [64, 3, 128])
            za = sb("za", [128, 256]); sa = sb("sa", [128, 256]); ld = sb("ld", [128, 128]); g_sb = sb("g_sb", [128, 128])
            kkv = sb("kkv", [128, 128]); rst = sb("rst", [128, 8]); k2 = sb("k2", [128, 128]); mm_ = sb("mm_", [128, 128])
            bvec = sb("bvec", [128, 128]); bs = sb("bs", [128, 2])
            cum = sb("cum", [128, 128]); ec = sb("ec", [128, 128]); eci = sb("eci", [128, 128]); ee = sb("ee", [128, 128])
            eh = sb("eh", [128, 128]); gC = sb("gC", [64, 2])
            rt = sb("rt", [128, 128]); bt = sb("bt", [128, 128]); ktl = sb("ktl", [128, 128])
            bh = sb("bh", [128, 128]); kh = sb("kh", [128, 128])
            FT = [sb("FT%d" % h, [64, 512]) for h in range(2)]
            GM = [sb("GM%d" % h, [128, 512]) for h in range(2)]
            Xa = [[sb("Xa%d_%d" % (h, i), [128, 128], BF16) for i in range(2)] for h in range(2)]
            Xb = [[sb("Xb%d_%d" % (h, i), [128, 128], BF16) for i in range(2)] for h in range(2)]
            XG = [sb("XG%d" % h, [128, 128], BF16) for h in range(2)]
            ACCB = [[sb("ACCB%d_%d" % (h, i), [128, 128], BF16) for i in range(2)] for h in range(2)]
            ACC = [[sb("ACC%d_%d" % (h, i), [128, 128]) for i in range(2)] for h in range(2)]
            RH = [sb("RH%d" % h, [128, 128]) for h in range(2)]
            PU = [sb("PU%d" % h, [128, 128]) for h in range(2)]
            Y1T = [sb("Y1T%d" % h, [64, 128]) for h in range(2)]
            Y2 = [sb("Y2%d" % h, [128, 64]) for h in range(2)]
            T1T = [sb("T1T%d" % h, [64, 64]) for h in range(2)]
            T2 = [sb("T2%d" % h, [64, 64]) for h in range(2)]
            Hs = [[sb("H%d_%d" % (h, i), [64, 64]) for i in range(2)] for h in range(2)]
            Hld = sb("Hld", [64, 2, 64]); Hout = sb("Hout", [64, 2, 64])
            yb = sb("yb", [128, 128]); yc = sb("yc", [128, 128]); ysq = sb("ysq", [128, 128]); obb = sb("obb", [128, 128], BF16)

            def bc3(ap2, n, a, b):
                return ap2.unsqueeze(2).to_broadcast([n, a, b])

            def rstd_chain(src_ap, dst_ap, n, k, scale, eps, bsrc, bdst):
                S.op("dve", lambda e: e.tensor_scalar(out=dst_ap, in0=src_ap, scalar1=scale, scalar2=eps, op0=ALU.mult,
                                                      op1=ALU.add), reads=[bsrc], writes=[bdst])
                S.op("act", lambda e: e.activation(out=dst_ap, in_=dst_ap, func=AF.Ln), reads=[bdst], writes=[bdst])
                S.op("act", lambda e: e.activation(out=dst_ap, in_=dst_ap, func=AF.Exp, scale=-0.5), reads=[bdst], writes=[bdst])

            def front(x_rows_ap, n, dst, dst_off, dstB, gt, slot):
                xtile = xt[slot]
                bx = B(xtile.name)
                S.dma("sp", xtile[:n, :], x_rows_ap, writes=[bx])
                S.op("dve", lambda e: e.scalar_tensor_tensor(out=junk[:n, :], in0=xtile[:n, :], scalar=1.0, in1=xtile[:n, :], op0=ALU.mult, op1=ALU.mult, accum_out=st4[:n, 0:1]),
                     reads=[bx], writes=[B("junk"), B("st4")])
                rstd_chain(st4[:n, 0:1], st4[:n, 0:1], n, 1, 1.0 / D, RMS_EPS, B("st4"), B("st4"))
                S.op("act", lambda e: e.activation(out=xsb[:n, :], in_=xtile[:n, :], func=AF.Copy, scale=st4[:n, 0:1]),
                     reads=[bx, B("st4")], writes=[B("xsb")])
                yield
                for half in range(2):
                    pb = gbank()
                    pv = PS[pb][:].bitcast(BF16)
                    for c8 in range(8):
                        c = half * 8 + c8
                        S.op("pe", lambda e, c=c, c8=c8, pv=pv: e.transpose(out=pv[:, c8 * n:(c8 + 1) * n],
                                                                           in_=xsb[:n, c * 128:(c + 1) * 128],
                                                                           identity=identb[:n, :n]),
                             reads=[B("xsb"), B("identb")], writes=[PB[pb]] if c8 == 0 else [], pwrites=[] if c8 == 0 else [PB[pb]],
                             inc=(c8 == 7))
                    S.op("dve", lambda e, half=half, pv=pv: e.tensor_tensor(
                        out=dst[:, half * 8:(half + 1) * 8, dst_off:dst_off + n],
                        in0=pv[:, 0:8 * n].rearrange("p (c n) -> p c n", n=n),
                        in1=bc3(gt[:, half * 8:(half + 1) * 8], 128, 8, n), op=ALU.mult),
                        reads=[PB[pb], B(gt.name)], writes=[] if half else [dstB], pwrites=[dstB] if half else [])
                    yield

            def load_unit_params(up_row, lw_ap, lg2_ap):
                S.dma("sp", parb[:], up_row.partition_broadcast(128), writes=[B("parb")])
                S.dma("sp", lwt[:], lw_ap.rearrange("(a p) n -> p a n", p=64), writes=[B("lwt")])
                S.dma("sp", lg2t[:], lg2_ap, writes=[B("lg2t")])
            mu_bc = parb[:, 0:576]; w0a0_bc = parb[:, 576:832]; kk_bc = parb[:, 832:960]; ka_bc = parb[:, 960:1088]
            rk_bc = parb[:, 1088:1216]; lg_bc = parb[:, 1216:1344]; lb_bc = parb[:, 1344:1472]

            def load_unit_w(wsrc):
                S.dma("sp", wub_sb[:], wsrc.rearrange("(c p) n -> p c n", p=128), reads=[B(wsrc_name[0])], writes=[B("wub_sb")])
            wsrc_name = [None]

            def project(xsrc, xoff, xB, n, pslot):
                pa, pb2 = gbank(), gbank()
                for k in range(NCH):
                    S.op("pe", lambda e, k=k: e.matmul(PS[pa][:n, 0:512], lhsT=xsrc[:, k, xoff:xoff + n], rhs=wub_sb[:, k, 0:512],
                                                       start=(k == 0), stop=(k == NCH - 1), skip_group_check=True),
                         reads=[xB, B("wub_sb")], writes=[PB[pa]] if k == 0 else [], pwrites=[] if k == 0 else [PB[pa]],
                         inc=(k == NCH - 1))
                for k in range(NCH):
                    S.op("pe", lambda e, k=k: e.matmul(PS[pb2][:n, 0:448], lhsT=xsrc[:, k, xoff:xoff + n], rhs=wub_sb[:, k, 512:960],
                                                       start=(k == 0), stop=(k == NCH - 1), skip_group_check=True),
                         reads=[xB, B("wub_sb")], writes=[PB[pb2]] if k == 0 else [], pwrites=[] if k == 0 else [PB[pb2]],
                         inc=(k == NCH - 1))
                pr = prw[pslot]
                yield
                S.op("act", lambda e: e.activation(out=gsb[:n, :], in_=PS[pa][:n, 0:384], func=AF.Copy),
                     reads=[PB[pa]], writes=[B("gsb")])
                S.op("act", lambda e: e.activation(out=pr[:n, 0:128], in_=PS[pa][:n, 384:512], func=AF.Copy),
                     reads=[PB[pa]], writes=[B(pr.name)])
                S.op("act", lambda e: e.activation(out=pr[:n, 128:576], in_=PS[pb2][:n, 0:448], func=AF.Copy),
                     reads=[PB[pb2]], pwrites=[B(pr.name)])

            def attn_prep(n, cs_ap, csB, k_out_ap, v_out_ap, kt_idx, kt_off, qoff, QT, QTB_):
                S.op("dve", lambda e: e.tensor_tensor(out=tq[:n, :], in0=gsb[:n, 0:256], in1=gsb[:n, 0:256], op=ALU.mult),
                     reads=[B("gsb")], writes=[B("tq")])
                S.op("dve", lambda e: e.tensor_reduce(out=st4[:n, 4:8], in_=tq[:n, :].rearrange("p (a b) -> p a b", b=64),
                                                      axis=AX.X, op=ALU.add), reads=[B("tq")], writes=[B("st4")])
                rstd_chain(st4[:n, 4:8], st4[:n, 4:8], n, 4, 1.0 / 64, RMS_EPS, B("st4"), B("st4"))
                q3 = qkn[:n, :].rearrange("p (a b) -> p a b", b=64)
                S.op("dve", lambda e: e.tensor_tensor(out=q3, in0=gsb[:n, 0:256].rearrange("p (a b) -> p a b", b=64),
                                                      in1=bc3(st4[:n, 4:8], n, 4, 64), op=ALU.mult),
                     reads=[B("gsb"), B("st4")], writes=[B("qkn")])
                S.op("dve", lambda e: e.tensor_tensor(out=qkn[:n, :], in0=qkn[:n, :], in1=qkgb[:n, :], op=ALU.mult),
                     reads=[B("qkn"), B("qkgb")], writes=[B("qkn")])
                yield
                x1 = q3[:, :, 0:8]; x2 = q3[:, :, 8:16]
                cosb = cs_ap[:, 0:8].unsqueeze(1).to_broadcast([n, 4, 8])
                sinb = cs_ap[:, 8:16].unsqueeze(1).to_broadcast([n, 4, 8])
                for idx, (a_, b_) in enumerate(((x1, cosb), (x2, sinb), (x2, cosb), (x1, sinb))):
                    S.op("dve", lambda e, idx=idx, a_=a_, b_=b_: e.tensor_tensor(out=rtmp[:n, :, idx, :], in0=a_, in1=b_, op=ALU.mult),
                         reads=[B("qkn"), csB], writes=[B("rtmp")] if idx == 0 else [], pwrites=[] if idx == 0 else [B("rtmp")])
                S.op("dve", lambda e: e.tensor_tensor(out=x1, in0=rtmp[:n, :, 0, :], in1=rtmp[:n, :, 1, :], op=ALU.subtract),
                     reads=[B("rtmp")], writes=[B("qkn")])
                S.op("dve", lambda e: e.tensor_tensor(out=x2, in0=rtmp[:n, :, 2, :], in1=rtmp[:n, :, 3, :], op=ALU.add),
                     reads=[B("rtmp")], writes=[B("qkn")])
                yield
                S.dma("sp", k_out_ap, qkn[:n, 128:256], reads=[B("qkn")])
                S.dma("sp", v_out_ap, gsb[:n, 256:384], reads=[B("gsb")])
                S.op("act", lambda e: e.activation(out=qkb[:n, :], in_=qkn[:n, :], func=AF.Copy), reads=[B("qkn")], writes=[B("qkb")])
                S.op("act", lambda e: e.activation(out=VV[:n, kt_idx, :], in_=gsb[:n, 256:384], func=AF.Copy),
                     reads=[B("gsb")], writes=[VB[kt_idx]])
                pb = gbank()
                pv = PS[pb][:].bitcast(BF16)
                S.op("pe", lambda e: e.transpose(out=pv[:, 0:n], in_=qkb[:n, 0:128], identity=identb[:n, :n]),
                     reads=[B("qkb"), B("identb")], writes=[PB[pb]], inc=False)
                S.op("pe", lambda e: e.transpose(out=pv[:, 128:128 + n], in_=qkb[:n, 128:256], identity=identb[:n, :n]),
                     reads=[B("qkb"), B("identb")], pwrites=[PB[pb]])
                S.op("act", lambda e: e.activation(out=QT[0:64, 0, qoff:qoff + n], in_=pv[0:64, 0:n], func=AF.Copy),
                     reads=[PB[pb]], writes=[] if qoff else [QTB_], pwrites=[QTB_] if qoff else [])
                S.op("act", lambda e: e.activation(out=QT[64:128, 1, qoff:qoff + n], in_=pv[64:128, 0:n], func=AF.Copy),
                     reads=[PB[pb]], pwrites=[QTB_])
                yield
                S.op("act", lambda e: e.activation(out=KT[:, kt_off:kt_off + n], in_=pv[:, 128:128 + n], func=AF.Copy),
                     reads=[PB[pb]], writes=[KTB[kt_idx]])

            def attention(nq, n, key_tiles, cat_writer, QT, QTB_, tile_base):
                W2 = 2 * nq
                for ki, (koff, nk, kidx, mk) in enumerate(key_tiles):
                    sbk = 4 + (ki % 2)
                    Ebt = Eb[ki % 2]
                    S.op("pe", lambda e: e.matmul(PS[sbk][:nk, 0:W2].rearrange("p (a b) -> p a b", a=2), lhsT=KT[:, koff:koff + nk], rhs=QT[:, :, 0:nq],
                                                  start=True, stop=True, skip_group_check=True),
                         reads=[KTB[kidx], QTB_], writes=[PB[sbk]])
                    S.op("act", lambda e: e.activation(out=Ebt[:nk, 0:W2], in_=PS[sbk][:nk, 0:W2], func=AF.Exp, scale=0.125),
                         reads=[PB[sbk]], writes=[B(Ebt.name)])
                    if mk is not None:
                        S.op("dve", lambda e: e.tensor_tensor(out=Ebt[:nk, 0:W2], in0=Ebt[:nk, 0:W2], in1=amask[:nk, mk, 0:W2], op=ALU.mult),
                             reads=[B(Ebt.name), B("amask")], writes=[B(Ebt.name)])
                    first = (ki == 0)
                    last = (ki == len(key_tiles) - 1)
                    S.op("pe", lambda e: e.matmul(PS[6][:, 0:W2], lhsT=VV[:nk, kidx, :], rhs=Ebt[:nk, 0:W2], start=first, stop=last,
                                                  skip_group_check=True),
                         reads=[VB[kidx], B(Ebt.name)], writes=[PB[6]] if first else [], pwrites=[] if first else [PB[6]], inc=True)
                    if first:
                        S.op("pool", lambda e: e.tensor_copy(out=Zacc[:nk, 0:W2], in_=Ebt[:nk, 0:W2]), reads=[B(Ebt.name)], writes=[B("Zacc")])
                    else:
                        S.op("pool", lambda e: e.tensor_tensor(out=Zacc[:nk, 0:W2], in0=Zacc[:nk, 0:W2], in1=Ebt[:nk, 0:W2], op=ALU.add),
                             reads=[B(Ebt.name), B("Zacc")], writes=[B("Zacc")])
                    yield
                S.op("pe", lambda e: e.matmul(PS[7][0:1, 0:W2], lhsT=ones[:, 0:1], rhs=Zacc[:, 0:W2], start=True, stop=True, skip_group_check=True),
                     reads=[B("ones"), B("Zacc")], writes=[PB[7]])
                S.op("act", lambda e: e.activation(out=OT[:, 0:W2], in_=PS[6][:, 0:W2], func=AF.Copy), reads=[PB[6]], writes=[B("OT")])
                S.op("dve", lambda e: e.tensor_copy(out=zsb[0:1, 0:W2], in_=PS[7][0:1, 0:W2]), reads=[PB[7]], writes=[B("zsb")])
                for qt in range(nq // n):
                    pb = gbank()
                    S.op("pe", lambda e: e.transpose(out=PS[pb][:n, 0:128], in_=OT[:, qt * n:(qt + 1) * n], identity=ident[:, :]),
                         reads=[B("OT"), B("ident")], writes=[PB[pb]], inc=False)
                    S.op("pe", lambda e: e.transpose(out=PS[pb][:n, 128:256], in_=OT[:, nq + qt * n:nq + (qt + 1) * n], identity=ident[:, :]),
                         reads=[B("OT"), B("ident")], pwrites=[PB[pb]], inc=False)
                    S.op("pe", lambda e: e.matmul(PS[pb][:n, 256:257], lhsT=zsb[0:1, qt * n:(qt + 1) * n], rhs=one11[0:1, 0:1],
                                                  start=False, stop=False, skip_group_check=True),
                         reads=[B("zsb"), B("one11")], pwrites=[PB[pb]], inc=False)
                    S.op("pe", lambda e: e.matmul(PS[pb][:n, 257:258], lhsT=zsb[0:1, nq + qt * n:nq + (qt + 1) * n], rhs=one11[0:1, 0:1],
                                                  start=False, stop=True, skip_group_check=True),
                         reads=[B("zsb"), B("one11")], pwrites=[PB[pb]])
                    S.op("dve", lambda e: e.reciprocal(out=ast[:n, 0:2], in_=PS[pb][:n, 256:258]), reads=[PB[pb]], writes=[B("ast")])
                    S.op("dve", lambda e: e.tensor_tensor(out=ast[:n, 2:3], in0=ast[:n, 1:2], in1=neglam[:n, 0:1], op=ALU.mult),
                         reads=[B("ast"), B("neglam")], writes=[B("ast")])
                    S.op("dve", lambda e: e.tensor_scalar(out=osb[:n, :], in0=PS[pb][:n, 0:128], scalar1=ast[:n, 0:1], scalar2=None,
                                                          op0=ALU.mult), reads=[PB[pb], B("ast")], writes=[B("osb")])
                    S.op("dve", lambda e: e.scalar_tensor_tensor(out=osb[:n, :], in0=PS[pb][:n, 128:256], scalar=ast[:n, 2:3],
                                                                 in1=osb[:n, :], op0=ALU.mult, op1=ALU.add),
                         reads=[PB[pb], B("ast"), B("osb")], writes=[B("osb")])
                    S.op("dve", lambda e: e.scalar_tensor_tensor(out=ajunk[:n, 0:128], in0=osb[:n, :], scalar=1.0, in1=osb[:n, :], op0=ALU.mult, op1=ALU.mult, accum_out=ast[:n, 4:5]),
                         reads=[B("osb"), B("ast")], writes=[B("ajunk"), B("ast")])
                    rstd_chain(ast[:n, 4:5], ast[:n, 4:5], n, 1, 1.0 / 128, RMS_EPS, B("ast"), B("ast"))
                    S.op("dve", lambda e: e.scalar_tensor_tensor(out=onb[:n, :], in0=osb[:n, :], scalar=ast[:n, 4:5], in1=subgb[:n, :],
                                                                 op0=ALU.mult, op1=ALU.mult),
                         reads=[B("osb"), B("ast"), B("subgb")], writes=[B("onb")])
                    pb2 = gbank()
                    pv = PS[pb2][:].bitcast(BF16)
                    S.op("pe", lambda e: e.transpose(out=pv[:, 0:n], in_=onb[:n, :], identity=identb[:n, :n]),
                         reads=[B("onb"), B("identb")], writes=[PB[pb2]])
                    cat_writer(0, tile_base + qt, pv[:, 0:n], PB[pb2])
                    yield

            def rwkv(n, pslot, prev_ap, prevB, cmat, first_tile, hslot, cat_writer, tile_idx):
                pr = prw[pslot]
                prB = B(pr.name)
                pa, pb2 = gbank(), gbank()
                for (bank, c0, c1) in ((pa, 0, 512), (pb2, 512, 576)):
                    S.op("pe", lambda e, bank=bank, c0=c0, c1=c1: e.matmul(PS[bank][:n, 0:c1 - c0], lhsT=s1m[:n, :n], rhs=pr[:n, c0:c1],
                                                                          start=True, stop=(prev_ap is None), skip_group_check=True),
                         reads=[B("s1m"), prB], writes=[PB[bank]], inc=(prev_ap is None))
                    if prev_ap is not None:
                        S.op("pe", lambda e, bank=bank, c0=c0, c1=c1: e.matmul(PS[bank][:n, 0:c1 - c0], lhsT=cmat, rhs=prev_ap[:, c0:c1],
                                                                              start=False, stop=True, skip_group_check=True),
                             reads=[prevB, B("c128"), B("c64")], pwrites=[PB[bank]])
                S.op("dve", lambda e: e.tensor_tensor(out=xs_[:n, 0:512], in0=PS[pa][:n, 0:512], in1=pr[:n, 0:512], op=ALU.subtract),
                     reads=[PB[pa], prB], writes=[B("xs_")])
                S.op("dve", lambda e: e.tensor_tensor(out=xs_[:n, 512:576], in0=PS[pb2][:n, 0:64], in1=pr[:n, 512:576], op=ALU.subtract),
                     reads=[PB[pb2], prB], pwrites=[B("xs_")])
                S.op("dve", lambda e: e.tensor_tensor(out=xs_[:n, :], in0=xs_[:n, :], in1=mu_bc[:n, :], op=ALU.mult),
                     reads=[B("xs_"), B("parb")], writes=[B("xs_")])
                S.op("dve", lambda e: e.tensor_tensor(out=xs_[:n, :], in0=xs_[:n, :], in1=pr[:n, :], op=ALU.add),
                     reads=[B("xs_"), prB], writes=[B("xs_")])
                xr_, xk, xv = xs_[:n, 0:128], xs_[:n, 128:256], xs_[:n, 256:384]
                yield
                S.op("act", lambda e: e.activation(out=Et[:n, 0:64], in_=xs_[:n, 384:448], func=AF.Exp, scale=-2.0),
                     reads=[B("xs_")], writes=[B("Et")])
                S.op("act", lambda e: e.activation(out=Et[:n, 64:128], in_=xs_[:n, 512:576], func=AF.Exp, scale=-1.0),
                     reads=[B("xs_")], pwrites=[B("Et")])
                S.op("dve", lambda e: e.tensor_scalar(out=Et[:n, :], in0=Et[:n, :], scalar1=1.0, scalar2=None, op0=ALU.add),
                     reads=[B("Et")], writes=[B("Et")])
                S.op("dve", lambda e: e.reciprocal(out=Et[:n, :], in_=Et[:n, :]), reads=[B("Et")], writes=[B("Et")])
                S.op("dve", lambda e: e.tensor_scalar(out=LT[:n, 0:64], in0=Et[:n, 0:64], scalar1=2.0, scalar2=-1.0, op0=ALU.mult, op1=ALU.add),
                     reads=[B("Et")], writes=[B("LT")])
                S.op("dve", lambda e: e.tensor_copy(out=LT[:n, 64:128], in_=xs_[:n, 448:512]), reads=[B("xs_")], pwrites=[B("LT")])
                S.op("dve", lambda e: e.tensor_copy(out=LT[:n, 128:192], in_=Et[:n, 64:128]), reads=[B("Et")], pwrites=[B("LT")])
                yield
                pb = gbank()
                for j3 in range(3):
                    S.op("pe", lambda e, j3=j3: e.transpose(out=PS[pb][0:64, j3 * 128:j3 * 128 + n], in_=LT[:n, j3 * 64:(j3 + 1) * 64], identity=ident[:n, :n]),
                         reads=[B("LT"), B("ident")], writes=[PB[pb]] if j3 == 0 else [], pwrites=[] if j3 == 0 else [PB[pb]], inc=(j3 == 2))
                S.op("act", lambda e: e.activation(out=LTT[:, :, 0:n], in_=PS[pb][0:64, 0:384].rearrange("p (a b) -> p a b", b=128)[:, :, 0:n], func=AF.Copy),
                     reads=[PB[pb]], writes=[B("LTT")])
                yield
                pl = gbank()
                S.op("pe", lambda e: e.matmul(PS[pl][:n, 0:128], lhsT=LTT[:, 0, 0:n], rhs=lwt[:, 0, :], start=True, stop=False, skip_group_check=True),
                     reads=[B("LTT"), B("lwt")], writes=[PB[pl]], inc=False)
                S.op("pe", lambda e: e.matmul(PS[pl][:n, 128:256], lhsT=LTT[:, 1, 0:n], rhs=lwt[:, 1, :], start=False, stop=False, skip_group_check=True),
                     reads=[B("LTT"), B("lwt")], pwrites=[PB[pl]], inc=False)
                S.op("pe", lambda e: e.matmul(PS[pl][:n, 256:384], lhsT=LTT[:, 2, 0:n], rhs=lg2t[:, :], start=False, stop=True, skip_group_check=True),
                     reads=[B("LTT"), B("lg2t")], pwrites=[PB[pl]])
                yield
                S.op("dve", lambda e: e.tensor_tensor(out=za[:n, :], in0=PS[pl][:n, 0:256], in1=w0a0_bc[:n, :], op=ALU.add),
                     reads=[PB[pl], B("parb")], writes=[B("za")])
                yield
                S.op("dve", lambda e: e.tensor_copy(out=g_sb[:n, :], in_=PS[pl][:n, 256:384]), reads=[PB[pl]], writes=[B("g_sb")])
                yield
                S.op("act", lambda e: e.activation(out=za[:n, :], in_=za[:n, :], func=AF.Exp, scale=-1.0), reads=[B("za")], writes=[B("za")])
                S.op("dve", lambda e: e.tensor_scalar(out=za[:n, :], in0=za[:n, :], scalar1=1.0, scalar2=None, op0=ALU.add),
                     reads=[B("za")], writes=[B("za")])
                yield
                S.op("dve", lambda e: e.reciprocal(out=sa[:n, :], in_=za[:n, :]), reads=[B("za")], writes=[B("sa")])
                yield
                S.op("dve", lambda e: e.tensor_scalar(out=ld[:n, :], in0=sa[:n, 0:128], scalar1=-math.exp(-0.5), scalar2=None, op0=ALU.mult),
                     reads=[B("sa")], writes=[B("ld")])
                av = sa[:n, 128:256]
                yield
                S.op("dve", lambda e: e.tensor_tensor(out=kkv[:n, :], in0=xk, in1=kk_bc[:n, :], op=ALU.mult), reads=[B("xs_"), B("parb")], writes=[B("kkv")])
                S.op("dve", lambda e: e.tensor_tensor(out=mm_[:n, :], in0=kkv[:n, :], in1=kkv[:n, :], op=ALU.mult), reads=[B("kkv")], writes=[B("mm_")])
                S.op("dve", lambda e: e.tensor_reduce(out=rst[:n, 0:2], in_=mm_[:n, :].rearrange("p (a b) -> p a b", b=64), axis=AX.X, op=ALU.add),
                     reads=[B("mm_")], writes=[B("rst")])
                S.op("dve", lambda e: e.tensor_scalar(out=rst[:n, 0:2], in0=rst[:n, 0:2], scalar1=1e-18, scalar2=None, op0=ALU.max),
                     reads=[B("rst")], writes=[B("rst")])
                S.op("act", lambda e: e.activation(out=rst[:n, 0:2], in_=rst[:n, 0:2], func=AF.Ln), reads=[B("rst")], writes=[B("rst")])
                S.op("act", lambda e: e.activation(out=rst[:n, 0:2], in_=rst[:n, 0:2], func=AF.Exp, scale=-0.5), reads=[B("rst")], writes=[B("rst")])
                k3 = kkv[:n, :].rearrange("p (a b) -> p a b", b=64)
                S.op("dve", lambda e: e.tensor_tensor(out=k3, in0=k3, in1=bc3(rst[:n, 0:2], n, 2, 64), op=ALU.mult),
                     reads=[B("kkv"), B("rst")], writes=[B("kkv")])
                S.op("dve", lambda e: e.scalar_tensor_tensor(out=mm_[:n, :], in0=av, scalar=-1.0, in1=ka_bc[:n, :], op0=ALU.add, op1=ALU.mult),
                     reads=[B("sa"), B("parb")], writes=[B("mm_")])
                S.op("dve", lambda e: e.scalar_tensor_tensor(out=k2[:n, :], in0=mm_[:n, :], scalar=1.0, in1=xk, op0=ALU.add, op1=ALU.mult),
                     reads=[B("mm_"), B("xs_")], writes=[B("k2")])
                S.op("dve", lambda e: e.tensor_tensor(out=bvec[:n, :], in0=kkv[:n, :], in1=av, op=ALU.mult), reads=[B("kkv"), B("sa")], writes=[B("bvec")])
                S.op("dve", lambda e: e.tensor_tensor(out=mm_[:n, :], in0=xr_, in1=k2[:n, :], op=ALU.mult), reads=[B("xs_"), B("k2")], writes=[B("mm_")])
                S.op("dve", lambda e: e.tensor_tensor(out=mm_[:n, :], in0=mm_[:n, :], in1=rk_bc[:n, :], op=ALU.mult), reads=[B("mm_"), B("parb")], writes=[B("mm_")])
                S.op("dve", lambda e: e.tensor_reduce(out=bs[:n, 0:2], in_=mm_[:n, :].rearrange("p (a b) -> p a b", b=64), axis=AX.X, op=ALU.add),
                     reads=[B("mm_")], writes=[B("bs")])
                yield
                pc = gbank()
                S.op("pe", lambda e: e.matmul(PS[pc][:n, 0:128], lhsT=ut[:n, :n], rhs=ld[:n, :], start=True, stop=False, skip_group_check=True),
                     reads=[B("ut"), B("ld")], writes=[PB[pc]], inc=False)
                S.op("pe", lambda e: e.matmul(PS[pc][:n, 128:256], lhsT=ones[:n, :n], rhs=ld[:n, :], start=False, stop=False, skip_group_check=True),
                     reads=[B("ones"), B("ld")], pwrites=[PB[pc]], inc=False)
                for hh in range(2):
                    S.op("pe", lambda e, hh=hh: e.matmul(PS[pc][0:64, 256 + hh:257 + hh], lhsT=ld[:n, hh * 64:(hh + 1) * 64], rhs=ones[:n, 0:1],
                                                         start=False, stop=(hh == 1), skip_group_check=True),
                         reads=[B("ones"), B("ld")], pwrites=[PB[pc]], inc=(hh == 1))
                S.op("act", lambda e: e.activation(out=cum[:n, :], in_=PS[pc][:n, 0:128], func=AF.Copy), reads=[PB[pc]], writes=[B("cum")])
                S.op("act", lambda e: e.activation(out=ec[:n, :], in_=PS[pc][:n, 0:128], func=AF.Exp), reads=[PB[pc]], writes=[B("ec")])
                S.op("act", lambda e: e.activation(out=eci[:n, :], in_=PS[pc][:n, 0:128], func=AF.Exp, scale=-1.0), reads=[PB[pc]], writes=[B("eci")])
                S.op("act", lambda e: e.activation(out=gC[:, 0:2], in_=PS[pc][0:64, 256:258], func=AF.Exp), reads=[PB[pc]], writes=[B("gC")])
                S.op("dve", lambda e: e.tensor_tensor(out=ee[:n, :], in0=cum[:n, :], in1=ld[:n, :], op=ALU.subtract), reads=[B("cum"), B("ld")], writes=[B("ee")])
                S.op("act", lambda e: e.activation(out=ee[:n, :], in_=ee[:n, :], func=AF.Exp), reads=[B("ee")], writes=[B("ee")])
                S.op("dve", lambda e: e.tensor_tensor(out=eh[:n, :], in0=PS[pc][:n, 128:256], in1=cum[:n, :], op=ALU.subtract), reads=[PB[pc], B("cum")], writes=[B("eh")])
                S.op("act", lambda e: e.activation(out=eh[:n, :], in_=eh[:n, :], func=AF.Exp), reads=[B("eh")], writes=[B("eh")])
                yield
                S.op("dve", lambda e: e.tensor_tensor(out=rt[:n, :], in0=xr_, in1=ec[:n, :], op=ALU.mult), reads=[B("xs_"), B("ec")], writes=[B("rt")])
                for hh in range(2):
                    S.op("dve", lambda e, hh=hh: e.scalar_tensor_tensor(out=RH[hh][:n, 0:64], in0=kkv[:n, hh * 64:(hh + 1) * 64], scalar=-1.0,
                                                                        in1=ee[:n, hh * 64:(hh + 1) * 64], op0=ALU.mult, op1=ALU.mult),
                         reads=[B("kkv"), B("ee")], writes=[B(RH[hh].name)])
                S.op("dve", lambda e: e.tensor_tensor(out=bt[:n, :], in0=bvec[:n, :], in1=eci[:n, :], op=ALU.mult), reads=[B("bvec"), B("eci")], writes=[B("bt")])
                S.op("dve", lambda e: e.tensor_tensor(out=ktl[:n, :], in0=k2[:n, :], in1=eci[:n, :], op=ALU.mult), reads=[B("k2"), B("eci")], writes=[B("ktl")])
                S.op("dve", lambda e: e.tensor_tensor(out=bh[:n, :], in0=bvec[:n, :], in1=eh[:n, :], op=ALU.mult), reads=[B("bvec"), B("eh")], writes=[B("bh")])
                S.op("dve", lambda e: e.tensor_tensor(out=kh[:n, :], in0=k2[:n, :], in1=eh[:n, :], op=ALU.mult), reads=[B("k2"), B("eh")], writes=[B("kh")])
                nlev = 7 if n == 128 else 6
                yield
                def head_gen(hh):
                    hs = slice(hh * 64, (hh + 1) * 64)
                    pf = gbank()
                    srcs = ((RH[hh][:n, 0:64], B(RH[hh].name)), (rt[:n, hs], B("rt")), (bt[:n, hs], B("bt")), (ktl[:n, hs], B("ktl")))
                    for i4, (sap, sB) in enumerate(srcs):
                        S.op("pe", lambda e, i4=i4, sap=sap: e.transpose(out=PS[pf][0:64, i4 * n:(i4 + 1) * n], in_=sap, identity=ident[:n, :n]),
                             reads=[sB, B("ident")], writes=[PB[pf]] if i4 == 0 else [], pwrites=[] if i4 == 0 else [PB[pf]], inc=(i4 == 3))
                    S.op("act", lambda e, hh=hh: e.activation(out=FT[hh][:, 0:4 * n], in_=PS[pf][0:64, 0:4 * n], func=AF.Copy),
                         reads=[PB[pf]], writes=[B(FT[hh].name)])
                    F_ = FT[hh]; FB = B(F_.name)
                    yield
                    pg = gbank()
                    S.op("pe", lambda e, F_=F_: e.matmul(PS[pg][:n, 0:2 * n], lhsT=F_[:, 2 * n:3 * n], rhs=F_[:, 0:2 * n], start=True, stop=False, skip_group_check=True),
                         reads=[FB], writes=[PB[pg]], inc=False)
                    S.op("pe", lambda e, F_=F_: e.matmul(PS[pg][:n, 2 * n:4 * n], lhsT=F_[:, 3 * n:4 * n], rhs=F_[:, 0:2 * n], start=False, stop=True, skip_group_check=True),
                         reads=[FB], pwrites=[PB[pg]])
                    G_ = GM[hh]; GB = B(G_.name)
                    S.op("dve", lambda e, G_=G_: e.tensor_tensor(out=G_[:n, 0:4 * n].rearrange("p (a b) -> p a b", b=n),
                                                                 in0=PS[pg][:n, 0:4 * n].rearrange("p (a b) -> p a b", b=n),
                                                                 in1=gmask[:n, :].rearrange("p (a b) -> p a b", b=128)[:, :, 0:n], op=ALU.mult),
                         reads=[PB[pg], B("gmask")], writes=[GB])
                    px = gbank()
                    S.op("pe", lambda e, F_=F_: e.matmul(PS[px][:n, 0:n], lhsT=F_[:, 0:n], rhs=F_[:, 2 * n:3 * n], start=True, stop=True, skip_group_check=True),
                         reads=[FB], writes=[PB[px]])
                    xa, xb_ = Xa[hh], Xb[hh]
                    S.op("dve", lambda e, xb_=xb_: e.tensor_tensor(out=xb_[0][:n, :n], in0=PS[px][:n, 0:n], in1=lowm[:n, :n], op=ALU.mult),
                         reads=[PB[px], B("lowm")], writes=[B(xb_[0].name)])
                    yield
                    acc = ACC[hh]
                    S.op("dve", lambda e, acc=acc, G_=G_: e.tensor_tensor(out=acc[0][:n, :n], in0=G_[:n, 0:n], in1=ident[:n, :n], op=ALU.add),
                         reads=[GB, B("ident")], writes=[B(acc[0].name)])
                    accb = ACCB[hh]
                    S.op("act", lambda e, acc=acc, accb=accb: e.activation(out=accb[0][:n, :n], in_=acc[0][:n, :n], func=AF.Copy),
                         reads=[B(acc[0].name)], writes=[B(accb[0].name)])
                    S.op("act", lambda e, G_=G_: e.activation(out=XG[hh][:n, :n], in_=G_[:n, 0:n], func=AF.Copy),
                         reads=[GB], writes=[B(XG[hh].name)])
                    curX_ap, curXB = XG[hh][:n, :n], B(XG[hh].name)
                    cs_ = 0
                    for lev in range(1, nlev):
                        curXp = xb_[cs_]
                        nxt = 1 - cs_
                        p2 = gbank()
                        lastlev = (lev == nlev - 1)
                        S.op("pe", lambda e, curX_ap=curX_ap, curXp=curXp: e.matmul(PS[p2][:n, 0:n], lhsT=curX_ap, rhs=curXp[:n, :n], start=True, stop=lastlev, skip_group_check=True),
                             reads=[curXB, B(curXp.name)], writes=[PB[p2]], inc=lastlev)
                        if not lastlev:
                            S.op("pe", lambda e, curX_ap=curX_ap, curXp=curXp: e.matmul(PS[p2][:n, 128:128 + n], lhsT=curXp[:n, :n], rhs=curX_ap, start=False, stop=True, skip_group_check=True),
                                 reads=[curXB, B(curXp.name)], pwrites=[PB[p2]])
                        S.op("act", lambda e, xb_=xb_, nxt=nxt: e.activation(out=xb_[nxt][:n, :n], in_=PS[p2][:n, 0:n], func=AF.Copy),
                             reads=[PB[p2]], writes=[B(xb_[nxt].name)])
                        if not lastlev:
                            S.op("dve", lambda e, xa=xa, nxt=nxt: e.tensor_copy(out=xa[nxt][:n, :n], in_=PS[p2][:n, 128:128 + n]),
                                 reads=[PB[p2]], writes=[B(xa[nxt].name)])
                        a_cur = acc[(lev - 1) % 2]; a_nxt = acc[lev % 2]
                        ab_cur = accb[(lev - 1) % 2]; ab_nxt = accb[lev % 2]
                        p3 = gbank()
                        S.op("pe", lambda e, xb_=xb_, nxt=nxt, ab_cur=ab_cur: e.matmul(PS[p3][:n, 0:n], lhsT=xb_[nxt][:n, :n], rhs=ab_cur[:n, :n], start=True, stop=True, skip_group_check=True),
                             reads=[B(xb_[nxt].name), B(ab_cur.name)], writes=[PB[p3]])
                        S.op("dve", lambda e, a_cur=a_cur, a_nxt=a_nxt: e.tensor_tensor(out=a_nxt[:n, :n], in0=PS[p3][:n, 0:n], in1=a_cur[:n, :n], op=ALU.add),
                             reads=[PB[p3], B(a_cur.name)], writes=[B(a_nxt.name)])
                        if not lastlev:
                            S.op("act", lambda e, a_nxt=a_nxt, ab_nxt=ab_nxt: e.activation(out=ab_nxt[:n, :n], in_=a_nxt[:n, :n], func=AF.Copy),
                                 reads=[B(a_nxt.name)], writes=[B(ab_nxt.name)])
                        curX_ap, curXB = xa[nxt][:n, :n], B(xa[nxt].name)
                        cs_ = nxt
                        yield
                    MT = acc[(nlev - 1) % 2]
                    yield
                    MTB = B(MT.name)
                    vh = xs_[:n, 256 + hh * 64:256 + (hh + 1) * 64]
                    pa_ = gbank()
                    S.op("pe", lambda e, G_=G_, vh=vh: e.matmul(PS[pa_][:n, 0:64], lhsT=G_[:n, 2 * n:3 * n], rhs=vh, start=True, stop=True, skip_group_check=True),
                         reads=[GB, B("xs_")], writes=[PB[pa_]])
                    S.op("act", lambda e, hh=hh: e.activation(out=RH[hh][:n, 64:128], in_=PS[pa_][:n, 0:64], func=AF.Copy),
                         reads=[PB[pa_]], pwrites=[B(RH[hh].name)])
                    ppu = gbank()
                    S.op("pe", lambda e, MT=MT, hh=hh: e.matmul(PS[ppu][:n, 0:128], lhsT=MT[:n, :n], rhs=RH[hh][:n, :], start=True, stop=True, skip_group_check=True),
                         reads=[MTB, B(RH[hh].name)], writes=[PB[ppu]])
                    S.op("act", lambda e, hh=hh: e.activation(out=PU[hh][:n, :], in_=PS[ppu][:n, 0:128], func=AF.Copy), reads=[PB[ppu]], writes=[B(PU[hh].name)])
                    PUB = B(PU[hh].name)
                    yield
                    py = gbank()
                    S.op("pe", lambda e, hh=hh, G_=G_: e.matmul(PS[py][:n, 128:192], lhsT=G_[:n, n:2 * n], rhs=PU[hh][:n, 64:128], start=True, stop=False, skip_group_check=True),
                         reads=[PUB, GB], writes=[PB[py]], inc=False)
                    S.op("pe", lambda e, hh=hh, G_=G_, vh=vh: e.matmul(PS[py][:n, 128:192], lhsT=G_[:n, 3 * n:4 * n], rhs=vh, start=False, stop=False, skip_group_check=True),
                         reads=[GB, B("xs_")], pwrites=[PB[py]], inc=False)
                    S.op("pe", lambda e, hh=hh, G_=G_: e.matmul(PS[py][0:64, 0:n], lhsT=PU[hh][:n, 0:64], rhs=G_[:n, n:2 * n], start=False, stop=False, skip_group_check=True),
                         reads=[PUB, GB], pwrites=[PB[py]], inc=False)
                    S.op("pe", lambda e, hh=hh, hs=hs: e.matmul(PS[py][0:64, 192:256], lhsT=PU[hh][:n, 0:64], rhs=bh[:n, hs], start=False, stop=False, skip_group_check=True),
                         reads=[PUB, B("bh")], pwrites=[PB[py]], inc=False)
                    S.op("pe", lambda e, hh=hh, hs=hs: e.matmul(PS[py][0:64, 256:320], lhsT=bh[:n, hs], rhs=PU[hh][:n, 64:128], start=False, stop=False, skip_group_check=True),
                         reads=[PUB, B("bh")], pwrites=[PB[py]], inc=False)
                    S.op("pe", lambda e, hh=hh, hs=hs, vh=vh: e.matmul(PS[py][0:64, 256:320], lhsT=kh[:n, hs], rhs=vh, start=False, stop=True, skip_group_check=True),
                         reads=[B("kh"), B("xs_")], pwrites=[PB[py]])
                    S.op("dve", lambda e, hh=hh, F_=F_: e.tensor_tensor(out=Y1T[hh][:, 0:n], in0=PS[py][0:64, 0:n], in1=F_[:, n:2 * n], op=ALU.add),
                         reads=[PB[py], FB], writes=[B(Y1T[hh].name)])
                    S.op("act", lambda e, hh=hh: e.activation(out=Y2[hh][:n, :], in_=PS[py][:n, 128:192], func=AF.Copy), reads=[PB[py]], writes=[B(Y2[hh].name)])
                    S.op("dve", lambda e, hh=hh: e.scalar_tensor_tensor(out=T1T[hh][:, :], in0=ident[0:64, 0:64], scalar=gC[:, hh:hh + 1], in1=PS[py][0:64, 192:256],
                                                                        op0=ALU.mult, op1=ALU.add),
                         reads=[PB[py], B("ident"), B("gC")], writes=[B(T1T[hh].name)])
                    S.op("act", lambda e, hh=hh: e.activation(out=T2[hh][:, :], in_=PS[py][0:64, 256:320], func=AF.Copy), reads=[PB[py]], writes=[B(T2[hh].name)])
                    yield
                    Hc = Hs[hh][hslot]; Hn = Hs[hh][1 - hslot]
                    ph = gbank()
                    S.op("pe", lambda e, hh=hh, Hc=Hc: e.matmul(PS[ph][:n, 0:64], lhsT=Y1T[hh][:, 0:n], rhs=Hc[:, :], start=True, stop=False, skip_group_check=True),
                         reads=[B(Y1T[hh].name), B(Hc.name)], writes=[PB[ph]], inc=False)
                    S.op("pe", lambda e, hh=hh, Hc=Hc: e.matmul(PS[ph][0:64, 64:128], lhsT=T1T[hh][:, :], rhs=Hc[:, :], start=False, stop=True, skip_group_check=True),
                         reads=[B(T1T[hh].name), B(Hc.name)], pwrites=[PB[ph]])
                    S.op("dve", lambda e, hh=hh, hs=hs: e.tensor_tensor(out=yb[:n, hs], in0=PS[ph][:n, 0:64], in1=Y2[hh][:n, :], op=ALU.add),
                         reads=[PB[ph], B(Y2[hh].name)], writes=[B("yb")] if hh == 0 else [], pwrites=[] if hh == 0 else [B("yb")])
                    S.op("dve", lambda e, hh=hh, Hn=Hn: e.tensor_tensor(out=Hn[:, :], in0=PS[ph][0:64, 64:128], in1=T2[hh][:, :], op=ALU.add),
                         reads=[PB[ph], B(T2[hh].name)], writes=[B(Hn.name)])
                gens = [head_gen(0), head_gen(1)]
                while gens:
                    for g_ in list(gens):
                        try:
                            next(g_)
                        except StopIteration:
                            gens.remove(g_)
                    yield
                yield
                y3 = yb[:n, :].rearrange("p (a b) -> p a b", b=64)
                yc3 = yc[:n, :].rearrange("p (a b) -> p a b", b=64)
                S.op("dve", lambda e: e.tensor_reduce(out=rst[:n, 2:4], in_=y3, axis=AX.X, op=ALU.add), reads=[B("yb")], writes=[B("rst")])
                S.op("dve", lambda e: e.tensor_scalar(out=rst[:n, 2:4], in0=rst[:n, 2:4], scalar1=-1.0 / 64, scalar2=None, op0=ALU.mult), reads=[B("rst")], writes=[B("rst")])
                S.op("dve", lambda e: e.tensor_tensor(out=yc3, in0=y3, in1=bc3(rst[:n, 2:4], n, 2, 64), op=ALU.add), reads=[B("yb"), B("rst")], writes=[B("yc")])
                S.op("dve", lambda e: e.tensor_tensor(out=ysq[:n, :], in0=yc[:n, :], in1=yc[:n, :], op=ALU.mult), reads=[B("yc")], writes=[B("ysq")])
                S.op("dve", lambda e: e.tensor_reduce(out=rst[:n, 4:6], in_=ysq[:n, :].rearrange("p (a b) -> p a b", b=64), axis=AX.X, op=ALU.add),
                     reads=[B("ysq")], writes=[B("rst")])
                rstd_chain(rst[:n, 4:6], rst[:n, 4:6], n, 2, 1.0 / 64, GN_EPS, B("rst"), B("rst"))
                S.op("dve", lambda e: e.tensor_tensor(out=yc3, in0=yc3, in1=bc3(rst[:n, 4:6], n, 2, 64), op=ALU.mult), reads=[B("yc"), B("rst")], writes=[B("yc")])
                S.op("dve", lambda e: e.tensor_tensor(out=yc[:n, :], in0=yc[:n, :], in1=lg_bc[:n, :], op=ALU.mult), reads=[B("yc"), B("parb")], writes=[B("yc")])
                S.op("dve", lambda e: e.tensor_tensor(out=yc[:n, :], in0=yc[:n, :], in1=lb_bc[:n, :], op=ALU.add), reads=[B("yc"), B("parb")], writes=[B("yc")])
                S.op("dve", lambda e: e.tensor_tensor(out=ysq[:n, :].rearrange("p (a b) -> p a b", b=64), in0=xv.rearrange("p (a b) -> p a b", b=64),
                                                      in1=bc3(bs[:n, 0:2], n, 2, 64), op=ALU.mult), reads=[B("xs_"), B("bs")], writes=[B("ysq")])
                S.op("dve", lambda e: e.tensor_tensor(out=yc[:n, :], in0=yc[:n, :], in1=ysq[:n, :], op=ALU.add), reads=[B("yc"), B("ysq")], writes=[B("yc")])
                S.op("dve", lambda e: e.tensor_tensor(out=obb[:n, :], in0=yc[:n, :], in1=g_sb[:n, :], op=ALU.mult), reads=[B("yc"), B("g_sb")], writes=[B("obb")])
                pb = gbank()
                pv = PS[pb][:].bitcast(BF16)
                S.op("pe", lambda e: e.transpose(out=pv[:, 0:n], in_=obb[:n, :], identity=identb[:n, :n]), reads=[B("obb"), B("identb")], writes=[PB[pb]])
                cat_writer(1, tile_idx, pv[:, 0:n], PB[pb])
                yield

            def state_out(hslot, dst_ap):
                pb = gbank()
                for hh in range(2):
                    Hc = Hs[hh][hslot]
                    S.op("pe", lambda e, hh=hh, Hc=Hc: e.transpose(out=PS[pb][0:64, hh * 64:(hh + 1) * 64], in_=Hc[:, :], identity=ident[0:64, 0:64]),
                         reads=[B(Hc.name), B("ident")], writes=[PB[pb]] if hh == 0 else [], pwrites=[] if hh == 0 else [PB[pb]], inc=(hh == 1))
                S.op("act", lambda e: e.activation(out=Hout[:, :, :], in_=PS[pb][0:64, 0:128].rearrange("p (a b) -> p a b", b=64), func=AF.Copy),
                     reads=[PB[pb]], writes=[B("Hout")])
                S.dma("sp", dst_ap.rearrange("h v k -> v h k"), Hout[:, :, :], reads=[B("Hout")])

            def run(g):
                for _ in g:
                    pass

            def drive(main, side, ratio):
                acc_ = 0.0
                for _ in main:
                    drive.n += 1
                    assert not any(S.uncommitted.values()), "yield inside an uncommitted group"
                    if side is not None:
                        acc_ += ratio
                        while acc_ >= 1.0 and side is not None:
                            acc_ -= 1.0
                            try:
                                next(side)
                            except StopIteration:
                                side = None
                if side is not None:
                    run(side)

            drive.n = 0
            XIv = XI.rearrange("(r c two p) t -> r c two p t", r=4, c=8, two=2, p=128)

            def make_writer(hp):
                def w_(kind, ti, src_ap, srcB):
                    rng_ = (ti * 128) // TR
                    toff_ = ti * 128 - rng_ * TR
                    ct = catA if kind == 0 else catB
                    cB = B(ct.name)
                    S.op("dve", lambda e: e.tensor_tensor(out=ct[:, :, :], in0=src_ap.unsqueeze(1).to_broadcast([128, 4, 128]),
                                                          in1=selt[:, 0:4].unsqueeze(2).to_broadcast([128, 4, 128]), op=ALU.mult),
                         reads=[srcB, B("selt")], writes=[cB])
                    c0 = 4 if kind == 1 else 0
                    S.dma("sp", XIv[rng_, c0:c0 + 4, hp, :, toff_:toff_ + 128].rearrange("j p t -> p j t"), ct[:, :, :],
                          reads=[cB], pwrites=[B("XI")])
                return w_

            def stage1(hp, i):
                slot = i % 2
                qb = (i // 2) % 2
                S.dma("sp", cstt[slot][:, :], csp[i * 128:(i + 1) * 128, :], writes=[B(cstt[slot].name)])
                yield from front(xb[i * 128:(i + 1) * 128, :], 128, xnTall, slot * 128, XNB[slot], g1t, slot)
                yield from project(xnTall, slot * 128, XNB[slot], 128, slot)
                yield from attn_prep(128, cstt[slot][:, :], B(cstt[slot].name), kpo[i * 128:(i + 1) * 128, hp, :], vpo[i * 128:(i + 1) * 128, hp, :],
                                     i, i * 128, (i % 2) * 128, QTs[qb], B(QTs[qb].name))

            def stage2(hp, i, writer):
                slot = i % 2
                prev_ap = None if i == 0 else prw[1 - slot]
                yield from rwkv(128, slot, prev_ap, None if i == 0 else B(prw[1 - slot].name), c128[:, :], i == 0, i % 2, writer, i)

            def interleave(g1, g2):
                gens = [g for g in (g1, g2) if g is not None]
                while gens:
                    for g_ in list(gens):
                        try:
                            next(g_)
                        except StopIteration:
                            gens.remove(g_)
                    yield

            def drive_keep(main, side, ratio):
                acc_ = 0.0
                for _ in main:
                    drive.n += 1
                    assert not any(S.uncommitted.values()), "yield inside an uncommitted group"
                    if side is not None:
                        acc_ += ratio
                        while acc_ >= 1.0 and side is not None:
                            acc_ -= 1.0
                            try:
                                next(side)
                            except StopIteration:
                                side = None
                return side

            def attn_stream(hp, Q, writer):
                kts = [(j * 128, 128, j, None) for j in range(2 * Q)]
                kts.append((2 * Q * 128, 128, 2 * Q, 0))
                kts.append(((2 * Q + 1) * 128, 128, 2 * Q + 1, 1))
                qb = Q % 2
                yield from attention(256, 128, kts, writer, QTs[qb], B(QTs[qb].name), 2 * Q)

            SEG_YIELDS = 25.0
            for hp in range(2):
                wsrc_name[0] = "wmyb%d" % hp
                load_unit_w(wmyb[hp])
                load_unit_params(upmy[hp:hp + 1, :], lwmy[hp], lg2my[hp])
                for hh in range(2):
                    S.op("dve", lambda e, hh=hh: e.memset(Hs[hh][0][:, :], 0.0), writes=[B(Hs[hh][0].name)])
                writer = make_writer(hp)
                run(stage1(hp, 0))
                side = None
                ratio = 0.0
                for i in range(NT):
                    if i % 2 == 1:
                        if side is not None:
                            run(side)
                        Q = i // 2
                        side = attn_stream(hp, Q, writer)
                        ratio = (2 * Q + 6) / (2 * SEG_YIELDS)
                    seg = interleave(stage2(hp, i, writer), stage1(hp, i + 1) if i + 1 < NT else None)
                    side = drive_keep(seg, side, ratio)
                if side is not None:
                    run(side)
                build_nc.main_yields = drive.n / float(NT) / (hp + 1)
                state_out(NT % 2, spo[hp])
                lastp = prw[(NT - 1) % 2]
                S.dma("sp", shpo[hp:hp + 1, :], lastp[127:128, :], reads=[B(lastp.name)])

            S.custom("pool", lambda e: e.collective_compute("ReduceScatter", ALU.add, replica_groups=[[0, 1, 2, 3], [4, 5, 6, 7]],
                                                             ins=[XI[:, :]], outs=[XO[:, :]]),
                     reads=[B("XI")], writes=[B("XO")])
            ckpt("rs")

            ckpt("casts")

            ntile_s = (NB * 64 + 127) // 128
            for t_ in range(ntile_s):
                n_ = min(128, NB * 64 - t_ * 128)
                run(front(xsm[t_ * 128:t_ * 128 + n_, :], n_, xnTall, t_ * 128, XNB[t_], g1t, t_ % 2))
            kst = sb("kst", [128, 128]); kstb = sb("kstb", [128, 128], BF16)
            kst2 = sb("kst2", [128, 128])

            def make_writer_s(u, bb):
                def cat_writer_s(kind, qt_, src_ap, srcB):
                    chunk = u + (8 if kind == 1 else 0)
                    ct = catA if kind == 0 else catB
                    cB = B(ct.name)
                    S.op("act", lambda e: e.activation(out=ct[:, 0, 0:64], in_=src_ap, func=AF.Copy), reads=[srcB], writes=[cB])
                    S.dma("sp", XS[chunk * 128:(chunk + 1) * 128, bb * 64:(bb + 1) * 64], ct[:, 0, 0:64], reads=[cB], pwrites=[B("XS")])
                return cat_writer_s

            def sample_pre(u, bb, idx):
                base = (idx % 2) * (NP + 1)
                qb = idx % 2
                for pt in range(NP):
                    kt_i = base + pt
                    S.dma("sp", kst[:, :], ck[bb, pt * 128:(pt + 1) * 128, u, :], writes=[B("kst")])
                    S.op("dve", lambda e: e.tensor_copy(out=kstb[:, :], in_=kst[:, :]), reads=[B("kst")], writes=[B("kstb")])
                    pb = gbank()
                    pv = PS[pb][:].bitcast(BF16)
                    S.op("pe", lambda e, pv=pv: e.transpose(out=pv[:, 0:128], in_=kstb[:, :], identity=identb[:, :]),
                         reads=[B("kstb"), B("identb")], writes=[PB[pb]])
                    S.op("act", lambda e, pv=pv: e.activation(out=KT[:, kt_i * 128:(kt_i + 1) * 128], in_=pv[:, 0:128], func=AF.Copy),
                         reads=[PB[pb]], writes=[KTB[kt_i]])
                    S.dma("sp", kst2[:, :], cv[bb, pt * 128:(pt + 1) * 128, u, :], writes=[B("kst2")])
                    S.op("dve", lambda e: e.tensor_copy(out=VV[:, kt_i, :], in_=kst2[:, :]), reads=[B("kst2")], writes=[VB[kt_i]])
                    if pt % 2 == 1:
                        yield
                S.dma("sp", Hld[:, :, :], st0[bb, 2 * u:2 * u + 2].rearrange("h v k -> v h k"), writes=[B("Hld")])
                pb = gbank()
                for hh in range(2):
                    S.op("pe", lambda e, hh=hh: e.transpose(out=PS[pb][0:64, hh * 64:(hh + 1) * 64], in_=Hld[:, hh, :], identity=ident[0:64, 0:64]),
                         reads=[B("Hld"), B("ident")], writes=[PB[pb]] if hh == 0 else [], pwrites=[] if hh == 0 else [PB[pb]], inc=(hh == 1))
                for hh in range(2):
                    S.op("act", lambda e, hh=hh: e.activation(out=Hs[hh][0][:, :], in_=PS[pb][0:64, hh * 64:(hh + 1) * 64], func=AF.Copy),
                         reads=[PB[pb]], writes=[B(Hs[hh][0].name)])
                S.op("dve", lambda e: e.memset(shbuf[0:63, :], 0.0), writes=[B("shbuf")])
                S.dma("sp", shbuf[63:64, :], sh0[bb, u:u + 1, :], reads=[], writes=[], pwrites=[B("shbuf")])
                yield
                yield from project(xnTall, bb * 64, XNB[bb // 2], 64, 0)
                yield from attn_prep(64, csst[:, :], B("csst"), kso[bb, :, u, :], vso[bb, :, u, :], base + NP, (base + NP) * 128, 0,
                                     QTs[qb], B(QTs[qb].name))
                yield from rwkv(64, 0, shbuf[0:64, :], B("shbuf"), c64[:, :], False, 0, make_writer_s(u, bb), 0)
                state_out(1, sso[bb, u])
                S.dma("sp", shso[bb, u:u + 1, :], prw[0][63:64, :], reads=[B(prw[0].name)])
                yield

            def sample_attn(u, bb, idx):
                base = (idx % 2) * (NP + 1)
                qb = idx % 2
                kts = [((base + j) * 128, 128, base + j, None) for j in range(NP)] + [((base + NP) * 128, 64, base + NP, None)]
                yield from attention(64, 64, kts, make_writer_s(u, bb), QTs[qb], B(QTs[qb].name), 0)

            side_s = None
            for u in range(8):
                wsrc_name[0] = "wub%d" % u
                load_unit_w(wub[u])
                load_unit_params(up[u:u + 1, :], lw[u], lg2[u])
                for bb in range(NB):
                    idx = u * NB + bb
                    side_s = drive_keep(sample_pre(u, bb, idx), side_s, (NP + 3) / 40.0)
                    if side_s is not None:
                        run(side_s)
                    side_s = sample_attn(u, bb, idx)
            if side_s is not None:
                run(side_s)

            ckpt("sample")
            S.barrier()
            es1.close()
            cur[0] = es
            NTOK = 512
            catT = sb("catT", [128, NCH, NTOK], BF16)
            hsb = sb("hsb", [128, NTOK // 128, D])
            hnT = sb("hnT", [128, NCH, NTOK], BF16)
            actT = sb("actT", [128, NFF, NTOK], BF16)
            wring = [sb("wring%d" % i, [128, 16 * 512], BF16) for i in range(3)]
            wrr = [0]
            sg = sb("sg", [128, 512])

            def wbuf():
                i = wrr[0]
                wrr[0] = (wrr[0] + 1) % 3
                return wring[i]

            blocks = []
            t0 = 0
            while t0 < TR:
                nb_ = min(NTOK, TR - t0)
                blocks.append(("p", t0, nb_))
                t0 += nb_
            t0 = 0
            while t0 < NB * 64:
                nb_ = min(NTOK, NB * 64 - t0)
                blocks.append(("s", t0, nb_))
                t0 += nb_
            for (kind, t0, nb_) in blocks:
                src = XO if kind == "p" else XS
                srcB = B("XO") if kind == "p" else B("XS")
                xsrc = xr if kind == "p" else xsm
                ydst = yp if kind == "p" else ys
                S.dma("sp", catT[:, :, 0:nb_], src[:, t0:t0 + nb_].rearrange("(c p) n -> p c n", p=128), reads=[srcB], writes=[B("catT")])
                ntl = (nb_ + 127) // 128
                tls = [(tt * 128, min(128, nb_ - tt * 128)) for tt in range(ntl)]
                for tt, (o_, n_) in enumerate(tls):
                    S.dma("sp", hsb[:n_, tt, :], xsrc[t0 + o_:t0 + o_ + n_, :], writes=[B("hsb%d" % tt)])
                for ng in range(4):
                    wb = wbuf(); wB = B(wb.name)
                    w3 = wb[:, :].rearrange("p (c n) -> p c n", n=512)
                    S.dma("sp", w3, wob[:, ng * 512:(ng + 1) * 512].rearrange("(c p) n -> p c n", p=128), reads=[B("wob")], writes=[wB])
                    for tt, (o_, n_) in enumerate(tls):
                        pb = tt % 4
                        for k in range(NCH):
                            S.op("pe", lambda e, k=k, pb=pb, o_=o_, n_=n_: e.matmul(PS[pb][:n_, 0:512], lhsT=catT[:, k, o_:o_ + n_], rhs=w3[:, k, :],
                                                                                  start=(k == 0), stop=(k == NCH - 1), skip_group_check=True),
                                 reads=[B("catT"), wB], writes=[PB[pb]] if k == 0 else [], pwrites=[] if k == 0 else [PB[pb]], inc=(k == NCH - 1))
                        S.op("dve", lambda e, tt=tt, pb=pb, n_=n_, ng=ng: e.tensor_tensor(out=hsb[:n_, tt, ng * 512:(ng + 1) * 512], in0=PS[pb][:n_, 0:512],
                                                                                         in1=hsb[:n_, tt, ng * 512:(ng + 1) * 512], op=ALU.add),
                             reads=[PB[pb], B("hsb%d" % tt)], writes=[B("hsb%d" % tt)])
                for tt, (o_, n_) in enumerate(tls):
                    hB = B("hsb%d" % tt)
                    S.op("dve", lambda e, tt=tt, n_=n_: e.scalar_tensor_tensor(out=junk[:n_, :], in0=hsb[:n_, tt, :], scalar=1.0, in1=hsb[:n_, tt, :], op0=ALU.mult, op1=ALU.mult, accum_out=st4[:n_, 0:1]),
                         reads=[hB], writes=[B("junk"), B("st4")])
                    rstd_chain(st4[:n_, 0:1], st4[:n_, 0:1], n_, 1, 1.0 / D, RMS_EPS, B("st4"), B("st4"))
                    S.op("act", lambda e, tt=tt, n_=n_: e.activation(out=xsb[:n_, :], in_=hsb[:n_, tt, :], func=AF.Copy, scale=st4[:n_, 0:1]),
                         reads=[hB, B("st4")], writes=[B("xsb")])
                    for half in range(2):
                        pb = 4 + half
                        pv = PS[pb][:].bitcast(BF16)
                        for c8 in range(8):
                            c = half * 8 + c8
                            S.op("pe", lambda e, c=c, c8=c8, pv=pv, n_=n_: e.transpose(out=pv[:, c8 * n_:(c8 + 1) * n_], in_=xsb[:n_, c * 128:(c + 1) * 128],
                                                                                     identity=identb[:n_, :n_]),
                                 reads=[B("xsb"), B("identb")], writes=[PB[pb]] if c8 == 0 else [], pwrites=[] if c8 == 0 else [PB[pb]], inc=(c8 == 7))
                        S.op("dve", lambda e, half=half, pv=pv, n_=n_, o_=o_: e.tensor_tensor(
                            out=hnT[:, half * 8:(half + 1) * 8, o_:o_ + n_], in0=pv[:, 0:8 * n_].rearrange("p (c n) -> p c n", n=n_),
                            in1=bc3(g2t[:, half * 8:(half + 1) * 8], 128, 8, n_), op=ALU.mult),
                            reads=[PB[pb], B("g2t")], writes=[B("hnT")] if (tt == 0 and half == 0) else [], pwrites=[] if (tt == 0 and half == 0) else [B("hnT")])
                for fg in range(NFF // 4):
                    wg_ = wbuf(); wgB = B(wg_.name)
                    wg3 = wg_[:, :].rearrange("p (c n) -> p c n", n=512)
                    S.dma("sp", wg3, wgb[:, fg * 512:(fg + 1) * 512].rearrange("(c p) n -> p c n", p=128), reads=[B("wgb")], writes=[wgB])
                    wu_ = wbuf(); wuB = B(wu_.name)
                    wu3 = wu_[:, :].rearrange("p (c n) -> p c n", n=512)
                    S.dma("sp", wu3, wupb[:, fg * 512:(fg + 1) * 512].rearrange("(c p) n -> p c n", p=128), reads=[B("wupb")], writes=[wuB])
                    for f4 in range(4):
                        f = fg * 4 + f4
                        pg_, pu_ = (f % 2) * 2, (f % 2) * 2 + 1
                        for k in range(NCH):
                            S.op("pe", lambda e, k=k, pg_=pg_, f4=f4: e.matmul(PS[pg_][:, 0:nb_], lhsT=wg3[:, k, f4 * 128:(f4 + 1) * 128], rhs=hnT[:, k, 0:nb_],
                                                                             start=(k == 0), stop=(k == NCH - 1), skip_group_check=True),
                                 reads=[B("hnT"), wgB], writes=[PB[pg_]] if k == 0 else [], pwrites=[] if k == 0 else [PB[pg_]], inc=(k == NCH - 1))
                        for k in range(NCH):
                            S.op("pe", lambda e, k=k, pu_=pu_, f4=f4: e.matmul(PS[pu_][:, 0:nb_], lhsT=wu3[:, k, f4 * 128:(f4 + 1) * 128], rhs=hnT[:, k, 0:nb_],
                                                                             start=(k == 0), stop=(k == NCH - 1), skip_group_check=True),
                                 reads=[B("hnT"), wuB], writes=[PB[pu_]] if k == 0 else [], pwrites=[] if k == 0 else [PB[pu_]], inc=(k == NCH - 1))
                        S.op("act", lambda e, pg_=pg_: e.activation(out=sg[:, 0:nb_], in_=PS[pg_][:, 0:nb_], func=AF.Silu), reads=[PB[pg_]], writes=[B("sg")])
                        S.op("dve", lambda e, pu_=pu_, f=f: e.tensor_tensor(out=actT[:, f, 0:nb_], in0=PS[pu_][:, 0:nb_], in1=sg[:, 0:nb_], op=ALU.mult),
                             reads=[PB[pu_], B("sg")], writes=[B("actT")] if f == 0 else [], pwrites=[] if f == 0 else [B("actT")])
                for ng in range(4):
                    for kq in range(4):
                        wb = wbuf(); wB = B(wb.name)
                        w3 = wb[:, 0:11 * 512].rearrange("p (c n) -> p c n", n=512)
                        S.dma("sp", w3, wdb[kq * 11 * 128:(kq + 1) * 11 * 128, ng * 512:(ng + 1) * 512].rearrange("(c p) n -> p c n", p=128),
                              reads=[B("wdb")], writes=[wB])
                        for tt, (o_, n_) in enumerate(tls):
                            pb = 4 + tt
                            for kc in range(11):
                                kk_ = kq * 11 + kc
                                first = (kk_ == 0); last = (kk_ == NFF - 1)
                                S.op("pe", lambda e, kc=kc, kk_=kk_, pb=pb, o_=o_, n_=n_, first=first, last=last: e.matmul(
                                    PS[pb][:n_, 0:512], lhsT=actT[:, kk_, o_:o_ + n_], rhs=w3[:, kc, :], start=first, stop=last, skip_group_check=True),
                                     reads=[B("actT"), wB], writes=[PB[pb]] if first else [], pwrites=[] if first else [PB[pb]], inc=(kc == 10))
                    for tt, (o_, n_) in enumerate(tls):
                        pb = 4 + tt
                        S.op("dve", lambda e, tt=tt, pb=pb, n_=n_, ng=ng: e.tensor_tensor(out=hsb[:n_, tt, ng * 512:(ng + 1) * 512], in0=PS[pb][:n_, 0:512],
                                                                                         in1=hsb[:n_, tt, ng * 512:(ng + 1) * 512], op=ALU.add),
                             reads=[PB[pb], B("hsb%d" % tt)], writes=[B("hsb%d" % tt)])
                for tt, (o_, n_) in enumerate(tls):
                    S.dma("sp", ydst[t0 + o_:t0 + o_ + n_, :], hsb[:n_, tt, :], reads=[B("hsb%d" % tt)])

        except _Stop:
            es1.close()
        for s, c in S.all_dma():
            nc.sync.wait_ge(s, c)
        for e in ("pe", "act", "dve", "pool"):
            if S.cnt[e] > 0:
                nc.sync.wait_ge(S.esem[e], S.cnt[e])
        build_nc.nops = S.nops
    return nc


def _unit_cols(u):
    A = 3072
    q = np.arange(128) + u * 128
    k = 1024 + q
    v = 2048 + q
    r = A + u * 128 + np.arange(128)
    kr = A + 1024 + u * 128 + np.arange(128)
    vr = A + 2048 + u * 128 + np.arange(128)
    lo = A + 3072 + np.arange(192)
    return np.concatenate([q, k, v, r, kr, vr, lo])


def _rw_cols(u):
    r = u * 128 + np.arange(128)
    return np.concatenate([r, 1024 + r, 2048 + r, 3072 + np.arange(192)])


def _consts(T):
    c = {}
    idx = np.arange(128)
    c["cid"] = np.eye(128, dtype=np.float32)
    c["cut"] = (idx[:, None] <= idx[None, :]).astype(np.float32)
    c["cs1"] = (idx[:, None] + 1 == idx[None, :]).astype(np.float32)
    cc = np.zeros((128, 128), np.float32); cc[127, 0] = 1.0
    c["cc128"] = cc
    cc = np.zeros((64, 64), np.float32); cc[63, 0] = 1.0
    c["cc64"] = cc
    mA = (idx[:, None] < idx[None, :]).astype(np.float32)
    mL = (idx[:, None] <= idx[None, :]).astype(np.float32)
    c["cgm"] = np.concatenate([mA, mL, mA, mL], axis=1)
    c["clow"] = (idx[None, :] < idx[:, None]).astype(np.float32)
    cm = np.ones((128, 128), np.float32); cm[64:, :64] = 0.0
    on = np.ones((128, 128), np.float32); ze = np.zeros((128, 128), np.float32)
    d0 = np.concatenate([cm, on], axis=1); d1 = np.concatenate([ze, cm], axis=1)
    c["cam"] = np.stack([np.concatenate([d0, d0], axis=1), np.concatenate([d1, d1], axis=1)]).astype(np.float32)
    inv = (np.float32(ROPE_THETA) ** (-np.arange(0, 16, 2, dtype=np.float32) / np.float32(16))).astype(np.float32)

    def tab(pos):
        ang = (pos.astype(np.float32)[:, None] * inv[None, :]).astype(np.float32)
        return np.concatenate([np.cos(ang), np.sin(ang)], axis=1).astype(np.float32)
    c["csp"] = tab(np.arange(T))
    return c, tab


_NC_CACHE = {}


def kernel(x_prompt, x_sample, cache_attn_k, cache_attn_v, state_rwkv, state_rwkv_shift,
           norm1_g, w_in, q_norm_g, k_norm_g, lambda_q1, lambda_k1, lambda_q2, lambda_k2, subln_g,
           mu_rwkv, w0, w2, a0, a2, g2, k_k, k_a, r_k, lnx_g, lnx_b,
           w_out, norm2_g, w_gate, w_up, w_down):
    f = lambda a: np.ascontiguousarray(np.asarray(a, dtype=np.float32))
    x_prompt, x_sample = f(x_prompt), f(x_sample)
    Bp, T, _ = x_prompt.shape
    Bs, Ts, _ = x_sample.shape
    PAST = cache_attn_k.shape[2]
    NB = Bs // 8
    assert Bp == 2 and Ts == 64
    dm = Dims(T, NB, PAST)
    key = (T, NB, PAST)
    if key not in _NC_CACHE:
        _NC_CACHE[key] = build_nc(dm)
    nc = _NC_CACHE[key]
    TR = T // 4
    w_in0 = f(w_in)[0]
    wu = np.stack([w_in0[:, _unit_cols(u)] for u in range(8)])
    mu0 = f(mu_rwkv)[0]; w00 = f(w0)[0]; a00 = f(a0)[0]; kk0 = f(k_k)[0]; ka0 = f(k_a)[0]
    rk0 = f(r_k)[0].reshape(-1); lg0 = f(lnx_g)[0]; lb0 = f(lnx_b)[0]
    w20 = f(w2)[0]; a20 = f(a2)[0]; g20 = f(g2)[0]
    ups, lws, lg2s = [], [], []
    for u in range(8):
        hs = slice(u * 128, (u + 1) * 128)
        ups.append(np.concatenate([mu0[_rw_cols(u)], w00[hs], a00[hs], kk0[hs], ka0[hs], rk0[hs], lg0[hs], lb0[hs]]))
        lws.append(np.concatenate([w20[:, hs], a20[:, hs]], axis=0))
        lg2s.append(g20[:, hs])
    up = np.stack(ups).astype(np.float32); lw = np.stack(lws).astype(np.float32); lg2_ = np.stack(lg2s).astype(np.float32)
    consts, tab = _consts(T)
    consts["css"] = tab(PAST + np.arange(64))
    common = dict(
        wu=wu, up=up, lw=lw, lg2=lg2_,
        g1T=np.ascontiguousarray(f(norm1_g)[0].reshape(16, 128).T), g2T=np.ascontiguousarray(f(norm2_g)[0].reshape(16, 128).T),
        qkg=np.concatenate([f(q_norm_g)[0].reshape(-1), f(k_norm_g)[0].reshape(-1)])[None, :],
        lamv=np.concatenate([f(lambda_q1)[0], f(lambda_k1)[0], f(lambda_q2)[0], f(lambda_k2)[0]])[None, :],
        subg=f(subln_g)[0][None, :],
        w_out=f(w_out)[0], w_gate=f(w_gate)[0], w_up=f(w_up)[0], w_down=f(w_down)[0], **consts)
    ck = f(cache_attn_k)[0]; cv = f(cache_attn_v)[0]; st = f(state_rwkv)[0]; sh = f(state_rwkv_shift)[0][:, 0, :]
    sh_u = np.stack([sh[:, _rw_cols(u)] for u in range(8)], axis=1)
    in_maps = []
    for c in range(8):
        b, j = c // 4, c % 4
        selm = np.zeros((128, 4), np.float32); selm[:, j] = 1.0
        m = dict(common)
        m.update(xb=x_prompt[b], xr=np.ascontiguousarray(x_prompt[b, j * TR:(j + 1) * TR]),
                 xsm=np.ascontiguousarray(x_sample[c * NB:(c + 1) * NB].reshape(NB * 64, D)),
                 ck=np.ascontiguousarray(ck[c * NB:(c + 1) * NB]), cv=np.ascontiguousarray(cv[c * NB:(c + 1) * NB]),
                 st0=np.ascontiguousarray(st[c * NB:(c + 1) * NB]), sh0=np.ascontiguousarray(sh_u[c * NB:(c + 1) * NB]),
                 wmy=np.ascontiguousarray(wu[2 * j:2 * j + 2]), upmy=np.ascontiguousarray(up[2 * j:2 * j + 2]),
                 lwmy=np.ascontiguousarray(lw[2 * j:2 * j + 2]), lg2my=np.ascontiguousarray(lg2_[2 * j:2 * j + 2]), sel=selm)
        in_maps.append({k: np.ascontiguousarray(v, dtype=np.float32) for k, v in m.items()})
    res = run_bass_kernel_spmd(nc, in_maps, core_ids=list(range(8)))
    R = res.results
    y_p = np.zeros((2, T, D), np.float32); y_s = np.zeros((Bs, 64, D), np.float32)
    k_p = np.zeros((1, 2, T, 8, 128), np.float32); v_p = np.zeros((1, 2, T, 8, 128), np.float32)
    S_p = np.zeros((1, 2, 16, 64, 64), np.float32); sh_p = np.zeros((1, 2, 1, 3264), np.float32)
    k_s = np.zeros((1, Bs, 64, 8, 128), np.float32); v_s = np.zeros((1, Bs, 64, 8, 128), np.float32)
    S_s = np.zeros((1, Bs, 16, 64, 64), np.float32); sh_s = np.zeros((1, Bs, 1, 3264), np.float32)
    for c in range(8):
        b, j = c // 4, c % 4
        r = R[c]
        y_p[b, j * TR:(j + 1) * TR] = r["yp"]
        y_s[c * NB:(c + 1) * NB] = np.asarray(r["ys"]).reshape(NB, 64, D)
        for hp in range(2):
            u = 2 * j + hp
            k_p[0, b, :, u, :] = r["kpo"][:, hp, :]
            v_p[0, b, :, u, :] = r["vpo"][:, hp, :]
            S_p[0, b, 2 * u:2 * u + 2] = r["spo"][hp]
            sh_p[0, b, 0, _rw_cols(u)] = r["shpo"][hp]
        k_s[0, c * NB:(c + 1) * NB] = r["kso"]
        v_s[0, c * NB:(c + 1) * NB] = r["vso"]
        S_s[0, c * NB:(c + 1) * NB] = np.asarray(r["sso"]).reshape(NB, 16, 64, 64)
        for u in range(8):
            sh_s[0, c * NB:(c + 1) * NB, 0, _rw_cols(u)] = np.asarray(r["shso"])[:, u, :].T if False else 0
        shso = np.asarray(r["shso"])
        for u in range(8):
            for bb in range(NB):
                sh_s[0, c * NB + bb, 0, _rw_cols(u)] = shso[bb, u]
    return (y_p, y_s, k_p, v_p, S_p, sh_p, k_s, v_s, S_s, sh_s)
```

```python
import math
import contextlib
import numpy as np
import concourse.bass as bass
import concourse.mybir as mybir
from concourse.bass_utils import run_bass_kernel_spmd

F32 = mybir.dt.float32
BF16 = mybir.dt.bfloat16
AF = mybir.ActivationFunctionType
ALU = mybir.AluOpType
AX = mybir.AxisListType

D = 2048
NCH = 16
UC = 960
DFF = 5632
NFF = 44
RMS_EPS = 1e-6
GN_EPS = 64e-5
LAM_INIT = 0.2
NPAR = 1472
ROPE_THETA = 500000.0


class _Stop(Exception):
    pass


def ckpt(name):
    import os
    if os.environ.get("KSTOP") == name:
        raise _Stop()


class Buf:
    __slots__ = ("name", "w", "r", "excl")

    def __init__(self, name, excl=False):
        self.name = name
        self.w = {}
        self.r = {}
        self.excl = excl


class Sched:
    LIMIT = 30000

    def __init__(self, nc, es, n_dma_sems=40):
        self.nc = nc
        self.es = es
        self.eng = {"pe": nc.tensor, "act": nc.scalar, "dve": nc.vector, "pool": nc.gpsimd, "sp": nc.sync}
        self.esem = {}
        self.cnt = {}
        self.seen = {e: {} for e in self.eng}
        self.uncommitted = {e: False for e in self.eng}
        self.nsem = 0
        for e in ("pe", "act", "dve", "pool"):
            self._new_esem(e)
        self.rings = {}
        for q, nq_ in (("sp", 30), ("pool", 12), ("cc", 1)):
            self.rings[q] = dict(sem=[self._sem("dma_%s%d" % (q, i)) for i in range(nq_)], cnt=[0] * nq_, nxt=0)
        self.nops = 0

    def _sem(self, name):
        self.nsem += 1
        return self.es.enter_context(self.nc.semaphore(name))

    def _new_esem(self, e):
        self.esem[e] = self._sem("e_%s_%d" % (e, self.nsem))
        self.cnt[e] = 0

    def _collect(self, reads, writes, pwrites):
        deps = {}

        def add(d):
            for k, (s, v) in d.items():
                if k not in deps or deps[k][1] < v:
                    deps[k] = (s, v)
        for b in reads:
            add(b.w)
            if b.excl:
                add(b.r)
        for b in writes:
            add(b.w)
            add(b.r)
        for b in pwrites:
            add(b.r)
        return deps

    def _emit_waits(self, e, deps):
        eng = self.eng[e]
        seen = self.seen[e]
        for k, (s, v) in deps.items():
            if seen.get(k, 0) >= v:
                continue
            if e in self.esem and s is self.esem[e] and v > self.cnt[e]:
                continue
            for e2 in self.esem:
                if s is self.esem[e2] and v > self.cnt[e2]:
                    raise RuntimeError("wait on uncommitted event of %s from %s" % (e2, e))
            eng.wait_ge(s, v)
            seen[k] = v

    def _record(self, ev, reads, writes, pwrites):
        k = id(ev[0])
        for b in reads:
            if k not in b.r or b.r[k][1] < ev[1]:
                b.r[k] = ev
        for b in writes:
            b.w = {k: ev}
            b.r = {}
        for b in pwrites:
            if k not in b.w or b.w[k][1] < ev[1]:
                b.w[k] = ev

    def op(self, e, fn, reads=(), writes=(), inc=True, pwrites=()):
        self.nops += 1
        if self.cnt[e] >= self.LIMIT and not self.uncommitted[e]:
            self._new_esem(e)
        deps = self._collect(reads, writes, pwrites)
        self._emit_waits(e, deps)
        ins = fn(self.eng[e])
        ev = (self.esem[e], self.cnt[e] + 1)
        if inc:
            self.cnt[e] += 1
            ins.then_inc(self.esem[e], 1)
            self.uncommitted[e] = False
        else:
            self.uncommitted[e] = True
        self._record(ev, reads, writes, pwrites)
        return ins

    def _ring_next(self, q):
        r = self.rings[q]
        i = r["nxt"]
        r["nxt"] = (i + 1) % len(r["sem"])
        return r, i

    def dma(self, q, out, in_, reads=(), writes=(), pwrites=(), **kw):
        self.nops += 1
        r, i = self._ring_next(q)
        s = r["sem"][i]
        deps = self._collect(reads, writes, pwrites)
        if r["cnt"][i] > 0:
            deps[id(s)] = (s, r["cnt"][i])
        self._emit_waits(q, deps)
        ins = self.eng[q].dma_start(out=out, in_=in_, **kw)
        r["cnt"][i] += 16
        ins.then_inc(s, 16)
        ev = (s, r["cnt"][i])
        self._record(ev, reads, writes, pwrites)
        return ev

    def custom(self, q, fn, reads=(), writes=(), pwrites=()):
        r, i = self._ring_next("cc")
        s = r["sem"][i]
        deps = self._collect(reads, writes, pwrites)
        if r["cnt"][i] > 0:
            deps[id(s)] = (s, r["cnt"][i])
        self._emit_waits(q, deps)
        ins = fn(self.eng[q])
        r["cnt"][i] += 1
        ins.then_inc(s, 1)
        ev = (s, r["cnt"][i])
        self._record(ev, reads, writes, pwrites)
        return ev

    def all_dma(self):
        for r in self.rings.values():
            for s, c in zip(r["sem"], r["cnt"]):
                if c > 0:
                    yield s, c

    def barrier(self):
        for e in self.eng:
            for e2 in self.esem:
                if self.cnt[e2] > 0 and self.seen[e].get(id(self.esem[e2]), 0) < self.cnt[e2]:
                    self.eng[e].wait_ge(self.esem[e2], self.cnt[e2])
                    self.seen[e][id(self.esem[e2])] = self.cnt[e2]
            for s, c in self.all_dma():
                if self.seen[e].get(id(s), 0) < c:
                    self.eng[e].wait_ge(s, c)
                    self.seen[e][id(s)] = c

    def wait_all(self, q, bufs):
        deps = self._collect(bufs, bufs, ())
        self._emit_waits(q, deps)


class Dims:
    def __init__(self, T, NB, PAST):
        self.T = T
        self.NB = NB
        self.PAST = PAST
        self.NT = T // 128
        self.TR = T // 4
        self.NP = PAST // 128


def build_nc(dm):
    T, NB, PAST, NT, TR, NP = dm.T, dm.NB, dm.PAST, dm.NT, dm.TR, dm.NP
    nc = bass.Bass("TRN2", target_bir_lowering=False)

    def din(name, shape, dt=F32):
        return nc.dram_tensor(name, list(shape), dt, kind="ExternalInput").ap()

    def dout(name, shape, dt=F32):
        return nc.dram_tensor(name, list(shape), dt, kind="ExternalOutput").ap()

    def dint(name, shape, dt):
        return nc.dram_tensor(name, list(shape), dt, kind="Internal").ap()

    xb = din("xb", [T, D])
    xr = din("xr", [TR, D])
    xsm = din("xsm", [NB * 64, D])
    ck = din("ck", [NB, PAST, 8, 128])
    cv = din("cv", [NB, PAST, 8, 128])
    st0 = din("st0", [NB, 16, 64, 64])
    sh0 = din("sh0", [NB, 8, 576])
    wu = din("wu", [8, D, UC])
    wmy = din("wmy", [2, D, UC])
    up = din("up", [8, NPAR])
    upmy = din("upmy", [2, NPAR])
    lw = din("lw", [8, 128, 128])
    lwmy = din("lwmy", [2, 128, 128])
    lg2 = din("lg2", [8, 64, 128])
    lg2my = din("lg2my", [2, 64, 128])
    g1T = din("g1T", [128, NCH])
    g2T = din("g2T", [128, NCH])
    qkg = din("qkg", [1, 256])
    lamv = din("lamv", [1, 256])
    subg = din("subg", [1, 128])
    sel = din("sel", [128, 4])
    w_out = din("w_out", [D, D])
    w_gate = din("w_gate", [D, DFF])
    w_up = din("w_up", [D, DFF])
    w_down = din("w_down", [DFF, D])
    csp = din("csp", [T, 16])
    css = din("css", [64, 16])
    cid = din("cid", [128, 128])
    cut = din("cut", [128, 128])
    cs1 = din("cs1", [128, 128])
    cc128 = din("cc128", [128, 128])
    cc64 = din("cc64", [64, 64])
    cgm = din("cgm", [128, 512])
    clow = din("clow", [128, 128])
    cam = din("cam", [2, 128, 512])
    yp = dout("yp", [TR, D])
    ys = dout("ys", [NB * 64, D])
    kpo = dout("kpo", [T, 2, 128])
    vpo = dout("vpo", [T, 2, 128])
    spo = dout("spo", [2, 2, 64, 64])
    shpo = dout("shpo", [2, 576])
    kso = dout("kso", [NB, 64, 8, 128])
    vso = dout("vso", [NB, 64, 8, 128])
    sso = dout("sso", [NB, 8, 2, 64, 64])
    shso = dout("shso", [NB, 8, 576])
    XI = dint("XI", [4 * 16 * 128, TR], BF16)
    XO = dint("XO", [16 * 128, TR], BF16)
    XS = dint("XS", [16 * 128, NB * 64], BF16)
    wub = dint("wub", [8, D, UC], BF16)
    wmyb = dint("wmyb", [2, D, UC], BF16)
    wob = dint("wob", [D, D], BF16)
    wgb = dint("wgb", [D, DFF], BF16)
    wupb = dint("wupb", [D, DFF], BF16)
    wdb = dint("wdb", [DFF, D], BF16)

    es = contextlib.ExitStack()
    with es:
        S = Sched(nc, es)
        bufs = {}

        es1 = contextlib.ExitStack()
        cur = [es]

        def sb(name, shape, dt=F32):
            t = cur[0].enter_context(nc.sbuf_tensor(name, list(shape), dt))
            bufs[name] = Buf(name)
            return t

        def B(name):
            if name not in bufs:
                bufs[name] = Buf(name)
            return bufs[name]

        PS = [es.enter_context(nc.psum_tensor("ps%d" % i, [128, 512], F32)) for i in range(8)]
        PB = [Buf("ps%d" % i, excl=True) for i in range(8)]
        gen_rr = [0]

        def gbank():
            i = gen_rr[0]
            gen_rr[0] = (gen_rr[0] + 1) % 4
            return i

        try:
            def cast_w(dst, src, rows, cols, bname):
                b = B(bname)
                r = 0
                while r < rows:
                    rr = min(128, rows - r)
                    c = 0
                    while c < cols:
                        cc = min(2048, cols - c)
                        S.dma("pool", dst[r:r + rr, c:c + cc], src[r:r + rr, c:c + cc], pwrites=[b])
                        c += cc
                    r += rr

            def cast_w_gen(dst, src, rows, cols, bname):
                b = B(bname)
                r = 0
                while r < rows:
                    rr = min(128, rows - r)
                    c = 0
                    while c < cols:
                        cc = min(2048, cols - c)
                        S.dma("pool", dst[r:r + rr, c:c + cc], src[r:r + rr, c:c + cc], pwrites=[b])
                        yield
                        c += cc
                    r += rr

            for u in range(2):
                cast_w(wmyb[u], wmy[u], D, UC, "wmyb%d" % u)
            def deferred_casts():
                for u in range(8):
                    yield from cast_w_gen(wub[u], wu[u], D, UC, "wub%d" % u)
                yield from cast_w_gen(wob, w_out, D, D, "wob")
                yield from cast_w_gen(wgb, w_gate, D, DFF, "wgb")
                yield from cast_w_gen(wupb, w_up, D, DFF, "wupb")
                yield from cast_w_gen(wdb, w_down, DFF, D, "wdb")
            dcast = [deferred_casts()]


            ident = sb("ident", [128, 128]); identb = sb("identb", [128, 128], BF16)
            ut = sb("ut", [128, 128]); s1m = sb("s1m", [128, 128]); c128 = sb("c128", [128, 128]); c64 = sb("c64", [64, 64])
            ones = sb("ones", [128, 128]); onesb = sb("onesb", [128, 1], BF16)
            gmask = sb("gmask", [128, 512]); lowm = sb("lowm", [128, 128])
            amask = sb("amask", [128, 2, 512], BF16)
            g1t = sb("g1t", [128, NCH]); g2t = sb("g2t", [128, NCH])
            qkgb = sb("qkgb", [128, 256]); lamb = sb("lamb", [128, 256]); subgb = sb("subgb", [128, 128])
            selt = sb("selt", [128, 4])
            cstt = [sb("cstt%d" % i, [128, 16]) for i in range(2)]; csst = sb("csst", [64, 16])
            lamt = sb("lamt", [128, 4])
            neglam = sb("neglam", [128, 1])
            for t_, src in ((ident, cid), (ut, cut), (s1m, cs1), (c128, cc128), (gmask, cgm), (lowm, clow),
                            (g1t, g1T), (g2t, g2T), (selt, sel)):
                S.dma("sp", t_[:], src[:, :], writes=[B(t_.name)])
            S.dma("sp", c64[:], cc64[:, :], writes=[B("c64")])
            S.dma("sp", csst[:], css[:, :], writes=[B("csst")])
            S.dma("sp", qkgb[:], qkg.partition_broadcast(128), writes=[B("qkgb")])
            S.dma("sp", lamb[:], lamv.partition_broadcast(128), writes=[B("lamb")])
            S.dma("sp", subgb[:], subg.partition_broadcast(128), writes=[B("subgb")])
            S.op("dve", lambda e: e.memset(ones[:], 1.0), writes=[B("ones")])
            S.op("dve", lambda e: e.memset(onesb[:], 1.0), writes=[B("onesb")])
            S.op("dve", lambda e: e.tensor_copy(out=identb[:], in_=ident[:]), reads=[B("ident")], writes=[B("identb")])
            S.dma("pool", amask[:], cam.rearrange("a p c -> p a c"), writes=[B("amask")])
            S.op("dve", lambda e: e.tensor_scalar(out=subgb[:], in0=subgb[:], scalar1=1.0 - LAM_INIT, scalar2=None,
                                                  op0=ALU.mult), reads=[B("subgb")], writes=[B("subgb")])
            lscr = sb("lscr", [128, 64])
            S.op("dve", lambda e: e.scalar_tensor_tensor(out=lscr[:], in0=lamb[:, 0:64], scalar=1.0, in1=lamb[:, 64:128], op0=ALU.mult, op1=ALU.mult, accum_out=lamt[:, 0:1]),
                 reads=[B("lamb")], writes=[B("lscr"), B("lamt")])
            S.op("dve", lambda e: e.scalar_tensor_tensor(out=lscr[:], in0=lamb[:, 128:192], scalar=1.0, in1=lamb[:, 192:256], op0=ALU.mult, op1=ALU.mult, accum_out=lamt[:, 1:2]),
                 reads=[B("lamb"), B("lamt")], writes=[B("lscr"), B("lamt")])
            S.op("act", lambda e: e.activation(out=lamt[:, 2:4], in_=lamt[:, 0:2], func=AF.Exp),
                 reads=[B("lamt")], writes=[B("lamt")])
            S.op("dve", lambda e: e.tensor_tensor(out=neglam[:], in0=lamt[:, 3:4], in1=lamt[:, 2:3], op=ALU.subtract),
                 reads=[B("lamt")], writes=[B("neglam")])
            S.op("dve", lambda e: e.tensor_scalar(out=neglam[:], in0=neglam[:], scalar1=-LAM_INIT, scalar2=None, op0=ALU.add),
                 reads=[B("neglam")], writes=[B("neglam")])
            ckpt("consts")

            NKT = max(NT, 2 * (NP + 1))
            xsb = sb("xsb", [128, D], BF16)
            junk = sb("junk", [128, D], BF16)
            st4 = sb("st4", [128, 8])
            cur[0] = es1
            wub_sb = sb("wub_sb", [128, NCH, UC], BF16)
            KT = sb("KT", [128, NKT * 128], BF16)
            VV = sb("VV", [128, NKT, 128], BF16)
            KTB = [Buf("KT%d" % i) for i in range(NKT)]
            VB = [Buf("V%d" % i) for i in range(NKT)]
            xt = [sb("xt%d" % i, [128, D]) for i in range(2)]
            xnTall = sb("xnTall", [128, NCH, 256], BF16)
            XNB = [Buf("xnTslot0"), Buf("xnTslot1")]
            gsb = sb("gsb", [128, 384])
            prw = [sb("prw%d" % i, [128, 576]) for i in range(2)]
            shbuf = sb("shbuf", [128, 576])
            tq = sb("tq", [128, 256]); qkn = sb("qkn", [128, 256]); rtmp = sb("rtmp", [128, 4, 4, 8])
            qkb = sb("qkb", [128, 256], BF16)
            QTs = [sb("QT%d" % i, [128, 2, 256], BF16) for i in range(2)]
            Eb = [sb("Eb%d" % i, [128, 512], BF16) for i in range(2)]
            OT = sb("OT", [128, 512]); zsb = sb("zsb", [1, 512]); Zacc = sb("Zacc", [128, 512]); ajunk = sb("ajunk", [128, 128], BF16); one11 = sb("one11", [1, 1])
            S.op("dve", lambda e: e.memset(one11[:], 1.0), writes=[B("one11")])
            for q_ in QTs:
                S.op("dve", lambda e, q_=q_: e.memset(q_[:, :, :], 0.0), writes=[B(q_.name)])
            osb = sb("osb", [128, 128]); onb = sb("onb", [128, 128], BF16); ast = sb("ast", [128, 8])
            catA = sb("catA", [128, 4, 128], BF16); catB = sb("catB", [128, 4, 128], BF16)
            parb = sb("parb", [128, NPAR])
            lwt = sb("lwt", [64, 2, 128]); lg2t = sb("lg2t", [64, 128])
            xs_ = sb("xs_", [128, 576])
            Et = sb("Et", [128, 128]); LT = sb("LT", [128, 192]); LTT = sb("LTT", [64, 3, 128])
            za = sb("za", [128, 256]); sa = sb("sa", [128, 256]); ld = sb("ld", [128, 128]); g_sb = sb("g_sb", [128, 128])
            kkv = sb("kkv", [128, 128]); rst = sb("rst", [128, 8]); k2 = sb("k2", [128, 128]); mm_ = sb("mm_", [128, 128])
            bvec = sb("bvec", [128, 128]); bs = sb("bs", [128, 2])
            cum = sb("cum", [128, 128]); ec = sb("ec", [128, 128]); eci = sb("eci", [128, 128]); ee = sb("ee", [128, 128])
            eh = sb("eh", [128, 128]); gC = sb("gC", [64, 2])
            rt = sb("rt", [128, 128]); bt = sb("bt", [128, 128]); ktl = sb("ktl", [128, 128])
            bh = sb("bh", [128, 128]); kh = sb("kh", [128, 128])
            FT = [sb("FT%d" % h, [64, 512]) for h in range(2)]
            GM = [sb("GM%d" % h, [128, 512]) for h in range(2)]
            Xa = [[sb("Xa%d_%d" % (h, i), [128, 128], BF16) for i in range(2)] for h in range(2)]
            Xb = [[sb("Xb%d_%d" % (h, i), [128, 128], BF16) for i in range(2)] for h in range(2)]
            XG = [sb("XG%d" % h, [128, 128], BF16) for h in range(2)]
            ACCB = [[sb("ACCB%d_%d" % (h, i), [128, 128], BF16) for i in range(2)] for h in range(2)]
            ACC = [[sb("ACC%d_%d" % (h, i), [128, 128]) for i in range(2)] for h in range(2)]
            RH = [sb("RH%d" % h, [128, 128]) for h in range(2)]
            PU = [sb("PU%d" % h, [128, 128]) for h in range(2)]
            Y1T = [sb("Y1T%d" % h, [64, 128]) for h in range(2)]
            Y2 = [sb("Y2%d" % h, [128, 64]) for h in range(2)]
            T1T = [sb("T1T%d" % h, [64, 64]) for h in range(2)]
            T2 = [sb("T2%d" % h, [64, 64]) for h in range(2)]
            Hs = [[sb("H%d_%d" % (h, i), [64, 64]) for i in range(2)] for h in range(2)]
            Hld = sb("Hld", [64, 2, 64]); Hout = sb("Hout", [64, 2, 64])
            yb = sb("yb", [128, 128]); yc = sb("yc", [128, 128]); ysq = sb("ysq", [128, 128]); obb = sb("obb", [128, 128], BF16)

            def bc3(ap2, n, a, b):
                return ap2.unsqueeze(2).to_broadcast([n, a, b])

            def rstd_chain(src_ap, dst_ap, n, k, scale, eps, bsrc, bdst):
                S.op("dve", lambda e: e.tensor_scalar(out=dst_ap, in0=src_ap, scalar1=scale, scalar2=eps, op0=ALU.mult,
                                                      op1=ALU.add), reads=[bsrc], writes=[bdst])
                S.op("act", lambda e: e.activation(out=dst_ap, in_=dst_ap, func=AF.Ln), reads=[bdst], writes=[bdst])
                S.op("act", lambda e: e.activation(out=dst_ap, in_=dst_ap, func=AF.Exp, scale=-0.5), reads=[bdst], writes=[bdst])

            def front(x_rows_ap, n, dst, dst_off, dstB, gt, slot):
                xtile = xt[slot]
                bx = B(xtile.name)
                S.dma("sp", xtile[:n, :], x_rows_ap, writes=[bx])
                S.op("dve", lambda e: e.scalar_tensor_tensor(out=junk[:n, :], in0=xtile[:n, :], scalar=1.0, in1=xtile[:n, :], op0=ALU.mult, op1=ALU.mult, accum_out=st4[:n, 0:1]),
                     reads=[bx], writes=[B("junk"), B("st4")])
                rstd_chain(st4[:n, 0:1], st4[:n, 0:1], n, 1, 1.0 / D, RMS_EPS, B("st4"), B("st4"))
                S.op("act", lambda e: e.activation(out=xsb[:n, :], in_=xtile[:n, :], func=AF.Copy, scale=st4[:n, 0:1]),
                     reads=[bx, B("st4")], writes=[B("xsb")])
                yield
                for half in range(2):
                    pb = gbank()
                    pv = PS[pb][:].bitcast(BF16)
                    for c8 in range(8):
                        c = half * 8 + c8
                        S.op("pe", lambda e, c=c, c8=c8, pv=pv: e.transpose(out=pv[:, c8 * n:(c8 + 1) * n],
                                                                           in_=xsb[:n, c * 128:(c + 1) * 128],
                                                                           identity=identb[:n, :n]),
                             reads=[B("xsb"), B("identb")], writes=[PB[pb]] if c8 == 0 else [], pwrites=[] if c8 == 0 else [PB[pb]],
                             inc=(c8 == 7))
                    S.op("dve", lambda e, half=half, pv=pv: e.tensor_tensor(
                        out=dst[:, half * 8:(half + 1) * 8, dst_off:dst_off + n],
                        in0=pv[:, 0:8 * n].rearrange("p (c n) -> p c n", n=n),
                        in1=bc3(gt[:, half * 8:(half + 1) * 8], 128, 8, n), op=ALU.mult),
                        reads=[PB[pb], B(gt.name)], writes=[] if half else [dstB], pwrites=[dstB] if half else [])
                    yield

            def load_unit_params(up_row, lw_ap, lg2_ap):
                S.dma("sp", parb[:], up_row.partition_broadcast(128), writes=[B("parb")])
                S.dma("sp", lwt[:], lw_ap.rearrange("(a p) n -> p a n", p=64), writes=[B("lwt")])
                S.dma("sp", lg2t[:], lg2_ap, writes=[B("lg2t")])
            mu_bc = parb[:, 0:576]; w0a0_bc = parb[:, 576:832]; kk_bc = parb[:, 832:960]; ka_bc = parb[:, 960:1088]
            rk_bc = parb[:, 1088:1216]; lg_bc = parb[:, 1216:1344]; lb_bc = parb[:, 1344:1472]

            def load_unit_w(wsrc):
                S.dma("sp", wub_sb[:], wsrc.rearrange("(c p) n -> p c n", p=128), reads=[B(wsrc_name[0])], writes=[B("wub_sb")])
            wsrc_name = [None]

            def project(xsrc, xoff, xB, n, pslot):
                pa, pb2 = gbank(), gbank()
                for k in range(NCH):
                    S.op("pe", lambda e, k=k: e.matmul(PS[pa][:n, 0:512], lhsT=xsrc[:, k, xoff:xoff + n], rhs=wub_sb[:, k, 0:512],
                                                       start=(k == 0), stop=(k == NCH - 1), skip_group_check=True),
                         reads=[xB, B("wub_sb")], writes=[PB[pa]] if k == 0 else [], pwrites=[] if k == 0 else [PB[pa]],
                         inc=(k == NCH - 1))
                for k in range(NCH):
                    S.op("pe", lambda e, k=k: e.matmul(PS[pb2][:n, 0:448], lhsT=xsrc[:, k, xoff:xoff + n], rhs=wub_sb[:, k, 512:960],
                                                       start=(k == 0), stop=(k == NCH - 1), skip_group_check=True),
                         reads=[xB, B("wub_sb")], writes=[PB[pb2]] if k == 0 else [], pwrites=[] if k == 0 else [PB[pb2]],
                         inc=(k == NCH - 1))
                pr = prw[pslot]
                yield
                S.op("act", lambda e: e.activation(out=gsb[:n, :], in_=PS[pa][:n, 0:384], func=AF.Copy),
                     reads=[PB[pa]], writes=[B("gsb")])
                S.op("act", lambda e: e.activation(out=pr[:n, 0:128], in_=PS[pa][:n, 384:512], func=AF.Copy),
                     reads=[PB[pa]], writes=[B(pr.name)])
                S.op("act", lambda e: e.activation(out=pr[:n, 128:576], in_=PS[pb2][:n, 0:448], func=AF.Copy),
                     reads=[PB[pb2]], pwrites=[B(pr.name)])

            def attn_prep(n, cs_ap, csB, k_out_ap, v_out_ap, kt_idx, kt_off, qoff, QT, QTB_):
                S.op("dve", lambda e: e.tensor_tensor(out=tq[:n, :], in0=gsb[:n, 0:256], in1=gsb[:n, 0:256], op=ALU.mult),
                     reads=[B("gsb")], writes=[B("tq")])
                S.op("dve", lambda e: e.tensor_reduce(out=st4[:n, 4:8], in_=tq[:n, :].rearrange("p (a b) -> p a b", b=64),
                                                      axis=AX.X, op=ALU.add), reads=[B("tq")], writes=[B("st4")])
                rstd_chain(st4[:n, 4:8], st4[:n, 4:8], n, 4, 1.0 / 64, RMS_EPS, B("st4"), B("st4"))
                q3 = qkn[:n, :].rearrange("p (a b) -> p a b", b=64)
                S.op("dve", lambda e: e.tensor_tensor(out=q3, in0=gsb[:n, 0:256].rearrange("p (a b) -> p a b", b=64),
                                                      in1=bc3(st4[:n, 4:8], n, 4, 64), op=ALU.mult),
                     reads=[B("gsb"), B("st4")], writes=[B("qkn")])
                S.op("dve", lambda e: e.tensor_tensor(out=qkn[:n, :], in0=qkn[:n, :], in1=qkgb[:n, :], op=ALU.mult),
                     reads=[B("qkn"), B("qkgb")], writes=[B("qkn")])
                yield
                x1 = q3[:, :, 0:8]; x2 = q3[:, :, 8:16]
                cosb = cs_ap[:, 0:8].unsqueeze(1).to_broadcast([n, 4, 8])
                sinb = cs_ap[:, 8:16].unsqueeze(1).to_broadcast([n, 4, 8])
                for idx, (a_, b_) in enumerate(((x1, cosb), (x2, sinb), (x2, cosb), (x1, sinb))):
                    S.op("dve", lambda e, idx=idx, a_=a_, b_=b_: e.tensor_tensor(out=rtmp[:n, :, idx, :], in0=a_, in1=b_, op=ALU.mult),
                         reads=[B("qkn"), csB], writes=[B("rtmp")] if idx == 0 else [], pwrites=[] if idx == 0 else [B("rtmp")])
                S.op("dve", lambda e: e.tensor_tensor(out=x1, in0=rtmp[:n, :, 0, :], in1=rtmp[:n, :, 1, :], op=ALU.subtract),
                     reads=[B("rtmp")], writes=[B("qkn")])
                S.op("dve", lambda e: e.tensor_tensor(out=x2, in0=rtmp[:n, :, 2, :], in1=rtmp[:n, :, 3, :], op=ALU.add),
                     reads=[B("rtmp")], writes=[B("qkn")])
                yield
                S.dma("sp", k_out_ap, qkn[:n, 128:256], reads=[B("qkn")])
                S.dma("sp", v_out_ap, gsb[:n, 256:384], reads=[B("gsb")])
                S.op("act", lambda e: e.activation(out=qkb[:n, :], in_=qkn[:n, :], func=AF.Copy), reads=[B("qkn")], writes=[B("qkb")])
                S.op("act", lambda e: e.activation(out=VV[:n, kt_idx, :], in_=gsb[:n, 256:384], func=AF.Copy),
                     reads=[B("gsb")], writes=[VB[kt_idx]])
                pb = gbank()
                pv = PS[pb][:].bitcast(BF16)
                S.op("pe", lambda e: e.transpose(out=pv[:, 0:n], in_=qkb[:n, 0:128], identity=identb[:n, :n]),
                     reads=[B("qkb"), B("identb")], writes=[PB[pb]], inc=False)
                S.op("pe", lambda e: e.transpose(out=pv[:, 128:128 + n], in_=qkb[:n, 128:256], identity=identb[:n, :n]),
                     reads=[B("qkb"), B("identb")], pwrites=[PB[pb]])
                S.op("act", lambda e: e.activation(out=QT[0:64, 0, qoff:qoff + n], in_=pv[0:64, 0:n], func=AF.Copy),
                     reads=[PB[pb]], writes=[] if qoff else [QTB_], pwrites=[QTB_] if qoff else [])
                S.op("act", lambda e: e.activation(out=QT[64:128, 1, qoff:qoff + n], in_=pv[64:128, 0:n], func=AF.Copy),
                     reads=[PB[pb]], pwrites=[QTB_])
                yield
                S.op("act", lambda e: e.activation(out=KT[:, kt_off:kt_off + n], in_=pv[:, 128:128 + n], func=AF.Copy),
                     reads=[PB[pb]], writes=[KTB[kt_idx]])

            def attention(nq, n, key_tiles, cat_writer, QT, QTB_, tile_base):
                W2 = 2 * nq
                for ki, (koff, nk, kidx, mk) in enumerate(key_tiles):
                    sbk = 4 + (ki % 2)
                    Ebt = Eb[ki % 2]
                    S.op("pe", lambda e: e.matmul(PS[sbk][:nk, 0:W2].rearrange("p (a b) -> p a b", a=2), lhsT=KT[:, koff:koff + nk], rhs=QT[:, :, 0:nq],
                                                  start=True, stop=True, skip_group_check=True),
                         reads=[KTB[kidx], QTB_], writes=[PB[sbk]])
                    S.op("act", lambda e: e.activation(out=Ebt[:nk, 0:W2], in_=PS[sbk][:nk, 0:W2], func=AF.Exp, scale=0.125),
                         reads=[PB[sbk]], writes=[B(Ebt.name)])
                    if mk is not None:
                        S.op("dve", lambda e: e.tensor_tensor(out=Ebt[:nk, 0:W2], in0=Ebt[:nk, 0:W2], in1=amask[:nk, mk, 0:W2], op=ALU.mult),
                             reads=[B(Ebt.name), B("amask")], writes=[B(Ebt.name)])
                    first = (ki == 0)
                    last = (ki == len(key_tiles) - 1)
                    S.op("pe", lambda e: e.matmul(PS[6][:, 0:W2], lhsT=VV[:nk, kidx, :], rhs=Ebt[:nk, 0:W2], start=first, stop=last,
                                                  skip_group_check=True),
                         reads=[VB[kidx], B(Ebt.name)], writes=[PB[6]] if first else [], pwrites=[] if first else [PB[6]], inc=True)
                    if first:
                        S.op("pool", lambda e: e.tensor_copy(out=Zacc[:nk, 0:W2], in_=Ebt[:nk, 0:W2]), reads=[B(Ebt.name)], writes=[B("Zacc")])
                    else:
                        S.op("pool", lambda e: e.tensor_tensor(out=Zacc[:nk, 0:W2], in0=Zacc[:nk, 0:W2], in1=Ebt[:nk, 0:W2], op=ALU.add),
                             reads=[B(Ebt.name), B("Zacc")], writes=[B("Zacc")])
                    yield
                S.op("pe", lambda e: e.matmul(PS[7][0:1, 0:W2], lhsT=ones[:, 0:1], rhs=Zacc[:, 0:W2], start=True, stop=True, skip_group_check=True),
                     reads=[B("ones"), B("Zacc")], writes=[PB[7]])
                S.op("act", lambda e: e.activation(out=OT[:, 0:W2], in_=PS[6][:, 0:W2], func=AF.Copy), reads=[PB[6]], writes=[B("OT")])
                S.op("dve", lambda e: e.tensor_copy(out=zsb[0:1, 0:W2], in_=PS[7][0:1, 0:W2]), reads=[PB[7]], writes=[B("zsb")])
                for qt in range(nq // n):
                    pb = gbank()
                    S.op("pe", lambda e: e.transpose(out=PS[pb][:n, 0:128], in_=OT[:, qt * n:(qt + 1) * n], identity=ident[:, :]),
                         reads=[B("OT"), B("ident")], writes=[PB[pb]], inc=False)
                    S.op("pe", lambda e: e.transpose(out=PS[pb][:n, 128:256], in_=OT[:, nq + qt * n:nq + (qt + 1) * n], identity=ident[:, :]),
                         reads=[B("OT"), B("ident")], pwrites=[PB[pb]], inc=False)
                    S.op("pe", lambda e: e.matmul(PS[pb][:n, 256:257], lhsT=zsb[0:1, qt * n:(qt + 1) * n], rhs=one11[0:1, 0:1],
                                                  start=False, stop=False, skip_group_check=True),
                         reads=[B("zsb"), B("one11")], pwrites=[PB[pb]], inc=False)
                    S.op("pe", lambda e: e.matmul(PS[pb][:n, 257:258], lhsT=zsb[0:1, nq + qt * n:nq + (qt + 1) * n], rhs=one11[0:1, 0:1],
                                                  start=False, stop=True, skip_group_check=True),
                         reads=[B("zsb"), B("one11")], pwrites=[PB[pb]])
                    S.op("dve", lambda e: e.reciprocal(out=ast[:n, 0:2], in_=PS[pb][:n, 256:258]), reads=[PB[pb]], writes=[B("ast")])
                    S.op("dve", lambda e: e.tensor_tensor(out=ast[:n, 2:3], in0=ast[:n, 1:2], in1=neglam[:n, 0:1], op=ALU.mult),
                         reads=[B("ast"), B("neglam")], writes=[B("ast")])
                    S.op("dve", lambda e: e.tensor_scalar(out=osb[:n, :], in0=PS[pb][:n, 0:128], scalar1=ast[:n, 0:1], scalar2=None,
                                                          op0=ALU.mult), reads=[PB[pb], B("ast")], writes=[B("osb")])
                    S.op("dve", lambda e: e.scalar_tensor_tensor(out=osb[:n, :], in0=PS[pb][:n, 128:256], scalar=ast[:n, 2:3],
                                                                 in1=osb[:n, :], op0=ALU.mult, op1=ALU.add),
                         reads=[PB[pb], B("ast"), B("osb")], writes=[B("osb")])
                    S.op("dve", lambda e: e.scalar_tensor_tensor(out=ajunk[:n, 0:128], in0=osb[:n, :], scalar=1.0, in1=osb[:n, :], op0=ALU.mult, op1=ALU.mult, accum_out=ast[:n, 4:5]),
                         reads=[B("osb"), B("ast")], writes=[B("ajunk"), B("ast")])
                    rstd_chain(ast[:n, 4:5], ast[:n, 4:5], n, 1, 1.0 / 128, RMS_EPS, B("ast"), B("ast"))
                    S.op("dve", lambda e: e.scalar_tensor_tensor(out=onb[:n, :], in0=osb[:n, :], scalar=ast[:n, 4:5], in1=subgb[:n, :],
                                                                 op0=ALU.mult, op1=ALU.mult),
                         reads=[B("osb"), B("ast"), B("subgb")], writes=[B("onb")])
                    pb2 = gbank()
                    pv = PS[pb2][:].bitcast(BF16)
                    S.op("pe", lambda e: e.transpose(out=pv[:, 0:n], in_=onb[:n, :], identity=identb[:n, :n]),
                         reads=[B("onb"), B("identb")], writes=[PB[pb2]])
                    cat_writer(0, tile_base + qt, pv[:, 0:n], PB[pb2])
                    yield

            def rwkv(n, pslot, prev_ap, prevB, cmat, first_tile, hslot, cat_writer, tile_idx):
                pr = prw[pslot]
                prB = B(pr.name)
                pa, pb2 = gbank(), gbank()
                for (bank, c0, c1) in ((pa, 0, 512), (pb2, 512, 576)):
                    S.op("pe", lambda e, bank=bank, c0=c0, c1=c1: e.matmul(PS[bank][:n, 0:c1 - c0], lhsT=s1m[:n, :n], rhs=pr[:n, c0:c1],
                                                                          start=True, stop=(prev_ap is None), skip_group_check=True),
                         reads=[B("s1m"), prB], writes=[PB[bank]], inc=(prev_ap is None))
                    if prev_ap is not None:
                        S.op("pe", lambda e, bank=bank, c0=c0, c1=c1: e.matmul(PS[bank][:n, 0:c1 - c0], lhsT=cmat, rhs=prev_ap[:, c0:c1],
                                                                              start=False, stop=True, skip_group_check=True),
                             reads=[prevB, B("c128"), B("c64")], pwrites=[PB[bank]])
                S.op("dve", lambda e: e.tensor_tensor(out=xs_[:n, 0:512], in0=PS[pa][:n, 0:512], in1=pr[:n, 0:512], op=ALU.subtract),
                     reads=[PB[pa], prB], writes=[B("xs_")])
                S.op("dve", lambda e: e.tensor_tensor(out=xs_[:n, 512:576], in0=PS[pb2][:n, 0:64], in1=pr[:n, 512:576], op=ALU.subtract),
                     reads=[PB[pb2], prB], pwrites=[B("xs_")])
                S.op("dve", lambda e: e.tensor_tensor(out=xs_[:n, :], in0=xs_[:n, :], in1=mu_bc[:n, :], op=ALU.mult),
                     reads=[B("xs_"), B("parb")], writes=[B("xs_")])
                S.op("dve", lambda e: e.tensor_tensor(out=xs_[:n, :], in0=xs_[:n, :], in1=pr[:n, :], op=ALU.add),
                     reads=[B("xs_"), prB], writes=[B("xs_")])
                xr_, xk, xv = xs_[:n, 0:128], xs_[:n, 128:256], xs_[:n, 256:384]
                yield
                S.op("act", lambda e: e.activation(out=Et[:n, 0:64], in_=xs_[:n, 384:448], func=AF.Exp, scale=-2.0),
                     reads=[B("xs_")], writes=[B("Et")])
                S.op("act", lambda e: e.activation(out=Et[:n, 64:128], in_=xs_[:n, 512:576], func=AF.Exp, scale=-1.0),
                     reads=[B("xs_")], pwrites=[B("Et")])
                S.op("dve", lambda e: e.tensor_scalar(out=Et[:n, :], in0=Et[:n, :], scalar1=1.0, scalar2=None, op0=ALU.add),
                     reads=[B("Et")], writes=[B("Et")])
                S.op("dve", lambda e: e.reciprocal(out=Et[:n, :], in_=Et[:n, :]), reads=[B("Et")], writes=[B("Et")])
                S.op("dve", lambda e: e.tensor_scalar(out=LT[:n, 0:64], in0=Et[:n, 0:64], scalar1=2.0, scalar2=-1.0, op0=ALU.mult, op1=ALU.add),
                     reads=[B("Et")], writes=[B("LT")])
                S.op("dve", lambda e: e.tensor_copy(out=LT[:n, 64:128], in_=xs_[:n, 448:512]), reads=[B("xs_")], pwrites=[B("LT")])
                S.op("dve", lambda e: e.tensor_copy(out=LT[:n, 128:192], in_=Et[:n, 64:128]), reads=[B("Et")], pwrites=[B("LT")])
                yield
                pb = gbank()
                for j3 in range(3):
                    S.op("pe", lambda e, j3=j3: e.transpose(out=PS[pb][0:64, j3 * 128:j3 * 128 + n], in_=LT[:n, j3 * 64:(j3 + 1) * 64], identity=ident[:n, :n]),
                         reads=[B("LT"), B("ident")], writes=[PB[pb]] if j3 == 0 else [], pwrites=[] if j3 == 0 else [PB[pb]], inc=(j3 == 2))
                S.op("act", lambda e: e.activation(out=LTT[:, :, 0:n], in_=PS[pb][0:64, 0:384].rearrange("p (a b) -> p a b", b=128)[:, :, 0:n], func=AF.Copy),
                     reads=[PB[pb]], writes=[B("LTT")])
                yield
                pl = gbank()
                S.op("pe", lambda e: e.matmul(PS[pl][:n, 0:128], lhsT=LTT[:, 0, 0:n], rhs=lwt[:, 0, :], start=True, stop=False, skip_group_check=True),
                     reads=[B("LTT"), B("lwt")], writes=[PB[pl]], inc=False)
                S.op("pe", lambda e: e.matmul(PS[pl][:n, 128:256], lhsT=LTT[:, 1, 0:n], rhs=lwt[:, 1, :], start=False, stop=False, skip_group_check=True),
                     reads=[B("LTT"), B("lwt")], pwrites=[PB[pl]], inc=False)
                S.op("pe", lambda e: e.matmul(PS[pl][:n, 256:384], lhsT=LTT[:, 2, 0:n], rhs=lg2t[:, :], start=False, stop=True, skip_group_check=True),
                     reads=[B("LTT"), B("lg2t")], pwrites=[PB[pl]])
                yield
                S.op("dve", lambda e: e.tensor_tensor(out=za[:n, :], in0=PS[pl][:n, 0:256], in1=w0a0_bc[:n, :], op=ALU.add),
                     reads=[PB[pl], B("parb")], writes=[B("za")])
                yield
                S.op("dve", lambda e: e.tensor_copy(out=g_sb[:n, :], in_=PS[pl][:n, 256:384]), reads=[PB[pl]], writes=[B("g_sb")])
                yield
                S.op("act", lambda e: e.activation(out=za[:n, :], in_=za[:n, :], func=AF.Exp, scale=-1.0), reads=[B("za")], writes=[B("za")])
                S.op("dve", lambda e: e.tensor_scalar(out=za[:n, :], in0=za[:n, :], scalar1=1.0, scalar2=None, op0=ALU.add),
                     reads=[B("za")], writes=[B("za")])
                yield
                S.op("dve", lambda e: e.reciprocal(out=sa[:n, :], in_=za[:n, :]), reads=[B("za")], writes=[B("sa")])
                yield
                S.op("dve", lambda e: e.tensor_scalar(out=ld[:n, :], in0=sa[:n, 0:128], scalar1=-math.exp(-0.5), scalar2=None, op0=ALU.mult),
                     reads=[B("sa")], writes=[B("ld")])
                av = sa[:n, 128:256]
                yield
                S.op("dve", lambda e: e.tensor_tensor(out=kkv[:n, :], in0=xk, in1=kk_bc[:n, :], op=ALU.mult), reads=[B("xs_"), B("parb")], writes=[B("kkv")])
                S.op("dve", lambda e: e.tensor_tensor(out=mm_[:n, :], in0=kkv[:n, :], in1=kkv[:n, :], op=ALU.mult), reads=[B("kkv")], writes=[B("mm_")])
                S.op("dve", lambda e: e.tensor_reduce(out=rst[:n, 0:2], in_=mm_[:n, :].rearrange("p (a b) -> p a b", b=64), axis=AX.X, op=ALU.add),
                     reads=[B("mm_")], writes=[B("rst")])
                S.op("dve", lambda e: e.tensor_scalar(out=rst[:n, 0:2], in0=rst[:n, 0:2], scalar1=1e-18, scalar2=None, op0=ALU.max),
                     reads=[B("rst")], writes=[B("rst")])
                S.op("act", lambda e: e.activation(out=rst[:n, 0:2], in_=rst[:n, 0:2], func=AF.Ln), reads=[B("rst")], writes=[B("rst")])
                S.op("act", lambda e: e.activation(out=rst[:n, 0:2], in_=rst[:n, 0:2], func=AF.Exp, scale=-0.5), reads=[B("rst")], writes=[B("rst")])
                k3 = kkv[:n, :].rearrange("p (a b) -> p a b", b=64)
                S.op("dve", lambda e: e.tensor_tensor(out=k3, in0=k3, in1=bc3(rst[:n, 0:2], n, 2, 64), op=ALU.mult),
                     reads=[B("kkv"), B("rst")], writes=[B("kkv")])
                S.op("dve", lambda e: e.scalar_tensor_tensor(out=mm_[:n, :], in0=av, scalar=-1.0, in1=ka_bc[:n, :], op0=ALU.add, op1=ALU.mult),
                     reads=[B("sa"), B("parb")], writes=[B("mm_")])
                S.op("dve", lambda e: e.scalar_tensor_tensor(out=k2[:n, :], in0=mm_[:n, :], scalar=1.0, in1=xk, op0=ALU.add, op1=ALU.mult),
                     reads=[B("mm_"), B("xs_")], writes=[B("k2")])
                S.op("dve", lambda e: e.tensor_tensor(out=bvec[:n, :], in0=kkv[:n, :], in1=av, op=ALU.mult), reads=[B("kkv"), B("sa")], writes=[B("bvec")])
                S.op("dve", lambda e: e.tensor_tensor(out=mm_[:n, :], in0=xr_, in1=k2[:n, :], op=ALU.mult), reads=[B("xs_"), B("k2")], writes=[B("mm_")])
                S.op("dve", lambda e: e.tensor_tensor(out=mm_[:n, :], in0=mm_[:n, :], in1=rk_bc[:n, :], op=ALU.mult), reads=[B("mm_"), B("parb")], writes=[B("mm_")])
                S.op("dve", lambda e: e.tensor_reduce(out=bs[:n, 0:2], in_=mm_[:n, :].rearrange("p (a b) -> p a b", b=64), axis=AX.X, op=ALU.add),
                     reads=[B("mm_")], writes=[B("bs")])
                yield
                pc = gbank()
                S.op("pe", lambda e: e.matmul(PS[pc][:n, 0:128], lhsT=ut[:n, :n], rhs=ld[:n, :], start=True, stop=False, skip_group_check=True),
                     reads=[B("ut"), B("ld")], writes=[PB[pc]], inc=False)
                S.op("pe", lambda e: e.matmul(PS[pc][:n, 128:256], lhsT=ones[:n, :n], rhs=ld[:n, :], start=False, stop=False, skip_group_check=True),
                     reads=[B("ones"), B("ld")], pwrites=[PB[pc]], inc=False)
                for hh in range(2):
                    S.op("pe", lambda e, hh=hh: e.matmul(PS[pc][0:64, 256 + hh:257 + hh], lhsT=ld[:n, hh * 64:(hh + 1) * 64], rhs=ones[:n, 0:1],
                                                         start=False, stop=(hh == 1), skip_group_check=True),
                         reads=[B("ones"), B("ld")], pwrites=[PB[pc]], inc=(hh == 1))
                S.op("act", lambda e: e.activation(out=cum[:n, :], in_=PS[pc][:n, 0:128], func=AF.Copy), reads=[PB[pc]], writes=[B("cum")])
                S.op("act", lambda e: e.activation(out=ec[:n, :], in_=PS[pc][:n, 0:128], func=AF.Exp), reads=[PB[pc]], writes=[B("ec")])
                S.op("act", lambda e: e.activation(out=eci[:n, :], in_=PS[pc][:n, 0:128], func=AF.Exp, scale=-1.0), reads=[PB[pc]], writes=[B("eci")])
                S.op("act", lambda e: e.activation(out=gC[:, 0:2], in_=PS[pc][0:64, 256:258], func=AF.Exp), reads=[PB[pc]], writes=[B("gC")])
                S.op("dve", lambda e: e.tensor_tensor(out=ee[:n, :], in0=cum[:n, :], in1=ld[:n, :], op=ALU.subtract), reads=[B("cum"), B("ld")], writes=[B("ee")])
                S.op("act", lambda e: e.activation(out=ee[:n, :], in_=ee[:n, :], func=AF.Exp), reads=[B("ee")], writes=[B("ee")])
                S.op("dve", lambda e: e.tensor_tensor(out=eh[:n, :], in0=PS[pc][:n, 128:256], in1=cum[:n, :], op=ALU.subtract), reads=[PB[pc], B("cum")], writes=[B("eh")])
                S.op("act", lambda e: e.activation(out=eh[:n, :], in_=eh[:n, :], func=AF.Exp), reads=[B("eh")], writes=[B("eh")])
                yield
                S.op("dve", lambda e: e.tensor_tensor(out=rt[:n, :], in0=xr_, in1=ec[:n, :], op=ALU.mult), reads=[B("xs_"), B("ec")], writes=[B("rt")])
                for hh in range(2):
                    S.op("dve", lambda e, hh=hh: e.scalar_tensor_tensor(out=RH[hh][:n, 0:64], in0=kkv[:n, hh * 64:(hh + 1) * 64], scalar=-1.0,
                                                                        in1=ee[:n, hh * 64:(hh + 1) * 64], op0=ALU.mult, op1=ALU.mult),
                         reads=[B("kkv"), B("ee")], writes=[B(RH[hh].name)])
                S.op("dve", lambda e: e.tensor_tensor(out=bt[:n, :], in0=bvec[:n, :], in1=eci[:n, :], op=ALU.mult), reads=[B("bvec"), B("eci")], writes=[B("bt")])
                S.op("dve", lambda e: e.tensor_tensor(out=ktl[:n, :], in0=k2[:n, :], in1=eci[:n, :], op=ALU.mult), reads=[B("k2"), B("eci")], writes=[B("ktl")])
                S.op("dve", lambda e: e.tensor_tensor(out=bh[:n, :], in0=bvec[:n, :], in1=eh[:n, :], op=ALU.mult), reads=[B("bvec"), B("eh")], writes=[B("bh")])
                S.op("dve", lambda e: e.tensor_tensor(out=kh[:n, :], in0=k2[:n, :], in1=eh[:n, :], op=ALU.mult), reads=[B("k2"), B("eh")], writes=[B("kh")])
                nlev = 7 if n == 128 else 6
                yield
                def head_gen(hh):
                    hs = slice(hh * 64, (hh + 1) * 64)
                    pf = gbank()
                    srcs = ((RH[hh][:n, 0:64], B(RH[hh].name)), (rt[:n, hs], B("rt")), (bt[:n, hs], B("bt")), (ktl[:n, hs], B("ktl")))
                    for i4, (sap, sB) in enumerate(srcs):
                        S.op("pe", lambda e, i4=i4, sap=sap: e.transpose(out=PS[pf][0:64, i4 * n:(i4 + 1) * n], in_=sap, identity=ident[:n, :n]),
                             reads=[sB, B("ident")], writes=[PB[pf]] if i4 == 0 else [], pwrites=[] if i4 == 0 else [PB[pf]], inc=(i4 == 3))
                    S.op("act", lambda e, hh=hh: e.activation(out=FT[hh][:, 0:4 * n], in_=PS[pf][0:64, 0:4 * n], func=AF.Copy),
                         reads=[PB[pf]], writes=[B(FT[hh].name)])
                    F_ = FT[hh]; FB = B(F_.name)
                    yield
                    pg = gbank()
                    S.op("pe", lambda e, F_=F_: e.matmul(PS[pg][:n, 0:2 * n], lhsT=F_[:, 2 * n:3 * n], rhs=F_[:, 0:2 * n], start=True, stop=False, skip_group_check=True),
                         reads=[FB], writes=[PB[pg]], inc=False)
                    S.op("pe", lambda e, F_=F_: e.matmul(PS[pg][:n, 2 * n:4 * n], lhsT=F_[:, 3 * n:4 * n], rhs=F_[:, 0:2 * n], start=False, stop=True, skip_group_check=True),
                         reads=[FB], pwrites=[PB[pg]])
                    G_ = GM[hh]; GB = B(G_.name)
                    S.op("dve", lambda e, G_=G_: e.tensor_tensor(out=G_[:n, 0:4 * n].rearrange("p (a b) -> p a b", b=n),
                                                                 in0=PS[pg][:n, 0:4 * n].rearrange("p (a b) -> p a b", b=n),
                                                                 in1=gmask[:n, :].rearrange("p (a b) -> p a b", b=128)[:, :, 0:n], op=ALU.mult),
                         reads=[PB[pg], B("gmask")], writes=[GB])
                    px = gbank()
                    S.op("pe", lambda e, F_=F_: e.matmul(PS[px][:n, 0:n], lhsT=F_[:, 0:n], rhs=F_[:, 2 * n:3 * n], start=True, stop=True, skip_group_check=True),
                         reads=[FB], writes=[PB[px]])
                    xa, xb_ = Xa[hh], Xb[hh]
                    S.op("dve", lambda e, xb_=xb_: e.tensor_tensor(out=xb_[0][:n, :n], in0=PS[px][:n, 0:n], in1=lowm[:n, :n], op=ALU.mult),
                         reads=[PB[px], B("lowm")], writes=[B(xb_[0].name)])
                    yield
                    acc = ACC[hh]
                    S.op("dve", lambda e, acc=acc, G_=G_: e.tensor_tensor(out=acc[0][:n, :n], in0=G_[:n, 0:n], in1=ident[:n, :n], op=ALU.add),
                         reads=[GB, B("ident")], writes=[B(acc[0].name)])
                    accb = ACCB[hh]
                    S.op("act", lambda e, acc=acc, accb=accb: e.activation(out=accb[0][:n, :n], in_=acc[0][:n, :n], func=AF.Copy),
                         reads=[B(acc[0].name)], writes=[B(accb[0].name)])
                    S.op("act", lambda e, G_=G_: e.activation(out=XG[hh][:n, :n], in_=G_[:n, 0:n], func=AF.Copy),
                         reads=[GB], writes=[B(XG[hh].name)])
                    curX_ap, curXB = XG[hh][:n, :n], B(XG[hh].name)
                    cs_ = 0
                    for lev in range(1, nlev):
                        curXp = xb_[cs_]
                        nxt = 1 - cs_
                        p2 = gbank()
                        lastlev = (lev == nlev - 1)
                        S.op("pe", lambda e, curX_ap=curX_ap, curXp=curXp: e.matmul(PS[p2][:n, 0:n], lhsT=curX_ap, rhs=curXp[:n, :n], start=True, stop=lastlev, skip_group_check=True),
                             reads=[curXB, B(curXp.name)], writes=[PB[p2]], inc=lastlev)
                        if not lastlev:
                            S.op("pe", lambda e, curX_ap=curX_ap, curXp=curXp: e.matmul(PS[p2][:n, 128:128 + n], lhsT=curXp[:n, :n], rhs=curX_ap, start=False, stop=True, skip_group_check=True),
                                 reads=[curXB, B(curXp.name)], pwrites=[PB[p2]])
                        S.op("act", lambda e, xb_=xb_, nxt=nxt: e.activation(out=xb_[nxt][:n, :n], in_=PS[p2][:n, 0:n], func=AF.Copy),
                             reads=[PB[p2]], writes=[B(xb_[nxt].name)])
                        if not lastlev:
                            S.op("dve", lambda e, xa=xa, nxt=nxt: e.tensor_copy(out=xa[nxt][:n, :n], in_=PS[p2][:n, 128:128 + n]),
                                 reads=[PB[p2]], writes=[B(xa[nxt].name)])
                        a_cur = acc[(lev - 1) % 2]; a_nxt = acc[lev % 2]
                        ab_cur = accb[(lev - 1) % 2]; ab_nxt = accb[lev % 2]
                        p3 = gbank()
                        S.op("pe", lambda e, xb_=xb_, nxt=nxt, ab_cur=ab_cur: e.matmul(PS[p3][:n, 0:n], lhsT=xb_[nxt][:n, :n], rhs=ab_cur[:n, :n], start=True, stop=True, skip_group_check=True),
                             reads=[B(xb_[nxt].name), B(ab_cur.name)], writes=[PB[p3]])
                        S.op("dve", lambda e, a_cur=a_cur, a_nxt=a_nxt: e.tensor_tensor(out=a_nxt[:n, :n], in0=PS[p3][:n, 0:n], in1=a_cur[:n, :n], op=ALU.add),
                             reads=[PB[p3], B(a_cur.name)], writes=[B(a_nxt.name)])
                        if not lastlev:
                            S.op("act", lambda e, a_nxt=a_nxt, ab_nxt=ab_nxt: e.activation(out=ab_nxt[:n, :n], in_=a_nxt[:n, :n], func=AF.Copy),
                                 reads=[B(a_nxt.name)], writes=[B(ab_nxt.name)])
                        curX_ap, curXB = xa[nxt][:n, :n], B(xa[nxt].name)
                        cs_ = nxt
                        yield
                    MT = acc[(nlev - 1) % 2]
                    yield
                    MTB = B(MT.name)
                    vh = xs_[:n, 256 + hh * 64:256 + (hh + 1) * 64]
                    pa_ = gbank()
                    S.op("pe", lambda e, G_=G_, vh=vh: e.matmul(PS[pa_][:n, 0:64], lhsT=G_[:n, 2 * n:3 * n], rhs=vh, start=True, stop=True, skip_group_check=True),
                         reads=[GB, B("xs_")], writes=[PB[pa_]])
                    S.op("act", lambda e, hh=hh: e.activation(out=RH[hh][:n, 64:128], in_=PS[pa_][:n, 0:64], func=AF.Copy),
                         reads=[PB[pa_]], pwrites=[B(RH[hh].name)])
                    ppu = gbank()
                    S.op("pe", lambda e, MT=MT, hh=hh: e.matmul(PS[ppu][:n, 0:128], lhsT=MT[:n, :n], rhs=RH[hh][:n, :], start=True, stop=True, skip_group_check=True),
                         reads=[MTB, B(RH[hh].name)], writes=[PB[ppu]])
                    S.op("act", lambda e, hh=hh: e.activation(out=PU[hh][:n, :], in_=PS[ppu][:n, 0:128], func=AF.Copy), reads=[PB[ppu]], writes=[B(PU[hh].name)])
                    PUB = B(PU[hh].name)
                    yield
                    py = gbank()
                    S.op("pe", lambda e, hh=hh, G_=G_: e.matmul(PS[py][:n, 128:192], lhsT=G_[:n, n:2 * n], rhs=PU[hh][:n, 64:128], start=True, stop=False, skip_group_check=True),
                         reads=[PUB, GB], writes=[PB[py]], inc=False)
                    S.op("pe", lambda e, hh=hh, G_=G_, vh=vh: e.matmul(PS[py][:n, 128:192], lhsT=G_[:n, 3 * n:4 * n], rhs=vh, start=False, stop=False, skip_group_check=True),
                         reads=[GB, B("xs_")], pwrites=[PB[py]], inc=False)
                    S.op("pe", lambda e, hh=hh, G_=G_: e.matmul(PS[py][0:64, 0:n], lhsT=PU[hh][:n, 0:64], rhs=G_[:n, n:2 * n], start=False, stop=False, skip_group_check=True),
                         reads=[PUB, GB], pwrites=[PB[py]], inc=False)
                    S.op("pe", lambda e, hh=hh, hs=hs: e.matmul(PS[py][0:64, 192:256], lhsT=PU[hh][:n, 0:64], rhs=bh[:n, hs], start=False, stop=False, skip_group_check=True),
                         reads=[PUB, B("bh")], pwrites=[PB[py]], inc=False)
                    S.op("pe", lambda e, hh=hh, hs=hs: e.matmul(PS[py][0:64, 256:320], lhsT=bh[:n, hs], rhs=PU[hh][:n, 64:128], start=False, stop=False, skip_group_check=True),
                         reads=[PUB, B("bh")], pwrites=[PB[py]], inc=False)
                    S.op("pe", lambda e, hh=hh, hs=hs, vh=vh: e.matmul(PS[py][0:64, 256:320], lhsT=kh[:n, hs], rhs=vh, start=False, stop=True, skip_group_check=True),
                         reads=[B("kh"), B("xs_")], pwrites=[PB[py]])
                    S.op("dve", lambda e, hh=hh, F_=F_: e.tensor_tensor(out=Y1T[hh][:, 0:n], in0=PS[py][0:64, 0:n], in1=F_[:, n:2 * n], op=ALU.add),
                         reads=[PB[py], FB], writes=[B(Y1T[hh].name)])
                    S.op("act", lambda e, hh=hh: e.activation(out=Y2[hh][:n, :], in_=PS[py][:n, 128:192], func=AF.Copy), reads=[PB[py]], writes=[B(Y2[hh].name)])
                    S.op("dve", lambda e, hh=hh: e.scalar_tensor_tensor(out=T1T[hh][:, :], in0=ident[0:64, 0:64], scalar=gC[:, hh:hh + 1], in1=PS[py][0:64, 192:256],
                                                                        op0=ALU.mult, op1=ALU.add),
                         reads=[PB[py], B("ident"), B("gC")], writes=[B(T1T[hh].name)])
                    S.op("act", lambda e, hh=hh: e.activation(out=T2[hh][:, :], in_=PS[py][0:64, 256:320], func=AF.Copy), reads=[PB[py]], writes=[B(T2[hh].name)])
                    yield
                    Hc = Hs[hh][hslot]; Hn = Hs[hh][1 - hslot]
                    ph = gbank()
                    S.op("pe", lambda e, hh=hh, Hc=Hc: e.matmul(PS[ph][:n, 0:64], lhsT=Y1T[hh][:, 0:n], rhs=Hc[:, :], start=True, stop=False, skip_group_check=True),
                         reads=[B(Y1T[hh].name), B(Hc.name)], writes=[PB[ph]], inc=False)
                    S.op("pe", lambda e, hh=hh, Hc=Hc: e.matmul(PS[ph][0:64, 64:128], lhsT=T1T[hh][:, :], rhs=Hc[:, :], start=False, stop=True, skip_group_check=True),
                         reads=[B(T1T[hh].name), B(Hc.name)], pwrites=[PB[ph]])
                    S.op("dve", lambda e, hh=hh, hs=hs: e.tensor_tensor(out=yb[:n, hs], in0=PS[ph][:n, 0:64], in1=Y2[hh][:n, :], op=ALU.add),
                         reads=[PB[ph], B(Y2[hh].name)], writes=[B("yb")] if hh == 0 else [], pwrites=[] if hh == 0 else [B("yb")])
                    S.op("dve", lambda e, hh=hh, Hn=Hn: e.tensor_tensor(out=Hn[:, :], in0=PS[ph][0:64, 64:128], in1=T2[hh][:, :], op=ALU.add),
                         reads=[PB[ph], B(T2[hh].name)], writes=[B(Hn.name)])
                gens = [head_gen(0), head_gen(1)]
                while gens:
                    for g_ in list(gens):
                        try:
                            next(g_)
                        except StopIteration:
                            gens.remove(g_)
                    yield
                yield
                y3 = yb[:n, :].rearrange("p (a b) -> p a b", b=64)
                yc3 = yc[:n, :].rearrange("p (a b) -> p a b", b=64)
                S.op("dve", lambda e: e.tensor_reduce(out=rst[:n, 2:4], in_=y3, axis=AX.X, op=ALU.add), reads=[B("yb")], writes=[B("rst")])
                S.op("dve", lambda e: e.tensor_scalar(out=rst[:n, 2:4], in0=rst[:n, 2:4], scalar1=-1.0 / 64, scalar2=None, op0=ALU.mult), reads=[B("rst")], writes=[B("rst")])
                S.op("dve", lambda e: e.tensor_tensor(out=yc3, in0=y3, in1=bc3(rst[:n, 2:4], n, 2, 64), op=ALU.add), reads=[B("yb"), B("rst")], writes=[B("yc")])
                S.op("dve", lambda e: e.tensor_tensor(out=ysq[:n, :], in0=yc[:n, :], in1=yc[:n, :], op=ALU.mult), reads=[B("yc")], writes=[B("ysq")])
                S.op("dve", lambda e: e.tensor_reduce(out=rst[:n, 4:6], in_=ysq[:n, :].rearrange("p (a b) -> p a b", b=64), axis=AX.X, op=ALU.add),
                     reads=[B("ysq")], writes=[B("rst")])
                rstd_chain(rst[:n, 4:6], rst[:n, 4:6], n, 2, 1.0 / 64, GN_EPS, B("rst"), B("rst"))
                S.op("dve", lambda e: e.tensor_tensor(out=yc3, in0=yc3, in1=bc3(rst[:n, 4:6], n, 2, 64), op=ALU.mult), reads=[B("yc"), B("rst")], writes=[B("yc")])
                S.op("dve", lambda e: e.tensor_tensor(out=yc[:n, :], in0=yc[:n, :], in1=lg_bc[:n, :], op=ALU.mult), reads=[B("yc"), B("parb")], writes=[B("yc")])
                S.op("dve", lambda e: e.tensor_tensor(out=yc[:n, :], in0=yc[:n, :], in1=lb_bc[:n, :], op=ALU.add), reads=[B("yc"), B("parb")], writes=[B("yc")])
                S.op("dve", lambda e: e.tensor_tensor(out=ysq[:n, :].rearrange("p (a b) -> p a b", b=64), in0=xv.rearrange("p (a b) -> p a b", b=64),
                                                      in1=bc3(bs[:n, 0:2], n, 2, 64), op=ALU.mult), reads=[B("xs_"), B("bs")], writes=[B("ysq")])
                S.op("dve", lambda e: e.tensor_tensor(out=yc[:n, :], in0=yc[:n, :], in1=ysq[:n, :], op=ALU.add), reads=[B("yc"), B("ysq")], writes=[B("yc")])
                S.op("dve", lambda e: e.tensor_tensor(out=obb[:n, :], in0=yc[:n, :], in1=g_sb[:n, :], op=ALU.mult), reads=[B("yc"), B("g_sb")], writes=[B("obb")])
                pb = gbank()
                pv = PS[pb][:].bitcast(BF16)
                S.op("pe", lambda e: e.transpose(out=pv[:, 0:n], in_=obb[:n, :], identity=identb[:n, :n]), reads=[B("obb"), B("identb")], writes=[PB[pb]])
                cat_writer(1, tile_idx, pv[:, 0:n], PB[pb])
                yield

            def state_out(hslot, dst_ap):
                pb = gbank()
                for hh in range(2):
                    Hc = Hs[hh][hslot]
                    S.op("pe", lambda e, hh=hh, Hc=Hc: e.transpose(out=PS[pb][0:64, hh * 64:(hh + 1) * 64], in_=Hc[:, :], identity=ident[0:64, 0:64]),
                         reads=[B(Hc.name), B("ident")], writes=[PB[pb]] if hh == 0 else [], pwrites=[] if hh == 0 else [PB[pb]], inc=(hh == 1))
                S.op("act", lambda e: e.activation(out=Hout[:, :, :], in_=PS[pb][0:64, 0:128].rearrange("p (a b) -> p a b", b=64), func=AF.Copy),
                     reads=[PB[pb]], writes=[B("Hout")])
                S.dma("sp", dst_ap.rearrange("h v k -> v h k"), Hout[:, :, :], reads=[B("Hout")])

            def run(g):
                for _ in g:
                    pass

            def drive(main, side, ratio):
                acc_ = 0.0
                for _ in main:
                    drive.n += 1
                    assert not any(S.uncommitted.values()), "yield inside an uncommitted group"
                    if side is not None:
                        acc_ += ratio
                        while acc_ >= 1.0 and side is not None:
                            acc_ -= 1.0
                            try:
                                next(side)
                            except StopIteration:
                                side = None
                if side is not None:
                    run(side)

            drive.n = 0
            XIv = XI.rearrange("(r c two p) t -> r c two p t", r=4, c=8, two=2, p=128)

            def make_writer(hp):
                def w_(kind, ti, src_ap, srcB):
                    rng_ = (ti * 128) // TR
                    toff_ = ti * 128 - rng_ * TR
                    ct = catA if kind == 0 else catB
                    cB = B(ct.name)
                    S.op("dve", lambda e: e.tensor_tensor(out=ct[:, :, :], in0=src_ap.unsqueeze(1).to_broadcast([128, 4, 128]),
                                                          in1=selt[:, 0:4].unsqueeze(2).to_broadcast([128, 4, 128]), op=ALU.mult),
                         reads=[srcB, B("selt")], writes=[cB])
                    c0 = 4 if kind == 1 else 0
                    S.dma("sp", XIv[rng_, c0:c0 + 4, hp, :, toff_:toff_ + 128].rearrange("j p t -> p j t"), ct[:, :, :],
                          reads=[cB], pwrites=[B("XI")])
                return w_

            def stage1(hp, i):
                slot = i % 2
                qb = (i // 2) % 2
                S.dma("sp", cstt[slot][:, :], csp[i * 128:(i + 1) * 128, :], writes=[B(cstt[slot].name)])
                yield from front(xb[i * 128:(i + 1) * 128, :], 128, xnTall, slot * 128, XNB[slot], g1t, slot)
                yield from project(xnTall, slot * 128, XNB[slot], 128, slot)
                yield from attn_prep(128, cstt[slot][:, :], B(cstt[slot].name), kpo[i * 128:(i + 1) * 128, hp, :], vpo[i * 128:(i + 1) * 128, hp, :],
                                     i, i * 128, (i % 2) * 128, QTs[qb], B(QTs[qb].name))

            def stage2(hp, i, writer):
                slot = i % 2
                prev_ap = None if i == 0 else prw[1 - slot]
                yield from rwkv(128, slot, prev_ap, None if i == 0 else B(prw[1 - slot].name), c128[:, :], i == 0, i % 2, writer, i)

            def interleave(g1, g2):
                gens = [g for g in (g1, g2) if g is not None]
                while gens:
                    for g_ in list(gens):
                        try:
                            next(g_)
                        except StopIteration:
                            gens.remove(g_)
                    yield

            def drive_keep(main, side, ratio):
                acc_ = 0.0
                for _ in main:
                    drive.n += 1
                    assert not any(S.uncommitted.values()), "yield inside an uncommitted group"
                    if side is not None:
                        acc_ += ratio
                        while acc_ >= 1.0 and side is not None:
                            acc_ -= 1.0
                            try:
                                next(side)
                            except StopIteration:
                                side = None
                return side

            def attn_stream(hp, Q, writer):
                kts = [(j * 128, 128, j, None) for j in range(2 * Q)]
                kts.append((2 * Q * 128, 128, 2 * Q, 0))
                kts.append(((2 * Q + 1) * 128, 128, 2 * Q + 1, 1))
                qb = Q % 2
                yield from attention(256, 128, kts, writer, QTs[qb], B(QTs[qb].name), 2 * Q)

            SEG_YIELDS = 25.0
            for hp in range(2):
                wsrc_name[0] = "wmyb%d" % hp
                load_unit_w(wmyb[hp])
                load_unit_params(upmy[hp:hp + 1, :], lwmy[hp], lg2my[hp])
                for hh in range(2):
                    S.op("dve", lambda e, hh=hh: e.memset(Hs[hh][0][:, :], 0.0), writes=[B(Hs[hh][0].name)])
                writer = make_writer(hp)
                run(stage1(hp, 0))
                side = None
                ratio = 0.0
                for i in range(NT):
                    if dcast[0] is not None and (i >= 4 or NT <= 8):
                        try:
                            for _ in range(12 if NT > 8 else 1000000):
                                next(dcast[0])
                        except StopIteration:
                            dcast[0] = None
                    if i % 2 == 1:
                        if side is not None:
                            run(side)
                        Q = i // 2
                        side = attn_stream(hp, Q, writer)
                        ratio = (2 * Q + 6) / (2 * SEG_YIELDS)
                    seg = interleave(stage2(hp, i, writer), stage1(hp, i + 1) if i + 1 < NT else None)
                    side = drive_keep(seg, side, ratio)
                if side is not None:
                    run(side)
                if hp == 1 and dcast[0] is not None:
                    run(dcast[0])
                    dcast[0] = None
                build_nc.main_yields = drive.n / float(NT) / (hp + 1)
                state_out(NT % 2, spo[hp])
                lastp = prw[(NT - 1) % 2]
                S.dma("sp", shpo[hp:hp + 1, :], lastp[127:128, :], reads=[B(lastp.name)])

            S.custom("pool", lambda e: e.collective_compute("ReduceScatter", ALU.add, replica_groups=[[0, 1, 2, 3], [4, 5, 6, 7]],
                                                             ins=[XI[:, :]], outs=[XO[:, :]]),
                     reads=[B("XI")], writes=[B("XO")])
            ckpt("rs")

            ckpt("casts")

            ntile_s = (NB * 64 + 127) // 128
            for t_ in range(ntile_s):
                n_ = min(128, NB * 64 - t_ * 128)
                run(front(xsm[t_ * 128:t_ * 128 + n_, :], n_, xnTall, t_ * 128, XNB[t_], g1t, t_ % 2))
            kst = sb("kst", [128, 128]); kstb = sb("kstb", [128, 128], BF16)
            kst2 = sb("kst2", [128, 128])

            def make_writer_s(u, bb):
                def cat_writer_s(kind, qt_, src_ap, srcB):
                    chunk = u + (8 if kind == 1 else 0)
                    ct = catA if kind == 0 else catB
                    cB = B(ct.name)
                    S.op("act", lambda e: e.activation(out=ct[:, 0, 0:64], in_=src_ap, func=AF.Copy), reads=[srcB], writes=[cB])
                    S.dma("sp", XS[chunk * 128:(chunk + 1) * 128, bb * 64:(bb + 1) * 64], ct[:, 0, 0:64], reads=[cB], pwrites=[B("XS")])
                return cat_writer_s

            def sample_pre(u, bb, idx):
                base = (idx % 2) * (NP + 1)
                qb = idx % 2
                for pt in range(NP):
                    kt_i = base + pt
                    S.dma("sp", kst[:, :], ck[bb, pt * 128:(pt + 1) * 128, u, :], writes=[B("kst")])
                    S.op("dve", lambda e: e.tensor_copy(out=kstb[:, :], in_=kst[:, :]), reads=[B("kst")], writes=[B("kstb")])
                    pb = gbank()
                    pv = PS[pb][:].bitcast(BF16)
                    S.op("pe", lambda e, pv=pv: e.transpose(out=pv[:, 0:128], in_=kstb[:, :], identity=identb[:, :]),
                         reads=[B("kstb"), B("identb")], writes=[PB[pb]])
                    S.op("act", lambda e, pv=pv: e.activation(out=KT[:, kt_i * 128:(kt_i + 1) * 128], in_=pv[:, 0:128], func=AF.Copy),
                         reads=[PB[pb]], writes=[KTB[kt_i]])
                    S.dma("sp", kst2[:, :], cv[bb, pt * 128:(pt + 1) * 128, u, :], writes=[B("kst2")])
                    S.op("dve", lambda e: e.tensor_copy(out=VV[:, kt_i, :], in_=kst2[:, :]), reads=[B("kst2")], writes=[VB[kt_i]])
                    if pt % 2 == 1:
                        yield
                S.dma("sp", Hld[:, :, :], st0[bb, 2 * u:2 * u + 2].rearrange("h v k -> v h k"), writes=[B("Hld")])
                pb = gbank()
                for hh in range(2):
                    S.op("pe", lambda e, hh=hh: e.transpose(out=PS[pb][0:64, hh * 64:(hh + 1) * 64], in_=Hld[:, hh, :], identity=ident[0:64, 0:64]),
                         reads=[B("Hld"), B("ident")], writes=[PB[pb]] if hh == 0 else [], pwrites=[] if hh == 0 else [PB[pb]], inc=(hh == 1))
                for hh in range(2):
                    S.op("act", lambda e, hh=hh: e.activation(out=Hs[hh][0][:, :], in_=PS[pb][0:64, hh * 64:(hh + 1) * 64], func=AF.Copy),
                         reads=[PB[pb]], writes=[B(Hs[hh][0].name)])
                S.op("dve", lambda e: e.memset(shbuf[0:63, :], 0.0), writes=[B("shbuf")])
                S.dma("sp", shbuf[63:64, :], sh0[bb, u:u + 1, :], reads=[], writes=[], pwrites=[B("shbuf")])
                yield
                yield from project(xnTall, bb * 64, XNB[bb // 2], 64, 0)
                yield from attn_prep(64, csst[:, :], B("csst"), kso[bb, :, u, :], vso[bb, :, u, :], base + NP, (base + NP) * 128, 0,
                                     QTs[qb], B(QTs[qb].name))
                yield from rwkv(64, 0, shbuf[0:64, :], B("shbuf"), c64[:, :], False, 0, make_writer_s(u, bb), 0)
                state_out(1, sso[bb, u])
                S.dma("sp", shso[bb, u:u + 1, :], prw[0][63:64, :], reads=[B(prw[0].name)])
                yield

            def sample_attn(u, bb, idx):
                base = (idx % 2) * (NP + 1)
                qb = idx % 2
                kts = [((base + j) * 128, 128, base + j, None) for j in range(NP)] + [((base + NP) * 128, 64, base + NP, None)]
                yield from attention(64, 64, kts, make_writer_s(u, bb), QTs[qb], B(QTs[qb].name), 0)

            side_s = None
            for u in range(8):
                wsrc_name[0] = "wub%d" % u
                load_unit_w(wub[u])
                load_unit_params(up[u:u + 1, :], lw[u], lg2[u])
                for bb in range(NB):
                    idx = u * NB + bb
                    side_s = drive_keep(sample_pre(u, bb, idx), side_s, (NP + 3) / 40.0)
                    if side_s is not None:
                        run(side_s)
                    side_s = sample_attn(u, bb, idx)
            if side_s is not None:
                run(side_s)

            ckpt("sample")
            S.barrier()
            es1.close()
            cur[0] = es
            NTOK = 512
            catT = sb("catT", [128, NCH, NTOK], BF16)
            hsb = sb("hsb", [128, NTOK // 128, D])
            hnT = sb("hnT", [128, NCH, NTOK], BF16)
            actT = sb("actT", [128, NFF, NTOK], BF16)
            wring = [sb("wring%d" % i, [128, 16 * 512], BF16) for i in range(3)]
            wrr = [0]
            sg = sb("sg", [128, 512])

            def wbuf():
                i = wrr[0]
                wrr[0] = (wrr[0] + 1) % 3
                return wring[i]

            blocks = []
            t0 = 0
            while t0 < TR:
                nb_ = min(NTOK, TR - t0)
                blocks.append(("p", t0, nb_))
                t0 += nb_
            t0 = 0
            while t0 < NB * 64:
                nb_ = min(NTOK, NB * 64 - t0)
                blocks.append(("s", t0, nb_))
                t0 += nb_
            for (kind, t0, nb_) in blocks:
                src = XO if kind == "p" else XS
                srcB = B("XO") if kind == "p" else B("XS")
                xsrc = xr if kind == "p" else xsm
                ydst = yp if kind == "p" else ys
                S.dma("sp", catT[:, :, 0:nb_], src[:, t0:t0 + nb_].rearrange("(c p) n -> p c n", p=128), reads=[srcB], writes=[B("catT")])
                ntl = (nb_ + 127) // 128
                tls = [(tt * 128, min(128, nb_ - tt * 128)) for tt in range(ntl)]
                for tt, (o_, n_) in enumerate(tls):
                    S.dma("sp", hsb[:n_, tt, :], xsrc[t0 + o_:t0 + o_ + n_, :], writes=[B("hsb%d" % tt)])
                for ng in range(4):
                    wb = wbuf(); wB = B(wb.name)
                    w3 = wb[:, :].rearrange("p (c n) -> p c n", n=512)
                    S.dma("sp", w3, wob[:, ng * 512:(ng + 1) * 512].rearrange("(c p) n -> p c n", p=128), reads=[B("wob")], writes=[wB])
                    for tt, (o_, n_) in enumerate(tls):
                        pb = tt % 4
                        for k in range(NCH):
                            S.op("pe", lambda e, k=k, pb=pb, o_=o_, n_=n_: e.matmul(PS[pb][:n_, 0:512], lhsT=catT[:, k, o_:o_ + n_], rhs=w3[:, k, :],
                                                                                  start=(k == 0), stop=(k == NCH - 1), skip_group_check=True),
                                 reads=[B("catT"), wB], writes=[PB[pb]] if k == 0 else [], pwrites=[] if k == 0 else [PB[pb]], inc=(k == NCH - 1))
                        S.op("dve", lambda e, tt=tt, pb=pb, n_=n_, ng=ng: e.tensor_tensor(out=hsb[:n_, tt, ng * 512:(ng + 1) * 512], in0=PS[pb][:n_, 0:512],
                                                                                         in1=hsb[:n_, tt, ng * 512:(ng + 1) * 512], op=ALU.add),
                             reads=[PB[pb], B("hsb%d" % tt)], writes=[B("hsb%d" % tt)])
                for tt, (o_, n_) in enumerate(tls):
                    hB = B("hsb%d" % tt)
                    S.op("dve", lambda e, tt=tt, n_=n_: e.scalar_tensor_tensor(out=junk[:n_, :], in0=hsb[:n_, tt, :], scalar=1.0, in1=hsb[:n_, tt, :], op0=ALU.mult, op1=ALU.mult, accum_out=st4[:n_, 0:1]),
                         reads=[hB], writes=[B("junk"), B("st4")])
                    rstd_chain(st4[:n_, 0:1], st4[:n_, 0:1], n_, 1, 1.0 / D, RMS_EPS, B("st4"), B("st4"))
                    S.op("act", lambda e, tt=tt, n_=n_: e.activation(out=xsb[:n_, :], in_=hsb[:n_, tt, :], func=AF.Copy, scale=st4[:n_, 0:1]),
                         reads=[hB, B("st4")], writes=[B("xsb")])
                    for half in range(2):
                        pb = 4 + half
                        pv = PS[pb][:].bitcast(BF16)
                        for c8 in range(8):
                            c = half * 8 + c8
                            S.op("pe", lambda e, c=c, c8=c8, pv=pv, n_=n_: e.transpose(out=pv[:, c8 * n_:(c8 + 1) * n_], in_=xsb[:n_, c * 128:(c + 1) * 128],
                                                                                     identity=identb[:n_, :n_]),
                                 reads=[B("xsb"), B("identb")], writes=[PB[pb]] if c8 == 0 else [], pwrites=[] if c8 == 0 else [PB[pb]], inc=(c8 == 7))
                        S.op("dve", lambda e, half=half, pv=pv, n_=n_, o_=o_: e.tensor_tensor(
                            out=hnT[:, half * 8:(half + 1) * 8, o_:o_ + n_], in0=pv[:, 0:8 * n_].rearrange("p (c n) -> p c n", n=n_),
                            in1=bc3(g2t[:, half * 8:(half + 1) * 8], 128, 8, n_), op=ALU.mult),
                            reads=[PB[pb], B("g2t")], writes=[B("hnT")] if (tt == 0 and half == 0) else [], pwrites=[] if (tt == 0 and half == 0) else [B("hnT")])
                for fg in range(NFF // 4):
                    wg_ = wbuf(); wgB = B(wg_.name)
                    wg3 = wg_[:, :].rearrange("p (c n) -> p c n", n=512)
                    S.dma("sp", wg3, wgb[:, fg * 512:(fg + 1) * 512].rearrange("(c p) n -> p c n", p=128), reads=[B("wgb")], writes=[wgB])
                    wu_ = wbuf(); wuB = B(wu_.name)
                    wu3 = wu_[:, :].rearrange("p (c n) -> p c n", n=512)
                    S.dma("sp", wu3, wupb[:, fg * 512:(fg + 1) * 512].rearrange("(c p) n -> p c n", p=128), reads=[B("wupb")], writes=[wuB])
                    for f4 in range(4):
                        f = fg * 4 + f4
                        pg_, pu_ = (f % 2) * 2, (f % 2) * 2 + 1
                        for k in range(NCH):
                            S.op("pe", lambda e, k=k, pg_=pg_, f4=f4: e.matmul(PS[pg_][:, 0:nb_], lhsT=wg3[:, k, f4 * 128:(f4 + 1) * 128], rhs=hnT[:, k, 0:nb_],
                                                                             start=(k == 0), stop=(k == NCH - 1), skip_group_check=True),
                                 reads=[B("hnT"), wgB], writes=[PB[pg_]] if k == 0 else [], pwrites=[] if k == 0 else [PB[pg_]], inc=(k == NCH - 1))
                        for k in range(NCH):
                            S.op("pe", lambda e, k=k, pu_=pu_, f4=f4: e.matmul(PS[pu_][:, 0:nb_], lhsT=wu3[:, k, f4 * 128:(f4 + 1) * 128], rhs=hnT[:, k, 0:nb_],
                                                                             start=(k == 0), stop=(k == NCH - 1), skip_group_check=True),
                                 reads=[B("hnT"), wuB], writes=[PB[pu_]] if k == 0 else [], pwrites=[] if k == 0 else [PB[pu_]], inc=(k == NCH - 1))
                        S.op("act", lambda e, pg_=pg_: e.activation(out=sg[:, 0:nb_], in_=PS[pg_][:, 0:nb_], func=AF.Silu), reads=[PB[pg_]], writes=[B("sg")])
                        S.op("dve", lambda e, pu_=pu_, f=f: e.tensor_tensor(out=actT[:, f, 0:nb_], in0=PS[pu_][:, 0:nb_], in1=sg[:, 0:nb_], op=ALU.mult),
                             reads=[PB[pu_], B("sg")], writes=[B("actT")] if f == 0 else [], pwrites=[] if f == 0 else [B("actT")])
                for ng in range(4):
                    for kq in range(4):
                        wb = wbuf(); wB = B(wb.name)
                        w3 = wb[:, 0:11 * 512].rearrange("p (c n) -> p c n", n=512)
                        S.dma("sp", w3, wdb[kq * 11 * 128:(kq + 1) * 11 * 128, ng * 512:(ng + 1) * 512].rearrange("(c p) n -> p c n", p=128),
                              reads=[B("wdb")], writes=[wB])
                        for tt, (o_, n_) in enumerate(tls):
                            pb = 4 + tt
                            for kc in range(11):
                                kk_ = kq * 11 + kc
                                first = (kk_ == 0); last = (kk_ == NFF - 1)
                                S.op("pe", lambda e, kc=kc, kk_=kk_, pb=pb, o_=o_, n_=n_, first=first, last=last: e.matmul(
                                    PS[pb][:n_, 0:512], lhsT=actT[:, kk_, o_:o_ + n_], rhs=w3[:, kc, :], start=first, stop=last, skip_group_check=True),
                                     reads=[B("actT"), wB], writes=[PB[pb]] if first else [], pwrites=[] if first else [PB[pb]], inc=(kc == 10))
                    for tt, (o_, n_) in enumerate(tls):
                        pb = 4 + tt
                        S.op("dve", lambda e, tt=tt, pb=pb, n_=n_, ng=ng: e.tensor_tensor(out=hsb[:n_, tt, ng * 512:(ng + 1) * 512], in0=PS[pb][:n_, 0:512],
                                                                                         in1=hsb[:n_, tt, ng * 512:(ng + 1) * 512], op=ALU.add),
                             reads=[PB[pb], B("hsb%d" % tt)], writes=[B("hsb%d" % tt)])
                for tt, (o_, n_) in enumerate(tls):
                    S.dma("sp", ydst[t0 + o_:t0 + o_ + n_, :], hsb[:n_, tt, :], reads=[B("hsb%d" % tt)])

        except _Stop:
            es1.close()
        for s, c in S.all_dma():
            nc.sync.wait_ge(s, c)
        for e in ("pe", "act", "dve", "pool"):
            if S.cnt[e] > 0:
                nc.sync.wait_ge(S.esem[e], S.cnt[e])
        build_nc.nops = S.nops
    return nc


def _unit_cols(u):
    A = 3072
    q = np.arange(128) + u * 128
    k = 1024 + q
    v = 2048 + q
    r = A + u * 128 + np.arange(128)
    kr = A + 1024 + u * 128 + np.arange(128)
    vr = A + 2048 + u * 128 + np.arange(128)
    lo = A + 3072 + np.arange(192)
    return np.concatenate([q, k, v, r, kr, vr, lo])


def _rw_cols(u):
    r = u * 128 + np.arange(128)
    return np.concatenate([r, 1024 + r, 2048 + r, 3072 + np.arange(192)])


def _consts(T):
    c = {}
    idx = np.arange(128)
    c["cid"] = np.eye(128, dtype=np.float32)
    c["cut"] = (idx[:, None] <= idx[None, :]).astype(np.float32)
    c["cs1"] = (idx[:, None] + 1 == idx[None, :]).astype(np.float32)
    cc = np.zeros((128, 128), np.float32); cc[127, 0] = 1.0
    c["cc128"] = cc
    cc = np.zeros((64, 64), np.float32); cc[63, 0] = 1.0
    c["cc64"] = cc
    mA = (idx[:, None] < idx[None, :]).astype(np.float32)
    mL = (idx[:, None] <= idx[None, :]).astype(np.float32)
    c["cgm"] = np.concatenate([mA, mL, mA, mL], axis=1)
    c["clow"] = (idx[None, :] < idx[:, None]).astype(np.float32)
    cm = np.ones((128, 128), np.float32); cm[64:, :64] = 0.0
    on = np.ones((128, 128), np.float32); ze = np.zeros((128, 128), np.float32)
    d0 = np.concatenate([cm, on], axis=1); d1 = np.concatenate([ze, cm], axis=1)
    c["cam"] = np.stack([np.concatenate([d0, d0], axis=1), np.concatenate([d1, d1], axis=1)]).astype(np.float32)
    inv = (np.float32(ROPE_THETA) ** (-np.arange(0, 16, 2, dtype=np.float32) / np.float32(16))).astype(np.float32)

    def tab(pos):
        ang = (pos.astype(np.float32)[:, None] * inv[None, :]).astype(np.float32)
        return np.concatenate([np.cos(ang), np.sin(ang)], axis=1).astype(np.float32)
    c["csp"] = tab(np.arange(T))
    return c, tab


_NC_CACHE = {}


def kernel(x_prompt, x_sample, cache_attn_k, cache_attn_v, state_rwkv, state_rwkv_shift,
           norm1_g, w_in, q_norm_g, k_norm_g, lambda_q1, lambda_k1, lambda_q2, lambda_k2, subln_g,
           mu_rwkv, w0, w2, a0, a2, g2, k_k, k_a, r_k, lnx_g, lnx_b,
           w_out, norm2_g, w_gate, w_up, w_down):
    f = lambda a: np.ascontiguousarray(np.asarray(a, dtype=np.float32))
    x_prompt, x_sample = f(x_prompt), f(x_sample)
    Bp, T, _ = x_prompt.shape
    Bs, Ts, _ = x_sample.shape
    PAST = cache_attn_k.shape[2]
    NB = Bs // 8
    assert Bp == 2 and Ts == 64
    dm = Dims(T, NB, PAST)
    key = (T, NB, PAST)
    if key not in _NC_CACHE:
        _NC_CACHE[key] = build_nc(dm)
    nc = _NC_CACHE[key]
    TR = T // 4
    w_in0 = f(w_in)[0]
    wu = np.stack([w_in0[:, _unit_cols(u)] for u in range(8)])
    mu0 = f(mu_rwkv)[0]; w00 = f(w0)[0]; a00 = f(a0)[0]; kk0 = f(k_k)[0]; ka0 = f(k_a)[0]
    rk0 = f(r_k)[0].reshape(-1); lg0 = f(lnx_g)[0]; lb0 = f(lnx_b)[0]
    w20 = f(w2)[0]; a20 = f(a2)[0]; g20 = f(g2)[0]
    ups, lws, lg2s = [], [], []
    for u in range(8):
        hs = slice(u * 128, (u + 1) * 128)
        ups.append(np.concatenate([mu0[_rw_cols(u)], w00[hs], a00[hs], kk0[hs], ka0[hs], rk0[hs], lg0[hs], lb0[hs]]))
        lws.append(np.concatenate([w20[:, hs], a20[:, hs]], axis=0))
        lg2s.append(g20[:, hs])
    up = np.stack(ups).astype(np.float32); lw = np.stack(lws).astype(np.float32); lg2_ = np.stack(lg2s).astype(np.float32)
    consts, tab = _consts(T)
    consts["css"] = tab(PAST + np.arange(64))
    common = dict(
        wu=wu, up=up, lw=lw, lg2=lg2_,
        g1T=np.ascontiguousarray(f(norm1_g)[0].reshape(16, 128).T), g2T=np.ascontiguousarray(f(norm2_g)[0].reshape(16, 128).T),
        qkg=np.concatenate([f(q_norm_g)[0].reshape(-1), f(k_norm_g)[0].reshape(-1)])[None, :],
        lamv=np.concatenate([f(lambda_q1)[0], f(lambda_k1)[0], f(lambda_q2)[0], f(lambda_k2)[0]])[None, :],
        subg=f(subln_g)[0][None, :],
        w_out=f(w_out)[0], w_gate=f(w_gate)[0], w_up=f(w_up)[0], w_down=f(w_down)[0], **consts)
    ck = f(cache_attn_k)[0]; cv = f(cache_attn_v)[0]; st = f(state_rwkv)[0]; sh = f(state_rwkv_shift)[0][:, 0, :]
    sh_u = np.stack([sh[:, _rw_cols(u)] for u in range(8)], axis=1)
    in_maps = []
    for c in range(8):
        b, j = c // 4, c % 4
        selm = np.zeros((128, 4), np.float32); selm[:, j] = 1.0
        m = dict(common)
        m.update(xb=x_prompt[b], xr=np.ascontiguousarray(x_prompt[b, j * TR:(j + 1) * TR]),
                 xsm=np.ascontiguousarray(x_sample[c * NB:(c + 1) * NB].reshape(NB * 64, D)),
                 ck=np.ascontiguousarray(ck[c * NB:(c + 1) * NB]), cv=np.ascontiguousarray(cv[c * NB:(c + 1) * NB]),
                 st0=np.ascontiguousarray(st[c * NB:(c + 1) * NB]), sh0=np.ascontiguousarray(sh_u[c * NB:(c + 1) * NB]),
                 wmy=np.ascontiguousarray(wu[2 * j:2 * j + 2]), upmy=np.ascontiguousarray(up[2 * j:2 * j + 2]),
                 lwmy=np.ascontiguousarray(lw[2 * j:2 * j + 2]), lg2my=np.ascontiguousarray(lg2_[2 * j:2 * j + 2]), sel=selm)
        in_maps.append({k: np.ascontiguousarray(v, dtype=np.float32) for k, v in m.items()})
    res = run_bass_kernel_spmd(nc, in_maps, core_ids=list(range(8)))
    R = res.results
    y_p = np.zeros((2, T, D), np.float32); y_s = np.zeros((Bs, 64, D), np.float32)
    k_p = np.zeros((1, 2, T, 8, 128), np.float32); v_p = np.zeros((1, 2, T, 8, 128), np.float32)
    S_p = np.zeros((1, 2, 16, 64, 64), np.float32); sh_p = np.zeros((1, 2, 1, 3264), np.float32)
    k_s = np.zeros((1, Bs, 64, 8, 128), np.float32); v_s = np.zeros((1, Bs, 64, 8, 128), np.float32)
    S_s = np.zeros((1, Bs, 16, 64, 64), np.float32); sh_s = np.zeros((1, Bs, 1, 3264), np.float32)
    for c in range(8):
        b, j = c // 4, c % 4
        r = R[c]
        y_p[b, j * TR:(j + 1) * TR] = r["yp"]
        y_s[c * NB:(c + 1) * NB] = np.asarray(r["ys"]).reshape(NB, 64, D)
        for hp in range(2):
            u = 2 * j + hp
            k_p[0, b, :, u, :] = r["kpo"][:, hp, :]
            v_p[0, b, :, u, :] = r["vpo"][:, hp, :]
            S_p[0, b, 2 * u:2 * u + 2] = r["spo"][hp]
            sh_p[0, b, 0, _rw_cols(u)] = r["shpo"][hp]
        k_s[0, c * NB:(c + 1) * NB] = r["kso"]
        v_s[0, c * NB:(c + 1) * NB] = r["vso"]
        S_s[0, c * NB:(c + 1) * NB] = np.asarray(r["sso"]).reshape(NB, 16, 64, 64)
        for u in range(8):
            sh_s[0, c * NB:(c + 1) * NB, 0, _rw_cols(u)] = np.asarray(r["shso"])[:, u, :].T if False else 0
        shso = np.asarray(r["shso"])
        for u in range(8):
            for bb in range(NB):
                sh_s[0, c * NB + bb, 0, _rw_cols(u)] = shso[bb, u]
    return (y_p, y_s, k_p, v_p, S_p, sh_p, k_s, v_s, S_s, sh_s)
```

```python
import math
import contextlib
import numpy as np
import concourse.bass as bass
import concourse.mybir as mybir
from concourse.bass_utils import run_bass_kernel_spmd

F32 = mybir.dt.float32
BF16 = mybir.dt.bfloat16
AF = mybir.ActivationFunctionType
ALU = mybir.AluOpType
AX = mybir.AxisListType

D = 2048
NCH = 16
UC = 960
DFF = 5632
NFF = 44
RMS_EPS = 1e-6
GN_EPS = 64e-5
LAM_INIT = 0.2
NPAR = 1472
ROPE_THETA = 500000.0


class _Stop(Exception):
    pass


def ckpt(name):
    return None


class Buf:
    __slots__ = ("name", "w", "r", "excl")

    def __init__(self, name, excl=False):
        self.name = name
        self.w = {}
        self.r = {}
        self.excl = excl


class Sched:
    LIMIT = 30000

    def __init__(self, nc, es, n_dma_sems=40):
        self.nc = nc
        self.es = es
        self.eng = {"pe": nc.tensor, "act": nc.scalar, "dve": nc.vector, "pool": nc.gpsimd, "sp": nc.sync}
        self.esem = {}
        self.cnt = {}
        self.seen = {e: {} for e in self.eng}
        self.uncommitted = {e: False for e in self.eng}
        self.nsem = 0
        for e in ("pe", "act", "dve", "pool"):
            self._new_esem(e)
        self.rings = {}
        for q, nq_ in (("sp", 30), ("pool", 12), ("cc", 1)):
            self.rings[q] = dict(sem=[self._sem("dma_%s%d" % (q, i)) for i in range(nq_)], cnt=[0] * nq_, nxt=0)
        self.nops = 0

    def _sem(self, name):
        self.nsem += 1
        return self.es.enter_context(self.nc.semaphore(name))

    def _new_esem(self, e):
        self.esem[e] = self._sem("e_%s_%d" % (e, self.nsem))
        self.cnt[e] = 0

    def _collect(self, reads, writes, pwrites):
        deps = {}

        def add(d):
            for k, (s, v) in d.items():
                if k not in deps or deps[k][1] < v:
                    deps[k] = (s, v)
        for b in reads:
            add(b.w)
            if b.excl:
                add(b.r)
        for b in writes:
            add(b.w)
            add(b.r)
        for b in pwrites:
            add(b.r)
        return deps

    def _emit_waits(self, e, deps):
        eng = self.eng[e]
        seen = self.seen[e]
        for k, (s, v) in deps.items():
            if seen.get(k, 0) >= v:
                continue
            if e in self.esem and s is self.esem[e] and v > self.cnt[e]:
                continue
            for e2 in self.esem:
                if s is self.esem[e2] and v > self.cnt[e2]:
                    raise RuntimeError("wait on uncommitted event of %s from %s" % (e2, e))
            eng.wait_ge(s, v)
            seen[k] = v

    def _record(self, ev, reads, writes, pwrites):
        k = id(ev[0])
        for b in reads:
            if k not in b.r or b.r[k][1] < ev[1]:
                b.r[k] = ev
        for b in writes:
            b.w = {k: ev}
            b.r = {}
        for b in pwrites:
            if k not in b.w or b.w[k][1] < ev[1]:
                b.w[k] = ev

    def op(self, e, fn, reads=(), writes=(), inc=True, pwrites=()):
        self.nops += 1
        if self.cnt[e] >= self.LIMIT and not self.uncommitted[e]:
            self._new_esem(e)
        deps = self._collect(reads, writes, pwrites)
        self._emit_waits(e, deps)
        ins = fn(self.eng[e])
        ev = (self.esem[e], self.cnt[e] + 1)
        if inc:
            self.cnt[e] += 1
            ins.then_inc(self.esem[e], 1)
            self.uncommitted[e] = False
        else:
            self.uncommitted[e] = True
        self._record(ev, reads, writes, pwrites)
        return ins

    def _ring_next(self, q):
        r = self.rings[q]
        i = r["nxt"]
        r["nxt"] = (i + 1) % len(r["sem"])
        return r, i

    def dma(self, q, out, in_, reads=(), writes=(), pwrites=(), **kw):
        self.nops += 1
        r, i = self._ring_next(q)
        s = r["sem"][i]
        deps = self._collect(reads, writes, pwrites)
        if r["cnt"][i] > 0:
            deps[id(s)] = (s, r["cnt"][i])
        self._emit_waits(q, deps)
        ins = self.eng[q].dma_start(out=out, in_=in_, **kw)
        r["cnt"][i] += 16
        ins.then_inc(s, 16)
        ev = (s, r["cnt"][i])
        self._record(ev, reads, writes, pwrites)
        return ev

    def custom(self, q, fn, reads=(), writes=(), pwrites=()):
        r, i = self._ring_next("cc")
        s = r["sem"][i]
        deps = self._collect(reads, writes, pwrites)
        if r["cnt"][i] > 0:
            deps[id(s)] = (s, r["cnt"][i])
        self._emit_waits(q, deps)
        ins = fn(self.eng[q])
        r["cnt"][i] += 1
        ins.then_inc(s, 1)
        ev = (s, r["cnt"][i])
        self._record(ev, reads, writes, pwrites)
        return ev

    def all_dma(self):
        for r in self.rings.values():
            for s, c in zip(r["sem"], r["cnt"]):
                if c > 0:
                    yield s, c

    def barrier(self):
        for e in self.eng:
            for e2 in self.esem:
                if self.cnt[e2] > 0 and self.seen[e].get(id(self.esem[e2]), 0) < self.cnt[e2]:
                    self.eng[e].wait_ge(self.esem[e2], self.cnt[e2])
                    self.seen[e][id(self.esem[e2])] = self.cnt[e2]
            for s, c in self.all_dma():
                if self.seen[e].get(id(s), 0) < c:
                    self.eng[e].wait_ge(s, c)
                    self.seen[e][id(s)] = c

    def wait_all(self, q, bufs):
        deps = self._collect(bufs, bufs, ())
        self._emit_waits(q, deps)


class Dims:
    def __init__(self, T, NB, PAST):
        self.T = T
        self.NB = NB
        self.PAST = PAST
        self.NT = T // 128
        self.TR = T // 4
        self.NP = PAST // 128


def build_nc(dm):
    T, NB, PAST, NT, TR, NP = dm.T, dm.NB, dm.PAST, dm.NT, dm.TR, dm.NP
    nc = bass.Bass("TRN2", target_bir_lowering=False)

    def din(name, shape, dt=F32):
        return nc.dram_tensor(name, list(shape), dt, kind="ExternalInput").ap()

    def dout(name, shape, dt=F32):
        return nc.dram_tensor(name, list(shape), dt, kind="ExternalOutput").ap()

    def dint(name, shape, dt):
        return nc.dram_tensor(name, list(shape), dt, kind="Internal").ap()

    xb = din("xb", [T, D])
    xr = din("xr", [TR, D])
    xsm = din("xsm", [NB * 64, D])
    ck = din("ck", [NB, PAST, 8, 128])
    cv = din("cv", [NB, PAST, 8, 128])
    st0 = din("st0", [NB, 16, 64, 64])
    sh0 = din("sh0", [NB, 8, 576])
    wu = din("wu", [8, D, UC])
    wmy = din("wmy", [2, D, UC])
    up = din("up", [8, NPAR])
    upmy = din("upmy", [2, NPAR])
    lw = din("lw", [8, 128, 128])
    lwmy = din("lwmy", [2, 128, 128])
    lg2 = din("lg2", [8, 64, 128])
    lg2my = din("lg2my", [2, 64, 128])
    g1T = din("g1T", [128, NCH])
    g2T = din("g2T", [128, NCH])
    qkg = din("qkg", [1, 256])
    lamv = din("lamv", [1, 256])
    subg = din("subg", [1, 128])
    sel = din("sel", [128, 4])
    w_out = din("w_out", [D, D])
    w_gate = din("w_gate", [D, DFF])
    w_up = din("w_up", [D, DFF])
    w_down = din("w_down", [DFF, D])
    csp = din("csp", [T, 16])
    css = din("css", [64, 16])
    cid = din("cid", [128, 128])
    cut = din("cut", [128, 128])
    cs1 = din("cs1", [128, 128])
    cc128 = din("cc128", [128, 128])
    cc64 = din("cc64", [64, 64])
    cgm = din("cgm", [128, 512])
    clow = din("clow", [128, 128])
    cam = din("cam", [2, 128, 512])
    yp = dout("yp", [TR, D])
    ys = dout("ys", [NB * 64, D])
    kpo = dout("kpo", [T, 2, 128])
    vpo = dout("vpo", [T, 2, 128])
    spo = dout("spo", [2, 2, 64, 64])
    shpo = dout("shpo", [2, 576])
    kso = dout("kso", [NB, 64, 8, 128])
    vso = dout("vso", [NB, 64, 8, 128])
    sso = dout("sso", [NB, 8, 2, 64, 64])
    shso = dout("shso", [NB, 8, 576])
    XI = dint("XI", [4 * 16 * 128, TR], BF16)
    XO = dint("XO", [16 * 128, TR], BF16)
    XS = dint("XS", [16 * 128, NB * 64], BF16)
    wub = dint("wub", [8, D, UC], BF16)
    wmyb = dint("wmyb", [2, D, UC], BF16)
    wob = dint("wob", [D, D], BF16)
    wgb = dint("wgb", [D, DFF], BF16)
    wupb = dint("wupb", [D, DFF], BF16)
    wdb = dint("wdb", [DFF, D], BF16)

    es = contextlib.ExitStack()
    with es:
        S = Sched(nc, es)
        bufs = {}

        es1 = contextlib.ExitStack()
        cur = [es]

        def sb(name, shape, dt=F32):
            t = cur[0].enter_context(nc.sbuf_tensor(name, list(shape), dt))
            bufs[name] = Buf(name)
            return t

        def B(name):
            if name not in bufs:
                bufs[name] = Buf(name)
            return bufs[name]

        PS = [es.enter_context(nc.psum_tensor("ps%d" % i, [128, 512], F32)) for i in range(8)]
        PB = [Buf("ps%d" % i, excl=True) for i in range(8)]
        gen_rr = [0]

        def gbank():
            i = gen_rr[0]
            gen_rr[0] = (gen_rr[0] + 1) % 4
            return i

        try:
            def cast_w(dst, src, rows, cols, bname):
                b = B(bname)
                r = 0
                while r < rows:
                    rr = min(128, rows - r)
                    c = 0
                    while c < cols:
                        cc = min(2048, cols - c)
                        S.dma("pool", dst[r:r + rr, c:c + cc], src[r:r + rr, c:c + cc], pwrites=[b])
                        c += cc
                    r += rr

            def cast_w_gen(dst, src, rows, cols, bname):
                b = B(bname)
                r = 0
                while r < rows:
                    rr = min(128, rows - r)
                    c = 0
                    while c < cols:
                        cc = min(2048, cols - c)
                        S.dma("pool", dst[r:r + rr, c:c + cc], src[r:r + rr, c:c + cc], pwrites=[b])
                        yield
                        c += cc
                    r += rr

            for u in range(2):
                cast_w(wmyb[u], wmy[u], D, UC, "wmyb%d" % u)
            def deferred_casts():
                for u in range(8):
                    yield from cast_w_gen(wub[u], wu[u], D, UC, "wub%d" % u)
                yield from cast_w_gen(wob, w_out, D, D, "wob")
                yield from cast_w_gen(wgb, w_gate, D, DFF, "wgb")
                yield from cast_w_gen(wupb, w_up, D, DFF, "wupb")
                yield from cast_w_gen(wdb, w_down, DFF, D, "wdb")
            dcast = [deferred_casts()]


            ident = sb("ident", [128, 128]); identb = sb("identb", [128, 128], BF16)
            ut = sb("ut", [128, 128]); s1m = sb("s1m", [128, 128]); c128 = sb("c128", [128, 128]); c64 = sb("c64", [64, 64])
            ones = sb("ones", [128, 128]); onesb = sb("onesb", [128, 1], BF16)
            gmask = sb("gmask", [128, 512]); lowm = sb("lowm", [128, 128])
            amask = sb("amask", [128, 2, 512], BF16)
            g1t = sb("g1t", [128, NCH]); g2t = sb("g2t", [128, NCH])
            qkgb = sb("qkgb", [128, 256]); lamb = sb("lamb", [128, 256]); subgb = sb("subgb", [128, 128])
            selt = sb("selt", [128, 4])
            cstt = [sb("cstt%d" % i, [128, 16]) for i in range(2)]; csst = sb("csst", [64, 16])
            lamt = sb("lamt", [128, 4])
            neglam = sb("neglam", [128, 1])
            for t_, src in ((ident, cid), (ut, cut), (s1m, cs1), (c128, cc128), (gmask, cgm), (lowm, clow),
                            (g1t, g1T), (g2t, g2T), (selt, sel)):
                S.dma("sp", t_[:], src[:, :], writes=[B(t_.name)])
            S.dma("sp", c64[:], cc64[:, :], writes=[B("c64")])
            S.dma("sp", csst[:], css[:, :], writes=[B("csst")])
            S.dma("sp", qkgb[:], qkg.partition_broadcast(128), writes=[B("qkgb")])
            S.dma("sp", lamb[:], lamv.partition_broadcast(128), writes=[B("lamb")])
            S.dma("sp", subgb[:], subg.partition_broadcast(128), writes=[B("subgb")])
            S.op("dve", lambda e: e.memset(ones[:], 1.0), writes=[B("ones")])
            S.op("dve", lambda e: e.memset(onesb[:], 1.0), writes=[B("onesb")])
            S.op("dve", lambda e: e.tensor_copy(out=identb[:], in_=ident[:]), reads=[B("ident")], writes=[B("identb")])
            S.dma("pool", amask[:], cam.rearrange("a p c -> p a c"), writes=[B("amask")])
            S.op("dve", lambda e: e.tensor_scalar(out=subgb[:], in0=subgb[:], scalar1=1.0 - LAM_INIT, scalar2=None,
                                                  op0=ALU.mult), reads=[B("subgb")], writes=[B("subgb")])
            lscr = sb("lscr", [128, 64])
            S.op("dve", lambda e: e.scalar_tensor_tensor(out=lscr[:], in0=lamb[:, 0:64], scalar=1.0, in1=lamb[:, 64:128], op0=ALU.mult, op1=ALU.mult, accum_out=lamt[:, 0:1]),
                 reads=[B("lamb")], writes=[B("lscr"), B("lamt")])
            S.op("dve", lambda e: e.scalar_tensor_tensor(out=lscr[:], in0=lamb[:, 128:192], scalar=1.0, in1=lamb[:, 192:256], op0=ALU.mult, op1=ALU.mult, accum_out=lamt[:, 1:2]),
                 reads=[B("lamb"), B("lamt")], writes=[B("lscr"), B("lamt")])
            S.op("act", lambda e: e.activation(out=lamt[:, 2:4], in_=lamt[:, 0:2], func=AF.Exp),
                 reads=[B("lamt")], writes=[B("lamt")])
            S.op("dve", lambda e: e.tensor_tensor(out=neglam[:], in0=lamt[:, 3:4], in1=lamt[:, 2:3], op=ALU.subtract),
                 reads=[B("lamt")], writes=[B("neglam")])
            S.op("dve", lambda e: e.tensor_scalar(out=neglam[:], in0=neglam[:], scalar1=-LAM_INIT, scalar2=None, op0=ALU.add),
                 reads=[B("neglam")], writes=[B("neglam")])
            ckpt("consts")

            NKT = max(NT, 2 * (NP + 1))
            xsb = sb("xsb", [128, D], BF16)
            junk = sb("junk", [128, D], BF16)
            st4 = sb("st4", [128, 8])
            cur[0] = es1
            wub_sb = sb("wub_sb", [128, NCH, UC], BF16)
            KT = sb("KT", [128, NKT * 128], BF16)
            VV = sb("VV", [128, NKT, 128], BF16)
            KTB = [Buf("KT%d" % i) for i in range(NKT)]
            VB = [Buf("V%d" % i) for i in range(NKT)]
            xt = [sb("xt%d" % i, [128, D]) for i in range(2)]
            xnTall = sb("xnTall", [128, NCH, 256], BF16)
            XNB = [Buf("xnTslot0"), Buf("xnTslot1")]
            gsb = sb("gsb", [128, 384])
            prw = [sb("prw%d" % i, [128, 576]) for i in range(2)]
            shbuf = sb("shbuf", [128, 576])
            tq = sb("tq", [128, 256]); qkn = sb("qkn", [128, 256]); rtmp = sb("rtmp", [128, 4, 4, 8])
            qkb = sb("qkb", [128, 256], BF16)
            QTs = [sb("QT%d" % i, [128, 2, 256], BF16) for i in range(2)]
            Eb = [sb("Eb%d" % i, [128, 512], BF16) for i in range(2)]
            OT = sb("OT", [128, 512]); zsb = sb("zsb", [1, 512]); Zacc = sb("Zacc", [128, 512]); ajunk = sb("ajunk", [128, 128], BF16); one11 = sb("one11", [1, 1])
            S.op("dve", lambda e: e.memset(one11[:], 1.0), writes=[B("one11")])
            for q_ in QTs:
                S.op("dve", lambda e, q_=q_: e.memset(q_[:, :, :], 0.0), writes=[B(q_.name)])
            osb = sb("osb", [128, 128]); onb = sb("onb", [128, 128], BF16); ast = sb("ast", [128, 8])
            catA = sb("catA", [128, 4, 128], BF16); catB = sb("catB", [128, 4, 128], BF16)
            parb = sb("parb", [128, NPAR])
            lwt = sb("lwt", [64, 2, 128]); lg2t = sb("lg2t", [64, 128])
            xs_ = sb("xs_", [128, 576])
            Et = sb("Et", [128, 128]); LT = sb("LT", [128, 192]); LTT = sb("LTT", [64, 3, 128])
            za = sb("za", [128, 256]); sa = sb("sa", [128, 256]); ld = sb("ld", [128, 128]); g_sb = sb("g_sb", [128, 128])
            kkv = sb("kkv", [128, 128]); rst = sb("rst", [128, 8]); k2 = sb("k2", [128, 128]); mm_ = sb("mm_", [128, 128])
            bvec = sb("bvec", [128, 128]); bs = sb("bs", [128, 2])
            cum = sb("cum", [128, 128]); ec = sb("ec", [128, 128]); eci = sb("eci", [128, 128]); ee = sb("ee", [128, 128])
            eh = sb("eh", [128, 128]); gC = sb("gC", [64, 2])
            rt = sb("rt", [128, 128]); bt = sb("bt", [128, 128]); ktl = sb("ktl", [128, 128])
            bh = sb("bh", [128, 128]); kh = sb("kh", [128, 128])
            FT = [sb("FT%d" % h, [64, 512]) for h in range(2)]
            GM = [sb("GM%d" % h, [128, 512]) for h in range(2)]
            Xa = [[sb("Xa%d_%d" % (h, i), [128, 128], BF16) for i in range(2)] for h in range(2)]
            Xb = [[sb("Xb%d_%d" % (h, i), [128, 128], BF16) for i in range(2)] for h in range(2)]
            XG = [sb("XG%d" % h, [128, 128], BF16) for h in range(2)]
            ACCB = [[sb("ACCB%d_%d" % (h, i), [128, 128], BF16) for i in range(2)] for h in range(2)]
            ACC = [[sb("ACC%d_%d" % (h, i), [128, 128]) for i in range(2)] for h in range(2)]
            RH = [sb("RH%d" % h, [128, 128]) for h in range(2)]
            PU = [sb("PU%d" % h, [128, 128]) for h in range(2)]
            Y1T = [sb("Y1T%d" % h, [64, 128]) for h in range(2)]
            Y2 = [sb("Y2%d" % h, [128, 64]) for h in range(2)]
            T1T = [sb("T1T%d" % h, [64, 64]) for h in range(2)]
            T2 = [sb("T2%d" % h, [64, 64]) for h in range(2)]
            Hs = [[sb("H%d_%d" % (h, i), [64, 64]) for i in range(2)] for h in range(2)]
            Hld = sb("Hld", [64, 2, 64]); Hout = sb("Hout", [64, 2, 64])
            yb = sb("yb", [128, 128]); yc = sb("yc", [128, 128]); ysq = sb("ysq", [128, 128]); obb = sb("obb", [128, 128], BF16)

            def bc3(ap2, n, a, b):
                return ap2.unsqueeze(2).to_broadcast([n, a, b])

            def rstd_chain(src_ap, dst_ap, n, k, scale, eps, bsrc, bdst):
                S.op("dve", lambda e: e.tensor_scalar(out=dst_ap, in0=src_ap, scalar1=scale, scalar2=eps, op0=ALU.mult,
                                                      op1=ALU.add), reads=[bsrc], writes=[bdst])
                S.op("act", lambda e: e.activation(out=dst_ap, in_=dst_ap, func=AF.Ln), reads=[bdst], writes=[bdst])
                S.op("act", lambda e: e.activation(out=dst_ap, in_=dst_ap, func=AF.Exp, scale=-0.5), reads=[bdst], writes=[bdst])

            def front(x_rows_ap, n, dst, dst_off, dstB, gt, slot):
                xtile = xt[slot]
                bx = B(xtile.name)
                S.dma("sp", xtile[:n, :], x_rows_ap, writes=[bx])
                S.op("dve", lambda e: e.scalar_tensor_tensor(out=junk[:n, :], in0=xtile[:n, :], scalar=1.0, in1=xtile[:n, :], op0=ALU.mult, op1=ALU.mult, accum_out=st4[:n, 0:1]),
                     reads=[bx], writes=[B("junk"), B("st4")])
                rstd_chain(st4[:n, 0:1], st4[:n, 0:1], n, 1, 1.0 / D, RMS_EPS, B("st4"), B("st4"))
                S.op("act", lambda e: e.activation(out=xsb[:n, :], in_=xtile[:n, :], func=AF.Copy, scale=st4[:n, 0:1]),
                     reads=[bx, B("st4")], writes=[B("xsb")])
                yield
                for half in range(2):
                    pb = gbank()
                    pv = PS[pb][:].bitcast(BF16)
                    for c8 in range(8):
                        c = half * 8 + c8
                        S.op("pe", lambda e, c=c, c8=c8, pv=pv: e.transpose(out=pv[:, c8 * n:(c8 + 1) * n],
                                                                           in_=xsb[:n, c * 128:(c + 1) * 128],
                                                                           identity=identb[:n, :n]),
                             reads=[B("xsb"), B("identb")], writes=[PB[pb]] if c8 == 0 else [], pwrites=[] if c8 == 0 else [PB[pb]],
                             inc=(c8 == 7))
                    S.op("dve", lambda e, half=half, pv=pv: e.tensor_tensor(
                        out=dst[:, half * 8:(half + 1) * 8, dst_off:dst_off + n],
                        in0=pv[:, 0:8 * n].rearrange("p (c n) -> p c n", n=n),
                        in1=bc3(gt[:, half * 8:(half + 1) * 8], 128, 8, n), op=ALU.mult),
                        reads=[PB[pb], B(gt.name)], writes=[] if half else [dstB], pwrites=[dstB] if half else [])
                    yield

            def load_unit_params(up_row, lw_ap, lg2_ap):
                S.dma("sp", parb[:], up_row.partition_broadcast(128), writes=[B("parb")])
                S.dma("sp", lwt[:], lw_ap.rearrange("(a p) n -> p a n", p=64), writes=[B("lwt")])
                S.dma("sp", lg2t[:], lg2_ap, writes=[B("lg2t")])
            mu_bc = parb[:, 0:576]; w0a0_bc = parb[:, 576:832]; kk_bc = parb[:, 832:960]; ka_bc = parb[:, 960:1088]
            rk_bc = parb[:, 1088:1216]; lg_bc = parb[:, 1216:1344]; lb_bc = parb[:, 1344:1472]

            def load_unit_w(wsrc):
                S.dma("sp", wub_sb[:], wsrc.rearrange("(c p) n -> p c n", p=128), reads=[B(wsrc_name[0])], writes=[B("wub_sb")])
            wsrc_name = [None]

            def project(xsrc, xoff, xB, n, pslot):
                pa, pb2 = gbank(), gbank()
                for k in range(NCH):
                    S.op("pe", lambda e, k=k: e.matmul(PS[pa][:n, 0:512], lhsT=xsrc[:, k, xoff:xoff + n], rhs=wub_sb[:, k, 0:512],
                                                       start=(k == 0), stop=(k == NCH - 1), skip_group_check=True),
                         reads=[xB, B("wub_sb")], writes=[PB[pa]] if k == 0 else [], pwrites=[] if k == 0 else [PB[pa]],
                         inc=(k == NCH - 1))
                for k in range(NCH):
                    S.op("pe", lambda e, k=k: e.matmul(PS[pb2][:n, 0:448], lhsT=xsrc[:, k, xoff:xoff + n], rhs=wub_sb[:, k, 512:960],
                                                       start=(k == 0), stop=(k == NCH - 1), skip_group_check=True),
                         reads=[xB, B("wub_sb")], writes=[PB[pb2]] if k == 0 else [], pwrites=[] if k == 0 else [PB[pb2]],
                         inc=(k == NCH - 1))
                pr = prw[pslot]
                yield
                S.op("act", lambda e: e.activation(out=gsb[:n, :], in_=PS[pa][:n, 0:384], func=AF.Copy),
                     reads=[PB[pa]], writes=[B("gsb")])
                S.op("act", lambda e: e.activation(out=pr[:n, 0:128], in_=PS[pa][:n, 384:512], func=AF.Copy),
                     reads=[PB[pa]], writes=[B(pr.name)])
                S.op("act", lambda e: e.activation(out=pr[:n, 128:576], in_=PS[pb2][:n, 0:448], func=AF.Copy),
                     reads=[PB[pb2]], pwrites=[B(pr.name)])

            def attn_prep(n, cs_ap, csB, k_out_ap, v_out_ap, kt_idx, kt_off, qoff, QT, QTB_):
                S.op("dve", lambda e: e.tensor_tensor(out=tq[:n, :], in0=gsb[:n, 0:256], in1=gsb[:n, 0:256], op=ALU.mult),
                     reads=[B("gsb")], writes=[B("tq")])
                S.op("dve", lambda e: e.tensor_reduce(out=st4[:n, 4:8], in_=tq[:n, :].rearrange("p (a b) -> p a b", b=64),
                                                      axis=AX.X, op=ALU.add), reads=[B("tq")], writes=[B("st4")])
                rstd_chain(st4[:n, 4:8], st4[:n, 4:8], n, 4, 1.0 / 64, RMS_EPS, B("st4"), B("st4"))
                q3 = qkn[:n, :].rearrange("p (a b) -> p a b", b=64)
                S.op("dve", lambda e: e.tensor_tensor(out=q3, in0=gsb[:n, 0:256].rearrange("p (a b) -> p a b", b=64),
                                                      in1=bc3(st4[:n, 4:8], n, 4, 64), op=ALU.mult),
                     reads=[B("gsb"), B("st4")], writes=[B("qkn")])
                S.op("dve", lambda e: e.tensor_tensor(out=qkn[:n, :], in0=qkn[:n, :], in1=qkgb[:n, :], op=ALU.mult),
                     reads=[B("qkn"), B("qkgb")], writes=[B("qkn")])
                yield
                x1 = q3[:, :, 0:8]; x2 = q3[:, :, 8:16]
                cosb = cs_ap[:, 0:8].unsqueeze(1).to_broadcast([n, 4, 8])
                sinb = cs_ap[:, 8:16].unsqueeze(1).to_broadcast([n, 4, 8])
                for idx, (a_, b_) in enumerate(((x1, cosb), (x2, sinb), (x2, cosb), (x1, sinb))):
                    S.op("dve", lambda e, idx=idx, a_=a_, b_=b_: e.tensor_tensor(out=rtmp[:n, :, idx, :], in0=a_, in1=b_, op=ALU.mult),
                         reads=[B("qkn"), csB], writes=[B("rtmp")] if idx == 0 else [], pwrites=[] if idx == 0 else [B("rtmp")])
                S.op("dve", lambda e: e.tensor_tensor(out=x1, in0=rtmp[:n, :, 0, :], in1=rtmp[:n, :, 1, :], op=ALU.subtract),
                     reads=[B("rtmp")], writes=[B("qkn")])
                S.op("dve", lambda e: e.tensor_tensor(out=x2, in0=rtmp[:n, :, 2, :], in1=rtmp[:n, :, 3, :], op=ALU.add),
                     reads=[B("rtmp")], writes=[B("qkn")])
                yield
                S.dma("sp", k_out_ap, qkn[:n, 128:256], reads=[B("qkn")])
                S.dma("sp", v_out_ap, gsb[:n, 256:384], reads=[B("gsb")])
                S.op("act", lambda e: e.activation(out=qkb[:n, :], in_=qkn[:n, :], func=AF.Copy), reads=[B("qkn")], writes=[B("qkb")])
                S.op("act", lambda e: e.activation(out=VV[:n, kt_idx, :], in_=gsb[:n, 256:384], func=AF.Copy),
                     reads=[B("gsb")], writes=[VB[kt_idx]])
                pb = gbank()
                pv = PS[pb][:].bitcast(BF16)
                S.op("pe", lambda e: e.transpose(out=pv[:, 0:n], in_=qkb[:n, 0:128], identity=identb[:n, :n]),
                     reads=[B("qkb"), B("identb")], writes=[PB[pb]], inc=False)
                S.op("pe", lambda e: e.transpose(out=pv[:, 128:128 + n], in_=qkb[:n, 128:256], identity=identb[:n, :n]),
                     reads=[B("qkb"), B("identb")], pwrites=[PB[pb]])
                S.op("act", lambda e: e.activation(out=QT[0:64, 0, qoff:qoff + n], in_=pv[0:64, 0:n], func=AF.Copy),
                     reads=[PB[pb]], writes=[] if qoff else [QTB_], pwrites=[QTB_] if qoff else [])
                S.op("act", lambda e: e.activation(out=QT[64:128, 1, qoff:qoff + n], in_=pv[64:128, 0:n], func=AF.Copy),
                     reads=[PB[pb]], pwrites=[QTB_])
                yield
                S.op("act", lambda e: e.activation(out=KT[:, kt_off:kt_off + n], in_=pv[:, 128:128 + n], func=AF.Copy),
                     reads=[PB[pb]], writes=[KTB[kt_idx]])

            def attention(nq, n, key_tiles, cat_writer, QT, QTB_, tile_base):
                W2 = 2 * nq
                for ki, (koff, nk, kidx, mk) in enumerate(key_tiles):
                    sbk = 4 + (ki % 2)
                    Ebt = Eb[ki % 2]
                    S.op("pe", lambda e: e.matmul(PS[sbk][:nk, 0:W2].rearrange("p (a b) -> p a b", a=2), lhsT=KT[:, koff:koff + nk], rhs=QT[:, :, 0:nq],
                                                  start=True, stop=True, skip_group_check=True),
                         reads=[KTB[kidx], QTB_], writes=[PB[sbk]])
                    S.op("act", lambda e: e.activation(out=Ebt[:nk, 0:W2], in_=PS[sbk][:nk, 0:W2], func=AF.Exp, scale=0.125),
                         reads=[PB[sbk]], writes=[B(Ebt.name)])
                    if mk is not None:
                        S.op("dve", lambda e: e.tensor_tensor(out=Ebt[:nk, 0:W2], in0=Ebt[:nk, 0:W2], in1=amask[:nk, mk, 0:W2], op=ALU.mult),
                             reads=[B(Ebt.name), B("amask")], writes=[B(Ebt.name)])
                    first = (ki == 0)
                    last = (ki == len(key_tiles) - 1)
                    S.op("pe", lambda e: e.matmul(PS[6][:, 0:W2], lhsT=VV[:nk, kidx, :], rhs=Ebt[:nk, 0:W2], start=first, stop=last,
                                                  skip_group_check=True),
                         reads=[VB[kidx], B(Ebt.name)], writes=[PB[6]] if first else [], pwrites=[] if first else [PB[6]], inc=True)
                    if first:
                        S.op("pool", lambda e: e.tensor_copy(out=Zacc[:nk, 0:W2], in_=Ebt[:nk, 0:W2]), reads=[B(Ebt.name)], writes=[B("Zacc")])
                    else:
                        S.op("pool", lambda e: e.tensor_tensor(out=Zacc[:nk, 0:W2], in0=Zacc[:nk, 0:W2], in1=Ebt[:nk, 0:W2], op=ALU.add),
                             reads=[B(Ebt.name), B("Zacc")], writes=[B("Zacc")])
                    yield
                S.op("pe", lambda e: e.matmul(PS[7][0:1, 0:W2], lhsT=ones[:, 0:1], rhs=Zacc[:, 0:W2], start=True, stop=True, skip_group_check=True),
                     reads=[B("ones"), B("Zacc")], writes=[PB[7]])
                S.op("act", lambda e: e.activation(out=OT[:, 0:W2], in_=PS[6][:, 0:W2], func=AF.Copy), reads=[PB[6]], writes=[B("OT")])
                S.op("dve", lambda e: e.tensor_copy(out=zsb[0:1, 0:W2], in_=PS[7][0:1, 0:W2]), reads=[PB[7]], writes=[B("zsb")])
                for qt in range(nq // n):
                    pb = gbank()
                    S.op("pe", lambda e: e.transpose(out=PS[pb][:n, 0:128], in_=OT[:, qt * n:(qt + 1) * n], identity=ident[:, :]),
                         reads=[B("OT"), B("ident")], writes=[PB[pb]], inc=False)
                    S.op("pe", lambda e: e.transpose(out=PS[pb][:n, 128:256], in_=OT[:, nq + qt * n:nq + (qt + 1) * n], identity=ident[:, :]),
                         reads=[B("OT"), B("ident")], pwrites=[PB[pb]], inc=False)
                    S.op("pe", lambda e: e.matmul(PS[pb][:n, 256:257], lhsT=zsb[0:1, qt * n:(qt + 1) * n], rhs=one11[0:1, 0:1],
                                                  start=False, stop=False, skip_group_check=True),
                         reads=[B("zsb"), B("one11")], pwrites=[PB[pb]], inc=False)
                    S.op("pe", lambda e: e.matmul(PS[pb][:n, 257:258], lhsT=zsb[0:1, nq + qt * n:nq + (qt + 1) * n], rhs=one11[0:1, 0:1],
                                                  start=False, stop=True, skip_group_check=True),
                         reads=[B("zsb"), B("one11")], pwrites=[PB[pb]])
                    S.op("dve", lambda e: e.reciprocal(out=ast[:n, 0:2], in_=PS[pb][:n, 256:258]), reads=[PB[pb]], writes=[B("ast")])
                    S.op("dve", lambda e: e.tensor_tensor(out=ast[:n, 2:3], in0=ast[:n, 1:2], in1=neglam[:n, 0:1], op=ALU.mult),
                         reads=[B("ast"), B("neglam")], writes=[B("ast")])
                    S.op("dve", lambda e: e.tensor_scalar(out=osb[:n, :], in0=PS[pb][:n, 0:128], scalar1=ast[:n, 0:1], scalar2=None,
                                                          op0=ALU.mult), reads=[PB[pb], B("ast")], writes=[B("osb")])
                    S.op("dve", lambda e: e.scalar_tensor_tensor(out=osb[:n, :], in0=PS[pb][:n, 128:256], scalar=ast[:n, 2:3],
                                                                 in1=osb[:n, :], op0=ALU.mult, op1=ALU.add),
                         reads=[PB[pb], B("ast"), B("osb")], writes=[B("osb")])
                    S.op("dve", lambda e: e.scalar_tensor_tensor(out=ajunk[:n, 0:128], in0=osb[:n, :], scalar=1.0, in1=osb[:n, :], op0=ALU.mult, op1=ALU.mult, accum_out=ast[:n, 4:5]),
                         reads=[B("osb"), B("ast")], writes=[B("ajunk"), B("ast")])
                    rstd_chain(ast[:n, 4:5], ast[:n, 4:5], n, 1, 1.0 / 128, RMS_EPS, B("ast"), B("ast"))
                    S.op("dve", lambda e: e.scalar_tensor_tensor(out=onb[:n, :], in0=osb[:n, :], scalar=ast[:n, 4:5], in1=subgb[:n, :],
                                                                 op0=ALU.mult, op1=ALU.mult),
                         reads=[B("osb"), B("ast"), B("subgb")], writes=[B("onb")])
                    pb2 = gbank()
                    pv = PS[pb2][:].bitcast(BF16)
                    S.op("pe", lambda e: e.transpose(out=pv[:, 0:n], in_=onb[:n, :], identity=identb[:n, :n]),
                         reads=[B("onb"), B("identb")], writes=[PB[pb2]])
                    cat_writer(0, tile_base + qt, pv[:, 0:n], PB[pb2])
                    yield

            def rwkv(n, pslot, prev_ap, prevB, cmat, first_tile, hslot, cat_writer, tile_idx):
                pr = prw[pslot]
                prB = B(pr.name)
                pa, pb2 = gbank(), gbank()
                for (bank, c0, c1) in ((pa, 0, 512), (pb2, 512, 576)):
                    S.op("pe", lambda e, bank=bank, c0=c0, c1=c1: e.matmul(PS[bank][:n, 0:c1 - c0], lhsT=s1m[:n, :n], rhs=pr[:n, c0:c1],
                                                                          start=True, stop=(prev_ap is None), skip_group_check=True),
                         reads=[B("s1m"), prB], writes=[PB[bank]], inc=(prev_ap is None))
                    if prev_ap is not None:
                        S.op("pe", lambda e, bank=bank, c0=c0, c1=c1: e.matmul(PS[bank][:n, 0:c1 - c0], lhsT=cmat, rhs=prev_ap[:, c0:c1],
                                                                              start=False, stop=True, skip_group_check=True),
                             reads=[prevB, B("c128"), B("c64")], pwrites=[PB[bank]])
                S.op("dve", lambda e: e.tensor_tensor(out=xs_[:n, 0:512], in0=PS[pa][:n, 0:512], in1=pr[:n, 0:512], op=ALU.subtract),
                     reads=[PB[pa], prB], writes=[B("xs_")])
                S.op("dve", lambda e: e.tensor_tensor(out=xs_[:n, 512:576], in0=PS[pb2][:n, 0:64], in1=pr[:n, 512:576], op=ALU.subtract),
                     reads=[PB[pb2], prB], pwrites=[B("xs_")])
                S.op("dve", lambda e: e.tensor_tensor(out=xs_[:n, :], in0=xs_[:n, :], in1=mu_bc[:n, :], op=ALU.mult),
                     reads=[B("xs_"), B("parb")], writes=[B("xs_")])
                S.op("dve", lambda e: e.tensor_tensor(out=xs_[:n, :], in0=xs_[:n, :], in1=pr[:n, :], op=ALU.add),
                     reads=[B("xs_"), prB], writes=[B("xs_")])
                xr_, xk, xv = xs_[:n, 0:128], xs_[:n, 128:256], xs_[:n, 256:384]
                yield
                S.op("act", lambda e: e.activation(out=Et[:n, 0:64], in_=xs_[:n, 384:448], func=AF.Exp, scale=-2.0),
                     reads=[B("xs_")], writes=[B("Et")])
                S.op("act", lambda e: e.activation(out=Et[:n, 64:128], in_=xs_[:n, 512:576], func=AF.Exp, scale=-1.0),
                     reads=[B("xs_")], pwrites=[B("Et")])
                S.op("dve", lambda e: e.tensor_scalar(out=Et[:n, :], in0=Et[:n, :], scalar1=1.0, scalar2=None, op0=ALU.add),
                     reads=[B("Et")], writes=[B("Et")])
                S.op("dve", lambda e: e.reciprocal(out=Et[:n, :], in_=Et[:n, :]), reads=[B("Et")], writes=[B("Et")])
                S.op("dve", lambda e: e.tensor_scalar(out=LT[:n, 0:64], in0=Et[:n, 0:64], scalar1=2.0, scalar2=-1.0, op0=ALU.mult, op1=ALU.add),
                     reads=[B("Et")], writes=[B("LT")])
                S.op("dve", lambda e: e.tensor_copy(out=LT[:n, 64:128], in_=xs_[:n, 448:512]), reads=[B("xs_")], pwrites=[B("LT")])
                S.op("dve", lambda e: e.tensor_copy(out=LT[:n, 128:192], in_=Et[:n, 64:128]), reads=[B("Et")], pwrites=[B("LT")])
                yield
                pb = gbank()
                for j3 in range(3):
                    S.op("pe", lambda e, j3=j3: e.transpose(out=PS[pb][0:64, j3 * 128:j3 * 128 + n], in_=LT[:n, j3 * 64:(j3 + 1) * 64], identity=ident[:n, :n]),
                         reads=[B("LT"), B("ident")], writes=[PB[pb]] if j3 == 0 else [], pwrites=[] if j3 == 0 else [PB[pb]], inc=(j3 == 2))
                S.op("act", lambda e: e.activation(out=LTT[:, :, 0:n], in_=PS[pb][0:64, 0:384].rearrange("p (a b) -> p a b", b=128)[:, :, 0:n], func=AF.Copy),
                     reads=[PB[pb]], writes=[B("LTT")])
                yield
                pl = gbank()
                S.op("pe", lambda e: e.matmul(PS[pl][:n, 0:128], lhsT=LTT[:, 0, 0:n], rhs=lwt[:, 0, :], start=True, stop=False, skip_group_check=True),
                     reads=[B("LTT"), B("lwt")], writes=[PB[pl]], inc=False)
                S.op("pe", lambda e: e.matmul(PS[pl][:n, 128:256], lhsT=LTT[:, 1, 0:n], rhs=lwt[:, 1, :], start=False, stop=False, skip_group_check=True),
                     reads=[B("LTT"), B("lwt")], pwrites=[PB[pl]], inc=False)
                S.op("pe", lambda e: e.matmul(PS[pl][:n, 256:384], lhsT=LTT[:, 2, 0:n], rhs=lg2t[:, :], start=False, stop=True, skip_group_check=True),
                     reads=[B("LTT"), B("lg2t")], pwrites=[PB[pl]])
                yield
                S.op("dve", lambda e: e.tensor_tensor(out=za[:n, :], in0=PS[pl][:n, 0:256], in1=w0a0_bc[:n, :], op=ALU.add),
                     reads=[PB[pl], B("parb")], writes=[B("za")])
                yield
                S.op("dve", lambda e: e.tensor_copy(out=g_sb[:n, :], in_=PS[pl][:n, 256:384]), reads=[PB[pl]], writes=[B("g_sb")])
                yield
                S.op("act", lambda e: e.activation(out=za[:n, :], in_=za[:n, :], func=AF.Exp, scale=-1.0), reads=[B("za")], writes=[B("za")])
                S.op("dve", lambda e: e.tensor_scalar(out=za[:n, :], in0=za[:n, :], scalar1=1.0, scalar2=None, op0=ALU.add),
                     reads=[B("za")], writes=[B("za")])
                yield
                S.op("dve", lambda e: e.reciprocal(out=sa[:n, :], in_=za[:n, :]), reads=[B("za")], writes=[B("sa")])
                yield
                S.op("dve", lambda e: e.tensor_scalar(out=ld[:n, :], in0=sa[:n, 0:128], scalar1=-math.exp(-0.5), scalar2=None, op0=ALU.mult),
                     reads=[B("sa")], writes=[B("ld")])
                av = sa[:n, 128:256]
                yield
                S.op("dve", lambda e: e.tensor_tensor(out=kkv[:n, :], in0=xk, in1=kk_bc[:n, :], op=ALU.mult), reads=[B("xs_"), B("parb")], writes=[B("kkv")])
                S.op("dve", lambda e: e.tensor_tensor(out=mm_[:n, :], in0=kkv[:n, :], in1=kkv[:n, :], op=ALU.mult), reads=[B("kkv")], writes=[B("mm_")])
                S.op("dve", lambda e: e.tensor_reduce(out=rst[:n, 0:2], in_=mm_[:n, :].rearrange("p (a b) -> p a b", b=64), axis=AX.X, op=ALU.add),
                     reads=[B("mm_")], writes=[B("rst")])
                S.op("dve", lambda e: e.tensor_scalar(out=rst[:n, 0:2], in0=rst[:n, 0:2], scalar1=1e-18, scalar2=None, op0=ALU.max),
                     reads=[B("rst")], writes=[B("rst")])
                S.op("act", lambda e: e.activation(out=rst[:n, 0:2], in_=rst[:n, 0:2], func=AF.Ln), reads=[B("rst")], writes=[B("rst")])
                S.op("act", lambda e: e.activation(out=rst[:n, 0:2], in_=rst[:n, 0:2], func=AF.Exp, scale=-0.5), reads=[B("rst")], writes=[B("rst")])
                k3 = kkv[:n, :].rearrange("p (a b) -> p a b", b=64)
                S.op("dve", lambda e: e.tensor_tensor(out=k3, in0=k3, in1=bc3(rst[:n, 0:2], n, 2, 64), op=ALU.mult),
                     reads=[B("kkv"), B("rst")], writes=[B("kkv")])
                S.op("dve", lambda e: e.scalar_tensor_tensor(out=mm_[:n, :], in0=av, scalar=-1.0, in1=ka_bc[:n, :], op0=ALU.add, op1=ALU.mult),
                     reads=[B("sa"), B("parb")], writes=[B("mm_")])
                S.op("dve", lambda e: e.scalar_tensor_tensor(out=k2[:n, :], in0=mm_[:n, :], scalar=1.0, in1=xk, op0=ALU.add, op1=ALU.mult),
                     reads=[B("mm_"), B("xs_")], writes=[B("k2")])
                S.op("dve", lambda e: e.tensor_tensor(out=bvec[:n, :], in0=kkv[:n, :], in1=av, op=ALU.mult), reads=[B("kkv"), B("sa")], writes=[B("bvec")])
                S.op("dve", lambda e: e.tensor_tensor(out=mm_[:n, :], in0=xr_, in1=k2[:n, :], op=ALU.mult), reads=[B("xs_"), B("k2")], writes=[B("mm_")])
                S.op("dve", lambda e: e.tensor_tensor(out=mm_[:n, :], in0=mm_[:n, :], in1=rk_bc[:n, :], op=ALU.mult), reads=[B("mm_"), B("parb")], writes=[B("mm_")])
                S.op("dve", lambda e: e.tensor_reduce(out=bs[:n, 0:2], in_=mm_[:n, :].rearrange("p (a b) -> p a b", b=64), axis=AX.X, op=ALU.add),
                     reads=[B("mm_")], writes=[B("bs")])
                yield
                pc = gbank()
                S.op("pe", lambda e: e.matmul(PS[pc][:n, 0:128], lhsT=ut[:n, :n], rhs=ld[:n, :], start=True, stop=False, skip_group_check=True),
                     reads=[B("ut"), B("ld")], writes=[PB[pc]], inc=False)
                S.op("pe", lambda e: e.matmul(PS[pc][:n, 128:256], lhsT=ones[:n, :n], rhs=ld[:n, :], start=False, stop=False, skip_group_check=True),
                     reads=[B("ones"), B("ld")], pwrites=[PB[pc]], inc=False)
                for hh in range(2):
                    S.op("pe", lambda e, hh=hh: e.matmul(PS[pc][0:64, 256 + hh:257 + hh], lhsT=ld[:n, hh * 64:(hh + 1) * 64], rhs=ones[:n, 0:1],
                                                         start=False, stop=(hh == 1), skip_group_check=True),
                         reads=[B("ones"), B("ld")], pwrites=[PB[pc]], inc=(hh == 1))
                S.op("act", lambda e: e.activation(out=cum[:n, :], in_=PS[pc][:n, 0:128], func=AF.Copy), reads=[PB[pc]], writes=[B("cum")])
                S.op("act", lambda e: e.activation(out=ec[:n, :], in_=PS[pc][:n, 0:128], func=AF.Exp), reads=[PB[pc]], writes=[B("ec")])
                S.op("act", lambda e: e.activation(out=eci[:n, :], in_=PS[pc][:n, 0:128], func=AF.Exp, scale=-1.0), reads=[PB[pc]], writes=[B("eci")])
                S.op("act", lambda e: e.activation(out=gC[:, 0:2], in_=PS[pc][0:64, 256:258], func=AF.Exp), reads=[PB[pc]], writes=[B("gC")])
                S.op("dve", lambda e: e.tensor_tensor(out=ee[:n, :], in0=cum[:n, :], in1=ld[:n, :], op=ALU.subtract), reads=[B("cum"), B("ld")], writes=[B("ee")])
                S.op("act", lambda e: e.activation(out=ee[:n, :], in_=ee[:n, :], func=AF.Exp), reads=[B("ee")], writes=[B("ee")])
                S.op("dve", lambda e: e.tensor_tensor(out=eh[:n, :], in0=PS[pc][:n, 128:256], in1=cum[:n, :], op=ALU.subtract), reads=[PB[pc], B("cum")], writes=[B("eh")])
                S.op("act", lambda e: e.activation(out=eh[:n, :], in_=eh[:n, :], func=AF.Exp), reads=[B("eh")], writes=[B("eh")])
                yield
                S.op("dve", lambda e: e.tensor_tensor(out=rt[:n, :], in0=xr_, in1=ec[:n, :], op=ALU.mult), reads=[B("xs_"), B("ec")], writes=[B("rt")])
                for hh in range(2):
                    S.op("dve", lambda e, hh=hh: e.scalar_tensor_tensor(out=RH[hh][:n, 0:64], in0=kkv[:n, hh * 64:(hh + 1) * 64], scalar=-1.0,
                                                                        in1=ee[:n, hh * 64:(hh + 1) * 64], op0=ALU.mult, op1=ALU.mult),
                         reads=[B("kkv"), B("ee")], writes=[B(RH[hh].name)])
                S.op("dve", lambda e: e.tensor_tensor(out=bt[:n, :], in0=bvec[:n, :], in1=eci[:n, :], op=ALU.mult), reads=[B("bvec"), B("eci")], writes=[B("bt")])
                S.op("dve", lambda e: e.tensor_tensor(out=ktl[:n, :], in0=k2[:n, :], in1=eci[:n, :], op=ALU.mult), reads=[B("k2"), B("eci")], writes=[B("ktl")])
                S.op("dve", lambda e: e.tensor_tensor(out=bh[:n, :], in0=bvec[:n, :], in1=eh[:n, :], op=ALU.mult), reads=[B("bvec"), B("eh")], writes=[B("bh")])
                S.op("dve", lambda e: e.tensor_tensor(out=kh[:n, :], in0=k2[:n, :], in1=eh[:n, :], op=ALU.mult), reads=[B("k2"), B("eh")], writes=[B("kh")])
                nlev = 7 if n == 128 else 6
                yield
                def head_gen(hh):
                    hs = slice(hh * 64, (hh + 1) * 64)
                    pf = gbank()
                    srcs = ((RH[hh][:n, 0:64], B(RH[hh].name)), (rt[:n, hs], B("rt")), (bt[:n, hs], B("bt")), (ktl[:n, hs], B("ktl")))
                    for i4, (sap, sB) in enumerate(srcs):
                        S.op("pe", lambda e, i4=i4, sap=sap: e.transpose(out=PS[pf][0:64, i4 * n:(i4 + 1) * n], in_=sap, identity=ident[:n, :n]),
                             reads=[sB, B("ident")], writes=[PB[pf]] if i4 == 0 else [], pwrites=[] if i4 == 0 else [PB[pf]], inc=(i4 == 3))
                    S.op("act", lambda e, hh=hh: e.activation(out=FT[hh][:, 0:4 * n], in_=PS[pf][0:64, 0:4 * n], func=AF.Copy),
                         reads=[PB[pf]], writes=[B(FT[hh].name)])
                    F_ = FT[hh]; FB = B(F_.name)
                    yield
                    pg = gbank()
                    S.op("pe", lambda e, F_=F_: e.matmul(PS[pg][:n, 0:2 * n], lhsT=F_[:, 2 * n:3 * n], rhs=F_[:, 0:2 * n], start=True, stop=False, skip_group_check=True),
                         reads=[FB], writes=[PB[pg]], inc=False)
                    S.op("pe", lambda e, F_=F_: e.matmul(PS[pg][:n, 2 * n:4 * n], lhsT=F_[:, 3 * n:4 * n], rhs=F_[:, 0:2 * n], start=False, stop=True, skip_group_check=True),
                         reads=[FB], pwrites=[PB[pg]])
                    G_ = GM[hh]; GB = B(G_.name)
                    S.op("dve", lambda e, G_=G_: e.tensor_tensor(out=G_[:n, 0:4 * n].rearrange("p (a b) -> p a b", b=n),
                                                                 in0=PS[pg][:n, 0:4 * n].rearrange("p (a b) -> p a b", b=n),
                                                                 in1=gmask[:n, :].rearrange("p (a b) -> p a b", b=128)[:, :, 0:n], op=ALU.mult),
                         reads=[PB[pg], B("gmask")], writes=[GB])
                    px = gbank()
                    S.op("pe", lambda e, F_=F_: e.matmul(PS[px][:n, 0:n], lhsT=F_[:, 0:n], rhs=F_[:, 2 * n:3 * n], start=True, stop=True, skip_group_check=True),
                         reads=[FB], writes=[PB[px]])
                    xa, xb_ = Xa[hh], Xb[hh]
                    S.op("dve", lambda e, xb_=xb_: e.tensor_tensor(out=xb_[0][:n, :n], in0=PS[px][:n, 0:n], in1=lowm[:n, :n], op=ALU.mult),
                         reads=[PB[px], B("lowm")], writes=[B(xb_[0].name)])
                    yield
                    acc = ACC[hh]
                    S.op("dve", lambda e, acc=acc, G_=G_: e.tensor_tensor(out=acc[0][:n, :n], in0=G_[:n, 0:n], in1=ident[:n, :n], op=ALU.add),
                         reads=[GB, B("ident")], writes=[B(acc[0].name)])
                    accb = ACCB[hh]
                    S.op("act", lambda e, acc=acc, accb=accb: e.activation(out=accb[0][:n, :n], in_=acc[0][:n, :n], func=AF.Copy),
                         reads=[B(acc[0].name)], writes=[B(accb[0].name)])
                    S.op("act", lambda e, G_=G_: e.activation(out=XG[hh][:n, :n], in_=G_[:n, 0:n], func=AF.Copy),
                         reads=[GB], writes=[B(XG[hh].name)])
                    curX_ap, curXB = XG[hh][:n, :n], B(XG[hh].name)
                    cs_ = 0
                    for lev in range(1, nlev):
                        curXp = xb_[cs_]
                        nxt = 1 - cs_
                        p2 = gbank()
                        lastlev = (lev == nlev - 1)
                        S.op("pe", lambda e, curX_ap=curX_ap, curXp=curXp: e.matmul(PS[p2][:n, 0:n], lhsT=curX_ap, rhs=curXp[:n, :n], start=True, stop=lastlev, skip_group_check=True),
                             reads=[curXB, B(curXp.name)], writes=[PB[p2]], inc=lastlev)
                        if not lastlev:
                            S.op("pe", lambda e, curX_ap=curX_ap, curXp=curXp: e.matmul(PS[p2][:n, 128:128 + n], lhsT=curXp[:n, :n], rhs=curX_ap, start=False, stop=True, skip_group_check=True),
                                 reads=[curXB, B(curXp.name)], pwrites=[PB[p2]])
                        S.op("act", lambda e, xb_=xb_, nxt=nxt: e.activation(out=xb_[nxt][:n, :n], in_=PS[p2][:n, 0:n], func=AF.Copy),
                             reads=[PB[p2]], writes=[B(xb_[nxt].name)])
                        if not lastlev:
                            S.op("dve", lambda e, xa=xa, nxt=nxt: e.tensor_copy(out=xa[nxt][:n, :n], in_=PS[p2][:n, 128:128 + n]),
                                 reads=[PB[p2]], writes=[B(xa[nxt].name)])
                        a_cur = acc[(lev - 1) % 2]; a_nxt = acc[lev % 2]
                        ab_cur = accb[(lev - 1) % 2]; ab_nxt = accb[lev % 2]
                        p3 = gbank()
                        S.op("pe", lambda e, xb_=xb_, nxt=nxt, ab_cur=ab_cur: e.matmul(PS[p3][:n, 0:n], lhsT=xb_[nxt][:n, :n], rhs=ab_cur[:n, :n], start=True, stop=True, skip_group_check=True),
                             reads=[B(xb_[nxt].name), B(ab_cur.name)], writes=[PB[p3]])
                        S.op("dve", lambda e, a_cur=a_cur, a_nxt=a_nxt: e.tensor_tensor(out=a_nxt[:n, :n], in0=PS[p3][:n, 0:n], in1=a_cur[:n, :n], op=ALU.add),
                             reads=[PB[p3], B(a_cur.name)], writes=[B(a_nxt.name)])
                        if not lastlev:
                            S.op("act", lambda e, a_nxt=a_nxt, ab_nxt=ab_nxt: e.activation(out=ab_nxt[:n, :n], in_=a_nxt[:n, :n], func=AF.Copy),
                                 reads=[B(a_nxt.name)], writes=[B(ab_nxt.name)])
                        curX_ap, curXB = xa[nxt][:n, :n], B(xa[nxt].name)
                        cs_ = nxt
                        yield
                    MT = acc[(nlev - 1) % 2]
                    yield
                    MTB = B(MT.name)
                    vh = xs_[:n, 256 + hh * 64:256 + (hh + 1) * 64]
                    pa_ = gbank()
                    S.op("pe", lambda e, G_=G_, vh=vh: e.matmul(PS[pa_][:n, 0:64], lhsT=G_[:n, 2 * n:3 * n], rhs=vh, start=True, stop=True, skip_group_check=True),
                         reads=[GB, B("xs_")], writes=[PB[pa_]])
                    S.op("act", lambda e, hh=hh: e.activation(out=RH[hh][:n, 64:128], in_=PS[pa_][:n, 0:64], func=AF.Copy),
                         reads=[PB[pa_]], pwrites=[B(RH[hh].name)])
                    ppu = gbank()
                    S.op("pe", lambda e, MT=MT, hh=hh: e.matmul(PS[ppu][:n, 0:128], lhsT=MT[:n, :n], rhs=RH[hh][:n, :], start=True, stop=True, skip_group_check=True),
                         reads=[MTB, B(RH[hh].name)], writes=[PB[ppu]])
                    S.op("act", lambda e, hh=hh: e.activation(out=PU[hh][:n, :], in_=PS[ppu][:n, 0:128], func=AF.Copy), reads=[PB[ppu]], writes=[B(PU[hh].name)])
                    PUB = B(PU[hh].name)
                    yield
                    py = gbank()
                    S.op("pe", lambda e, hh=hh, G_=G_: e.matmul(PS[py][:n, 128:192], lhsT=G_[:n, n:2 * n], rhs=PU[hh][:n, 64:128], start=True, stop=False, skip_group_check=True),
                         reads=[PUB, GB], writes=[PB[py]], inc=False)
                    S.op("pe", lambda e, hh=hh, G_=G_, vh=vh: e.matmul(PS[py][:n, 128:192], lhsT=G_[:n, 3 * n:4 * n], rhs=vh, start=False, stop=False, skip_group_check=True),
                         reads=[GB, B("xs_")], pwrites=[PB[py]], inc=False)
                    S.op("pe", lambda e, hh=hh, G_=G_: e.matmul(PS[py][0:64, 0:n], lhsT=PU[hh][:n, 0:64], rhs=G_[:n, n:2 * n], start=False, stop=False, skip_group_check=True),
                         reads=[PUB, GB], pwrites=[PB[py]], inc=False)
                    S.op("pe", lambda e, hh=hh, hs=hs: e.matmul(PS[py][0:64, 192:256], lhsT=PU[hh][:n, 0:64], rhs=bh[:n, hs], start=False, stop=False, skip_group_check=True),
                         reads=[PUB, B("bh")], pwrites=[PB[py]], inc=False)
                    S.op("pe", lambda e, hh=hh, hs=hs: e.matmul(PS[py][0:64, 256:320], lhsT=bh[:n, hs], rhs=PU[hh][:n, 64:128], start=False, stop=False, skip_group_check=True),
                         reads=[PUB, B("bh")], pwrites=[PB[py]], inc=False)
                    S.op("pe", lambda e, hh=hh, hs=hs, vh=vh: e.matmul(PS[py][0:64, 256:320], lhsT=kh[:n, hs], rhs=vh, start=False, stop=True, skip_group_check=True),
                         reads=[B("kh"), B("xs_")], pwrites=[PB[py]])
                    S.op("dve", lambda e, hh=hh, F_=F_: e.tensor_tensor(out=Y1T[hh][:, 0:n], in0=PS[py][0:64, 0:n], in1=F_[:, n:2 * n], op=ALU.add),
                         reads=[PB[py], FB], writes=[B(Y1T[hh].name)])
                    S.op("act", lambda e, hh=hh: e.activation(out=Y2[hh][:n, :], in_=PS[py][:n, 128:192], func=AF.Copy), reads=[PB[py]], writes=[B(Y2[hh].name)])
                    S.op("dve", lambda e, hh=hh: e.scalar_tensor_tensor(out=T1T[hh][:, :], in0=ident[0:64, 0:64], scalar=gC[:, hh:hh + 1], in1=PS[py][0:64, 192:256],
                                                                        op0=ALU.mult, op1=ALU.add),
                         reads=[PB[py], B("ident"), B("gC")], writes=[B(T1T[hh].name)])
                    S.op("act", lambda e, hh=hh: e.activation(out=T2[hh][:, :], in_=PS[py][0:64, 256:320], func=AF.Copy), reads=[PB[py]], writes=[B(T2[hh].name)])
                    yield
                    Hc = Hs[hh][hslot]; Hn = Hs[hh][1 - hslot]
                    ph = gbank()
                    S.op("pe", lambda e, hh=hh, Hc=Hc: e.matmul(PS[ph][:n, 0:64], lhsT=Y1T[hh][:, 0:n], rhs=Hc[:, :], start=True, stop=False, skip_group_check=True),
                         reads=[B(Y1T[hh].name), B(Hc.name)], writes=[PB[ph]], inc=False)
                    S.op("pe", lambda e, hh=hh, Hc=Hc: e.matmul(PS[ph][0:64, 64:128], lhsT=T1T[hh][:, :], rhs=Hc[:, :], start=False, stop=True, skip_group_check=True),
                         reads=[B(T1T[hh].name), B(Hc.name)], pwrites=[PB[ph]])
                    S.op("dve", lambda e, hh=hh, hs=hs: e.tensor_tensor(out=yb[:n, hs], in0=PS[ph][:n, 0:64], in1=Y2[hh][:n, :], op=ALU.add),
                         reads=[PB[ph], B(Y2[hh].name)], writes=[B("yb")] if hh == 0 else [], pwrites=[] if hh == 0 else [B("yb")])
                    S.op("dve", lambda e, hh=hh, Hn=Hn: e.tensor_tensor(out=Hn[:, :], in0=PS[ph][0:64, 64:128], in1=T2[hh][:, :], op=ALU.add),
                         reads=[PB[ph], B(T2[hh].name)], writes=[B(Hn.name)])
                gens = [head_gen(0), head_gen(1)]
                while gens:
                    for g_ in list(gens):
                        try:
                            next(g_)
                        except StopIteration:
                            gens.remove(g_)
                    yield
                yield
                y3 = yb[:n, :].rearrange("p (a b) -> p a b", b=64)
                yc3 = yc[:n, :].rearrange("p (a b) -> p a b", b=64)
                S.op("dve", lambda e: e.tensor_reduce(out=rst[:n, 2:4], in_=y3, axis=AX.X, op=ALU.add), reads=[B("yb")], writes=[B("rst")])
                S.op("dve", lambda e: e.tensor_scalar(out=rst[:n, 2:4], in0=rst[:n, 2:4], scalar1=-1.0 / 64, scalar2=None, op0=ALU.mult), reads=[B("rst")], writes=[B("rst")])
                S.op("dve", lambda e: e.tensor_tensor(out=yc3, in0=y3, in1=bc3(rst[:n, 2:4], n, 2, 64), op=ALU.add), reads=[B("yb"), B("rst")], writes=[B("yc")])
                S.op("dve", lambda e: e.tensor_tensor(out=ysq[:n, :], in0=yc[:n, :], in1=yc[:n, :], op=ALU.mult), reads=[B("yc")], writes=[B("ysq")])
                S.op("dve", lambda e: e.tensor_reduce(out=rst[:n, 4:6], in_=ysq[:n, :].rearrange("p (a b) -> p a b", b=64), axis=AX.X, op=ALU.add),
                     reads=[B("ysq")], writes=[B("rst")])
                rstd_chain(rst[:n, 4:6], rst[:n, 4:6], n, 2, 1.0 / 64, GN_EPS, B("rst"), B("rst"))
                S.op("dve", lambda e: e.tensor_tensor(out=yc3, in0=yc3, in1=bc3(rst[:n, 4:6], n, 2, 64), op=ALU.mult), reads=[B("yc"), B("rst")], writes=[B("yc")])
                S.op("dve", lambda e: e.tensor_tensor(out=yc[:n, :], in0=yc[:n, :], in1=lg_bc[:n, :], op=ALU.mult), reads=[B("yc"), B("parb")], writes=[B("yc")])
                S.op("dve", lambda e: e.tensor_tensor(out=yc[:n, :], in0=yc[:n, :], in1=lb_bc[:n, :], op=ALU.add), reads=[B("yc"), B("parb")], writes=[B("yc")])
                S.op("dve", lambda e: e.tensor_tensor(out=ysq[:n, :].rearrange("p (a b) -> p a b", b=64), in0=xv.rearrange("p (a b) -> p a b", b=64),
                                                      in1=bc3(bs[:n, 0:2], n, 2, 64), op=ALU.mult), reads=[B("xs_"), B("bs")], writes=[B("ysq")])
                S.op("dve", lambda e: e.tensor_tensor(out=yc[:n, :], in0=yc[:n, :], in1=ysq[:n, :], op=ALU.add), reads=[B("yc"), B("ysq")], writes=[B("yc")])
                S.op("dve", lambda e: e.tensor_tensor(out=obb[:n, :], in0=yc[:n, :], in1=g_sb[:n, :], op=ALU.mult), reads=[B("yc"), B("g_sb")], writes=[B("obb")])
                pb = gbank()
                pv = PS[pb][:].bitcast(BF16)
                S.op("pe", lambda e: e.transpose(out=pv[:, 0:n], in_=obb[:n, :], identity=identb[:n, :n]), reads=[B("obb"), B("identb")], writes=[PB[pb]])
                cat_writer(1, tile_idx, pv[:, 0:n], PB[pb])
                yield

            def state_out(hslot, dst_ap):
                pb = gbank()
                for hh in range(2):
                    Hc = Hs[hh][hslot]
                    S.op("pe", lambda e, hh=hh, Hc=Hc: e.transpose(out=PS[pb][0:64, hh * 64:(hh + 1) * 64], in_=Hc[:, :], identity=ident[0:64, 0:64]),
                         reads=[B(Hc.name), B("ident")], writes=[PB[pb]] if hh == 0 else [], pwrites=[] if hh == 0 else [PB[pb]], inc=(hh == 1))
                S.op("act", lambda e: e.activation(out=Hout[:, :, :], in_=PS[pb][0:64, 0:128].rearrange("p (a b) -> p a b", b=64), func=AF.Copy),
                     reads=[PB[pb]], writes=[B("Hout")])
                S.dma("sp", dst_ap.rearrange("h v k -> v h k"), Hout[:, :, :], reads=[B("Hout")])

            def run(g):
                for _ in g:
                    pass

            def drive(main, side, ratio):
                acc_ = 0.0
                for _ in main:
                    drive.n += 1
                    assert not any(S.uncommitted.values()), "yield inside an uncommitted group"
                    if side is not None:
                        acc_ += ratio
                        while acc_ >= 1.0 and side is not None:
                            acc_ -= 1.0
                            try:
                                next(side)
                            except StopIteration:
                                side = None
                if side is not None:
                    run(side)

            drive.n = 0
            XIv = XI.rearrange("(r c two p) t -> r c two p t", r=4, c=8, two=2, p=128)

            def make_writer(hp):
                def w_(kind, ti, src_ap, srcB):
                    rng_ = (ti * 128) // TR
                    toff_ = ti * 128 - rng_ * TR
                    ct = catA if kind == 0 else catB
                    cB = B(ct.name)
                    S.op("dve", lambda e: e.tensor_tensor(out=ct[:, :, :], in0=src_ap.unsqueeze(1).to_broadcast([128, 4, 128]),
                                                          in1=selt[:, 0:4].unsqueeze(2).to_broadcast([128, 4, 128]), op=ALU.mult),
                         reads=[srcB, B("selt")], writes=[cB])
                    c0 = 4 if kind == 1 else 0
                    S.dma("sp", XIv[rng_, c0:c0 + 4, hp, :, toff_:toff_ + 128].rearrange("j p t -> p j t"), ct[:, :, :],
                          reads=[cB], pwrites=[B("XI")])
                return w_

            def stage1(hp, i):
                slot = i % 2
                qb = (i // 2) % 2
                S.dma("sp", cstt[slot][:, :], csp[i * 128:(i + 1) * 128, :], writes=[B(cstt[slot].name)])
                yield from front(xb[i * 128:(i + 1) * 128, :], 128, xnTall, slot * 128, XNB[slot], g1t, slot)
                yield from project(xnTall, slot * 128, XNB[slot], 128, slot)
                yield from attn_prep(128, cstt[slot][:, :], B(cstt[slot].name), kpo[i * 128:(i + 1) * 128, hp, :], vpo[i * 128:(i + 1) * 128, hp, :],
                                     i, i * 128, (i % 2) * 128, QTs[qb], B(QTs[qb].name))

            def stage2(hp, i, writer):
                slot = i % 2
                prev_ap = None if i == 0 else prw[1 - slot]
                yield from rwkv(128, slot, prev_ap, None if i == 0 else B(prw[1 - slot].name), c128[:, :], i == 0, i % 2, writer, i)

            def interleave(g1, g2):
                gens = [g for g in (g1, g2) if g is not None]
                while gens:
                    for g_ in list(gens):
                        try:
                            next(g_)
                        except StopIteration:
                            gens.remove(g_)
                    yield

            def drive_keep(main, side, ratio):
                acc_ = 0.0
                for _ in main:
                    drive.n += 1
                    assert not any(S.uncommitted.values()), "yield inside an uncommitted group"
                    if side is not None:
                        acc_ += ratio
                        while acc_ >= 1.0 and side is not None:
                            acc_ -= 1.0
                            try:
                                next(side)
                            except StopIteration:
                                side = None
                return side

            def attn_stream(hp, Q, writer):
                kts = [(j * 128, 128, j, None) for j in range(2 * Q)]
                kts.append((2 * Q * 128, 128, 2 * Q, 0))
                kts.append(((2 * Q + 1) * 128, 128, 2 * Q + 1, 1))
                qb = Q % 2
                yield from attention(256, 128, kts, writer, QTs[qb], B(QTs[qb].name), 2 * Q)

            SEG_YIELDS = 25.0
            for hp in range(2):
                wsrc_name[0] = "wmyb%d" % hp
                load_unit_w(wmyb[hp])
                load_unit_params(upmy[hp:hp + 1, :], lwmy[hp], lg2my[hp])
                for hh in range(2):
                    S.op("dve", lambda e, hh=hh: e.memset(Hs[hh][0][:, :], 0.0), writes=[B(Hs[hh][0].name)])
                writer = make_writer(hp)
                run(stage1(hp, 0))
                side = None
                ratio = 0.0
                for i in range(NT):
                    if dcast[0] is not None and (i >= 4 or NT <= 8):
                        try:
                            for _ in range(12 if NT > 8 else 1000000):
                                next(dcast[0])
                        except StopIteration:
                            dcast[0] = None
                    if i % 2 == 1:
                        if side is not None:
                            run(side)
                        Q = i // 2
                        side = attn_stream(hp, Q, writer)
                        ratio = (2 * Q + 6) / (2 * SEG_YIELDS)
                    seg = interleave(stage2(hp, i, writer), stage1(hp, i + 1) if i + 1 < NT else None)
                    side = drive_keep(seg, side, ratio)
                if side is not None:
                    run(side)
                if hp == 1 and dcast[0] is not None:
                    run(dcast[0])
                    dcast[0] = None
                build_nc.main_yields = drive.n / float(NT) / (hp + 1)
                state_out(NT % 2, spo[hp])
                lastp = prw[(NT - 1) % 2]
                S.dma("sp", shpo[hp:hp + 1, :], lastp[127:128, :], reads=[B(lastp.name)])

            S.custom("pool", lambda e: e.collective_compute("ReduceScatter", ALU.add, replica_groups=[[0, 1, 2, 3], [4, 5, 6, 7]],
                                                             ins=[XI[:, :]], outs=[XO[:, :]]),
                     reads=[B("XI")], writes=[B("XO")])
            ckpt("rs")

            ckpt("casts")

            ntile_s = (NB * 64 + 127) // 128
            for t_ in range(ntile_s):
                n_ = min(128, NB * 64 - t_ * 128)
                run(front(xsm[t_ * 128:t_ * 128 + n_, :], n_, xnTall, t_ * 128, XNB[t_], g1t, t_ % 2))
            kst = sb("kst", [128, 128]); kstb = sb("kstb", [128, 128], BF16)
            kst2 = sb("kst2", [128, 128])

            def make_writer_s(u, bb):
                def cat_writer_s(kind, qt_, src_ap, srcB):
                    chunk = u + (8 if kind == 1 else 0)
                    ct = catA if kind == 0 else catB
                    cB = B(ct.name)
                    S.op("act", lambda e: e.activation(out=ct[:, 0, 0:64], in_=src_ap, func=AF.Copy), reads=[srcB], writes=[cB])
                    S.dma("sp", XS[chunk * 128:(chunk + 1) * 128, bb * 64:(bb + 1) * 64], ct[:, 0, 0:64], reads=[cB], pwrites=[B("XS")])
                return cat_writer_s

            def sample_pre(u, bb, idx):
                base = (idx % 2) * (NP + 1)
                qb = idx % 2
                for pt in range(NP):
                    kt_i = base + pt
                    S.dma("sp", kst[:, :], ck[bb, pt * 128:(pt + 1) * 128, u, :], writes=[B("kst")])
                    S.op("dve", lambda e: e.tensor_copy(out=kstb[:, :], in_=kst[:, :]), reads=[B("kst")], writes=[B("kstb")])
                    pb = gbank()
                    pv = PS[pb][:].bitcast(BF16)
                    S.op("pe", lambda e, pv=pv: e.transpose(out=pv[:, 0:128], in_=kstb[:, :], identity=identb[:, :]),
                         reads=[B("kstb"), B("identb")], writes=[PB[pb]])
                    S.op("act", lambda e, pv=pv: e.activation(out=KT[:, kt_i * 128:(kt_i + 1) * 128], in_=pv[:, 0:128], func=AF.Copy),
                         reads=[PB[pb]], writes=[KTB[kt_i]])
                    S.dma("sp", kst2[:, :], cv[bb, pt * 128:(pt + 1) * 128, u, :], writes=[B("kst2")])
                    S.op("dve", lambda e: e.tensor_copy(out=VV[:, kt_i, :], in_=kst2[:, :]), reads=[B("kst2")], writes=[VB[kt_i]])
                    if pt % 2 == 1:
                        yield
                S.dma("sp", Hld[:, :, :], st0[bb, 2 * u:2 * u + 2].rearrange("h v k -> v h k"), writes=[B("Hld")])
                pb = gbank()
                for hh in range(2):
                    S.op("pe", lambda e, hh=hh: e.transpose(out=PS[pb][0:64, hh * 64:(hh + 1) * 64], in_=Hld[:, hh, :], identity=ident[0:64, 0:64]),
                         reads=[B("Hld"), B("ident")], writes=[PB[pb]] if hh == 0 else [], pwrites=[] if hh == 0 else [PB[pb]], inc=(hh == 1))
                for hh in range(2):
                    S.op("act", lambda e, hh=hh: e.activation(out=Hs[hh][0][:, :], in_=PS[pb][0:64, hh * 64:(hh + 1) * 64], func=AF.Copy),
                         reads=[PB[pb]], writes=[B(Hs[hh][0].name)])
                S.op("dve", lambda e: e.memset(shbuf[0:63, :], 0.0), writes=[B("shbuf")])
                S.dma("sp", shbuf[63:64, :], sh0[bb, u:u + 1, :], reads=[], writes=[], pwrites=[B("shbuf")])
                yield
                yield from project(xnTall, bb * 64, XNB[bb // 2], 64, 0)
                yield from attn_prep(64, csst[:, :], B("csst"), kso[bb, :, u, :], vso[bb, :, u, :], base + NP, (base + NP) * 128, 0,
                                     QTs[qb], B(QTs[qb].name))
                yield from rwkv(64, 0, shbuf[0:64, :], B("shbuf"), c64[:, :], False, 0, make_writer_s(u, bb), 0)
                state_out(1, sso[bb, u])
                S.dma("sp", shso[bb, u:u + 1, :], prw[0][63:64, :], reads=[B(prw[0].name)])
                yield

            def sample_attn(u, bb, idx):
                base = (idx % 2) * (NP + 1)
                qb = idx % 2
                kts = [((base + j) * 128, 128, base + j, None) for j in range(NP)] + [((base + NP) * 128, 64, base + NP, None)]
                yield from attention(64, 64, kts, make_writer_s(u, bb), QTs[qb], B(QTs[qb].name), 0)

            side_s = None
            for u in range(8):
                wsrc_name[0] = "wub%d" % u
                load_unit_w(wub[u])
                load_unit_params(up[u:u + 1, :], lw[u], lg2[u])
                for bb in range(NB):
                    idx = u * NB + bb
                    side_s = drive_keep(sample_pre(u, bb, idx), side_s, (NP + 3) / 40.0)
                    if side_s is not None:
                        run(side_s)
                    side_s = sample_attn(u, bb, idx)
            if side_s is not None:
                run(side_s)

            ckpt("sample")
            S.barrier()
            es1.close()
            cur[0] = es
            NTOK = 512
            catT = sb("catT", [128, NCH, NTOK], BF16)
            hsb = sb("hsb", [128, NTOK // 128, D])
            hnT = sb("hnT", [128, NCH, NTOK], BF16)
            actT = sb("actT", [128, NFF, NTOK], BF16)
            wring = [sb("wring%d" % i, [128, 16 * 512], BF16) for i in range(3)]
            wrr = [0]
            sg = sb("sg", [128, 512])

            def wbuf():
                i = wrr[0]
                wrr[0] = (wrr[0] + 1) % 3
                return wring[i]

            blocks = []
            t0 = 0
            while t0 < TR:
                nb_ = min(NTOK, TR - t0)
                blocks.append(("p", t0, nb_))
                t0 += nb_
            t0 = 0
            while t0 < NB * 64:
                nb_ = min(NTOK, NB * 64 - t0)
                blocks.append(("s", t0, nb_))
                t0 += nb_
            for (kind, t0, nb_) in blocks:
                src = XO if kind == "p" else XS
                srcB = B("XO") if kind == "p" else B("XS")
                xsrc = xr if kind == "p" else xsm
                ydst = yp if kind == "p" else ys
                S.dma("sp", catT[:, :, 0:nb_], src[:, t0:t0 + nb_].rearrange("(c p) n -> p c n", p=128), reads=[srcB], writes=[B("catT")])
                ntl = (nb_ + 127) // 128
                tls = [(tt * 128, min(128, nb_ - tt * 128)) for tt in range(ntl)]
                for tt, (o_, n_) in enumerate(tls):
                    S.dma("sp", hsb[:n_, tt, :], xsrc[t0 + o_:t0 + o_ + n_, :], writes=[B("hsb%d" % tt)])
                for ng in range(4):
                    wb = wbuf(); wB = B(wb.name)
                    w3 = wb[:, :].rearrange("p (c n) -> p c n", n=512)
                    S.dma("sp", w3, wob[:, ng * 512:(ng + 1) * 512].rearrange("(c p) n -> p c n", p=128), reads=[B("wob")], writes=[wB])
                    for tt, (o_, n_) in enumerate(tls):
                        pb = tt % 4
                        for k in range(NCH):
                            S.op("pe", lambda e, k=k, pb=pb, o_=o_, n_=n_: e.matmul(PS[pb][:n_, 0:512], lhsT=catT[:, k, o_:o_ + n_], rhs=w3[:, k, :],
                                                                                  start=(k == 0), stop=(k == NCH - 1), skip_group_check=True),
                                 reads=[B("catT"), wB], writes=[PB[pb]] if k == 0 else [], pwrites=[] if k == 0 else [PB[pb]], inc=(k == NCH - 1))
                        S.op("dve", lambda e, tt=tt, pb=pb, n_=n_, ng=ng: e.tensor_tensor(out=hsb[:n_, tt, ng * 512:(ng + 1) * 512], in0=PS[pb][:n_, 0:512],
                                                                                         in1=hsb[:n_, tt, ng * 512:(ng + 1) * 512], op=ALU.add),
                             reads=[PB[pb], B("hsb%d" % tt)], writes=[B("hsb%d" % tt)])
                for tt, (o_, n_) in enumerate(tls):
                    hB = B("hsb%d" % tt)
                    S.op("dve", lambda e, tt=tt, n_=n_: e.scalar_tensor_tensor(out=junk[:n_, :], in0=hsb[:n_, tt, :], scalar=1.0, in1=hsb[:n_, tt, :], op0=ALU.mult, op1=ALU.mult, accum_out=st4[:n_, 0:1]),
                         reads=[hB], writes=[B("junk"), B("st4")])
                    rstd_chain(st4[:n_, 0:1], st4[:n_, 0:1], n_, 1, 1.0 / D, RMS_EPS, B("st4"), B("st4"))
                    S.op("act", lambda e, tt=tt, n_=n_: e.activation(out=xsb[:n_, :], in_=hsb[:n_, tt, :], func=AF.Copy, scale=st4[:n_, 0:1]),
                         reads=[hB, B("st4")], writes=[B("xsb")])
                    for half in range(2):
                        pb = 4 + half
                        pv = PS[pb][:].bitcast(BF16)
                        for c8 in range(8):
                            c = half * 8 + c8
                            S.op("pe", lambda e, c=c, c8=c8, pv=pv, n_=n_: e.transpose(out=pv[:, c8 * n_:(c8 + 1) * n_], in_=xsb[:n_, c * 128:(c + 1) * 128],
                                                                                     identity=identb[:n_, :n_]),
                                 reads=[B("xsb"), B("identb")], writes=[PB[pb]] if c8 == 0 else [], pwrites=[] if c8 == 0 else [PB[pb]], inc=(c8 == 7))
                        S.op("dve", lambda e, half=half, pv=pv, n_=n_, o_=o_: e.tensor_tensor(
                            out=hnT[:, half * 8:(half + 1) * 8, o_:o_ + n_], in0=pv[:, 0:8 * n_].rearrange("p (c n) -> p c n", n=n_),
                            in1=bc3(g2t[:, half * 8:(half + 1) * 8], 128, 8, n_), op=ALU.mult),
                            reads=[PB[pb], B("g2t")], writes=[B("hnT")] if (tt == 0 and half == 0) else [], pwrites=[] if (tt == 0 and half == 0) else [B("hnT")])
                for fg in range(NFF // 4):
                    wg_ = wbuf(); wgB = B(wg_.name)
                    wg3 = wg_[:, :].rearrange("p (c n) -> p c n", n=512)
                    S.dma("sp", wg3, wgb[:, fg * 512:(fg + 1) * 512].rearrange("(c p) n -> p c n", p=128), reads=[B("wgb")], writes=[wgB])
                    wu_ = wbuf(); wuB = B(wu_.name)
                    wu3 = wu_[:, :].rearrange("p (c n) -> p c n", n=512)
                    S.dma("sp", wu3, wupb[:, fg * 512:(fg + 1) * 512].rearrange("(c p) n -> p c n", p=128), reads=[B("wupb")], writes=[wuB])
                    for f4 in range(4):
                        f = fg * 4 + f4
                        pg_, pu_ = (f % 2) * 2, (f % 2) * 2 + 1
                        for k in range(NCH):
                            S.op("pe", lambda e, k=k, pg_=pg_, f4=f4: e.matmul(PS[pg_][:, 0:nb_], lhsT=wg3[:, k, f4 * 128:(f4 + 1) * 128], rhs=hnT[:, k, 0:nb_],
                                                                             start=(k == 0), stop=(k == NCH - 1), skip_group_check=True),
                                 reads=[B("hnT"), wgB], writes=[PB[pg_]] if k == 0 else [], pwrites=[] if k == 0 else [PB[pg_]], inc=(k == NCH - 1))
                        for k in range(NCH):
                            S.op("pe", lambda e, k=k, pu_=pu_, f4=f4: e.matmul(PS[pu_][:, 0:nb_], lhsT=wu3[:, k, f4 * 128:(f4 + 1) * 128], rhs=hnT[:, k, 0:nb_],
                                                                             start=(k == 0), stop=(k == NCH - 1), skip_group_check=True),
                                 reads=[B("hnT"), wuB], writes=[PB[pu_]] if k == 0 else [], pwrites=[] if k == 0 else [PB[pu_]], inc=(k == NCH - 1))
                        S.op("act", lambda e, pg_=pg_: e.activation(out=sg[:, 0:nb_], in_=PS[pg_][:, 0:nb_], func=AF.Silu), reads=[PB[pg_]], writes=[B("sg")])
                        S.op("dve", lambda e, pu_=pu_, f=f: e.tensor_tensor(out=actT[:, f, 0:nb_], in0=PS[pu_][:, 0:nb_], in1=sg[:, 0:nb_], op=ALU.mult),
                             reads=[PB[pu_], B("sg")], writes=[B("actT")] if f == 0 else [], pwrites=[] if f == 0 else [B("actT")])
                for ng in range(4):
                    for kq in range(4):
                        wb = wbuf(); wB = B(wb.name)
                        w3 = wb[:, 0:11 * 512].rearrange("p (c n) -> p c n", n=512)
                        S.dma("sp", w3, wdb[kq * 11 * 128:(kq + 1) * 11 * 128, ng * 512:(ng + 1) * 512].rearrange("(c p) n -> p c n", p=128),
                              reads=[B("wdb")], writes=[wB])
                        for tt, (o_, n_) in enumerate(tls):
                            pb = 4 + tt
                            for kc in range(11):
                                kk_ = kq * 11 + kc
                                first = (kk_ == 0); last = (kk_ == NFF - 1)
                                S.op("pe", lambda e, kc=kc, kk_=kk_, pb=pb, o_=o_, n_=n_, first=first, last=last: e.matmul(
                                    PS[pb][:n_, 0:512], lhsT=actT[:, kk_, o_:o_ + n_], rhs=w3[:, kc, :], start=first, stop=last, skip_group_check=True),
                                     reads=[B("actT"), wB], writes=[PB[pb]] if first else [], pwrites=[] if first else [PB[pb]], inc=(kc == 10))
                    for tt, (o_, n_) in enumerate(tls):
                        pb = 4 + tt
                        S.op("dve", lambda e, tt=tt, pb=pb, n_=n_, ng=ng: e.tensor_tensor(out=hsb[:n_, tt, ng * 512:(ng + 1) * 512], in0=PS[pb][:n_, 0:512],
                                                                                         in1=hsb[:n_, tt, ng * 512:(ng + 1) * 512], op=ALU.add),
                             reads=[PB[pb], B("hsb%d" % tt)], writes=[B("hsb%d" % tt)])
                for tt, (o_, n_) in enumerate(tls):
                    S.dma("sp", ydst[t0 + o_:t0 + o_ + n_, :], hsb[:n_, tt, :], reads=[B("hsb%d" % tt)])

        except _Stop:
            es1.close()
        for s, c in S.all_dma():
            nc.sync.wait_ge(s, c)
        for e in ("pe", "act", "dve", "pool"):
            if S.cnt[e] > 0:
                nc.sync.wait_ge(S.esem[e], S.cnt[e])
        build_nc.nops = S.nops
    return nc


def _unit_cols(u):
    A = 3072
    q = np.arange(128) + u * 128
    k = 1024 + q
    v = 2048 + q
    r = A + u * 128 + np.arange(128)
    kr = A + 1024 + u * 128 + np.arange(128)
    vr = A + 2048 + u * 128 + np.arange(128)
    lo = A + 3072 + np.arange(192)
    return np.concatenate([q, k, v, r, kr, vr, lo])


def _rw_cols(u):
    r = u * 128 + np.arange(128)
    return np.concatenate([r, 1024 + r, 2048 + r, 3072 + np.arange(192)])


def _consts(T):
    c = {}
    idx = np.arange(128)
    c["cid"] = np.eye(128, dtype=np.float32)
    c["cut"] = (idx[:, None] <= idx[None, :]).astype(np.float32)
    c["cs1"] = (idx[:, None] + 1 == idx[None, :]).astype(np.float32)
    cc = np.zeros((128, 128), np.float32); cc[127, 0] = 1.0
    c["cc128"] = cc
    cc = np.zeros((64, 64), np.float32); cc[63, 0] = 1.0
    c["cc64"] = cc
    mA = (idx[:, None] < idx[None, :]).astype(np.float32)
    mL = (idx[:, None] <= idx[None, :]).astype(np.float32)
    c["cgm"] = np.concatenate([mA, mL, mA, mL], axis=1)
    c["clow"] = (idx[None, :] < idx[:, None]).astype(np.float32)
    cm = np.ones((128, 128), np.float32); cm[64:, :64] = 0.0
    on = np.ones((128, 128), np.float32); ze = np.zeros((128, 128), np.float32)
    d0 = np.concatenate([cm, on], axis=1); d1 = np.concatenate([ze, cm], axis=1)
    c["cam"] = np.stack([np.concatenate([d0, d0], axis=1), np.concatenate([d1, d1], axis=1)]).astype(np.float32)
    inv = (np.float32(ROPE_THETA) ** (-np.arange(0, 16, 2, dtype=np.float32) / np.float32(16))).astype(np.float32)

    def tab(pos):
        ang = (pos.astype(np.float32)[:, None] * inv[None, :]).astype(np.float32)
        return np.concatenate([np.cos(ang), np.sin(ang)], axis=1).astype(np.float32)
    c["csp"] = tab(np.arange(T))
    return c, tab


_NC_CACHE = {}


def kernel(x_prompt, x_sample, cache_attn_k, cache_attn_v, state_rwkv, state_rwkv_shift,
           norm1_g, w_in, q_norm_g, k_norm_g, lambda_q1, lambda_k1, lambda_q2, lambda_k2, subln_g,
           mu_rwkv, w0, w2, a0, a2, g2, k_k, k_a, r_k, lnx_g, lnx_b,
           w_out, norm2_g, w_gate, w_up, w_down):
    f = lambda a: np.ascontiguousarray(np.asarray(a, dtype=np.float32))
    x_prompt, x_sample = f(x_prompt), f(x_sample)
    Bp, T, _ = x_prompt.shape
    Bs, Ts, _ = x_sample.shape
    PAST = cache_attn_k.shape[2]
    NB = Bs // 8
    assert Bp == 2 and Ts == 64
    dm = Dims(T, NB, PAST)
    key = (T, NB, PAST)
    if key not in _NC_CACHE:
        _NC_CACHE[key] = build_nc(dm)
    nc = _NC_CACHE[key]
    TR = T // 4
    w_in0 = f(w_in)[0]
    wu = np.stack([w_in0[:, _unit_cols(u)] for u in range(8)])
    mu0 = f(mu_rwkv)[0]; w00 = f(w0)[0]; a00 = f(a0)[0]; kk0 = f(k_k)[0]; ka0 = f(k_a)[0]
    rk0 = f(r_k)[0].reshape(-1); lg0 = f(lnx_g)[0]; lb0 = f(lnx_b)[0]
    w20 = f(w2)[0]; a20 = f(a2)[0]; g20 = f(g2)[0]
    ups, lws, lg2s = [], [], []
    for u in range(8):
        hs = slice(u * 128, (u + 1) * 128)
        ups.append(np.concatenate([mu0[_rw_cols(u)], w00[hs], a00[hs], kk0[hs], ka0[hs], rk0[hs], lg0[hs], lb0[hs]]))
        lws.append(np.concatenate([w20[:, hs], a20[:, hs]], axis=0))
        lg2s.append(g20[:, hs])
    up = np.stack(ups).astype(np.float32); lw = np.stack(lws).astype(np.float32); lg2_ = np.stack(lg2s).astype(np.float32)
    consts, tab = _consts(T)
    consts["css"] = tab(PAST + np.arange(64))
    common = dict(
        wu=wu, up=up, lw=lw, lg2=lg2_,
        g1T=np.ascontiguousarray(f(norm1_g)[0].reshape(16, 128).T), g2T=np.ascontiguousarray(f(norm2_g)[0].reshape(16, 128).T),
        qkg=np.concatenate([f(q_norm_g)[0].reshape(-1), f(k_norm_g)[0].reshape(-1)])[None, :],
        lamv=np.concatenate([f(lambda_q1)[0], f(lambda_k1)[0], f(lambda_q2)[0], f(lambda_k2)[0]])[None, :],
        subg=f(subln_g)[0][None, :],
        w_out=f(w_out)[0], w_gate=f(w_gate)[0], w_up=f(w_up)[0], w_down=f(w_down)[0], **consts)
    ck = f(cache_attn_k)[0]; cv = f(cache_attn_v)[0]; st = f(state_rwkv)[0]; sh = f(state_rwkv_shift)[0][:, 0, :]
    sh_u = np.stack([sh[:, _rw_cols(u)] for u in range(8)], axis=1)
    in_maps = []
    for c in range(8):
        b, j = c // 4, c % 4
        selm = np.zeros((128, 4), np.float32); selm[:, j] = 1.0
        m = dict(common)
        m.update(xb=x_prompt[b], xr=np.ascontiguousarray(x_prompt[b, j * TR:(j + 1) * TR]),
                 xsm=np.ascontiguousarray(x_sample[c * NB:(c + 1) * NB].reshape(NB * 64, D)),
                 ck=np.ascontiguousarray(ck[c * NB:(c + 1) * NB]), cv=np.ascontiguousarray(cv[c * NB:(c + 1) * NB]),
                 st0=np.ascontiguousarray(st[c * NB:(c + 1) * NB]), sh0=np.ascontiguousarray(sh_u[c * NB:(c + 1) * NB]),
                 wmy=np.ascontiguousarray(wu[2 * j:2 * j + 2]), upmy=np.ascontiguousarray(up[2 * j:2 * j + 2]),
                 lwmy=np.ascontiguousarray(lw[2 * j:2 * j + 2]), lg2my=np.ascontiguousarray(lg2_[2 * j:2 * j + 2]), sel=selm)
        in_maps.append({k: np.ascontiguousarray(v, dtype=np.float32) for k, v in m.items()})
    res = run_bass_kernel_spmd(nc, in_maps, core_ids=list(range(8)))
    R = res.results
    y_p = np.zeros((2, T, D), np.float32); y_s = np.zeros((Bs, 64, D), np.float32)
    k_p = np.zeros((1, 2, T, 8, 128), np.float32); v_p = np.zeros((1, 2, T, 8, 128), np.float32)
    S_p = np.zeros((1, 2, 16, 64, 64), np.float32); sh_p = np.zeros((1, 2, 1, 3264), np.float32)
    k_s = np.zeros((1, Bs, 64, 8, 128), np.float32); v_s = np.zeros((1, Bs, 64, 8, 128), np.float32)
    S_s = np.zeros((1, Bs, 16, 64, 64), np.float32); sh_s = np.zeros((1, Bs, 1, 3264), np.float32)
    for c in range(8):
        b, j = c // 4, c % 4
        r = R[c]
        y_p[b, j * TR:(j + 1) * TR] = r["yp"]
        y_s[c * NB:(c + 1) * NB] = np.asarray(r["ys"]).reshape(NB, 64, D)
        for hp in range(2):
            u = 2 * j + hp
            k_p[0, b, :, u, :] = r["kpo"][:, hp, :]
            v_p[0, b, :, u, :] = r["vpo"][:, hp, :]
            S_p[0, b, 2 * u:2 * u + 2] = r["spo"][hp]
            sh_p[0, b, 0, _rw_cols(u)] = r["shpo"][hp]
        k_s[0, c * NB:(c + 1) * NB] = r["kso"]
        v_s[0, c * NB:(c + 1) * NB] = r["vso"]
        S_s[0, c * NB:(c + 1) * NB] = np.asarray(r["sso"]).reshape(NB, 16, 64, 64)
        for u in range(8):
            sh_s[0, c * NB:(c + 1) * NB, 0, _rw_cols(u)] = np.asarray(r["shso"])[:, u, :].T if False else 0
        shso = np.asarray(r["shso"])
        for u in range(8):
            for bb in range(NB):
                sh_s[0, c * NB + bb, 0, _rw_cols(u)] = shso[bb, u]
    return (y_p, y_s, k_p, v_p, S_p, sh_p, k_s, v_s, S_s, sh_s)
```
